# Optimizing a Trainium2 kernel written in Bass

```python
import jax
import jax.numpy as jnp
from jax import lax
import numpy as np

D_MODEL = 1024
BATCH = 2
SEQ = 8192
DEPTH = 2

HEAD_DIM = 64
BRANCH_HEADS = 6
BRANCH_WIDTH = BRANCH_HEADS * HEAD_DIM
N_BRANCH = 5
MOBA_BLOCK = 256
MOBA_TOPK = 3
MOBA_QCHUNK = 64
DILATED_GROUPS = ((128, 1), (512, 4), (2048, 16))
N_DIL_GROUPS = 3
BAND = 128
FOX_BLOCK = 128
GDN_CHUNK = 64
CONV_K = 4
MEM_LEN = 256
MEM_HEADS = 4
MEM_HEAD_DIM = BRANCH_WIDTH // MEM_HEADS

A_QKV = 3 * BRANCH_WIDTH
B_QK = 2 * N_DIL_GROUPS * BRANCH_WIDTH
B_V = BRANCH_WIDTH
C_QKV = 3 * BRANCH_WIDTH
C_F = BRANCH_HEADS
D_QKV = 3 * BRANCH_WIDTH
D_BETA = BRANCH_HEADS
D_DECAY = BRANCH_HEADS
E_Q = BRANCH_WIDTH
Z_W = N_BRANCH * BRANCH_WIDTH
MERGE_W = N_BRANCH * D_MODEL
IN_SPLIT_SIZES = (A_QKV, B_QK, B_V, C_QKV, C_F, D_QKV, D_BETA, D_DECAY, E_Q, Z_W, MERGE_W)
D_IN = A_QKV + B_QK + B_V + C_QKV + C_F + D_QKV + D_BETA + D_DECAY + E_Q + Z_W + MERGE_W

DEEPNORM_ALPHA = (2 * DEPTH) ** 0.25
DEEPNORM_BETA = (8 * DEPTH) ** -0.25
NEG = -1e30
LN_EPS = 1e-5
RMS_EPS = 1e-6

kernel_name = 'hybrid_gated_parallel_moba_dilated_fox_gdn'


def _layer_norm(x, g, b):
    xf = x.astype(jnp.float32)
    mu = xf.mean(-1, keepdims=True)
    var = jnp.square(xf - mu).mean(-1, keepdims=True)
    y = (xf - mu) * lax.rsqrt(var + LN_EPS) * g.astype(jnp.float32) + b.astype(jnp.float32)
    return y.astype(x.dtype)


def _split_heads(t, n):
    B, S, _ = t.shape
    return t.reshape(B, S, n, -1).transpose(0, 2, 1, 3)


def _merge_heads(o):
    B, H, S, dh = o.shape
    return o.transpose(0, 2, 1, 3).reshape(B, S, H * dh)


def _l2norm(t):
    return t * lax.rsqrt(jnp.sum(jnp.square(t), -1, keepdims=True) + RMS_EPS)


def _moba_attention(q, k, v):
    B, H, S, dh = q.shape
    nblk = -(-S // MOBA_BLOCK)
    Sp = nblk * MOBA_BLOCK
    pad = ((0, 0), (0, 0), (0, Sp - S), (0, 0))
    q = jnp.pad(q.astype(jnp.float32) * dh ** -0.5, pad)
    k = jnp.pad(k.astype(jnp.float32), pad)
    v = jnp.pad(v.astype(jnp.float32), pad)
    kb = k.reshape(B, H, nblk, MOBA_BLOCK, dh)
    vb = v.reshape(B, H, nblk, MOBA_BLOCK, dh)
    qblk = jnp.arange(Sp) // MOBA_BLOCK
    gate = jnp.einsum('bhsd,bhnd->bhsn', q, kb.mean(axis=3))
    gate = jnp.where(jnp.arange(nblk)[None, :] < qblk[:, None], gate, NEG)
    ksel = min(MOBA_TOPK, nblk)
    _, sel = lax.top_k(gate, ksel)
    sel_ok = sel < qblk[:, None]
    bi = jnp.arange(B)[:, None, None, None]
    hi = jnp.arange(H)[None, :, None, None]
    qoff = jnp.arange(MOBA_QCHUNK)
    koff = jnp.arange(MOBA_BLOCK)

    def chunk(c):
        start = c * MOBA_QCHUNK
        own = (start // MOBA_BLOCK) * MOBA_BLOCK
        qc = lax.dynamic_slice_in_dim(q, start, MOBA_QCHUNK, axis=2)
        sc = lax.dynamic_slice_in_dim(sel, start, MOBA_QCHUNK, axis=2)
        okc = lax.dynamic_slice_in_dim(sel_ok, start, MOBA_QCHUNK, axis=2)
        kg = kb[bi, hi, sc]
        vg = vb[bi, hi, sc]
        l_sel = jnp.einsum('bhqd,bhqnkd->bhqnk', qc, kg)
        l_sel = jnp.where(okc[..., None], l_sel, NEG).reshape(B, H, MOBA_QCHUNK, ksel * MOBA_BLOCK)
        ko = lax.dynamic_slice_in_dim(k, own, MOBA_BLOCK, axis=2)
        vo = lax.dynamic_slice_in_dim(v, own, MOBA_BLOCK, axis=2)
        l_own = jnp.einsum('bhqd,bhkd->bhqk', qc, ko)
        l_own = jnp.where((start + qoff)[:, None] >= (own + koff)[None, :], l_own, NEG)
        p = jax.nn.softmax(jnp.concatenate([l_sel, l_own], axis=-1), axis=-1)
        p_sel = p[..., :ksel * MOBA_BLOCK].reshape(B, H, MOBA_QCHUNK, ksel, MOBA_BLOCK)
        p_own = p[..., ksel * MOBA_BLOCK:]
        return jnp.einsum('bhqnk,bhqnkd->bhqd', p_sel, vg) + jnp.einsum('bhqk,bhkd->bhqd', p_own, vo)

    out = lax.map(chunk, jnp.arange(Sp // MOBA_QCHUNK))
    return jnp.moveaxis(out, 0, 2).reshape(B, H, Sp, dh)[:, :, :S]


def _dilated_band(q, k, v, window, dilation):
    B, H, S, dh = q.shape
    L = S // dilation
    nb = -(-L // BAND)
    Lp = nb * BAND
    reach = window // dilation

    def streams(t):
        t = t.reshape(B, H, L, dilation, dh).transpose(0, 1, 3, 2, 4)
        t = jnp.pad(t, ((0, 0), (0, 0), (0, 0), (0, Lp - L), (0, 0)))
        return t.reshape(B, H, dilation, nb, BAND, dh)

    def with_prev(t):
        prev = jnp.pad(t, ((0, 0), (0, 0), (0, 0), (1, 0), (0, 0), (0, 0)))[:, :, :, :nb]
        return jnp.concatenate([prev, t], axis=4)

    qs = streams(q)
    kk = with_prev(streams(k))
    vv = with_prev(streams(v))
    logits = jnp.einsum('bhrnqd,bhrnkd->bhrnqk', qs, kk)
    qi = jnp.arange(BAND)[:, None]
    kj = jnp.arange(2 * BAND)[None, :]
    dist = BAND + qi - kj
    blk = jnp.arange(nb)[:, None, None]
    valid = (dist >= 0) & (dist <= reach) & ((blk - 1) * BAND + kj >= 0)
    logits = jnp.where(valid, logits, NEG)
    m = logits.max(-1)
    p = jnp.exp(logits - m[..., None])
    s = p.sum(-1)
    num = jnp.einsum('bhrnqk,bhrnkd->bhrnqd', p, vv)

    def unstream(t):
        t = t.reshape(B, H, dilation, Lp, *t.shape[5:])[:, :, :, :L]
        return jnp.swapaxes(t, 2, 3).reshape(B, H, S, *t.shape[4:])

    return unstream(num), unstream(s), unstream(m)


def _dilated_mixture(qg, kg, v):
    dh = v.shape[-1]
    qg = qg.astype(jnp.float32) * dh ** -0.5
    kg = kg.astype(jnp.float32)
    v = v.astype(jnp.float32)
    parts = [_dilated_band(qg[i], kg[i], v, w, d) for i, (w, d) in enumerate(DILATED_GROUPS)]
    m_max = jnp.max(jnp.stack([pt[2] for pt in parts]), axis=0)
    num, den = None, None
    for pn, ps, pm in parts:
        wgt = jnp.exp(pm - m_max)
        num = pn * wgt[..., None] if num is None else num + pn * wgt[..., None]
        den = ps * wgt if den is None else den + ps * wgt
    return num / den[..., None]


def _forgetting_attention(q, k, v, log_f):
    B, H, S, dh = q.shape
    q = q.astype(jnp.float32) * dh ** -0.5
    k = k.astype(jnp.float32)
    v = v.astype(jnp.float32)
    F = lax.cumsum(log_f, axis=2)
    kpos = jnp.arange(S)
    qoff = jnp.arange(FOX_BLOCK)

    def block(c):
        start = c * FOX_BLOCK
        qc = lax.dynamic_slice_in_dim(q, start, FOX_BLOCK, axis=2)
        Fc = lax.dynamic_slice_in_dim(F, start, FOX_BLOCK, axis=2)
        l = jnp.einsum('bhqd,bhkd->bhqk', qc, k) + Fc[..., :, None] - F[..., None, :]
        l = jnp.where((start + qoff)[:, None] >= kpos[None, :], l, NEG)
        p = jax.nn.softmax(l, axis=-1)
        return jnp.einsum('bhqk,bhkd->bhqd', p, v)

    out = lax.map(block, jnp.arange(S // FOX_BLOCK))
    return jnp.moveaxis(out, 0, 2).reshape(B, H, S, dh)


def _causal_depthwise_conv(x, w):
    C = x.shape[-1]
    return lax.conv_general_dilated(x, w[:, None, :], window_strides=(1,), padding=[(CONV_K - 1, 0)], dimension_numbers=('NWC', 'WIO', 'NWC'), feature_group_count=C)


def _gated_delta_rule(q, k, v, beta, g):
    B, H, S, dk = q.shape
    dv = v.shape[-1]
    C = GDN_CHUNK
    n = S // C
    ch = lambda t: t.reshape(B, H, n, C, *t.shape[3:])
    q, k, v, beta, g = ch(q), ch(k), ch(v), ch(beta), ch(g)
    gc = lax.cumsum(g, axis=3)
    incl = jnp.tril(jnp.ones((C, C), dtype=bool))
    strict = jnp.tril(jnp.ones((C, C), dtype=bool), -1)
    diff = gc[..., :, None] - gc[..., None, :]
    decay = jnp.where(incl, jnp.exp(jnp.where(incl, diff, 0.0)), 0.0)
    kb = k * beta[..., None]
    M = jnp.where(strict, jnp.einsum('bhnid,bhnjd->bhnij', kb, k) * decay, 0.0)
    eye = jnp.eye(C, dtype=q.dtype)
    T = lax.linalg.triangular_solve(M + eye, jnp.broadcast_to(eye, M.shape), left_side=True, lower=True)
    u = T @ (v * beta[..., None])
    w = T @ (kb * jnp.exp(gc)[..., None])
    qk = jnp.einsum('bhnid,bhnjd->bhnij', q, k) * decay
    q_dec = q * jnp.exp(gc)[..., None]
    k_dec = k * jnp.exp(gc[..., -1:] - gc)[..., None]
    g_last = jnp.exp(gc[..., -1])
    xs = (jnp.moveaxis(u, 2, 0), jnp.moveaxis(w, 2, 0), jnp.moveaxis(qk, 2, 0), jnp.moveaxis(q_dec, 2, 0), jnp.moveaxis(k_dec, 2, 0), jnp.moveaxis(g_last, 2, 0))

    def step(state, inp):
        u_i, w_i, qk_i, qd_i, kd_i, gl_i = inp
        v_new = u_i - w_i @ state
        o = qd_i @ state + qk_i @ v_new
        state = state * gl_i[..., None, None] + jnp.swapaxes(kd_i, -1, -2) @ v_new
        return state, o

    state0 = jnp.zeros((B, H, dk, dv), dtype=q.dtype)
    _, o = lax.scan(step, state0, xs)
    return jnp.moveaxis(o, 0, 2).reshape(B, H, S, dv)


def _memory_cross_attention(q, mem_n, w_kv):
    B, S, _ = q.shape
    Mn = mem_n.shape[1]
    k, v = jnp.split(mem_n @ w_kv, 2, axis=-1)
    q = q.reshape(B, S, MEM_HEADS, MEM_HEAD_DIM).astype(jnp.float32) * MEM_HEAD_DIM ** -0.5
    k = k.reshape(B, Mn, MEM_HEADS, MEM_HEAD_DIM).astype(jnp.float32)
    v = v.reshape(B, Mn, MEM_HEADS, MEM_HEAD_DIM).astype(jnp.float32)
    p = jax.nn.softmax(jnp.einsum('bshd,bmhd->bhsm', q, k), axis=-1)
    return jnp.einsum('bhsm,bmhd->bshd', p, v).reshape(B, S, BRANCH_WIDTH)


def _hybrid_layer(x, mem_n, w_in, b_in, conv_w, a_log, dt_bias, gdn_norm_w, w_mem_kv, w_branch, w_out, ln_g, ln_b):
    B, S, _ = x.shape
    f32 = jnp.float32
    h = x @ w_in + b_in
    offs = [int(o) for o in np.cumsum(IN_SPLIT_SIZES)[:-1]]
    (a_qkv, b_qk, b_v, c_qkv, c_f, d_qkv, d_beta, d_decay, e_q, z, merge_logits) = jnp.split(h, offs, axis=-1)

    aq, ak, av = [_split_heads(t, BRANCH_HEADS) for t in jnp.split(a_qkv, 3, axis=-1)]
    o_a = _moba_attention(aq, ak, av)

    bqk = b_qk.reshape(B, S, 2, N_DIL_GROUPS, BRANCH_HEADS, HEAD_DIM).transpose(2, 3, 0, 4, 1, 5)
    o_b = _dilated_mixture(bqk[0], bqk[1], _split_heads(b_v, BRANCH_HEADS))

    cq, ck, cv = [_split_heads(t, BRANCH_HEADS) for t in jnp.split(c_qkv, 3, axis=-1)]
    log_f = jax.nn.log_sigmoid(c_f.astype(f32)).transpose(0, 2, 1)
    o_c = _forgetting_attention(cq, ck, cv, log_f)

    dqkv = jax.nn.silu(_causal_depthwise_conv(d_qkv.astype(f32), conv_w.astype(f32)))
    dq, dk, dv = [_split_heads(t, BRANCH_HEADS) for t in jnp.split(dqkv, 3, axis=-1)]
    dq = _l2norm(dq) * HEAD_DIM ** -0.5
    dk = _l2norm(dk)
    beta = jax.nn.sigmoid(d_beta.astype(f32)).transpose(0, 2, 1)
    g = (-jnp.exp(a_log.astype(f32)) * jax.nn.softplus(d_decay.astype(f32) + dt_bias.astype(f32))).transpose(0, 2, 1)
    o_d = _gated_delta_rule(dq, dk, dv, beta, g)
    o_d = o_d * lax.rsqrt(jnp.square(o_d).mean(-1, keepdims=True) + RMS_EPS) * gdn_norm_w.astype(f32)

    o_e = _memory_cross_attention(e_q, mem_n, w_mem_kv)

    o = jnp.stack([_merge_heads(o_a), _merge_heads(o_b), _merge_heads(o_c), _merge_heads(o_d), o_e], axis=2).astype(x.dtype)
    o = o * jax.nn.silu(z.reshape(B, S, N_BRANCH, BRANCH_WIDTH))
    y = jnp.einsum('bsnc,ncd->bsnd', o, w_branch)
    y = jnp.sum(jax.nn.sigmoid(merge_logits.reshape(B, S, N_BRANCH, D_MODEL)) * y, axis=2)
    y = y @ w_out
    return _layer_norm(DEEPNORM_ALPHA * x + y, ln_g, ln_b)


def setup_inputs(seed: int = 0) -> dict:
    key = jax.random.key(seed)
    ks = jax.random.split(key, 16)
    nrm = jax.random.normal
    x = nrm(ks[0], (BATCH, SEQ, D_MODEL), jnp.float32)
    mem = nrm(ks[1], (BATCH, MEM_LEN, D_MODEL), jnp.float32)
    mem_ln_g = 1.0 + 0.02 * nrm(ks[2], (D_MODEL,), jnp.float32)
    mem_ln_b = 0.02 * nrm(ks[3], (D_MODEL,), jnp.float32)
    w_in = nrm(ks[4], (DEPTH, D_MODEL, D_IN), jnp.float32) * D_MODEL ** -0.5
    b_in = 0.02 * nrm(ks[5], (DEPTH, D_IN), jnp.float32)
    conv_w = nrm(ks[6], (DEPTH, CONV_K, D_QKV), jnp.float32) * CONV_K ** -0.5
    a_log = jnp.log(jax.random.uniform(ks[7], (DEPTH, BRANCH_HEADS), jnp.float32, 1.0, 16.0))
    dt = jnp.exp(jax.random.uniform(ks[8], (DEPTH, BRANCH_HEADS), jnp.float32, np.log(1e-3), np.log(1e-1)))
    dt_bias = dt + jnp.log(-jnp.expm1(-dt))
    gdn_norm_w = 1.0 + 0.02 * nrm(ks[9], (DEPTH, HEAD_DIM), jnp.float32)
    w_mem_kv = nrm(ks[10], (DEPTH, D_MODEL, 2 * BRANCH_WIDTH), jnp.float32) * D_MODEL ** -0.5
    w_branch = nrm(ks[11], (DEPTH, N_BRANCH, BRANCH_WIDTH, D_MODEL), jnp.float32) * (BRANCH_WIDTH ** -0.5 * DEEPNORM_BETA)
    w_out = nrm(ks[12], (DEPTH, D_MODEL, D_MODEL), jnp.float32) * (D_MODEL ** -0.5 * DEEPNORM_BETA)
    ln_g = 1.0 + 0.02 * nrm(ks[13], (DEPTH, D_MODEL), jnp.float32)
    ln_b = 0.02 * nrm(ks[14], (DEPTH, D_MODEL), jnp.float32)
    return {'x': x, 'mem': mem, 'mem_ln_g': mem_ln_g, 'mem_ln_b': mem_ln_b, 'w_in': w_in, 'b_in': b_in, 'conv_w': conv_w, 'a_log': a_log, 'dt_bias': dt_bias, 'gdn_norm_w': gdn_norm_w, 'w_mem_kv': w_mem_kv, 'w_branch': w_branch, 'w_out': w_out, 'ln_g': ln_g, 'ln_b': ln_b}


def reference(x, mem, mem_ln_g, mem_ln_b, w_in, b_in, conv_w, a_log, dt_bias, gdn_norm_w, w_mem_kv, w_branch, w_out, ln_g, ln_b):
    mem_n = _layer_norm(mem, mem_ln_g, mem_ln_b)
    for l in range(DEPTH):
        x = _hybrid_layer(x, mem_n, w_in[l], b_in[l], conv_w[l], a_log[l], dt_bias[l], gdn_norm_w[l], w_mem_kv[l], w_branch[l], w_out[l], ln_g[l], ln_b[l])
    return x
```

```python
import os
import numpy as np
from concourse.bass_utils import run_bass_kernel_spmd
import contextlib
import numpy as np
import concourse.bass as bass
import concourse.mybir as mybir

F32 = mybir.dt.float32
BF16 = mybir.dt.bfloat16
AF = mybir.ActivationFunctionType
ALU = mybir.AluOpType
AX = mybir.AxisListType

ENGS = ("pe", "act", "dve", "pool", "sp")


class _Op:
    __slots__ = ("eng", "fn", "reads", "writes", "dma", "lane", "n", "idx", "waits", "signal", "cnt")


class Prog:
    def __init__(self, nc, same_engine_raw=True):
        self.nc = nc
        self.ops = []
        self.last_w = {}
        self.readers = {}
        self.same_engine_raw = same_engine_raw
        self.es = contextlib.ExitStack()
        self.cur_es = self.es
        self._tn = 0
        self.bar_deps = ()
        self.bar_seen = set()
        self.last_on_eng = {}
        self.last_on_lane = {}

    def sbuf(self, shape, dtype, name=None):
        self._tn += 1
        return self.cur_es.enter_context(self.nc.sbuf_tensor(f"s{self._tn}_" + (name or "t"), list(shape), dtype))

    def psum(self, shape, dtype=F32, name=None):
        self._tn += 1
        return self.es.enter_context(self.nc.psum_tensor("p_" + (name or f"ps{self._tn}"), list(shape), dtype))

    @contextlib.contextmanager
    def scope(self):
        prev = self.cur_es
        with contextlib.ExitStack() as es:
            self.cur_es = es
            try:
                yield
            finally:
                self.cur_es = prev
        self.barrier()

    def barrier(self):
        self.bar_deps = tuple(self.last_on_eng.values()) + tuple(self.last_on_lane.values())
        self.bar_seen = set()

    def op(self, eng, fn, reads=(), writes=()):
        o = _Op()
        o.eng, o.fn, o.reads, o.writes = eng, fn, tuple(reads), tuple(writes)
        o.dma, o.lane, o.n = False, None, 1
        self._add(o)

    def dma(self, eng, fn, reads=(), writes=(), lane=None, n=1):
        o = _Op()
        o.eng, o.fn, o.reads, o.writes = eng, fn, tuple(reads), tuple(writes)
        o.dma, o.n = True, n
        o.lane = lane if lane is not None else (o.writes[0] if o.writes else o.reads[0])
        self._add(o)

    def _add(self, o):
        o.idx = len(self.ops)
        deps = set()
        for r in o.reads:
            w = self.last_w.get(r)
            if w is not None:
                deps.add(w)
        for r in o.writes:
            w = self.last_w.get(r)
            if w is not None:
                deps.add(w)
            for rd in self.readers.get(r, ()):
                deps.add(rd)
        if o.eng not in self.bar_seen:
            self.bar_seen.add(o.eng)
            deps.update(self.bar_deps)
        deps.discard(o.idx)
        self.last_on_eng[o.eng] = o.idx
        if o.dma:
            self.last_on_lane[o.lane] = o.idx
        o.waits = deps
        o.signal = False
        for r in o.reads:
            self.readers.setdefault(r, []).append(o.idx)
        for r in o.writes:
            self.last_w[r] = o.idx
            self.readers[r] = []
        self.ops.append(o)

    def emit(self, final_wait_eng="sp"):
        nc = self.nc
        ops = self.ops
        for o in ops:
            keep = set()
            for d in o.waits:
                p = ops[d]
                if p.dma:
                    keep.add(d)
                elif p.eng != o.eng or o.dma:
                    keep.add(d)
                else:
                    if self.same_engine_raw and o.eng != "pe":
                        if any(r in p.writes for r in o.reads):
                            keep.add(d)
            o.waits = keep
            for d in keep:
                ops[d].signal = True
        eng_cnt = {e: 0 for e in ENGS}
        lane_cnt = {}
        for o in ops:
            if o.dma:
                lane_cnt[o.lane] = lane_cnt.get(o.lane, 0) + 16 * o.n
                o.cnt = lane_cnt[o.lane]
            elif o.signal:
                eng_cnt[o.eng] += 1
                o.cnt = eng_cnt[o.eng]
            else:
                o.cnt = None
        lanes = list(lane_cnt.keys())
        es = self.es
        eng_sem = {e: es.enter_context(nc.semaphore(f"s_{e}")) for e in ENGS}
        lane_sem = {l: es.enter_context(nc.semaphore(f"l_{i}")) for i, l in enumerate(lanes)}
        self.n_sems = len(lanes) + len(ENGS)
        per_eng = {e: [] for e in ENGS}
        for o in ops:
            per_eng[o.eng].append(o)

        def run(e, engine):
            seen = {}
            for o in per_eng[e]:
                need = {}
                for d in o.waits:
                    p = ops[d]
                    if p.dma:
                        key = ("l", p.lane)
                        sem = lane_sem[p.lane]
                    else:
                        key = ("e", p.eng)
                        sem = eng_sem[p.eng]
                    if need.get(key, (None, -1))[1] < p.cnt:
                        need[key] = (sem, p.cnt)
                for key, (sem, val) in need.items():
                    if seen.get(key, -1) < val:
                        engine.wait_ge(sem, val)
                        seen[key] = val
                if o.dma:
                    instrs = o.fn(engine)
                    assert len(instrs) == o.n, (len(instrs), o.n)
                    for ins in instrs:
                        ins.then_inc(lane_sem[o.lane], 16)
                else:
                    ins = o.fn(engine)
                    if o.signal:
                        ins.then_inc(eng_sem[o.eng], 1)

        with nc.Block() as block:
            @block.tensor
            def _(eng):
                run("pe", eng)

            @block.scalar
            def _(eng):
                run("act", eng)

            @block.vector
            def _(eng):
                run("dve", eng)

            @block.gpsimd
            def _(eng):
                run("pool", eng)

            @block.sync
            def _(eng):
                run("sp", eng)
        self.es.close()


import types
import numpy as np

S = 8192
NCHK = 16
NEG = -30000.0


class Ctx:
    def __init__(self, nc):
        self.nc = nc
        self.P = Prog(nc)
        P = self.P
        self.pp = [P.psum([128, 512], F32, name=f"pp{i}") for i in range(2)]
        self.ps = [P.psum([128, 512], F32, name=f"pss{i}") for i in range(3)]
        self.po = [P.psum([128, 512], F32, name=f"po{i}") for i in range(2)]
        self.pm = P.psum([128, 512], F32, name="pm")
        self.ipp = 0
        self.ips = 0
        self.ipo = 0
        self.loc = types.SimpleNamespace()
        self.out_res = []
        self.ones_f = P.sbuf([128, 128], F32, name="ones_f")
        self.ones_b = P.sbuf([128, 128], BF16, name="ones_b")
        P.op("pool", lambda e: e.memset(self.ones_f[:], 1.0), writes=["ones_f"])
        P.op("pool", lambda e: e.memset(self.ones_b[:], 1.0), writes=["ones_b"])

    def new_scope(self):
        self.loc = types.SimpleNamespace()

    def dram_in(self, name, shape, dtype=F32):
        return self.nc.dram_tensor(name, list(shape), dtype, kind="ExternalInput").ap()

    def dram_out(self, name, shape, dtype=F32):
        return self.nc.dram_tensor(name, list(shape), dtype, kind="ExternalOutput").ap()

    def load_const(self, dram_ap, shape, name, dtype=F32, cast=None):
        P = self.P
        t = P.sbuf(shape, dtype, name=name)
        P.dma("sp", lambda e: [e.dma_start(out=t[:], in_=dram_ap)], writes=[name])
        if cast is not None:
            tb = P.sbuf(shape, cast, name=name + "_b")
            P.op("pool", lambda e: e.tensor_copy(out=tb[:], in_=t[:]), reads=[name], writes=[name + "_b"])
            return t, tb
        return t

    def next_pp(self):
        i = self.ipp
        self.ipp = (i + 1) % 2
        return self.pp[i], ("pp", i)

    def next_ps(self):
        i = self.ips
        self.ips = (i + 1) % 3
        return self.ps[i], ("ps", i)

    def next_po(self):
        i = self.ipo
        self.ipo = (i + 1) % 2
        return self.po[i], ("po", i)


def load_weights(C, w_d, ncols, name, wb=None):
    P = C.P
    if wb is None:
        wb = P.sbuf([128, 8, ncols], BF16, name=name)
    if not hasattr(C.loc, "lw_stg"):
        C.loc.lw_stg = [P.sbuf([128, 8, 128], F32, name=f"lw_stg{i}") for i in range(2)]
        C.loc.lw_k = 0
    stg = C.loc.lw_stg
    k = C.loc.lw_k
    for c0 in range(0, ncols, 128):
        n = min(128, ncols - c0)
        st = stg[k % 2]
        rs = ("lw_stg", k % 2)
        P.dma("act", lambda e, st=st, c0=c0, n=n: [e.dma_start(out=st[:, :, :], in_=w_d[c0 // 128])], writes=[rs])
        P.op("pool", lambda e, st=st, c0=c0, n=n: e.tensor_copy(out=wb[:, :, c0:c0 + n], in_=st[:, :, 0:n]), reads=[rs], writes=[name])
        k += 1
    C.loc.lw_k = k
    return wb


class _Shift:
    def __init__(self, t, off):
        self.t, self.off = t, off

    def __getitem__(self, key):
        ps, rest = key[0], key[1:]
        return self.t[(slice(ps.start + self.off, ps.stop + self.off),) + tuple(rest)]


def convert_xT(C, xT_d, xTb_d):
    P = C.P
    xs = [P.sbuf([128, 8, 512], F32, name=f"cv_xs{i}") for i in range(2)]
    xb = [P.sbuf([128, 8, 512], BF16, name=f"cv_xb{i}") for i in range(2)]
    for ci in range(NCHK):
        t0, b = ci * 512, ci % 2
        P.dma("sp", lambda e, b=b, t0=t0: [e.dma_start(out=xs[b][:, 0:4, :], in_=xT_d[:, 0:4, t0:t0 + 512]),
                                          e.dma_start(out=xs[b][:, 4:8, :], in_=xT_d[:, 4:8, t0:t0 + 512])], writes=[("cvxs", b)], n=2)
        P.op("pool", lambda e, b=b: e.tensor_copy(out=xb[b][:, 0:3, :], in_=xs[b][:, 0:3, :]), reads=[("cvxs", b)], writes=[("cvxb", b, 0)])
        P.op("dve", lambda e, b=b: e.tensor_copy(out=xb[b][:, 3:6, :], in_=xs[b][:, 3:6, :]), reads=[("cvxs", b)], writes=[("cvxb", b, 1)])
        P.op("act", lambda e, b=b: e.activation(out=xb[b][:, 6:8, :], in_=xs[b][:, 6:8, :], func=AF.Copy), reads=[("cvxs", b)], writes=[("cvxb", b, 2)])
        P.dma("act", lambda e, b=b, t0=t0: [e.dma_start(out=xTb_d[:, :, t0:t0 + 512], in_=xb[b][:, :, :])], reads=[("cvxb", b, k) for k in range(3)],
              writes=[("xTb", ci)], lane=("cvst", b))


def proj_pass(C, xT_d, wb, wname, specs, tag, hook=None, xTb_d=None):
    P = C.P
    use_b = xTb_d is not None and hook is None and all(len(sp) == 3 for sp in specs)
    if not hasattr(C.loc, "xb_bufs"):
        C.loc.xb_bufs = [P.sbuf([128, 8, 512], BF16, name=f"xb_{i}") for i in range(3 if use_b else 2)]
        if not use_b:
            C.loc.xs_bufs = [P.sbuf([128, 8, 512], F32, name=f"xs_{i}") for i in range(2)]
    xb = C.loc.xb_bufs
    xs = None if use_b else C.loc.xs_bufs
    tag = "x"
    if use_b:
        for ci in range(NCHK):
            t0 = ci * 512
            b = ci % 3
            rxb = ("xb", tag, b)
            P.dma("sp", lambda e, b=b, t0=t0: [e.dma_start(out=xb[b][:, 0:4, :], in_=xTb_d[:, 0:4, t0:t0 + 512]),
                                              e.dma_start(out=xb[b][:, 4:8, :], in_=xTb_d[:, 4:8, t0:t0 + 512])], writes=[(rxb, 0), (rxb, 1)], n=2,
                  lane=("xbl", b))
            for spec in specs:
                col0, M, evac = spec
                pt, pres = C.next_pp()
                for kc in range(8):
                    P.op("pe", lambda e, pt=pt, kc=kc, col0=col0, M=M, b=b: e.matmul(
                        pt[0:M, :], lhsT=wb[:, kc, col0:col0 + M], rhs=xb[b][:, kc, :], start=(kc == 0), stop=(kc == 7)),
                        reads=[wname, (rxb, 0), (rxb, 1)], writes=[pres])
                if callable(evac):
                    evac(pt, pres, ci, t0)
                else:
                    for off, fn in evac:
                        fn(pt if off == 0 else _Shift(pt, off), pres, ci, t0)
        return
    for ci in range(NCHK):
        t0 = ci * 512
        b = ci % 2
        rxs = ("xs", tag, b)
        rxb = ("xb", tag, b)
        P.dma("sp", lambda e, b=b, t0=t0: [e.dma_start(out=xs[b][:, 0:4, :], in_=xT_d[:, 0:4, t0:t0 + 512]),
                                          e.dma_start(out=xs[b][:, 4:8, :], in_=xT_d[:, 4:8, t0:t0 + 512])],
              writes=[rxs], n=2)
        P.op("pool", lambda e, b=b: e.tensor_copy(out=xb[b][:, 0:4, :], in_=xs[b][:, 0:4, :]), reads=[rxs], writes=[(rxb, 0)])
        P.op("dve", lambda e, b=b: e.tensor_copy(out=xb[b][:, 4:8, :], in_=xs[b][:, 4:8, :]), reads=[rxs], writes=[(rxb, 1)])
        if hook is not None:
            hook(ci, xs[b], rxs)
        for spec in specs:
            col0, M, evac = spec[0], spec[1], spec[2]
            pt, pres = C.next_pp()
            if len(spec) > 3:
                w32, w32_res = spec[3], spec[4]
                for kc in range(8):
                    P.op("pe", lambda e, pt=pt, kc=kc, M=M, b=b, w32=w32: e.matmul(
                        pt[0:M, :], lhsT=w32[:, kc, 0:M], rhs=xs[b][:, kc, :], start=(kc == 0), stop=(kc == 7)),
                        reads=[w32_res, rxs], writes=[pres])
            else:
                for kc in range(8):
                    P.op("pe", lambda e, pt=pt, kc=kc, col0=col0, M=M, b=b: e.matmul(
                        pt[0:M, :], lhsT=wb[:, kc, col0:col0 + M], rhs=xb[b][:, kc, :], start=(kc == 0), stop=(kc == 7)),
                        reads=[wname, (rxb, 0), (rxb, 1)], writes=[pres])
            if callable(evac):
                evac(pt, pres, ci, t0)
            else:
                for off, fn in evac:
                    fn(pt if off == 0 else _Shift(pt, off), pres, ci, t0)


def make_vprime(C, vT, vT_res, vp, vp_res, nblk, colsel, dh=64):
    P = C.P
    P.op("pool", lambda e: e.memset(vp[:, :, dh:dh + 1], 1.0), writes=[(vp_res, "ones")])
    pmb = C.pm_b
    G = 8
    for g0 in range(0, nblk, G):
        for k in range(G):
            blk = g0 + k
            P.op("pe", lambda e, blk=blk, k=k: e.transpose(pmb[:, k * dh:(k + 1) * dh], vT[0:dh, colsel(blk)], C.ident_b[0:dh, 0:dh]),
                 reads=vT_res(blk) + ["ident_b"], writes=["pm"])
        P.op("act", lambda e, g0=g0: e.activation(out=vp[:, g0:g0 + G, 0:dh], in_=pmb[:, 0:G * dh].rearrange("p (g d) -> p g d", g=G), func=AF.Copy),
             reads=["pm"], writes=[(vp_res, g0)])


def attn_block(C, S_mm, nq_lo, nq_hi, bias_ap, bias_res, masks, PV_mm, po_res, extra_reads=()):
    raise NotImplementedError


def causal_attention(C, qT, q_res, kT, k_res, KC, vp, vp_res, negc, bias_fn, out_cb, tag, LA=2):
    P = C.P
    NB = LA + 1
    pts = [P.sbuf([128, 512], BF16, name=f"pt_{tag}{i}") for i in range(NB)]
    tmps = [P.sbuf([128, 128], F32, name=f"tmp_{tag}{i}") for i in range(2)]
    st = {"itmp": 0}
    tasks = []
    pos = {}
    for I in range(NCHK):
        pos[I] = C.next_po()
        for j in range(4 * I + 4):
            tasks.append((I, j))

    def stage1(t, I, j):
        r = j - 4 * I
        lo = 128 * r if r > 0 else 0
        pst, ps_res = C.ps[t % 3], ("ps", t % 3)
        pt = pts[t % NB]
        pt_res = ("pt", tag, t % NB)
        q0 = 512 * I
        P.op("pe", lambda e: e.matmul(pst[:, lo:512], lhsT=kT[0:KC, 128 * j:128 * j + 128], rhs=qT[0:KC, q0 + lo:q0 + 512], start=True, stop=True),
             reads=q_res(I) + k_res(j), writes=[ps_res])
        b = bias_fn(I, j) if bias_fn is not None else None
        bkw = {} if b is None else {"bias": b[0]}
        brd = [] if b is None else [b[1]]
        if r >= 0:
            tmp = tmps[st["itmp"] % 2]
            tmp_res = ("tmp", tag, st["itmp"] % 2)
            st["itmp"] += 1
            P.op("dve", lambda e: e.tensor_tensor(out=tmp[:], in0=pst[:, lo:lo + 128], in1=negc[:], op=ALU.add), reads=[ps_res, "negc"], writes=[tmp_res])
            P.op("act", lambda e: e.activation(out=pt[:, lo:lo + 128], in_=tmp[:], func=AF.Exp, **bkw), reads=[tmp_res] + brd, writes=[(pt_res, "d")])
            if lo + 128 < 512:
                P.op("act", lambda e: e.activation(out=pt[:, lo + 128:512], in_=pst[:, lo + 128:512], func=AF.Exp, **bkw), reads=[ps_res] + brd, writes=[(pt_res, "o")])
        else:
            P.op("act", lambda e: e.activation(out=pt[:, :], in_=pst[:, :], func=AF.Exp, **bkw), reads=[ps_res] + brd, writes=[(pt_res, "d"), (pt_res, "o")])

    def stage2(t, I, j):
        r = j - 4 * I
        lo = 128 * r if r > 0 else 0
        nkb = 4 * I + 4
        po, po_res = pos[I]
        pt = pts[t % NB]
        pt_res = ("pt", tag, t % NB)
        P.op("pe", lambda e: e.matmul(po[0:65, lo:512], lhsT=vp[:, j, 0:65], rhs=pt[:, lo:512], start=(j == 0), stop=(j == nkb - 1), skip_group_check=True),
             reads=vp_res(j) + [(pt_res, "d"), (pt_res, "o")], writes=[po_res])
        if j == nkb - 1:
            out_cb(I, po, po_res)

    n = len(tasks)
    for t in range(n + LA):
        if t < n:
            stage1(t, *tasks[t])
        if t - LA >= 0:
            stage2(t - LA, *tasks[t - LA])


def norm_store(C, onum_pool, out_d, tag):
    P = C.P
    st = {"i": 0}

    def cb(I, po, po_res):
        i = st["i"] % 2
        st["i"] += 1
        on = onum_pool[i]
        on_res = ("onum", tag, i)
        P.op("act", lambda e: e.activation(out=on[0:65, :], in_=po[0:65, :], func=AF.Copy), reads=[po_res], writes=[on_res])
        P.op("dve", lambda e: e.reciprocal(out=on[64:65, :], in_=on[64:65, :]), reads=[on_res], writes=[on_res])
        P.op("pe", lambda e: e.matmul(C.pm[0:64, :], lhsT=C.ones_f[64:65, 0:64], rhs=on[64:65, :], start=True, stop=True),
             reads=["ones_f", on_res], writes=["pm"])
        P.op("dve", lambda e: e.tensor_tensor(out=on[0:64, :], in0=on[0:64, :], in1=C.pm[0:64, :], op=ALU.mult),
             reads=[on_res, "pm"], writes=[on_res])
        P.dma("sp", lambda e: [e.dma_start(out=out_d[:, 512 * I:512 * I + 512], in_=on[0:64, :])], reads=[on_res], writes=[("out", tag, I)],
              lane=("onum_st", tag, i))
        C.out_res.append(("out", tag, I))
    return cb


def build_fox(nc=None, C=None, io=None):
    standalone = C is None
    if standalone:
        C = Ctx(nc)
        io = {"xT": C.dram_in("xT", [128, 8, S]), "w": C.dram_in("w", [128, 8, 193]), "bias": C.dram_in("bias", [64, 4]),
              "out": C.dram_out("oT", [64, S])}
        C.negc = C.load_const(C.dram_in("negc", [128, 128]), [128, 128], "negc")
        C.identf, C.ident_b = C.load_const(C.dram_in("ident", [128, 128]), [128, 128], "ident", cast=BF16)
        C.pm_b = C.pm[:].bitcast(BF16)
    P = C.P
    xT_d, w_d, b_d, out_d = io["xT"], io["w"], io["bias"], io["out"]
    negc = C.negc
    bia = C.load_const(b_d, [64, 4], "bia")
    bq8 = P.sbuf([64, 1], F32, name="bq8")
    nbcf = P.sbuf([1, 1], F32, name="nbcf")
    P.op("dve", lambda e: e.tensor_scalar(out=bq8[:], in0=bia[:, 0:1], scalar1=0.125, scalar2=None, op0=ALU.mult), reads=["bia"], writes=["bq8"])
    P.op("dve", lambda e: e.tensor_scalar(out=nbcf[:], in0=bia[0:1, 3:4], scalar1=-1.0, scalar2=None, op0=ALU.mult), reads=["bia"], writes=["nbcf"])
    wb = load_weights(C, w_d, 193, "w")

    qC = P.sbuf([65, S], BF16, name="qC")
    kC = P.sbuf([65, S], BF16, name="kC")
    vT = P.sbuf([64, S], BF16, name="vT")
    Frow = P.sbuf([1, S], F32, name="Frow")
    erow = P.sbuf([1, 512], F32, name="erow")
    vp = P.sbuf([128, 64, 65], BF16, name="vp")

    def ev_q(pt, pres, ci, t0):
        P.op("act", lambda e: e.activation(out=qC[0:64, t0:t0 + 512], in_=pt[0:64, :], func=AF.Identity, bias=bq8[:, 0:1], scale=0.125),
             reads=[pres, "bq8"], writes=[("qC", ci)])

    def ev_k(pt, pres, ci, t0):
        P.op("act", lambda e: e.activation(out=kC[0:64, t0:t0 + 512], in_=pt[0:64, :], func=AF.Identity, bias=bia[:, 1:2], scale=1.0),
             reads=[pres, "bia"], writes=[("kC", ci)])

    def ev_v(pt, pres, ci, t0):
        P.op("act", lambda e: e.activation(out=vT[0:64, t0:t0 + 512], in_=pt[0:64, :], func=AF.Identity, bias=bia[:, 2:3], scale=1.0),
             reads=[pres, "bia"], writes=[("vT", ci)])

    def ev_cf(pt, pres, ci, t0):
        P.op("act", lambda e: e.activation(out=erow[:], in_=pt[0:1, :], func=AF.Exp, bias=nbcf[:, 0:1], scale=-1.0),
             reads=[pres, "nbcf"], writes=["erow"])
        P.op("act", lambda e: e.activation(out=erow[:], in_=erow[:], func=AF.Ln, bias=1.0, scale=1.0), reads=["erow"], writes=["erow"])
        init = 0.0 if ci == 0 else Frow[0:1, t0 - 1:t0]
        P.op("dve", lambda e: e.tensor_tensor_scan(out=Frow[0:1, t0:t0 + 512], data0=ones_row[0:1, :], data1=erow[:],
                                                  initial=init, op0=ALU.mult, op1=ALU.subtract),
             reads=["erow", "ones_row", "Frow"], writes=["Frow"])

    ones_row = P.sbuf([1, 512], F32, name="ones_row")
    P.op("pool", lambda e: e.memset(ones_row[:], 1.0), writes=["ones_row"])
    specs = [(0, 128, [(0, ev_q), (64, ev_k)]), (128, 65, [(0, ev_v), (64, ev_cf)])]
    proj_pass(C, xT_d, wb, "w", specs, "c", xTb_d=io.get("xTb"))
    P.op("pool", lambda e: e.memset(kC[64:65, :], 1.0), writes=["kC64"])
    for I in range(NCHK):
        P.op("dve", lambda e, I=I: e.tensor_scalar(out=qC[64:65, 512 * I:512 * I + 512], in0=Frow[0:1, 512 * I:512 * I + 512],
                                                   scalar1=Frow[0:1, 512 * I:512 * I + 1], scalar2=None, op0=ALU.subtract),
             reads=["Frow"], writes=["qC64"])
    for j in range(64):
        P.op("pe", lambda e, j=j: e.matmul(C.pm[0:128, j:j + 1], lhsT=Frow[0:1, 128 * j:128 * j + 128], rhs=C.ones_f[0:1, 0:1], start=True, stop=True),
             reads=["Frow", "ones_f"], writes=["pm"])
    P.op("pe", lambda e: e.matmul(C.pm[0:128, 64:80], lhsT=C.ones_f[0:1, 0:128], rhs=Frow[0:1, 0:S:512], start=True, stop=True),
         reads=["Frow", "ones_f"], writes=["pm"])
    Ftr = P.sbuf([128, 80], F32, name="Ftr")
    biasC = P.sbuf([128, 16, 64], F32, name="biasC")
    P.op("dve", lambda e: e.tensor_copy(out=Ftr[:], in_=C.pm[:, 0:80]), reads=["pm"], writes=["Ftr"])
    for I in range(NCHK):
        P.op("dve", lambda e, I=I: e.tensor_scalar(out=biasC[:, I, :], in0=Ftr[:, 0:64], scalar1=-1.0, scalar2=Ftr[:, 64 + I:65 + I], op0=ALU.mult, op1=ALU.add),
             reads=["Ftr"], writes=["biasC"])
    make_vprime(C, vT, lambda blk: [("vT", blk // 4)], vp, "vp", 64, lambda blk: slice(128 * blk, 128 * blk + 128))
    onum = [P.sbuf([65, 512], F32, name=f"onum{i}") for i in range(2)]
    cb = norm_store(C, onum, out_d, "c")
    causal_attention(C, qC, lambda I: [("qC", I), "qC64"], kC, lambda j: [("kC", j // 4), "kC64"], 65, vp,
                     lambda j: [("vp", (j // 8) * 8), ("vp", "ones")], negc,
                     lambda I, j: (biasC[:, I, j:j + 1], "biasC"), cb, "c")
    if standalone:
        P.op("sp", lambda e: None, reads=C.out_res)
        P.emit()
    return nc


def build_moba(nc=None, C=None, io=None):
    standalone = C is None
    if standalone:
        C = Ctx(nc)
        io = {"xT": C.dram_in("xT", [128, 8, S]), "w": C.dram_in("w", [2, 128, 8, 128]), "bias": C.dram_in("bias", [64, 4]),
              "ind": C.dram_in("ind", [32, S]), "blkc": C.dram_in("blkc", [128, 3, 32, 32]), "out": C.dram_out("oT", [64, S])}
        C.negc = C.load_const(C.dram_in("negc", [128, 128]), [128, 128], "negc")
        C.identf, C.ident_b = C.load_const(C.dram_in("ident", [128, 128]), [128, 128], "ident", cast=BF16)
        C.pm_b = C.pm[:].bitcast(BF16)
    P = C.P
    xT_d, w_d, b_d, ind_d, blk_d, out_d = io["xT"], io["w"], io["bias"], io["ind"], io["blkc"], io["out"]
    negc = C.negc
    bia = C.load_const(b_d, [64, 4], "bia")
    blkc = C.load_const(blk_d, [128, 3, 32, 32], "blkc")
    bq8 = P.sbuf([64, 1], F32, name="bq8")
    P.op("dve", lambda e: e.tensor_scalar(out=bq8[:], in0=bia[:, 0:1], scalar1=0.125, scalar2=None, op0=ALU.mult), reads=["bia"], writes=["bq8"])
    wb = load_weights(C, w_d, 192, "w")

    qs = P.sbuf([96, S], BF16, name="qs")
    ke = P.sbuf([96, S], BF16, name="ke")
    vT = P.sbuf([64, S], BF16, name="vT")
    vp = P.sbuf([128, 64, 65], BF16, name="vp")
    ksum = P.sbuf([64, 32], F32, name="ksum")
    kmT = P.sbuf([64, 32], BF16, name="kmT")
    indst = P.sbuf([32, 2048], F32, name="indst")
    for i4 in range(4):
        P.dma("act", lambda e, i4=i4: [e.dma_start(out=indst[:, :], in_=ind_d[:, 2048 * i4:2048 * i4 + 2048])], writes=["indst"])
        P.op("pool", lambda e, i4=i4: e.tensor_copy(out=ke[64:96, 2048 * i4:2048 * i4 + 2048], in_=indst[:]), reads=["indst"], writes=["ke_ind"])
    q32 = P.sbuf([64, S], F32, name="q32")
    w32q = P.sbuf([128, 8, 64], F32, name="w32q")
    w32k = P.sbuf([128, 8, 64], F32, name="w32k")
    P.dma("act", lambda e: [e.dma_start(out=w32q[:], in_=w_d[0][:, :, 0:64])], writes=["w32q"])
    P.dma("act", lambda e: [e.dma_start(out=w32k[:], in_=w_d[0][:, :, 64:128])], writes=["w32k"])
    xsum = P.sbuf([128, 8, 32], F32, name="xsum")
    km32 = P.sbuf([64, 32], F32, name="km32")

    def ev_q32(pt, pres, ci, t0):
        P.op("act", lambda e: e.activation(out=q32[0:64, t0:t0 + 512], in_=pt[0:64, :], func=AF.Identity, bias=bq8[:, 0:1], scale=0.125),
             reads=[pres, "bq8"], writes=[("q32", ci)])

    def xhook(ci, xs_t, rxs):
        P.op("dve", lambda e: e.tensor_reduce(out=xsum[:, :, 2 * ci:2 * ci + 2], in_=xs_t[:, :, :].rearrange("p k (a b) -> p k a b", a=2), axis=AX.X, op=ALU.add),
             reads=[rxs], writes=["xsum"])

    def ev_q(pt, pres, ci, t0):
        P.op("act", lambda e: e.activation(out=qs[0:64, t0:t0 + 512], in_=pt[0:64, :], func=AF.Identity, bias=bq8[:, 0:1], scale=0.125),
             reads=[pres, "bq8"], writes=[("qs", ci)])

    def ev_k(pt, pres, ci, t0):
        P.op("act", lambda e: e.activation(out=ke[0:64, t0:t0 + 512], in_=pt[0:64, :], func=AF.Identity, bias=bia[:, 1:2], scale=1.0),
             reads=[pres, "bia"], writes=[("ke", ci)])

    def ev_v(pt, pres, ci, t0):
        P.op("act", lambda e: e.activation(out=vT[0:64, t0:t0 + 512], in_=pt[0:64, :], func=AF.Identity, bias=bia[:, 2:3], scale=1.0),
             reads=[pres, "bia"], writes=[("vT", ci)])

    specs = [(0, 128, [(0, ev_q), (64, ev_k)]), (128, 64, ev_v), (0, 64, ev_q32, w32q, "w32q")]
    proj_pass(C, xT_d, wb, "w", specs, "a", hook=xhook)
    P.op("dve", lambda e: e.tensor_scalar(out=xsum[:], in0=xsum[:], scalar1=1.0 / 256.0, scalar2=None, op0=ALU.mult), reads=["xsum"], writes=["xsum"])
    for kc in range(8):
        P.op("pe", lambda e, kc=kc: e.matmul(C.pm[0:64, 0:32], lhsT=w32k[:, kc, 0:64], rhs=xsum[:, kc, 0:32], start=(kc == 0), stop=(kc == 7)),
             reads=["w32k", "xsum"], writes=["pm"])
    P.op("dve", lambda e: e.tensor_scalar(out=km32[:], in0=C.pm[0:64, 0:32], scalar1=bia[:, 1:2], scalar2=None, op0=ALU.add), reads=["pm", "bia"], writes=["km32"])
    NL = 4
    lane_banks = [C.pp[0], C.pp[1], C.po[0], C.po[1]]
    lane_res = [("pp", 0), ("pp", 1), ("po", 0), ("po", 1)]
    gms = [P.sbuf([128, 32], F32, name=f"gm{i}") for i in range(NL)]
    top8s = [P.sbuf([128, 8], F32, name=f"top8{i}") for i in range(NL)]
    selms = [P.sbuf([128, 32], F32, name=f"selm{i}") for i in range(NL)]
    stages = [P.sbuf([128, 96], BF16, name=f"stage{i}") for i in range(NL)]
    for i in range(NL):
        P.op("pool", lambda e, i=i: e.memset(stages[i][:], 0.0), writes=[("stage", i)])

    def gate_lane(ln):
        bank, bres = lane_banks[ln], lane_res[ln]
        bank_b = bank[:].bitcast(BF16)
        gm, top8, selm, stage = gms[ln], top8s[ln], selms[ln], stages[ln]
        rg, rt, rs, rst = ("gm", ln), ("top8", ln), ("selm", ln), ("stage", ln)

        def tile(ti):
            qb = ti // 2
            c0 = 128 * ti
            P.op("pe", lambda e: e.matmul(bank[0:128, 0:32], lhsT=q32[0:64, c0:c0 + 128], rhs=km32[0:64, 0:32], start=True, stop=True),
                 reads=[("q32", ti // 4), "km32"], writes=[bres])
            yield
            P.op("dve", lambda e: e.tensor_tensor(out=gm[:], in0=bank[0:128, 0:32], in1=blkc[:, 0, qb, :], op=ALU.add), reads=["blkc"], writes=[rg, bres])
            yield
            P.op("dve", lambda e: e.max(out=top8[:], in_=gm[:]), reads=[rg], writes=[rt])
            yield
            P.op("dve", lambda e: e.tensor_scalar(out=selm[:], in0=gm[:], scalar1=top8[:, 2:3], scalar2=None, op0=ALU.is_ge), reads=[rg, rt], writes=[rs])
            yield
            P.op("dve", lambda e: e.tensor_tensor(out=selm[:], in0=selm[:], in1=blkc[:, 1, qb, :], op=ALU.mult), reads=[rs, "blkc"], writes=[rs])
            yield
            P.op("dve", lambda e: e.tensor_tensor(out=selm[:], in0=selm[:], in1=blkc[:, 2, qb, :], op=ALU.add), reads=[rs, "blkc"], writes=[rs])
            yield
            P.op("dve", lambda e: e.tensor_scalar(out=stage[:, 64:96], in0=selm[:], scalar1=-1.0, scalar2=30000.0, op0=ALU.add, op1=ALU.mult), reads=[rs], writes=[rst])
            yield
            P.op("pe", lambda e: e.transpose(bank_b[0:96, 0:128], stage[:, 0:96], C.ident_b[:, :]), reads=[rst, "ident_b"], writes=[bres])
            yield
            P.op("act", lambda e: e.activation(out=qs[64:96, c0:c0 + 128], in_=bank_b[64:96, 0:128], func=AF.Copy), reads=[], writes=[("qsel", ti), bres])
            yield

        for ti in range(ln, 64, NL):
            yield from tile(ti)

    alive = [gate_lane(ln) for ln in range(NL)]
    while alive:
        nxt = []
        for gen in alive:
            try:
                next(gen)
                nxt.append(gen)
            except StopIteration:
                pass
        alive = nxt

    make_vprime(C, vT, lambda blk: [("vT", blk // 4)], vp, "vp", 64, lambda blk: slice(128 * blk, 128 * blk + 128))
    onum = [P.sbuf([65, 512], F32, name=f"onum{i}") for i in range(2)]
    cb = norm_store(C, onum, out_d, "a")
    causal_attention(C, qs, lambda I: [("qs", I)] + [("qsel", 4 * I + k) for k in range(4)], ke, lambda j: [("ke", j // 4), "ke_ind"], 96, vp,
                     lambda j: [("vp", (j // 8) * 8), ("vp", "ones")], negc, None, cb, "a")
    if standalone:
        P.op("sp", lambda e: None, reads=C.out_res)
        P.emit()
    return nc


DILS = (1, 4, 16)


def build_dil(nc=None, C=None, io=None):
    standalone = C is None
    if standalone:
        C = Ctx(nc)
        io = {"xT": C.dram_in("xT", [128, 8, S]), "w": C.dram_in("w", [128, 8, 448]), "bias": C.dram_in("bias", [64, 8]),
              "out": C.dram_out("oT", [64, S])}
        C.band = C.load_const(C.dram_in("band", [128, 256]), [128, 256], "band")
        C.identf, C.ident_b = C.load_const(C.dram_in("ident", [128, 128]), [128, 128], "ident", cast=BF16)
        C.pm_b = C.pm[:].bitcast(BF16)
    P = C.P
    xT_d, w_d, b_d, out_d = io["xT"], io["w"], io["bias"], io["out"]
    band = C.band
    bia = C.load_const(b_d, [64, 8], "bia")
    bq8 = P.sbuf([64, 8], F32, name="bq8")
    P.op("dve", lambda e: e.tensor_scalar(out=bq8[:], in0=bia[:], scalar1=0.125, scalar2=None, op0=ALU.mult), reads=["bia"], writes=["bq8"])
    wb = load_weights(C, w_d, 448, "w")

    qTs = [P.sbuf([64, S], BF16, name=f"qT{g}") for g in range(3)]
    kTs = [P.sbuf([64, S], BF16, name=f"kT{g}") for g in range(3)]
    vT = P.sbuf([64, S], BF16, name="vT")
    vp = P.sbuf([128, 64, 65], BF16, name="vp")
    acc = P.sbuf([65, S], F32, name="acc")
    pts = [P.sbuf([128, 256], BF16, name=f"ptb{i}") for i in range(3)]
    tmps = [P.sbuf([128, 256], F32, name=f"tmpb{i}") for i in range(3)]
    cnt = {"pt": 0}

    def mk_ev(g):
        def ev_q(pt, pres, ci, t0):
            P.op("act", lambda e: e.activation(out=qTs[g][0:64, t0:t0 + 512], in_=pt[0:64, :], func=AF.Identity, bias=bq8[:, 2 * g:2 * g + 1], scale=0.125),
                 reads=[pres, "bq8"], writes=[("qT", g, ci)])

        def ev_k(pt, pres, ci, t0):
            P.op("act", lambda e: e.activation(out=kTs[g][0:64, t0:t0 + 512], in_=pt[0:64, :], func=AF.Identity, bias=bia[:, 2 * g + 1:2 * g + 2], scale=1.0),
                 reads=[pres, "bia"], writes=[("kT", g, ci)])
        return ev_q, ev_k

    def ev_v(pt, pres, ci, t0):
        P.op("act", lambda e: e.activation(out=vT[0:64, t0:t0 + 512], in_=pt[0:64, :], func=AF.Identity, bias=bia[:, 6:7], scale=1.0),
             reads=[pres, "bia"], writes=[("vT", ci)])

    specs = []
    for g in range(3):
        eq_, ek_ = mk_ev(g)
        specs.append((128 * g, 128, [(0, eq_), (64, ek_)]))
    specs.append((384, 64, ev_v))
    proj_pass(C, xT_d, wb, "w", specs, "b", xTb_d=io.get("xTb"))

    for g, d in enumerate(DILS):
        L = S // d
        qT, kT = qTs[g], kTs[g]
        allq = [("qT", g, ci) for ci in range(NCHK)]
        allk = [("kT", g, ci) for ci in range(NCHK)]
        allv = [("vT", ci) for ci in range(NCHK)]
        nlb = L // 128

        def colsel(blk, d=d, nlb=nlb):
            r, lb = divmod(blk, nlb)
            st = r + d * 128 * lb
            return slice(st, st + d * 127 + 1, d)

        make_vprime(C, vT, (lambda blk: allv) if d > 1 else (lambda blk: [("vT", blk // 4)]), vp, ("vp", g), 64, colsel)
        vpres = [(("vp", g), g0) for g0 in range(0, 64, 8)] + [(("vp", g), "ones")]
        tasks = []
        chunks = []
        for r in range(d):
            for c0 in range(0, L, 512):
                ck = (r, c0, C.next_po())
                chunks.append(ck)
                cl = [c for c in range(-1, 4) if c0 + 128 * c >= 0]
                for c in cl:
                    tasks.append((ck, c, c == cl[-1]))

        def stage1(t, ck, c, last, qT=qT, kT=kT, d=d):
            r, c0, (po, po_res) = ck
            qst = r + d * c0
            kl = c0 + 128 * c
            lo = max(0, 128 * c)
            hi = min(512, 128 * c + 256)
            n = hi - lo
            mlo = 128 if c == -1 else 0
            pst, ps_res = C.ps[t % 3], ("ps", t % 3)
            i = t % 3
            pt, tmp = pts[i], tmps[i]
            pt_res, tmp_res = ("ptb", i), ("tmpb", i)
            kst = r + d * kl
            ksl = slice(kst, kst + d * 127 + 1, d)
            qsl = slice(qst + d * lo, qst + d * (hi - 1) + 1, d)
            P.op("pe", lambda e: e.matmul(pst[:, 0:n], lhsT=kT[0:64, ksl], rhs=qT[0:64, qsl], start=True, stop=True),
                 reads=(allq + allk) if d > 1 else [("qT", g, c0 // 512), ("kT", g, kl // 512)], writes=[ps_res])
            P.op("dve", lambda e: e.tensor_tensor(out=tmp[:, 0:n], in0=pst[:, 0:n], in1=band[:, mlo:mlo + n], op=ALU.add), reads=[ps_res, "band"], writes=[tmp_res])
            P.op("act", lambda e: e.activation(out=pt[:, 0:n], in_=tmp[:, 0:n], func=AF.Exp), reads=[tmp_res], writes=[pt_res])

        def stage2(t, ck, c, last, g=g, d=d, nlb=nlb, vpres=vpres):
            r, c0, (po, po_res) = ck
            qst = r + d * c0
            kl = c0 + 128 * c
            i = t % 3
            pt, pt_res = pts[i], ("ptb", i)
            vidx = r * nlb + kl // 128
            if c >= 0:
                first = (c == 0 and c0 == 0)
                P.op("pe", lambda e: e.matmul(po[0:65, 128 * c:128 * c + 128], lhsT=vp[:, vidx, 0:65], rhs=pt[:, 0:128], start=first, stop=True, skip_group_check=True),
                     reads=vpres + [pt_res], writes=[po_res])
            if c <= 2:
                off = 0 if c == -1 else 128
                P.op("pe", lambda e: e.matmul(po[0:65, 128 * (c + 1):128 * (c + 1) + 128], lhsT=vp[:, vidx, 0:65], rhs=pt[:, off:off + 128], start=True, stop=False, skip_group_check=True),
                     reads=vpres + [pt_res], writes=[po_res])
            if last:
                asl = slice(qst, qst + d * 511 + 1, d)
                if g == 0:
                    P.op("act", lambda e: e.activation(out=acc[0:65, asl], in_=po[0:65, :], func=AF.Copy), reads=[po_res], writes=["acc"])
                else:
                    P.op("dve", lambda e: e.tensor_tensor(out=acc[0:65, asl], in0=acc[0:65, asl], in1=po[0:65, :], op=ALU.add), reads=[po_res, "acc"], writes=["acc"])

        LA = 2
        nt = len(tasks)
        for t in range(nt + LA):
            if t < nt:
                stage1(t, *tasks[t])
            if t - LA >= 0:
                stage2(t - LA, *tasks[t - LA])
    for I in range(NCHK):
        sl = slice(512 * I, 512 * I + 512)
        P.op("dve", lambda e, sl=sl: e.reciprocal(out=acc[64:65, sl], in_=acc[64:65, sl]), reads=["acc"], writes=["acc"])
        P.op("pe", lambda e, sl=sl: e.matmul(C.pm[0:64, :], lhsT=C.ones_f[64:65, 0:64], rhs=acc[64:65, sl], start=True, stop=True),
             reads=["ones_f", "acc"], writes=["pm"])
        P.op("dve", lambda e, sl=sl: e.tensor_tensor(out=acc[0:64, sl], in0=acc[0:64, sl], in1=C.pm[0:64, :], op=ALU.mult),
             reads=["acc", "pm"], writes=["acc"])
        P.dma("sp", lambda e, sl=sl: [e.dma_start(out=out_d[:, sl], in_=acc[0:64, sl])], reads=["acc"], writes=[("out", I)], lane="ost")
        C.out_res.append(("out", I))
    if standalone:
        P.op("sp", lambda e: None, reads=C.out_res)
        P.emit()
    return nc


NC_ = 64


def gdn_phase1(C, io, hd, st8, scr):
    P = C.P
    xT_d, w_d, b_d, cw_d, sc_d = io["xT"], io["w"], io["bias"], io["convw"], io["scal"]
    tri = C.tri
    bia = C.load_const(b_d, [64, 8], "bia")
    cw = C.load_const(cw_d, [64, 12], "cw")
    scal = C.load_const(sc_d, [128, 2], "scal")
    wb = load_weights(C, w_d, 194, "w")
    Utri = tri[:, 0, :]
    nA = P.sbuf([1, 1], F32, name="nA")
    bdec = P.sbuf([1, 1], F32, name="bdec")
    P.op("act", lambda e: e.activation(out=nA[:], in_=scal[0:1, 0:1], func=AF.Exp), reads=["scal"], writes=["nA"])
    P.op("dve", lambda e: e.tensor_scalar(out=nA[:], in0=nA[:], scalar1=-1.0, scalar2=None, op0=ALU.mult), reads=["nA"], writes=["nA"])
    P.op("dve", lambda e: e.tensor_tensor(out=bdec[:], in0=bia[0:1, 4:5], in1=scal[0:1, 1:2], op=ALU.add), reads=["bia", "scal"], writes=["bdec"])

    stg3 = [[P.sbuf([64, 512], F32, name=f"qkvst{i}_{b}") for b in range(2)] for i in range(3)]
    brow = P.sbuf([1, 512], F32, name="brow")
    grow = P.sbuf([1, 512], F32, name="grow")
    pBG = C.po[0]
    raws = [P.sbuf([64, 515], F32, name=f"raw{i}") for i in range(3)]
    cys = [P.sbuf([64, 512], F32, name=f"cy{i}") for i in range(3)]
    czs = [P.sbuf([64, 512], F32, name=f"cz{i}") for i in range(2)]
    sqs = [P.sbuf([64, 512], F32, name=f"sq{i}") for i in range(2)]
    rns = [P.sbuf([64, 512], F32, name=f"rn{i}") for i in range(2)]
    nbanks = [(C.pm, "pm"), (C.ps[0], ("ps", 0))]
    erow = P.sbuf([1, 512], F32, name="erow")
    eps_t = P.sbuf([128, 1], F32, name="eps_t")
    P.op("pool", lambda e: e.memset(eps_t[:], 1e-6), writes=["eps_t"])
    for i in range(3):
        P.op("pool", lambda e, i=i: e.memset(raws[i][:, 0:3], 0.0), writes=[("raw", i)])

    def ev_qkv(i):
        def ev(pt, pres, ci, t0):
            dst = stg3[i][ci % 2]
            dres = ("qkvst", i, ci % 2)
            cy, rcy = cys[i], ("cy", i)
            if i < 2:
                cz, sq, rn = czs[i], sqs[i], rns[i]
                rcz, rsq, rrn = ("cz", i), ("sq", i), ("rn", i)
                nb, rnb = nbanks[i]
            raw = raws[i]
            rr = ("raw", i)
            P.op("act", lambda e: e.activation(out=raw[:, 3:515], in_=pt[0:64, :], func=AF.Identity, bias=bia[:, i:i + 1], scale=1.0),
                 reads=[pres, "bia", rr], writes=[rr])
            P.op("dve", lambda e: e.tensor_scalar(out=cy[:], in0=raw[:, 0:512], scalar1=cw[:, 4 * i:4 * i + 1], scalar2=None, op0=ALU.mult),
                 reads=[rr, "cw"], writes=[rcy])
            for j in range(1, 4):
                P.op("dve", lambda e, j=j: e.scalar_tensor_tensor(out=cy[:], in0=raw[:, j:j + 512], scalar=cw[:, 4 * i + j:4 * i + j + 1], in1=cy[:],
                                                                 op0=ALU.mult, op1=ALU.add), reads=[rr, "cw", rcy], writes=[rcy])
            P.op("dve", lambda e: e.tensor_copy(out=raw[:, 0:3], in_=raw[:, 512:515]), reads=[rr], writes=[rr])
            if i == 2:
                P.op("act", lambda e: e.activation(out=dst[:, :], in_=cy[:], func=AF.Silu), reads=[rcy], writes=[dres])
                P.dma("sp", lambda e: [e.dma_start(out=scr[i, :, t0:t0 + 512], in_=dst[:, :])], reads=[dres], writes=[("scr", hd, i, ci)], lane=("qkvst", i, ci % 2))
                return
            P.op("act", lambda e: e.activation(out=cz[:], in_=cy[:], func=AF.Silu), reads=[rcy], writes=[rcz])
            P.op("pool", lambda e: e.tensor_tensor(out=sq[:], in0=cz[:], in1=cz[:], op=ALU.mult), reads=[rcz], writes=[rsq])
            P.op("pe", lambda e: e.matmul(nb[0:64, :], lhsT=C.ones_f[0:64, 0:64], rhs=sq[:], start=True, stop=True), reads=["ones_f", rsq], writes=[rnb])
            P.op("act", lambda e: e.activation(out=rn[:], in_=nb[0:64, :], func=AF.Ln, bias=eps_t[0:64, 0:1], scale=1.0), reads=[rnb, "eps_t"], writes=[rrn])
            P.op("act", lambda e: e.activation(out=rn[:], in_=rn[:], func=AF.Exp, scale=-0.5), reads=[rrn], writes=[rrn])
            if i == 0:
                P.op("dve", lambda e: e.scalar_tensor_tensor(out=dst[:, :], in0=cz[:], scalar=0.125, in1=rn[:], op0=ALU.mult, op1=ALU.mult),
                     reads=[rcz, rrn], writes=[dres])
            else:
                P.op("dve", lambda e: e.tensor_tensor(out=dst[:, :], in0=cz[:], in1=rn[:], op=ALU.mult), reads=[rcz, rrn], writes=[dres])
            P.dma("sp", lambda e: [e.dma_start(out=scr[i, :, t0:t0 + 512], in_=dst[:, :])], reads=[dres], writes=[("scr", hd, i, ci)], lane=("qkvst", i, ci % 2))
        return ev

    def ev_beta(pt, pres, ci, t0):
        P.op("act", lambda e: e.activation(out=brow[0:1, :], in_=pt[0:1, :], func=AF.Sigmoid, bias=bia[0:1, 3:4], scale=1.0),
             reads=[pres, "bia"], writes=["brow"])
        for j in range(4):
            P.op("pe", lambda e, j=j: e.matmul(pBG[0:128, 4 * ci + j:4 * ci + j + 1], lhsT=brow[0:1, 128 * j:128 * j + 128], rhs=C.ones_f[0:1, 0:1], start=True, stop=True),
                 reads=["brow", "ones_f"], writes=["pBG"])

    def ev_dec(pt, pres, ci, t0):
        P.op("act", lambda e: e.activation(out=erow[:], in_=pt[0:1, :], func=AF.Exp, bias=bdec[0:1, 0:1], scale=1.0), reads=[pres, "bdec"], writes=["erow"])
        P.op("act", lambda e: e.activation(out=erow[:], in_=erow[:], func=AF.Ln, bias=1.0, scale=1.0), reads=["erow"], writes=["erow"])
        P.op("dve", lambda e: e.tensor_scalar(out=grow[0:1, :], in0=erow[:], scalar1=nA[0:1, 0:1], scalar2=None, op0=ALU.mult),
             reads=["erow", "nA"], writes=["grow"])
        for j in range(4):
            P.op("pe", lambda e, j=j: e.matmul(pBG[0:128, 64 + 4 * ci + j:64 + 4 * ci + j + 1], lhsT=grow[0:1, 128 * j:128 * j + 128], rhs=C.ones_f[0:1, 0:1], start=True, stop=True),
                 reads=["grow", "ones_f"], writes=["pBG"])

    specs = [(0, 128, [(0, ev_qkv(0)), (64, ev_qkv(1))]), (128, 64, ev_qkv(2)), (192, 1, ev_beta), (193, 1, ev_dec)]
    proj_pass(C, xT_d, wb, "w", specs, "d", xTb_d=io.get("xTb"))

    betaT, gT, gcT, gl, egc, e2, egl, be = [st8[:, k, :] for k in range(8)]
    P.op("dve", lambda e: e.tensor_copy(out=betaT, in_=pBG[:, 0:64]), reads=["pBG"], writes=[("betaT", hd)])
    P.op("dve", lambda e: e.tensor_copy(out=gT, in_=pBG[:, 64:128]), reads=["pBG"], writes=[("gT", hd)])
    P.op("pe", lambda e: e.matmul(C.pm[:, 0:NC_], lhsT=Utri, rhs=gT, start=True, stop=True), reads=["tri", ("gT", hd)], writes=["pm"])
    P.op("dve", lambda e: e.tensor_copy(out=gcT, in_=C.pm[:, 0:NC_]), reads=["pm"], writes=[("gcT", hd)])
    P.op("pe", lambda e: e.matmul(C.pm[:, 64:64 + NC_], lhsT=C.ones_f[:, :], rhs=gT, start=True, stop=True), reads=["ones_f", ("gT", hd)], writes=["pm"])
    P.op("dve", lambda e: e.tensor_copy(out=gl, in_=C.pm[:, 64:64 + NC_]), reads=["pm"], writes=[("gl", hd)])
    P.op("act", lambda e: e.activation(out=egc, in_=gcT, func=AF.Exp), reads=[("gcT", hd)], writes=[("egc", hd)])
    P.op("act", lambda e: e.activation(out=egl, in_=gl, func=AF.Exp), reads=[("gl", hd)], writes=[("egl", hd)])
    P.op("dve", lambda e: e.tensor_tensor(out=e2, in0=gl, in1=gcT, op=ALU.subtract), reads=[("gl", hd), ("gcT", hd)], writes=[("e2", hd)])
    P.op("act", lambda e: e.activation(out=e2, in_=e2, func=AF.Exp), reads=[("e2", hd)], writes=[("e2", hd)])
    P.op("dve", lambda e: e.tensor_tensor(out=be, in0=betaT, in1=egc, op=ALU.mult), reads=[("betaT", hd), ("egc", hd)], writes=[("be", hd)])


def psum_slot(C, k):
    banks = C.pp + C.ps + C.po + [C.pm]
    return banks[k // 4][:, 128 * (k % 4):128 * (k % 4) + 128], ("slot", k)


def gdn_chunk_gen(C, hd, st8, scr, gnw, out_d, slots, T):
    P = C.P
    tri = C.tri
    identf = C.identf
    Utri, negSU, negSL = tri[:, 0, :], tri[:, 1, :], tri[:, 2, :]
    betaT, gT, gcT, gl, egc, e2, egl, be = [st8[:, k, :] for k in range(8)]
    R = lambda nm: (nm, hd)
    (sA, rA), (sB, rB), (sC, rC), (sD, rD), (sE, rE) = slots
    bank_keys = {rA, rB, rC, rD, rE}
    _P = P

    class _W:
        def op(self, eng, fn, reads=(), writes=()):
            rd = [r for r in reads if r not in bank_keys]
            wr = list(writes) + [r for r in reads if r in bank_keys]
            _P.op(eng, fn, reads=rd, writes=wr)

        def dma(self, *a, **k):
            _P.dma(*a, **k)
    P = _W()
    kv_t, kb_t, R32, GU, T1, T2, decST, decS, decTI, kbf = T["kv_t"], T["kb_t"], T["R32"], T["GU"], T["T1"], T["T2"], T["decST"], T["decS"], T["decTI"], T["kbf"]
    Mk, Nk, wf, u32, qkT, kdec_t = T["Mk"], T["Nk"], T["wf"], T["u32"], T["qkT"], T["kdec_t"]
    S32, v32, tmpB, o32, osq, ss, on32, ostage, qkv = T["S32"], T["v32"], T["tmpB"], T["o32"], T["osq"], T["ss"], T["on32"], T["ostage"], T["qkv"]
    eps_t = C.eps6
    P.op("pool", lambda e: e.memset(S32[:], 0.0), writes=[R("S32")])

    def chunk(c):
        cs = slice(128 * c, 128 * c + 128)
        cc = slice(c, c + 1)
        b2 = c % 2
        qc, kc, vc = qkv[b2][:, 0, :], qkv[b2][:, 1, :], qkv[b2][:, 2, :]
        rin = ("qkvc", hd, b2)
        P.dma("sp", lambda e: [e.dma_start(out=qkv[b2][:, :, :], in_=scr[:, :, cs].rearrange("i d t -> d i t"))], writes=[rin])
        yield
        P.op("pe", lambda e: e.transpose(sA[:, 0:64], kc, identf[0:64, 0:64]), reads=[rin, "ident"], writes=[rA])
        P.op("pe", lambda e: e.transpose(sA[:, 64:128], vc, identf[0:64, 0:64]), reads=[rin, "ident"], writes=[rA])
        yield
        P.op("act", lambda e: e.activation(out=kv_t[:], in_=sA[:, 0:128], func=AF.Copy), reads=[rA], writes=[R("kv_t")])
        yield
        P.op("dve", lambda e: e.tensor_scalar(out=kb_t[:], in0=kv_t[:, 0:64], scalar1=betaT[:, cc], scalar2=None, op0=ALU.mult), reads=[R("kv_t"), R("betaT")], writes=[R("kb_t")])
        P.op("pool", lambda e: e.tensor_scalar(out=kdec_t[b2][:], in0=kv_t[:, 0:64], scalar1=e2[:, cc], scalar2=None, op0=ALU.mult), reads=[R("kv_t"), R("e2")], writes=[("kdec_t", hd, b2)])
        P.op("dve", lambda e: e.tensor_scalar(out=R32[:, 0:64], in0=kv_t[:, 64:128], scalar1=betaT[:, cc], scalar2=None, op0=ALU.mult), reads=[R("kv_t"), R("betaT")], writes=[R("R32")])
        P.op("dve", lambda e: e.tensor_scalar(out=R32[:, 64:128], in0=kv_t[:, 0:64], scalar1=be[:, cc], scalar2=None, op0=ALU.mult), reads=[R("kv_t"), R("be")], writes=[R("R32")])
        P.op("pool", lambda e: e.tensor_scalar(out=GU[:], in0=Utri, scalar1=gT[:, cc], scalar2=None, op0=ALU.mult), reads=["tri", R("gT")], writes=[R("GU")])
        yield
        P.op("pe", lambda e: e.transpose(sA[0:64, 0:128], kb_t[:, 0:64], identf[:, :]), reads=[R("kb_t"), "ident"], writes=[rA])
        P.op("pe", lambda e: e.matmul(sB[:, 0:128], lhsT=C.ones_f[:, :], rhs=GU[:], start=True, stop=True), reads=["ones_f", R("GU")], writes=[rB])
        yield
        P.op("act", lambda e: e.activation(out=kbf[:], in_=sA[0:64, 0:128], func=AF.Copy), reads=[rA], writes=[R("kbf")])
        P.op("dve", lambda e: e.scalar_tensor_tensor(out=T1[:], in0=sB[:, 0:128], scalar=gcT[:, cc], in1=negSU, op0=ALU.subtract, op1=ALU.add),
             reads=[rB, R("gcT"), "tri"], writes=[R("T1")])
        P.op("dve", lambda e: e.scalar_tensor_tensor(out=T2[:], in0=sB[:, 0:128], scalar=gcT[:, cc], in1=negSL, op0=ALU.subtract, op1=ALU.subtract),
             reads=[rB, R("gcT"), "tri"], writes=[R("T2")])
        yield
        P.op("act", lambda e: e.activation(out=decST[:], in_=T1[:], func=AF.Exp), reads=[R("T1")], writes=[R("decST")])
        P.op("act", lambda e: e.activation(out=decS[:], in_=T2[:], func=AF.Exp, scale=-1.0), reads=[R("T2")], writes=[R("decS")])
        P.op("pe", lambda e: e.matmul(sC[:, 0:128], lhsT=kbf[:, :], rhs=kc, start=True, stop=True), reads=[R("kbf"), rin], writes=[rC])
        P.op("pe", lambda e: e.matmul(sD[:, 0:128], lhsT=kc, rhs=kbf[:, :], start=True, stop=True), reads=[R("kbf"), rin], writes=[rD])
        P.op("pe", lambda e: e.matmul(sE[:, 0:128], lhsT=kc, rhs=qc, start=True, stop=True), reads=[rin], writes=[rE])
        yield
        P.op("pool", lambda e: e.tensor_tensor(out=decTI[:], in0=decST[:], in1=identf[:], op=ALU.add), reads=[R("decST"), "ident"], writes=[R("decTI")])
        P.op("dve", lambda e: e.tensor_tensor(out=Mk[0][:], in0=sC[:, 0:128], in1=decS[:], op=ALU.mult), reads=[rC, R("decS")], writes=[("Mk", hd, 0)])
        P.op("dve", lambda e: e.tensor_tensor(out=Nk[0][:], in0=sD[:, 0:128], in1=decST[:], op=ALU.mult), reads=[rD, R("decST")], writes=[("Nk", hd, 0)])
        yield
        P.op("dve", lambda e: e.tensor_tensor(out=qkT[b2][:], in0=sE[:, 0:128], in1=decTI[:], op=ALU.mult), reads=[rE, R("decTI")], writes=[("qkT", hd, b2)])
        for k in range(1, 7):
            P.op("pe", lambda e, k=k: e.matmul(sD[:, 0:128], lhsT=Mk[k - 1][:], rhs=Nk[k - 1][:], start=True, stop=True), reads=[("Mk", hd, k - 1), ("Nk", hd, k - 1)], writes=[rD])
            if k < 6:
                P.op("pe", lambda e, k=k: e.matmul(sC[:, 0:128], lhsT=Nk[k - 1][:], rhs=Mk[k - 1][:], start=True, stop=True), reads=[("Mk", hd, k - 1), ("Nk", hd, k - 1)], writes=[rC])
            yield
            P.op("act", lambda e, k=k: e.activation(out=Nk[k][:], in_=sD[:, 0:128], func=AF.Copy), reads=[rD], writes=[("Nk", hd, k)])
            if k < 6:
                P.op("dve", lambda e, k=k: e.tensor_copy(out=Mk[k][:], in_=sC[:, 0:128]), reads=[rC], writes=[("Mk", hd, k)])
            yield
        for k in range(6, -1, -1):
            P.op("pe", lambda e, k=k: e.matmul(sB[:, 0:128], lhsT=Nk[k][:], rhs=R32[:], start=True, stop=True), reads=[("Nk", hd, k), R("R32")], writes=[rB])
            yield
            op = ALU.add if k > 0 else ALU.subtract
            P.op("dve", lambda e, op=op: e.tensor_tensor(out=R32[:], in0=R32[:], in1=sB[:, 0:128], op=op), reads=[rB, R("R32")], writes=[R("R32")])
            yield
        P.op("pool", lambda e: e.tensor_copy(out=u32[b2][:], in_=R32[:, 0:64]), reads=[R("R32")], writes=[("u32", hd, b2)])
        P.op("pe", lambda e: e.transpose(sA[0:64, 0:128], R32[:, 64:128], identf[:, :]), reads=[R("R32"), "ident"], writes=[rA])
        yield
        P.op("act", lambda e: e.activation(out=wf[b2][:], in_=sA[0:64, 0:128], func=AF.Copy), reads=[rA], writes=[("wf", hd, b2)])
        yield
        P.op("pe", lambda e: e.matmul(sE[:, 0:64], lhsT=wf[b2][:, :], rhs=S32[:, :], start=True, stop=True), reads=[("wf", hd, b2), R("S32")], writes=[rE])
        P.op("pe", lambda e: e.matmul(sC[:, 0:64], lhsT=qc, rhs=S32[:, :], start=True, stop=True), reads=[rin, R("S32")], writes=[rC])
        yield
        P.op("dve", lambda e: e.tensor_tensor(out=v32[:], in0=u32[b2][:], in1=sE[:, 0:64], op=ALU.subtract), reads=[("u32", hd, b2), rE], writes=[R("v32")])
        yield
        P.op("pe", lambda e: e.matmul(sC[:, 64:128], lhsT=qkT[b2][:, :], rhs=v32[:, :], start=True, stop=True), reads=[("qkT", hd, b2), R("v32")], writes=[rC])
        P.op("pe", lambda e: e.matmul(sE[0:64, 64:128], lhsT=kdec_t[b2][:, :], rhs=v32[:, :], start=True, stop=True), reads=[("kdec_t", hd, b2), R("v32")], writes=[rE])
        yield
        P.op("dve", lambda e: e.scalar_tensor_tensor(out=S32[:], in0=S32[:], scalar=egl[0:64, cc], in1=sE[0:64, 64:128], op0=ALU.mult, op1=ALU.add),
             reads=[R("S32"), R("egl"), rE], writes=[R("S32")])
        P.op("act", lambda e: e.activation(out=tmpB[:], in_=sC[:, 64:128], func=AF.Copy), reads=[rC], writes=[R("tmpB")])
        yield
        P.op("dve", lambda e: e.scalar_tensor_tensor(out=o32[:], in0=sC[:, 0:64], scalar=egc[:, cc], in1=tmpB[:], op0=ALU.mult, op1=ALU.add),
             reads=[rC, R("egc"), R("tmpB")], writes=[R("o32")])
        yield
        P.op("pool", lambda e: e.tensor_tensor(out=osq[:], in0=o32[:], in1=o32[:], op=ALU.mult), reads=[R("o32")], writes=[R("osq")])
        yield
        P.op("dve", lambda e: e.tensor_reduce(out=ss[:], in_=osq[:], axis=AX.X, op=ALU.add), reads=[R("osq")], writes=[R("ss")])
        yield
        P.op("act", lambda e: e.activation(out=ss[:], in_=ss[:], func=AF.Ln, bias=eps_t[:, 0:1], scale=1.0 / 64.0), reads=[R("ss"), "eps6"], writes=[R("ss")])
        P.op("act", lambda e: e.activation(out=ss[:], in_=ss[:], func=AF.Exp, scale=-0.5), reads=[R("ss")], writes=[R("ss")])
        yield
        P.op("dve", lambda e: e.scalar_tensor_tensor(out=on32[:], in0=o32[:], scalar=ss[:, 0:1], in1=gnw[:], op0=ALU.mult, op1=ALU.mult), reads=[R("o32"), R("ss"), "gnw"], writes=[R("on32")])
        yield
        og = (c // 4) % 2
        P.op("pe", lambda e: e.transpose(sD[0:64, 0:128], on32[:, 0:64], identf[:, :]), reads=[R("on32"), "ident"], writes=[rD])
        yield
        P.op("act", lambda e: e.activation(out=ostage[og][:, 128 * (c % 4):128 * (c % 4) + 128], in_=sD[0:64, 0:128], func=AF.Copy),
             reads=[rD], writes=[("ostage", hd, og)])
        if c % 4 == 3:
            I = c // 4
            P.dma("sp", lambda e: [e.dma_start(out=out_d[:, 512 * I:512 * I + 512], in_=ostage[og][:, :])], reads=[("ostage", hd, og)], writes=[("gout", hd, I)],
                  lane=("gost", hd, og))
            C.out_res.append(("gout", hd, I))
        yield

    import os
    lim = int(os.environ.get("GDN_LIM", "1000000000"))
    cnt = 0
    for c in range(int(os.environ.get("GDN_NCH", NC_))):
        for _ in chunk(c):
            cnt += 1
            if cnt >= lim:
                return
            yield


def gdn_alloc_tiles(P, hd):
    sb = lambda shape, nm: P.sbuf(shape, F32, name=f"{nm}_{hd}")
    T = {}
    for nm in ("kv_t", "R32", "GU", "T1", "T2", "decST", "decS", "decTI"):
        T[nm] = sb([128, 128], nm)
    T["kb_t"] = sb([128, 64], "kb_t")
    T["kbf"] = sb([64, 128], "kbf")
    T["kdec_t"] = [sb([128, 64], f"kdec{i}") for i in range(2)]
    T["Mk"] = [sb([128, 128], f"Mk{i}") for i in range(7)]
    T["Nk"] = [sb([128, 128], f"Nk{i}") for i in range(7)]
    T["wf"] = [sb([64, 128], f"wf{i}") for i in range(2)]
    T["u32"] = [sb([128, 64], f"u32{i}") for i in range(2)]
    T["qkT"] = [sb([128, 128], f"qkT{i}") for i in range(2)]
    T["S32"] = sb([64, 64], "S32")
    for nm in ("v32", "tmpB", "o32", "osq", "on32"):
        T[nm] = sb([128, 64], nm)
    T["ss"] = sb([128, 1], "ss")
    T["ostage"] = [sb([64, 512], f"ostage{i}") for i in range(2)]
    T["qkv"] = [sb([64, 3, 128], f"qkvc{i}") for i in range(2)]
    return T


def gdn_group(C, ios, scrs, gnw_d, outs):
    import itertools
    P = C.P
    G = len(ios)
    C.eps6 = P.sbuf([128, 1], F32, name="eps6")
    P.op("pool", lambda e: e.memset(C.eps6[:], 1e-6), writes=["eps6"])
    st8s = [P.sbuf([128, 8, NC_], F32, name=f"st8_{g}") for g in range(G)]
    gnw = C.load_const(gnw_d, [128, 64], "gnw")
    for g in range(G):
        with P.scope():
            C.new_scope()
            gdn_phase1(C, ios[g], g, st8s[g], scrs[g])
    assert G <= 8
    Ts = [gdn_alloc_tiles(P, g) for g in range(G)]
    banks = C.pp + C.ps + C.po + [C.pm]

    def slots_for(g):
        bk, key = banks[g], ("gbank", g)
        sl = [(bk[:, 128 * k:128 * k + 128], key) for k in range(4)]
        return sl + [sl[0]]

    gens = [gdn_chunk_gen(C, g, st8s[g], scrs[g], gnw, outs[g], slots_for(g), Ts[g]) for g in range(G)]
    alive = list(gens)
    while alive:
        nxt = []
        for gen in alive:
            try:
                next(gen)
                nxt.append(gen)
            except StopIteration:
                pass
        alive = nxt


def build_gdn(nc=None, C=None, io=None):
    C = Ctx(nc)
    io = {"xT": C.dram_in("xT", [128, 8, S]), "w": C.dram_in("w", [2, 128, 8, 128]), "bias": C.dram_in("bias", [64, 8]),
          "convw": C.dram_in("convw", [64, 12]), "scal": C.dram_in("scal", [128, 2])}
    gnw_d = C.dram_in("gnw", [128, 64])
    out_d = C.dram_out("oT", [64, S])
    C.tri = C.load_const(C.dram_in("tri", [128, 3, 128]), [128, 3, 128], "tri")
    C.identf, C.ident_b = C.load_const(C.dram_in("ident", [128, 128]), [128, 128], "ident", cast=BF16)
    C.pm_b = C.pm[:].bitcast(BF16)
    scr = nc.dram_tensor("gdn_scr", [3, 64, S], F32).ap()
    gdn_group(C, [io], [scr], gnw_d, [out_d])
    C.P.barrier()
    C.P.op("sp", lambda e: None)
    C.P.emit()
    return nc


T_ = 1024
NTC = 2
ALPHA = (2 * 2) ** 0.25
LN_EPS = 1e-5


def stream_w(C, w_d, c0, n, tag):
    P = C.P
    L = C.loc
    if not hasattr(L, "ws_bufs"):
        L.ws_stg = [P.sbuf([128, 8, 128], F32, name=f"wsstg{i}") for i in range(2)]
        L.ws_bufs = [P.sbuf([128, 8, 128], BF16, name=f"wsb{i}") for i in range(2)]
        L.ws_i = 0
    i = L.ws_i
    L.ws_i += 1
    st, rs = L.ws_stg[i % 2], ("wsstg", i % 2)
    wt, rw = L.ws_bufs[i % 2], ("wsb", i % 2)
    P.dma("act", lambda e: [e.dma_start(out=st[:, :, :], in_=w_d[c0])], writes=[rs])
    P.op("pool", lambda e: e.tensor_copy(out=wt[:, :, 0:n], in_=st[:, :, 0:n]), reads=[rs], writes=[rw])
    return wt, rw


def build_merge(nc=None, C=None, io=None):
    standalone = C is None
    if standalone:
        C = Ctx(nc)
        io = {"xT": C.dram_in("xT", [128, 8, T_]), "xtok": C.dram_in("xtok", [T_, 1024]), "oT_in": C.dram_in("oT_in", [4, 384, T_]),
              "wz": C.dram_in("wz", [128, 8, 1920]), "bz": C.dram_in("bz", [128, 16]), "wm": C.dram_in("wm", [128, 8, 5120]),
              "bm": C.dram_in("bm", [128, 40]), "weq": C.dram_in("weq", [128, 8, 384]), "beq": C.dram_in("beq", [96, 4]),
              "mem": C.dram_in("mem", [256, 1024]), "memg": C.dram_in("memg", [128, 2, 1024]), "wkv": C.dram_in("wkv", [128, 8, 768]),
              "wbr": C.dram_in("wbr", [128, 12, 1024]), "wbe": C.dram_in("wbe", [96, 4, 1024]), "wo": C.dram_in("wo", [128, 8, 1024]),
              "lng": C.dram_in("lng", [128, 2, 1024]), "out": C.dram_out("out", [T_, 1024]), "outT": None}
        C.identf, C.ident_b = C.load_const(C.dram_in("ident", [128, 128]), [128, 128], "ident", cast=BF16)
        C.pm_b = C.pm[:].bitcast(BF16)
    P = C.P
    wz_d, bz_d, wm_d, bm_d = io["wz"], io["bz"], io["wm"], io["bm"]
    weq_d, beq_d, mem_d, mg_d, wkv_d, wbr_d, wbe_d = io["weq"], io["beq"], io["mem"], io["memg"], io["wkv"], io["wbr"], io["wbe"]
    wo_d, lng_d = io["wo"], io["lng"]
    shards = io.get("shards") or [{"xT": io["xT"], "xtok": io["xtok"], "oT_in": io["oT_in"], "out": io["out"], "outT": io["outT"]}]
    _cache = {}

    def A(shape, dtype, name=None):
        if name not in _cache:
            _cache[name] = P.sbuf(shape, dtype, name=name)
        return _cache[name]
    identf = C.identf
    bz = C.load_const(bz_d, [128, 16], "bz")
    bm = C.load_const(bm_d, [128, 40], "bm")
    beq = C.load_const(beq_d, [96, 4], "beq")
    SC = 96 ** -0.5
    beqs = A([96, 4], F32, name="beqs")
    P.op("dve", lambda e: e.tensor_scalar(out=beqs[:], in0=beq[:], scalar1=SC, scalar2=None, op0=ALU.mult), reads=["beq"], writes=["beqs"])
    eps_t = A([128, 1], F32, name="eps_t")
    P.op("pool", lambda e: e.memset(eps_t[:], LN_EPS), writes=["eps_t"])

    big = [A([128, 1024], F32, name=f"big{i}") for i in range(3)]
    stat = A([128, 4], F32, name="stat")

    def layer_norm(src, src_res, gb, gb_res, dst, dst_res, tmp, tmp_res):
        P.op("dve", lambda e: e.tensor_reduce(out=stat[:, 0:1], in_=src[:], axis=AX.X, op=ALU.add), reads=[src_res], writes=["stat0"])
        P.op("dve", lambda e: e.tensor_scalar(out=stat[:, 0:1], in0=stat[:, 0:1], scalar1=1.0 / 1024.0, scalar2=None, op0=ALU.mult), reads=["stat0"], writes=["stat0"])
        P.op("dve", lambda e: e.tensor_scalar(out=src[:], in0=src[:], scalar1=stat[:, 0:1], scalar2=None, op0=ALU.subtract), reads=[src_res, "stat0"], writes=[src_res])
        P.op("pool", lambda e: e.tensor_tensor(out=tmp[:], in0=src[:], in1=src[:], op=ALU.mult), reads=[src_res], writes=[tmp_res])
        P.op("dve", lambda e: e.tensor_reduce(out=stat[:, 1:2], in_=tmp[:], axis=AX.X, op=ALU.add), reads=[tmp_res], writes=["stat1"])
        P.op("act", lambda e: e.activation(out=stat[:, 1:2], in_=stat[:, 1:2], func=AF.Ln, bias=eps_t[:, 0:1], scale=1.0 / 1024.0), reads=["stat1", "eps_t"], writes=["stat1"])
        P.op("act", lambda e: e.activation(out=stat[:, 1:2], in_=stat[:, 1:2], func=AF.Exp, scale=-0.5), reads=["stat1"], writes=["stat1"])
        P.op("dve", lambda e: e.scalar_tensor_tensor(out=tmp[:], in0=src[:], scalar=stat[:, 1:2], in1=gb[:, 0, :], op0=ALU.mult, op1=ALU.mult),
             reads=[src_res, "stat1", gb_res], writes=[tmp_res])
        P.op("pool", lambda e: e.tensor_tensor(out=dst[:], in0=tmp[:], in1=gb[:, 1, :], op=ALU.add), reads=[tmp_res, gb_res], writes=[dst_res])

    gbt = A([128, 2, 1024], F32, name="gbt")
    P.dma("sp", lambda e: [e.dma_start(out=gbt[:], in_=mg_d)], writes=["gbt"])
    mg = gbt
    memT = A([128, 8, 256], BF16, name="memT")
    memn_b = A([128, 1024], BF16, name="memn_b")
    for t in range(2):
        P.dma("sp", lambda e, t=t: [e.dma_start(out=big[0][:], in_=mem_d[128 * t:128 * t + 128, :])], writes=["big0"])
        layer_norm(big[0], "big0", mg, "gbt", big[1], "big1", big[2], "big2")
        P.op("act", lambda e: e.activation(out=memn_b[:], in_=big[1][:], func=AF.Copy), reads=["big1"], writes=["memn_b"])
        for kc in range(8):
            P.op("pe", lambda e, kc=kc: e.transpose(C.pm_b[:, 128 * kc:128 * kc + 128], memn_b[:, 128 * kc:128 * kc + 128], C.ident_b[:, :]),
                 reads=["memn_b", "ident_b"], writes=["pm"])
        P.op("act", lambda e, t=t: e.activation(out=memT[:, :, 128 * t:128 * t + 128], in_=C.pm_b[:, 0:1024].rearrange("p (k m) -> p k m", k=8), func=AF.Copy),
             reads=["pm"], writes=[("memT", t)])
    wbig = A([128, 8, 1024], BF16, name="wbig")
    wkv = load_weights(C, wkv_d, 768, "wbig", wb=wbig)
    kmT = A([96, 4, 256], BF16, name="kmT")
    vm = A([128, 2, 4, 97], BF16, name="vm")
    P.op("pool", lambda e: e.memset(vm[:, :, :, 96:97], 1.0), writes=["vm_ones"])
    for h in range(4):
        pt, pres = C.next_pp()
        for kc in range(8):
            P.op("pe", lambda e, pt=pt, kc=kc, h=h: e.matmul(pt[0:96, 0:256], lhsT=wkv[:, kc, 96 * h:96 * h + 96], rhs=memT[:, kc, :], start=(kc == 0), stop=(kc == 7)),
                 reads=["wbig", ("memT", 0), ("memT", 1)], writes=[pres])
        P.op("act", lambda e, pt=pt, h=h: e.activation(out=kmT[:, h, :], in_=pt[0:96, 0:256], func=AF.Copy), reads=[pres], writes=[("kmT", h)])
    for t in range(2):
        pt, pres = C.next_pp()
        for kc in range(8):
            P.op("pe", lambda e, pt=pt, kc=kc, t=t: e.matmul(pt[:, 0:384], lhsT=memT[:, kc, 128 * t:128 * t + 128], rhs=wkv[:, kc, 384:768], start=(kc == 0), stop=(kc == 7)),
                 reads=["wbig", ("memT", t)], writes=[pres])
        P.op("act", lambda e, pt=pt, t=t: e.activation(out=vm[:, t, :, 0:96], in_=pt[:, 0:384].rearrange("p (h d) -> p h d", h=4), func=AF.Copy),
             reads=[pres], writes=[("vm", t)])

    wbr = A([128, 12, 1024], BF16, name="wbr")
    wbe = A([96, 4, 1024], BF16, name="wbe")
    for q in range(16):
        bt, br = big[1 + q % 2], f"big{1 + q % 2}"
        if q < 12:
            P.dma("act", lambda e, q=q, bt=bt: [e.dma_start(out=bt[:, :], in_=wbr_d[:, q, :])], writes=[br])
            P.op("pool", lambda e, q=q, bt=bt: e.tensor_copy(out=wbr[:, q, :], in_=bt[:, :]), reads=[br], writes=["wbr"])
        else:
            P.dma("act", lambda e, q=q, bt=bt: [e.dma_start(out=bt[0:96, :], in_=wbe_d[:, q - 12, :])], writes=[br])
            P.op("pool", lambda e, q=q, bt=bt: e.tensor_copy(out=wbe[:, q - 12, :], in_=bt[0:96, :]), reads=[br], writes=["wbe"])
    wo = load_weights(C, wo_d, 1024, "wbig", wb=wbig)
    P.dma("sp", lambda e: [e.dma_start(out=gbt[:], in_=lng_d)], reads=[], writes=["gbt"])
    lng = gbt
    def do_shard(sh):
        xT_d, xtok_d, oT_d, out_d, outT_d = sh["xT"], sh["xtok"], sh["oT_in"], sh["out"], sh["outT"]
        xb = A([128, 8, T_], BF16, name="xb")
        xs = [A([128, 2, 512], F32, name=f"xs{i}") for i in range(2)]
        for ci in range(NTC):
            t0 = 512 * ci
            for q4 in range(4):
                hh = q4 % 2
                P.dma("sp", lambda e, hh=hh, q4=q4, t0=t0: [e.dma_start(out=xs[hh][:, :, :], in_=xT_d[:, 2 * q4:2 * q4 + 2, t0:t0 + 512])], writes=[("xs", hh)])
                P.op("pool" if hh == 0 else "dve", lambda e, hh=hh, q4=q4, t0=t0: e.tensor_copy(out=xb[:, 2 * q4:2 * q4 + 2, t0:t0 + 512], in_=xs[hh][:, :, :]), reads=[("xs", hh)], writes=[("xb", ci, q4)])

        def xres(ci):
            return [("xb", ci, q4) for q4 in range(4)]


        gE = A([96, 4, T_], BF16, name="gE")
        eq = A([96, 512], BF16, name="eq")
        szE = A([96, 512], F32, name="szE")
        ptE = [A([128, 512], BF16, name=f"ptE{i}") for i in range(2)]
        onE = A([97, 512], F32, name="onE")
        rdenE = A([1, 512], F32, name="rdenE")
        for h in range(4):
            wq_t, wq_r = stream_w(C, weq_d, h, 96, "weq")
            wz_t, wz_r = stream_w(C, wz_d, 12 + h, 96, "wzE")
            for ci in range(NTC):
                cs = slice(512 * ci, 512 * ci + 512)
                pt, pres = C.next_pp()
                for kc in range(8):
                    P.op("pe", lambda e, pt=pt, kc=kc, cs=cs: e.matmul(pt[0:96, :], lhsT=wq_t[:, kc, 0:96], rhs=xb[:, kc, cs], start=(kc == 0), stop=(kc == 7)),
                         reads=[wq_r] + xres(ci), writes=[pres])
                P.op("act", lambda e, pt=pt, h=h: e.activation(out=eq[:], in_=pt[0:96, :], func=AF.Identity, bias=beqs[:, h:h + 1], scale=SC), reads=[pres, "beqs"], writes=["eq"])
                pt2, pres2 = C.next_pp()
                for kc in range(8):
                    P.op("pe", lambda e, pt2=pt2, kc=kc, cs=cs: e.matmul(pt2[0:96, :], lhsT=wz_t[:, kc, 0:96], rhs=xb[:, kc, cs], start=(kc == 0), stop=(kc == 7)),
                         reads=[wz_r] + xres(ci), writes=[pres2])
                P.op("act", lambda e, pt2=pt2, h=h: e.activation(out=szE[:], in_=pt2[0:96, :], func=AF.Silu, bias=bz[0:96, 12 + h:13 + h], scale=1.0), reads=[pres2, "bz"], writes=["szE"])
                po, po_res = C.next_po()
                for t in range(2):
                    pst, ps_res = C.next_ps()
                    P.op("pe", lambda e, pst=pst, t=t, h=h: e.matmul(pst[:, :], lhsT=kmT[:, h, 128 * t:128 * t + 128], rhs=eq[:, :], start=True, stop=True),
                         reads=[("kmT", h), "eq"], writes=[ps_res])
                    P.op("act", lambda e, pst=pst, t=t: e.activation(out=ptE[t][:], in_=pst[:, :], func=AF.Exp), reads=[ps_res], writes=[("ptE", t)])
                    P.op("pe", lambda e, po=po, t=t, h=h: e.matmul(po[0:97, :], lhsT=vm[:, t, h, 0:97], rhs=ptE[t][:], start=(t == 0), stop=(t == 1)),
                         reads=[("vm", t), "vm_ones", ("ptE", t)], writes=[po_res])
                P.op("act", lambda e, po=po: e.activation(out=onE[0:97, :], in_=po[0:97, :], func=AF.Copy), reads=[po_res], writes=["onE"])
                P.op("dve", lambda e: e.reciprocal(out=rdenE[0:1, :], in_=onE[96:97, :]), reads=["onE"], writes=["rdenE"])
                P.op("pe", lambda e: e.matmul(C.pm[0:96, :], lhsT=C.ones_f[0:1, 0:96], rhs=rdenE[0:1, :], start=True, stop=True), reads=["ones_f", "rdenE"], writes=["pm"])
                P.op("dve", lambda e: e.tensor_tensor(out=onE[0:96, :], in0=onE[0:96, :], in1=C.pm[0:96, :], op=ALU.mult), reads=["onE", "pm"], writes=["onE"])
                P.op("dve", lambda e, h=h, cs=cs: e.tensor_tensor(out=gE[:, h, cs], in0=onE[0:96, :], in1=szE[:], op=ALU.mult), reads=["onE", "szE"], writes=[("gE", h, ci)])

        g4 = A([128, 12, T_], BF16, name="g4")
        oin = [A([128, 512], F32, name=f"oin{i}") for i in range(2)]
        sz = [A([128, 512], F32, name=f"sz{i}") for i in range(2)]
        it = 0
        for n in range(4):
            for k in range(3):
                wz_t, wz_r = stream_w(C, wz_d, 3 * n + k, 128, "wz")
                for ci in range(NTC):
                    cs = slice(512 * ci, 512 * ci + 512)
                    b = it % 2
                    it += 1
                    P.dma("sp", lambda e, b=b, n=n, k=k, cs=cs: [e.dma_start(out=oin[b][:], in_=oT_d[n, 128 * k:128 * k + 128, cs])], writes=[("oin", b)])
                    pt, pres = C.next_pp()
                    for kc in range(8):
                        P.op("pe", lambda e, pt=pt, kc=kc, cs=cs, wz_t=wz_t: e.matmul(pt[:, :], lhsT=wz_t[:, kc, 0:128], rhs=xb[:, kc, cs], start=(kc == 0), stop=(kc == 7)),
                             reads=[wz_r] + xres(ci), writes=[pres])
                    P.op("act", lambda e, pt=pt, b=b, n=n, k=k: e.activation(out=sz[b][:], in_=pt[:, :], func=AF.Silu, bias=bz[:, 3 * n + k:3 * n + k + 1], scale=1.0),
                         reads=[pres, "bz"], writes=[("sz", b)])
                    P.op("dve", lambda e, b=b, n=n, k=k, cs=cs: e.tensor_tensor(out=g4[:, 3 * n + k, cs], in0=oin[b][:], in1=sz[b][:], op=ALU.mult),
                         reads=[("oin", b), ("sz", b)], writes=[("g4", n, k, ci)])

        wbr, wbe = _cache["wbr"], _cache["wbe"]
        yb = A([128, 8, T_], BF16, name="yb")
        yacc = A([128, T_], F32, name="yacc")
        sig = [A([128, 512], F32, name=f"sig{i}") for i in range(2)]
        ytmp = [A([128, 512], F32, name=f"ytmp{i}") for i in range(2)]
        it = 0
        for dmt in range(8):
            dsl = slice(128 * dmt, 128 * dmt + 128)
            for n in range(5):
                wm_t, wm_r = stream_w(C, wm_d, 8 * n + dmt, 128, "wm")
                for ci in range(NTC):
                    cs = slice(512 * ci, 512 * ci + 512)
                    b = it % 2
                    it += 1
                    pt, pres = C.next_pp()
                    for kc in range(8):
                        P.op("pe", lambda e, pt=pt, kc=kc, cs=cs, wm_t=wm_t: e.matmul(pt[:, :], lhsT=wm_t[:, kc, 0:128], rhs=xb[:, kc, cs], start=(kc == 0), stop=(kc == 7)),
                             reads=[wm_r] + xres(ci), writes=[pres])
                    P.op("act", lambda e, pt=pt, b=b, n=n, dmt=dmt: e.activation(out=sig[b][:], in_=pt[:, :], func=AF.Sigmoid, bias=bm[:, 8 * n + dmt:8 * n + dmt + 1], scale=1.0),
                         reads=[pres, "bm"], writes=[("sig", b)])
                    pst, ps_res = C.next_ps()
                    if n < 4:
                        for k in range(3):
                            P.op("pe", lambda e, pst=pst, k=k, n=n, cs=cs, dsl=dsl: e.matmul(pst[:, :], lhsT=wbr[:, 3 * n + k, dsl], rhs=g4[:, 3 * n + k, cs], start=(k == 0), stop=(k == 2)),
                                 reads=["wbr", ("g4", n, k, ci)], writes=[ps_res])
                    else:
                        for h in range(4):
                            P.op("pe", lambda e, pst=pst, h=h, cs=cs, dsl=dsl: e.matmul(pst[:, :], lhsT=wbe[:, h, dsl], rhs=gE[:, h, cs], start=(h == 0), stop=(h == 3)),
                                 reads=["wbe", ("gE", h, ci)], writes=[ps_res])
                    if n == 0:
                        P.op("dve", lambda e, pst=pst, b=b, cs=cs: e.tensor_tensor(out=yacc[:, cs], in0=sig[b][:], in1=pst[:, :], op=ALU.mult), reads=[("sig", b), ps_res], writes=[("yacc", ci)])
                    else:
                        P.op("dve", lambda e, pst=pst, b=b: e.tensor_tensor(out=ytmp[b][:], in0=sig[b][:], in1=pst[:, :], op=ALU.mult), reads=[("sig", b), ps_res], writes=[("ytmp", b)])
                        P.op("pool", lambda e, b=b, cs=cs: e.tensor_tensor(out=yacc[:, cs], in0=yacc[:, cs], in1=ytmp[b][:], op=ALU.add), reads=[("ytmp", b), ("yacc", ci)], writes=[("yacc", ci)])
            for ci in range(NTC):
                cs = slice(512 * ci, 512 * ci + 512)
                P.op("act", lambda e, dmt=dmt, cs=cs: e.activation(out=yb[:, dmt, cs], in_=yacc[:, cs], func=AF.Copy), reads=[("yacc", ci)], writes=[("yb", dmt, ci)])

        wo, lng = wbig, gbt
        for tt in range(T_ // 128):
            ts = slice(128 * tt, 128 * tt + 128)
            ci = tt // 4
            P.dma("sp", lambda e, ts=ts: [e.dma_start(out=big[0][:], in_=xtok_d[ts, :])], writes=["big0"])
            for half in range(2):
                pt, pres = C.next_pp()
                hs = slice(512 * half, 512 * half + 512)
                for kc in range(8):
                    P.op("pe", lambda e, pt=pt, kc=kc, ts=ts, hs=hs: e.matmul(pt[:, :], lhsT=yb[:, kc, ts], rhs=wo[:, kc, hs], start=(kc == 0), stop=(kc == 7)),
                         reads=["wbig"] + [("yb", kc, ci) for kc in range(8)], writes=[pres])
                P.op("dve", lambda e, pt=pt, hs=hs: e.scalar_tensor_tensor(out=big[1][:, hs], in0=big[0][:, hs], scalar=ALPHA, in1=pt[:, :], op0=ALU.mult, op1=ALU.add),
                     reads=["big0", pres], writes=["big1"])
            layer_norm(big[1], "big1", lng, "gbt", big[0], "big0", big[2], "big2")
            P.dma("sp", lambda e, ts=ts: [e.dma_start(out=out_d[ts, :], in_=big[0][:])], reads=["big0"], writes=[("out", tt)], lane="ost")
            C.out_res.append(("out", tt))
            if outT_d is not None:
                xTn = big[2][:, :].rearrange("p (k t) -> p k t", k=8)
                for half in range(2):
                    for k4 in range(4):
                        kc = 4 * half + k4
                        P.op("pe", lambda e, kc=kc, k4=k4: e.transpose(C.pm[:, 128 * k4:128 * k4 + 128], big[0][:, 128 * kc:128 * kc + 128], identf[:, :]),
                             reads=["big0", "ident"], writes=["pm"])
                    P.op("act", lambda e, half=half: e.activation(out=xTn[:, 4 * half:4 * half + 4, :], in_=C.pm[:, 0:512].rearrange("p (k t) -> p k t", k=4), func=AF.Copy),
                         reads=["pm"], writes=["big2"])
                P.dma("sp", lambda e, ts=ts: [e.dma_start(out=outT_d[:, :, ts], in_=xTn)], reads=["big2"], writes=[("outT", tt)], lane="ostT")

    for sh in shards:
        do_shard(sh)
    if standalone:
        P.op("sp", lambda e: None, reads=C.out_res)
        P.emit()
    return nc


DEPTH = 2
NH = 6


def build_fused(nc, depth=DEPTH, heads=NH, nshards=S // T_):
    C = Ctx(nc)
    P = C.P
    di = C.dram_in
    xT0 = di("xT0", [128, 8, S])
    xtok0 = di("xtok0", [S, 1024])
    mem = di("mem", [256, 1024])
    memg = di("memg", [128, 2, 1024])
    wfox, bfox = di("wfox", [DEPTH, NH, 2, 128, 8, 128]), di("bfox", [DEPTH, NH, 64, 4])
    wmoba, bmoba = di("wmoba", [DEPTH, NH, 2, 128, 8, 128]), di("bmoba", [DEPTH, NH, 64, 4])
    wdil, bdil = di("wdil", [DEPTH, NH, 4, 128, 8, 128]), di("bdil", [DEPTH, NH, 64, 8])
    wgdn, bgdn = di("wgdn", [DEPTH, NH, 2, 128, 8, 128]), di("bgdn", [DEPTH, NH, 64, 8])
    convw, scal, gnw = di("convw", [DEPTH, NH, 64, 12]), di("scal", [DEPTH, NH, 128, 2]), di("gnw", [DEPTH, 128, 64])
    wz, bz = di("wz", [DEPTH, 16, 128, 8, 128]), di("bz", [DEPTH, 128, 16])
    wm, bm = di("wm", [DEPTH, 40, 128, 8, 128]), di("bm", [DEPTH, 128, 40])
    weq, beq = di("weq", [DEPTH, 4, 128, 8, 128]), di("beq", [DEPTH, 96, 4])
    wkv = di("wkv", [DEPTH, 6, 128, 8, 128])
    wbr, wbe = di("wbr", [DEPTH, 128, 12, 1024]), di("wbe", [DEPTH, 96, 4, 1024])
    wo, lng = di("wo", [DEPTH, 8, 128, 8, 128]), di("lng", [DEPTH, 128, 2, 1024])
    ind, blkc = di("ind", [32, S]), di("blkc", [128, 3, 32, 32])
    out = C.dram_out("out", [S, 1024])
    oT_all = nc.dram_tensor("oT_all", [4, 384, S], F32).ap()
    xT1 = nc.dram_tensor("xT1", [128, 8, S], F32).ap()
    xtok1 = nc.dram_tensor("xtok1", [S, 1024], F32).ap()
    gscr = nc.dram_tensor("gdn_scr", [NH, 3, 64, S], F32).ap()
    xTb = nc.dram_tensor("xTb", [128, 8, S], BF16).ap()
    C.negc = C.load_const(di("negc", [128, 128]), [128, 128], "negc")
    C.band = C.load_const(di("band", [128, 256]), [128, 256], "band")
    C.tri = C.load_const(di("tri", [128, 3, 128]), [128, 3, 128], "tri")
    C.identf, C.ident_b = C.load_const(di("ident", [128, 128]), [128, 128], "ident", cast=BF16)
    C.pm_b = C.pm[:].bitcast(BF16)
    for l in range(depth):
        xT = xT0 if l == 0 else xT1
        xtok = xtok0 if l == 0 else xtok1
        with P.scope():
            C.new_scope()
            convert_xT(C, xT, xTb)
        for h in range(heads):
            osl = lambda n: oT_all[n, 64 * h:64 * h + 64, :]
            with P.scope():
                C.new_scope()
                build_moba(C=C, io={"xT": xT, "w": wmoba[l, h], "bias": bmoba[l, h], "ind": ind, "blkc": blkc, "out": osl(0)})
            with P.scope():
                C.new_scope()
                build_dil(C=C, io={"xT": xT, "xTb": xTb, "w": wdil[l, h], "bias": bdil[l, h], "out": osl(1)})
            with P.scope():
                C.new_scope()
                build_fox(C=C, io={"xT": xT, "xTb": xTb, "w": wfox[l, h], "bias": bfox[l, h], "out": osl(2)})
        with P.scope():
            C.new_scope()
            gdn_group(C, [{"xT": xT, "xTb": xTb, "w": wgdn[l, h], "bias": bgdn[l, h], "convw": convw[l, h], "scal": scal[l, h]} for h in range(heads)],
                      [gscr[h] for h in range(heads)], gnw[l], [oT_all[3, 64 * h:64 * h + 64, :] for h in range(heads)])
        last = (l == depth - 1)
        shards = []
        for s_ in range(nshards):
            ts = slice(T_ * s_, T_ * s_ + T_)
            shards.append({"xT": xT[:, :, ts], "xtok": xtok[ts, :], "oT_in": oT_all[:, :, ts], "out": (out if last else xtok1)[ts, :],
                           "outT": None if last else xT1[:, :, ts]})
        with P.scope():
            C.new_scope()
            build_merge(C=C, io={"shards": shards, "wz": wz[l], "bz": bz[l], "wm": wm[l], "bm": bm[l], "weq": weq[l], "beq": beq[l], "mem": mem, "memg": memg,
                                 "wkv": wkv[l], "wbr": wbr[l], "wbe": wbe[l], "wo": wo[l], "lng": lng[l]})
    P.barrier()
    P.op("sp", lambda e: None)
    print("fused ops:", len(P.ops), flush=True)
    P.emit()
    return nc


H = 6
BW = 384
OFF_A = 0
OFF_BQK = 1152
OFF_BV = 3456
OFF_C = 3840
OFF_CF = 4992
OFF_D = 4998
OFF_DBETA = 6150
OFF_DDEC = 6156
OFF_EQ = 6162
OFF_Z = 6546
OFF_M = 8466


def _xT_layout(xb):
    T = xb.shape[0]
    return np.ascontiguousarray(xb.T.reshape(8, 128, T).transpose(1, 0, 2))


def _w_layout(Wc):
    n = Wc.shape[1]
    return np.ascontiguousarray(Wc.reshape(8, 128, n).transpose(1, 0, 2))


def _w_tiles(Wc, bounds=None):
    n = Wc.shape[1]
    if bounds is None:
        bounds = [(c0, min(c0 + 128, n)) for c0 in range(0, n, 128)]
    out = np.zeros((len(bounds), 128, 8, 128), np.float32)
    for t, (a, b) in enumerate(bounds):
        out[t, :, :, 0:b - a] = Wc[:, a:b].reshape(8, 128, b - a).transpose(1, 0, 2)
    return out


def _consts():
    s_ = np.arange(128)[:, None]
    t_ = np.arange(128)[None, :]
    c = {}
    c["negc"] = np.where(t_ >= s_, 0.0, -30000.0).astype(np.float32)
    c["band"] = np.concatenate([np.where(t_ >= s_, 0.0, -30000.0), np.where(t_ <= s_, 0.0, -30000.0)], 1).astype(np.float32)
    c["ind"] = (np.arange(S)[None, :] // 256 == np.arange(32)[:, None]).astype(np.float32)
    qb = np.arange(32)[:, None]
    n = np.arange(32)[None, :]
    blk = np.stack([np.where(n < qb, 0.0, -1e9), (n < qb).astype(np.float32), (n == qb).astype(np.float32)]).astype(np.float32)
    c["blkc"] = np.ascontiguousarray(np.broadcast_to(blk[None], (128, 3, 32, 32)))
    c["ident"] = np.eye(128, dtype=np.float32)
    c["tri"] = np.stack([(s_ <= t_).astype(np.float32), np.where(t_ > s_, 0.0, -30000.0), np.where(s_ > t_, 0.0, -30000.0)], 1).astype(np.float32)
    return c


def _pack_weights(w_in, b_in, conv_w, a_log, dt_bias, gdn_norm_w, w_mem_kv, w_branch, w_out, ln_g, ln_b, mem_ln_g, mem_ln_b):
    f32 = np.float32
    L = w_in.shape[0]
    d = {k: [] for k in ("wfox", "bfox", "wmoba", "bmoba", "wdil", "bdil", "wgdn", "bgdn", "convw", "scal", "gnw",
                         "wz", "bz", "wm", "bm", "weq", "beq", "wkv", "wbr", "wbe", "wo", "lng")}
    rep2 = lambda a, b: np.ascontiguousarray(np.broadcast_to(np.stack([np.asarray(a, f32), np.asarray(b, f32)])[None], (128, 2, 1024)))
    for l in range(L):
        Wl = np.asarray(w_in[l], f32)
        bl = np.asarray(b_in[l], f32)
        cwl = np.asarray(conv_w[l], f32)
        per = {k: [] for k in ("wfox", "bfox", "wmoba", "bmoba", "wdil", "bdil", "wgdn", "bgdn", "convw", "scal")}
        for h in range(H):
            cols = np.concatenate([OFF_C + k * BW + 64 * h + np.arange(64) for k in range(3)] + [np.array([OFF_CF + h])])
            bias = np.zeros((64, 4), f32)
            for k in range(3):
                bias[:, k] = bl[cols[64 * k:64 * k + 64]]
            bias[0, 3] = bl[OFF_CF + h]
            per["wfox"].append(_w_tiles(Wl[:, cols]))
            per["bfox"].append(bias)
            cols = np.concatenate([OFF_A + k * BW + 64 * h + np.arange(64) for k in range(3)])
            bias = np.zeros((64, 4), f32)
            for k in range(3):
                bias[:, k] = bl[cols[64 * k:64 * k + 64]]
            per["wmoba"].append(_w_tiles(Wl[:, cols]))
            per["bmoba"].append(bias)
            segs = []
            for g in range(3):
                segs.append(OFF_BQK + ((0 * 3 + g) * H + h) * 64 + np.arange(64))
                segs.append(OFF_BQK + ((1 * 3 + g) * H + h) * 64 + np.arange(64))
            segs.append(OFF_BV + 64 * h + np.arange(64))
            cols = np.concatenate(segs)
            bias = np.zeros((64, 8), f32)
            for k in range(7):
                bias[:, k] = bl[cols[64 * k:64 * k + 64]]
            per["wdil"].append(_w_tiles(Wl[:, cols]))
            per["bdil"].append(bias)
            cols = np.concatenate([OFF_D + k * BW + 64 * h + np.arange(64) for k in range(3)] + [np.array([OFF_DBETA + h, OFF_DDEC + h])])
            bias = np.zeros((64, 8), f32)
            for k in range(3):
                bias[:, k] = bl[cols[64 * k:64 * k + 64]]
            bias[0, 3] = bl[OFF_DBETA + h]
            bias[0, 4] = bl[OFF_DDEC + h]
            cw = np.zeros((64, 12), f32)
            for i in range(3):
                for j in range(4):
                    cw[:, 4 * i + j] = cwl[j, i * BW + 64 * h:i * BW + 64 * h + 64]
            sc = np.zeros((128, 2), f32)
            sc[:, 0] = np.asarray(a_log[l], f32)[h]
            sc[:, 1] = np.asarray(dt_bias[l], f32)[h]
            per["wgdn"].append(_w_tiles(Wl[:, cols]))
            per["bgdn"].append(bias)
            per["convw"].append(cw)
            per["scal"].append(sc)
        for k in per:
            d[k].append(np.stack(per[k]))
        d["gnw"].append(np.ascontiguousarray(np.broadcast_to(np.asarray(gdn_norm_w[l], f32)[None], (128, 64))))
        bzv = bl[OFF_Z:OFF_Z + 1920]
        bmv = bl[OFF_M:OFF_M + 5120]
        beqv = bl[OFF_EQ:OFF_EQ + 384]
        bz = np.zeros((128, 16), f32)
        for i in range(12):
            bz[:, i] = bzv[128 * i:128 * i + 128]
        for h in range(4):
            bz[0:96, 12 + h] = bzv[1536 + 96 * h:1536 + 96 * h + 96]
        bm = np.zeros((128, 40), f32)
        for n in range(5):
            for dd in range(8):
                bm[:, 8 * n + dd] = bmv[1024 * n + 128 * dd:1024 * n + 128 * dd + 128]
        Wbr = np.asarray(w_branch[l], f32)
        d["wz"].append(_w_tiles(Wl[:, OFF_Z:OFF_Z + 1920], [(128 * i, 128 * i + 128) for i in range(12)] + [(1536 + 96 * i, 1536 + 96 * i + 96) for i in range(4)]))
        d["bz"].append(bz)
        d["wm"].append(_w_tiles(Wl[:, OFF_M:OFF_M + 5120]))
        d["bm"].append(bm)
        d["weq"].append(_w_tiles(Wl[:, OFF_EQ:OFF_EQ + 384], [(96 * i, 96 * i + 96) for i in range(4)]))
        d["beq"].append(np.ascontiguousarray(beqv.reshape(4, 96).T))
        d["wkv"].append(_w_tiles(np.asarray(w_mem_kv[l], f32)))
        d["wbr"].append(np.ascontiguousarray(Wbr[0:4].reshape(4, 3, 128, 1024).transpose(2, 0, 1, 3).reshape(128, 12, 1024)))
        d["wbe"].append(np.ascontiguousarray(Wbr[4].reshape(4, 96, 1024).transpose(1, 0, 2)))
        d["wo"].append(_w_tiles(np.asarray(w_out[l], f32)))
        d["lng"].append(rep2(ln_g[l], ln_b[l]))
    out = {k: np.ascontiguousarray(np.stack(v)).astype(f32) for k, v in d.items()}
    out["memg"] = rep2(mem_ln_g, mem_ln_b)
    return out


def kernel(x, mem, mem_ln_g, mem_ln_b, w_in, b_in, conv_w, a_log, dt_bias, gdn_norm_w, w_mem_kv, w_branch, w_out, ln_g, ln_b):
    f32 = np.float32
    x = np.asarray(x, f32)
    mem = np.asarray(mem, f32)
    B = x.shape[0]
    shared = _pack_weights(np.asarray(w_in), np.asarray(b_in), np.asarray(conv_w), np.asarray(a_log), np.asarray(dt_bias), np.asarray(gdn_norm_w),
                           np.asarray(w_mem_kv), np.asarray(w_branch), np.asarray(w_out), np.asarray(ln_g), np.asarray(ln_b), mem_ln_g, mem_ln_b)
    shared.update(_consts())
    nc = bass.Bass("TRN2", target_bir_lowering=False)
    build_fused(nc)
    per_batch = []
    for b in range(B):
        d = dict(shared)
        d["xT0"] = _xT_layout(x[b])
        d["xtok0"] = np.ascontiguousarray(x[b])
        d["mem"] = np.ascontiguousarray(mem[b])
        per_batch.append(d)
    n_cores = 8
    grp = n_cores // B
    res = run_bass_kernel_spmd(nc, [per_batch[c // grp] for c in range(n_cores)], core_ids=list(range(n_cores)))
    return np.stack([np.asarray(res.results[b * grp]["out"]) for b in range(B)]).astype(f32)
```

```python
import os
import numpy as np
from concourse.bass_utils import run_bass_kernel_spmd
import contextlib
import numpy as np
import concourse.bass as bass
import concourse.mybir as mybir

F32 = mybir.dt.float32
BF16 = mybir.dt.bfloat16
AF = mybir.ActivationFunctionType
ALU = mybir.AluOpType
AX = mybir.AxisListType

ENGS = ("pe", "act", "dve", "pool", "sp")


class _Op:
    __slots__ = ("eng", "fn", "reads", "writes", "dma", "lane", "n", "idx", "waits", "signal", "cnt")


class Prog:
    def __init__(self, nc, same_engine_raw=True):
        self.nc = nc
        self.ops = []
        self.last_w = {}
        self.readers = {}
        self.same_engine_raw = same_engine_raw
        self.es = contextlib.ExitStack()
        self.cur_es = self.es
        self._tn = 0
        self.bar_deps = ()
        self.bar_seen = set()
        self.last_on_eng = {}
        self.last_on_lane = {}

    def sbuf(self, shape, dtype, name=None):
        self._tn += 1
        return self.cur_es.enter_context(self.nc.sbuf_tensor(f"s{self._tn}_" + (name or "t"), list(shape), dtype))

    def psum(self, shape, dtype=F32, name=None):
        self._tn += 1
        return self.es.enter_context(self.nc.psum_tensor("p_" + (name or f"ps{self._tn}"), list(shape), dtype))

    @contextlib.contextmanager
    def scope(self):
        prev = self.cur_es
        with contextlib.ExitStack() as es:
            self.cur_es = es
            try:
                yield
            finally:
                self.cur_es = prev
        self.barrier()

    def barrier(self):
        self.bar_deps = tuple(self.last_on_eng.values()) + tuple(self.last_on_lane.values())
        self.bar_seen = set()

    def op(self, eng, fn, reads=(), writes=()):
        o = _Op()
        o.eng, o.fn, o.reads, o.writes = eng, fn, tuple(reads), tuple(writes)
        o.dma, o.lane, o.n = False, None, 1
        self._add(o)

    def dma(self, eng, fn, reads=(), writes=(), lane=None, n=1):
        o = _Op()
        o.eng, o.fn, o.reads, o.writes = eng, fn, tuple(reads), tuple(writes)
        o.dma, o.n = True, n
        o.lane = lane if lane is not None else (o.writes[0] if o.writes else o.reads[0])
        self._add(o)

    def _add(self, o):
        o.idx = len(self.ops)
        deps = set()
        for r in o.reads:
            w = self.last_w.get(r)
            if w is not None:
                deps.add(w)
        for r in o.writes:
            w = self.last_w.get(r)
            if w is not None:
                deps.add(w)
            for rd in self.readers.get(r, ()):
                deps.add(rd)
        if o.eng not in self.bar_seen:
            self.bar_seen.add(o.eng)
            deps.update(self.bar_deps)
        deps.discard(o.idx)
        self.last_on_eng[o.eng] = o.idx
        if o.dma:
            self.last_on_lane[o.lane] = o.idx
        o.waits = deps
        o.signal = False
        for r in o.reads:
            self.readers.setdefault(r, []).append(o.idx)
        for r in o.writes:
            self.last_w[r] = o.idx
            self.readers[r] = []
        self.ops.append(o)

    def emit(self, final_wait_eng="sp"):
        nc = self.nc
        ops = self.ops
        for o in ops:
            keep = set()
            for d in o.waits:
                p = ops[d]
                if p.dma:
                    keep.add(d)
                elif p.eng != o.eng or o.dma:
                    keep.add(d)
                else:
                    if self.same_engine_raw and o.eng != "pe":
                        if any(r in p.writes for r in o.reads):
                            keep.add(d)
            o.waits = keep
            for d in keep:
                ops[d].signal = True
        eng_cnt = {e: 0 for e in ENGS}
        lane_cnt = {}
        for o in ops:
            if o.dma:
                lane_cnt[o.lane] = lane_cnt.get(o.lane, 0) + 16 * o.n
                o.cnt = lane_cnt[o.lane]
            elif o.signal:
                eng_cnt[o.eng] += 1
                o.cnt = eng_cnt[o.eng]
            else:
                o.cnt = None
        lanes = list(lane_cnt.keys())
        es = self.es
        eng_sem = {e: es.enter_context(nc.semaphore(f"s_{e}")) for e in ENGS}
        lane_sem = {l: es.enter_context(nc.semaphore(f"l_{i}")) for i, l in enumerate(lanes)}
        self.n_sems = len(lanes) + len(ENGS)
        per_eng = {e: [] for e in ENGS}
        for o in ops:
            per_eng[o.eng].append(o)

        def run(e, engine):
            seen = {}
            for o in per_eng[e]:
                need = {}
                for d in o.waits:
                    p = ops[d]
                    if p.dma:
                        key = ("l", p.lane)
                        sem = lane_sem[p.lane]
                    else:
                        key = ("e", p.eng)
                        sem = eng_sem[p.eng]
                    if need.get(key, (None, -1))[1] < p.cnt:
                        need[key] = (sem, p.cnt)
                for key, (sem, val) in need.items():
                    if seen.get(key, -1) < val:
                        engine.wait_ge(sem, val)
                        seen[key] = val
                if o.dma:
                    instrs = o.fn(engine)
                    assert len(instrs) == o.n, (len(instrs), o.n)
                    for ins in instrs:
                        ins.then_inc(lane_sem[o.lane], 16)
                else:
                    ins = o.fn(engine)
                    if o.signal:
                        ins.then_inc(eng_sem[o.eng], 1)

        with nc.Block() as block:
            @block.tensor
            def _(eng):
                run("pe", eng)

            @block.scalar
            def _(eng):
                run("act", eng)

            @block.vector
            def _(eng):
                run("dve", eng)

            @block.gpsimd
            def _(eng):
                run("pool", eng)

            @block.sync
            def _(eng):
                run("sp", eng)
        self.es.close()


import types
import numpy as np

S = 8192
NCHK = 16
NEG = -30000.0


class Ctx:
    def __init__(self, nc):
        self.nc = nc
        self.P = Prog(nc)
        P = self.P
        self.pp = [P.psum([128, 512], F32, name=f"pp{i}") for i in range(2)]
        self.ps = [P.psum([128, 512], F32, name=f"pss{i}") for i in range(3)]
        self.po = [P.psum([128, 512], F32, name=f"po{i}") for i in range(2)]
        self.pm = P.psum([128, 512], F32, name="pm")
        self.ipp = 0
        self.ips = 0
        self.ipo = 0
        self.loc = types.SimpleNamespace()
        self.out_res = []
        self.ones_f = P.sbuf([128, 128], F32, name="ones_f")
        self.ones_b = P.sbuf([128, 128], BF16, name="ones_b")
        P.op("pool", lambda e: e.memset(self.ones_f[:], 1.0), writes=["ones_f"])
        P.op("pool", lambda e: e.memset(self.ones_b[:], 1.0), writes=["ones_b"])

    def new_scope(self):
        self.loc = types.SimpleNamespace()

    def dram_in(self, name, shape, dtype=F32):
        return self.nc.dram_tensor(name, list(shape), dtype, kind="ExternalInput").ap()

    def dram_out(self, name, shape, dtype=F32):
        return self.nc.dram_tensor(name, list(shape), dtype, kind="ExternalOutput").ap()

    def load_const(self, dram_ap, shape, name, dtype=F32, cast=None):
        P = self.P
        t = P.sbuf(shape, dtype, name=name)
        P.dma("sp", lambda e: [e.dma_start(out=t[:], in_=dram_ap)], writes=[name])
        if cast is not None:
            tb = P.sbuf(shape, cast, name=name + "_b")
            P.op("pool", lambda e: e.tensor_copy(out=tb[:], in_=t[:]), reads=[name], writes=[name + "_b"])
            return t, tb
        return t

    def next_pp(self):
        i = self.ipp
        self.ipp = (i + 1) % 2
        return self.pp[i], ("pp", i)

    def next_ps(self):
        i = self.ips
        self.ips = (i + 1) % 3
        return self.ps[i], ("ps", i)

    def next_po(self):
        i = self.ipo
        self.ipo = (i + 1) % 2
        return self.po[i], ("po", i)


def load_weights(C, w_d, ncols, name, wb=None):
    P = C.P
    if wb is None:
        wb = P.sbuf([128, 8, ncols], BF16, name=name)
    if not hasattr(C.loc, "lw_stg"):
        C.loc.lw_stg = [P.sbuf([128, 8, 128], F32, name=f"lw_stg{i}") for i in range(2)]
        C.loc.lw_k = 0
    stg = C.loc.lw_stg
    k = C.loc.lw_k
    for c0 in range(0, ncols, 128):
        n = min(128, ncols - c0)
        st = stg[k % 2]
        rs = ("lw_stg", k % 2)
        P.dma("act", lambda e, st=st, c0=c0, n=n: [e.dma_start(out=st[:, :, :], in_=w_d[c0 // 128])], writes=[rs])
        P.op("pool", lambda e, st=st, c0=c0, n=n: e.tensor_copy(out=wb[:, :, c0:c0 + n], in_=st[:, :, 0:n]), reads=[rs], writes=[name])
        k += 1
    C.loc.lw_k = k
    return wb


class _Shift:
    def __init__(self, t, off):
        self.t, self.off = t, off

    def __getitem__(self, key):
        ps, rest = key[0], key[1:]
        return self.t[(slice(ps.start + self.off, ps.stop + self.off),) + tuple(rest)]


def convert_xT(C, xT_d, xTb_d):
    P = C.P
    xs = [P.sbuf([128, 8, 512], F32, name=f"cv_xs{i}") for i in range(2)]
    xb = [P.sbuf([128, 8, 512], BF16, name=f"cv_xb{i}") for i in range(2)]
    for ci in range(NCHK):
        t0, b = ci * 512, ci % 2
        P.dma("sp", lambda e, b=b, t0=t0: [e.dma_start(out=xs[b][:, 0:4, :], in_=xT_d[:, 0:4, t0:t0 + 512]),
                                          e.dma_start(out=xs[b][:, 4:8, :], in_=xT_d[:, 4:8, t0:t0 + 512])], writes=[("cvxs", b)], n=2)
        P.op("pool", lambda e, b=b: e.tensor_copy(out=xb[b][:, 0:3, :], in_=xs[b][:, 0:3, :]), reads=[("cvxs", b)], writes=[("cvxb", b, 0)])
        P.op("dve", lambda e, b=b: e.tensor_copy(out=xb[b][:, 3:6, :], in_=xs[b][:, 3:6, :]), reads=[("cvxs", b)], writes=[("cvxb", b, 1)])
        P.op("act", lambda e, b=b: e.activation(out=xb[b][:, 6:8, :], in_=xs[b][:, 6:8, :], func=AF.Copy), reads=[("cvxs", b)], writes=[("cvxb", b, 2)])
        P.dma("act", lambda e, b=b, t0=t0: [e.dma_start(out=xTb_d[:, :, t0:t0 + 512], in_=xb[b][:, :, :])], reads=[("cvxb", b, k) for k in range(3)],
              writes=[("xTb", ci)], lane=("cvst", b))


def proj_pass(C, xT_d, wb, wname, specs, tag, hook=None, xTb_d=None):
    P = C.P
    use_b = xTb_d is not None and hook is None and all(len(sp) == 3 for sp in specs)
    if not hasattr(C.loc, "xb_bufs"):
        C.loc.xb_bufs = [P.sbuf([128, 8, 512], BF16, name=f"xb_{i}") for i in range(3 if use_b else 2)]
        if not use_b:
            C.loc.xs_bufs = [P.sbuf([128, 8, 512], F32, name=f"xs_{i}") for i in range(2)]
    xb = C.loc.xb_bufs
    xs = None if use_b else C.loc.xs_bufs
    tag = "x"
    if use_b:
        for ci in range(NCHK):
            t0 = ci * 512
            b = ci % 3
            rxb = ("xb", tag, b)
            P.dma("sp", lambda e, b=b, t0=t0: [e.dma_start(out=xb[b][:, 0:4, :], in_=xTb_d[:, 0:4, t0:t0 + 512]),
                                              e.dma_start(out=xb[b][:, 4:8, :], in_=xTb_d[:, 4:8, t0:t0 + 512])], writes=[(rxb, 0), (rxb, 1)], n=2,
                  lane=("xbl", b))
            for spec in specs:
                col0, M, evac = spec
                pt, pres = C.next_pp()
                for kc in range(8):
                    P.op("pe", lambda e, pt=pt, kc=kc, col0=col0, M=M, b=b: e.matmul(
                        pt[0:M, :], lhsT=wb[:, kc, col0:col0 + M], rhs=xb[b][:, kc, :], start=(kc == 0), stop=(kc == 7)),
                        reads=[wname, (rxb, 0), (rxb, 1)], writes=[pres])
                if callable(evac):
                    evac(pt, pres, ci, t0)
                else:
                    for off, fn in evac:
                        fn(pt if off == 0 else _Shift(pt, off), pres, ci, t0)
        return
    for ci in range(NCHK):
        t0 = ci * 512
        b = ci % 2
        rxs = ("xs", tag, b)
        rxb = ("xb", tag, b)
        P.dma("sp", lambda e, b=b, t0=t0: [e.dma_start(out=xs[b][:, 0:4, :], in_=xT_d[:, 0:4, t0:t0 + 512]),
                                          e.dma_start(out=xs[b][:, 4:8, :], in_=xT_d[:, 4:8, t0:t0 + 512])],
              writes=[rxs], n=2)
        P.op("pool", lambda e, b=b: e.tensor_copy(out=xb[b][:, 0:4, :], in_=xs[b][:, 0:4, :]), reads=[rxs], writes=[(rxb, 0)])
        P.op("dve", lambda e, b=b: e.tensor_copy(out=xb[b][:, 4:8, :], in_=xs[b][:, 4:8, :]), reads=[rxs], writes=[(rxb, 1)])
        if hook is not None:
            hook(ci, xs[b], rxs)
        for spec in specs:
            col0, M, evac = spec[0], spec[1], spec[2]
            pt, pres = C.next_pp()
            if len(spec) > 3:
                w32, w32_res = spec[3], spec[4]
                for kc in range(8):
                    P.op("pe", lambda e, pt=pt, kc=kc, M=M, b=b, w32=w32: e.matmul(
                        pt[0:M, :], lhsT=w32[:, kc, 0:M], rhs=xs[b][:, kc, :], start=(kc == 0), stop=(kc == 7)),
                        reads=[w32_res, rxs], writes=[pres])
            else:
                for kc in range(8):
                    P.op("pe", lambda e, pt=pt, kc=kc, col0=col0, M=M, b=b: e.matmul(
                        pt[0:M, :], lhsT=wb[:, kc, col0:col0 + M], rhs=xb[b][:, kc, :], start=(kc == 0), stop=(kc == 7)),
                        reads=[wname, (rxb, 0), (rxb, 1)], writes=[pres])
            if callable(evac):
                evac(pt, pres, ci, t0)
            else:
                for off, fn in evac:
                    fn(pt if off == 0 else _Shift(pt, off), pres, ci, t0)


def make_vprime(C, vT, vT_res, vp, vp_res, nblk, colsel, dh=64):
    P = C.P
    P.op("pool", lambda e: e.memset(vp[:, :, dh:dh + 1], 1.0), writes=[(vp_res, "ones")])
    pmb = C.pm_b
    G = 8
    for g0 in range(0, nblk, G):
        for k in range(G):
            blk = g0 + k
            P.op("pe", lambda e, blk=blk, k=k: e.transpose(pmb[:, k * dh:(k + 1) * dh], vT[0:dh, colsel(blk)], C.ident_b[0:dh, 0:dh]),
                 reads=vT_res(blk) + ["ident_b"], writes=["pm"])
        P.op("act", lambda e, g0=g0: e.activation(out=vp[:, g0:g0 + G, 0:dh], in_=pmb[:, 0:G * dh].rearrange("p (g d) -> p g d", g=G), func=AF.Copy),
             reads=["pm", (vp_res, "ones")], writes=[(vp_res, g0)])


def attn_block(C, S_mm, nq_lo, nq_hi, bias_ap, bias_res, masks, PV_mm, po_res, extra_reads=()):
    raise NotImplementedError


def causal_attention(C, qT, q_res, kT, k_res, KC, vp, vp_res, negc, bias_fn, out_cb, tag, LA=2):
    P = C.P
    NB = LA + 1
    pts = [P.sbuf([128, 512], BF16, name=f"pt_{tag}{i}") for i in range(NB)]
    tmps = [P.sbuf([128, 128], F32, name=f"tmp_{tag}{i}") for i in range(2)]
    st = {"itmp": 0}
    tasks = []
    pos = {}
    for I in range(NCHK):
        pos[I] = C.next_po()
        for j in range(4 * I + 4):
            tasks.append((I, j))

    def stage1(t, I, j):
        r = j - 4 * I
        lo = 128 * r if r > 0 else 0
        pst, ps_res = C.ps[t % 3], ("ps", t % 3)
        pt = pts[t % NB]
        pt_res = ("pt", tag, t % NB)
        q0 = 512 * I
        P.op("pe", lambda e: e.matmul(pst[:, lo:512], lhsT=kT[0:KC, 128 * j:128 * j + 128], rhs=qT[0:KC, q0 + lo:q0 + 512], start=True, stop=True),
             reads=q_res(I) + k_res(j), writes=[ps_res])
        b = bias_fn(I, j) if bias_fn is not None else None
        bkw = {} if b is None else {"bias": b[0]}
        brd = [] if b is None else [b[1]]
        if r >= 0:
            tmp = tmps[st["itmp"] % 2]
            tmp_res = ("tmp", tag, st["itmp"] % 2)
            st["itmp"] += 1
            P.op("dve", lambda e: e.tensor_tensor(out=tmp[:], in0=pst[:, lo:lo + 128], in1=negc[:], op=ALU.add), reads=[ps_res, "negc"], writes=[tmp_res])
            P.op("act", lambda e: e.activation(out=pt[:, lo:lo + 128], in_=tmp[:], func=AF.Exp, **bkw), reads=[tmp_res] + brd, writes=[(pt_res, "d")])
            if lo + 128 < 512:
                P.op("act", lambda e: e.activation(out=pt[:, lo + 128:512], in_=pst[:, lo + 128:512], func=AF.Exp, **bkw), reads=[ps_res] + brd, writes=[(pt_res, "o")])
        else:
            P.op("act", lambda e: e.activation(out=pt[:, :], in_=pst[:, :], func=AF.Exp, **bkw), reads=[ps_res] + brd, writes=[(pt_res, "d"), (pt_res, "o")])

    def stage2(t, I, j):
        r = j - 4 * I
        lo = 128 * r if r > 0 else 0
        nkb = 4 * I + 4
        po, po_res = pos[I]
        pt = pts[t % NB]
        pt_res = ("pt", tag, t % NB)
        P.op("pe", lambda e: e.matmul(po[0:65, lo:512], lhsT=vp[:, j, 0:65], rhs=pt[:, lo:512], start=(j == 0), stop=(j == nkb - 1), skip_group_check=True),
             reads=vp_res(j) + [(pt_res, "d"), (pt_res, "o")], writes=[po_res])
        if j == nkb - 1:
            out_cb(I, po, po_res)

    n = len(tasks)
    for t in range(n + LA):
        if t < n:
            stage1(t, *tasks[t])
        if t - LA >= 0:
            stage2(t - LA, *tasks[t - LA])


def norm_store(C, onum_pool, out_d, tag):
    P = C.P
    st = {"i": 0}

    def cb(I, po, po_res):
        i = st["i"] % 2
        st["i"] += 1
        on = onum_pool[i]
        on_res = ("onum", tag, i)
        P.op("act", lambda e: e.activation(out=on[0:65, :], in_=po[0:65, :], func=AF.Copy), reads=[po_res], writes=[on_res])
        P.op("dve", lambda e: e.reciprocal(out=on[64:65, :], in_=on[64:65, :]), reads=[on_res], writes=[on_res])
        P.op("pe", lambda e: e.matmul(C.pm[0:64, :], lhsT=C.ones_f[64:65, 0:64], rhs=on[64:65, :], start=True, stop=True),
             reads=["ones_f", on_res], writes=["pm"])
        P.op("dve", lambda e: e.tensor_tensor(out=on[0:64, :], in0=on[0:64, :], in1=C.pm[0:64, :], op=ALU.mult),
             reads=[on_res, "pm"], writes=[on_res])
        P.dma("sp", lambda e: [e.dma_start(out=out_d[:, 512 * I:512 * I + 512], in_=on[0:64, :])], reads=[on_res], writes=[("out", tag, I)],
              lane=("onum_st", tag, i))
        C.out_res.append(("out", tag, I))
    return cb


def build_fox(nc=None, C=None, io=None):
    standalone = C is None
    if standalone:
        C = Ctx(nc)
        io = {"xT": C.dram_in("xT", [128, 8, S]), "w": C.dram_in("w", [128, 8, 193]), "bias": C.dram_in("bias", [64, 4]),
              "out": C.dram_out("oT", [64, S])}
        C.negc = C.load_const(C.dram_in("negc", [128, 128]), [128, 128], "negc")
        C.identf, C.ident_b = C.load_const(C.dram_in("ident", [128, 128]), [128, 128], "ident", cast=BF16)
        C.pm_b = C.pm[:].bitcast(BF16)
    P = C.P
    xT_d, w_d, b_d, out_d = io["xT"], io["w"], io["bias"], io["out"]
    negc = C.negc
    bia = C.load_const(b_d, [64, 4], "bia")
    bq8 = P.sbuf([64, 1], F32, name="bq8")
    nbcf = P.sbuf([1, 1], F32, name="nbcf")
    P.op("dve", lambda e: e.tensor_scalar(out=bq8[:], in0=bia[:, 0:1], scalar1=0.125, scalar2=None, op0=ALU.mult), reads=["bia"], writes=["bq8"])
    P.op("dve", lambda e: e.tensor_scalar(out=nbcf[:], in0=bia[0:1, 3:4], scalar1=-1.0, scalar2=None, op0=ALU.mult), reads=["bia"], writes=["nbcf"])
    wb = load_weights(C, w_d, 193, "w")

    qC = P.sbuf([65, S], BF16, name="qC")
    kC = P.sbuf([65, S], BF16, name="kC")
    vT = P.sbuf([64, S], BF16, name="vT")
    Frow = P.sbuf([1, S], F32, name="Frow")
    erow = P.sbuf([1, 512], F32, name="erow")
    vp = P.sbuf([128, 64, 65], BF16, name="vp")

    def ev_q(pt, pres, ci, t0):
        P.op("act", lambda e: e.activation(out=qC[0:64, t0:t0 + 512], in_=pt[0:64, :], func=AF.Identity, bias=bq8[:, 0:1], scale=0.125),
             reads=[pres, "bq8"], writes=[("qC", ci)])

    def ev_k(pt, pres, ci, t0):
        P.op("act", lambda e: e.activation(out=kC[0:64, t0:t0 + 512], in_=pt[0:64, :], func=AF.Identity, bias=bia[:, 1:2], scale=1.0),
             reads=[pres, "bia"], writes=[("kC", ci)])

    def ev_v(pt, pres, ci, t0):
        P.op("act", lambda e: e.activation(out=vT[0:64, t0:t0 + 512], in_=pt[0:64, :], func=AF.Identity, bias=bia[:, 2:3], scale=1.0),
             reads=[pres, "bia"], writes=[("vT", ci)])

    def ev_cf(pt, pres, ci, t0):
        P.op("act", lambda e: e.activation(out=erow[:], in_=pt[0:1, :], func=AF.Exp, bias=nbcf[:, 0:1], scale=-1.0),
             reads=[pres, "nbcf"], writes=["erow"])
        P.op("act", lambda e: e.activation(out=erow[:], in_=erow[:], func=AF.Ln, bias=1.0, scale=1.0), reads=["erow"], writes=["erow"])
        init = 0.0 if ci == 0 else Frow[0:1, t0 - 1:t0]
        P.op("dve", lambda e: e.tensor_tensor_scan(out=Frow[0:1, t0:t0 + 512], data0=ones_row[0:1, :], data1=erow[:],
                                                  initial=init, op0=ALU.mult, op1=ALU.subtract),
             reads=["erow", "ones_row", "Frow"], writes=["Frow"])

    ones_row = P.sbuf([1, 512], F32, name="ones_row")
    P.op("pool", lambda e: e.memset(ones_row[:], 1.0), writes=["ones_row"])
    specs = [(0, 128, [(0, ev_q), (64, ev_k)]), (128, 65, [(0, ev_v), (64, ev_cf)])]
    proj_pass(C, xT_d, wb, "w", specs, "c", xTb_d=io.get("xTb"))
    P.op("pool", lambda e: e.memset(kC[64:65, :], 1.0), writes=["kC64"])
    for I in range(NCHK):
        P.op("dve", lambda e, I=I: e.tensor_scalar(out=qC[64:65, 512 * I:512 * I + 512], in0=Frow[0:1, 512 * I:512 * I + 512],
                                                   scalar1=Frow[0:1, 512 * I:512 * I + 1], scalar2=None, op0=ALU.subtract),
             reads=["Frow"], writes=["qC64"])
    for j in range(64):
        P.op("pe", lambda e, j=j: e.matmul(C.pm[0:128, j:j + 1], lhsT=Frow[0:1, 128 * j:128 * j + 128], rhs=C.ones_f[0:1, 0:1], start=True, stop=True),
             reads=["Frow", "ones_f"], writes=["pm"])
    P.op("pe", lambda e: e.matmul(C.pm[0:128, 64:80], lhsT=C.ones_f[0:1, 0:128], rhs=Frow[0:1, 0:S:512], start=True, stop=True),
         reads=["Frow", "ones_f"], writes=["pm"])
    Ftr = P.sbuf([128, 80], F32, name="Ftr")
    biasC = P.sbuf([128, 16, 64], F32, name="biasC")
    P.op("dve", lambda e: e.tensor_copy(out=Ftr[:], in_=C.pm[:, 0:80]), reads=["pm"], writes=["Ftr"])
    for I in range(NCHK):
        P.op("dve", lambda e, I=I: e.tensor_scalar(out=biasC[:, I, :], in0=Ftr[:, 0:64], scalar1=-1.0, scalar2=Ftr[:, 64 + I:65 + I], op0=ALU.mult, op1=ALU.add),
             reads=["Ftr"], writes=["biasC"])
    make_vprime(C, vT, lambda blk: [("vT", blk // 4)], vp, "vp", 64, lambda blk: slice(128 * blk, 128 * blk + 128))
    onum = [P.sbuf([65, 512], F32, name=f"onum{i}") for i in range(2)]
    cb = norm_store(C, onum, out_d, "c")
    causal_attention(C, qC, lambda I: [("qC", I), "qC64"], kC, lambda j: [("kC", j // 4), "kC64"], 65, vp,
                     lambda j: [("vp", (j // 8) * 8), ("vp", "ones")], negc,
                     lambda I, j: (biasC[:, I, j:j + 1], "biasC"), cb, "c")
    if standalone:
        P.op("sp", lambda e: None, reads=C.out_res)
        P.emit()
    return nc


def build_moba(nc=None, C=None, io=None):
    standalone = C is None
    if standalone:
        C = Ctx(nc)
        io = {"xT": C.dram_in("xT", [128, 8, S]), "w": C.dram_in("w", [2, 128, 8, 128]), "bias": C.dram_in("bias", [64, 4]),
              "ind": C.dram_in("ind", [32, S]), "blkc": C.dram_in("blkc", [128, 3, 32, 32]), "out": C.dram_out("oT", [64, S])}
        C.negc = C.load_const(C.dram_in("negc", [128, 128]), [128, 128], "negc")
        C.identf, C.ident_b = C.load_const(C.dram_in("ident", [128, 128]), [128, 128], "ident", cast=BF16)
        C.pm_b = C.pm[:].bitcast(BF16)
    P = C.P
    xT_d, w_d, b_d, ind_d, blk_d, out_d = io["xT"], io["w"], io["bias"], io["ind"], io["blkc"], io["out"]
    negc = C.negc
    bia = C.load_const(b_d, [64, 4], "bia")
    blkc = C.load_const(blk_d, [128, 3, 32, 32], "blkc")
    bq8 = P.sbuf([64, 1], F32, name="bq8")
    P.op("dve", lambda e: e.tensor_scalar(out=bq8[:], in0=bia[:, 0:1], scalar1=0.125, scalar2=None, op0=ALU.mult), reads=["bia"], writes=["bq8"])
    wb = load_weights(C, w_d, 192, "w")

    qs = P.sbuf([96, S], BF16, name="qs")
    ke = P.sbuf([96, S], BF16, name="ke")
    vT = P.sbuf([64, S], BF16, name="vT")
    vp = P.sbuf([128, 64, 65], BF16, name="vp")
    ksum = P.sbuf([64, 32], F32, name="ksum")
    kmT = P.sbuf([64, 32], BF16, name="kmT")
    indst = P.sbuf([32, 2048], F32, name="indst")
    for i4 in range(4):
        P.dma("act", lambda e, i4=i4: [e.dma_start(out=indst[:, :], in_=ind_d[:, 2048 * i4:2048 * i4 + 2048])], writes=["indst"])
        P.op("pool", lambda e, i4=i4: e.tensor_copy(out=ke[64:96, 2048 * i4:2048 * i4 + 2048], in_=indst[:]), reads=["indst"], writes=["ke_ind"])
    q32 = P.sbuf([64, S], F32, name="q32")
    w32q = P.sbuf([128, 8, 64], F32, name="w32q")
    w32k = P.sbuf([128, 8, 64], F32, name="w32k")
    P.dma("act", lambda e: [e.dma_start(out=w32q[:], in_=w_d[0][:, :, 0:64])], writes=["w32q"])
    P.dma("act", lambda e: [e.dma_start(out=w32k[:], in_=w_d[0][:, :, 64:128])], writes=["w32k"])
    xsum = P.sbuf([128, 8, 32], F32, name="xsum")
    km32 = P.sbuf([64, 32], F32, name="km32")

    def ev_q32(pt, pres, ci, t0):
        P.op("act", lambda e: e.activation(out=q32[0:64, t0:t0 + 512], in_=pt[0:64, :], func=AF.Identity, bias=bq8[:, 0:1], scale=0.125),
             reads=[pres, "bq8"], writes=[("q32", ci)])

    def xhook(ci, xs_t, rxs):
        P.op("dve", lambda e: e.tensor_reduce(out=xsum[:, :, 2 * ci:2 * ci + 2], in_=xs_t[:, :, :].rearrange("p k (a b) -> p k a b", a=2), axis=AX.X, op=ALU.add),
             reads=[rxs], writes=["xsum"])

    def ev_q(pt, pres, ci, t0):
        P.op("act", lambda e: e.activation(out=qs[0:64, t0:t0 + 512], in_=pt[0:64, :], func=AF.Identity, bias=bq8[:, 0:1], scale=0.125),
             reads=[pres, "bq8"], writes=[("qs", ci)])

    def ev_k(pt, pres, ci, t0):
        P.op("act", lambda e: e.activation(out=ke[0:64, t0:t0 + 512], in_=pt[0:64, :], func=AF.Identity, bias=bia[:, 1:2], scale=1.0),
             reads=[pres, "bia"], writes=[("ke", ci)])

    def ev_v(pt, pres, ci, t0):
        P.op("act", lambda e: e.activation(out=vT[0:64, t0:t0 + 512], in_=pt[0:64, :], func=AF.Identity, bias=bia[:, 2:3], scale=1.0),
             reads=[pres, "bia"], writes=[("vT", ci)])

    specs = [(0, 128, [(0, ev_q), (64, ev_k)]), (128, 64, ev_v), (0, 64, ev_q32, w32q, "w32q")]
    proj_pass(C, xT_d, wb, "w", specs, "a", hook=xhook)
    P.op("dve", lambda e: e.tensor_scalar(out=xsum[:], in0=xsum[:], scalar1=1.0 / 256.0, scalar2=None, op0=ALU.mult), reads=["xsum"], writes=["xsum"])
    for kc in range(8):
        P.op("pe", lambda e, kc=kc: e.matmul(C.pm[0:64, 0:32], lhsT=w32k[:, kc, 0:64], rhs=xsum[:, kc, 0:32], start=(kc == 0), stop=(kc == 7)),
             reads=["w32k", "xsum"], writes=["pm"])
    P.op("dve", lambda e: e.tensor_scalar(out=km32[:], in0=C.pm[0:64, 0:32], scalar1=bia[:, 1:2], scalar2=None, op0=ALU.add), reads=["pm", "bia"], writes=["km32"])
    NL = 4
    lane_banks = [C.pp[0], C.pp[1], C.po[0], C.po[1]]
    lane_res = [("pp", 0), ("pp", 1), ("po", 0), ("po", 1)]
    gms = [P.sbuf([128, 32], F32, name=f"gm{i}") for i in range(NL)]
    top8s = [P.sbuf([128, 8], F32, name=f"top8{i}") for i in range(NL)]
    selms = [P.sbuf([128, 32], F32, name=f"selm{i}") for i in range(NL)]
    stages = [P.sbuf([128, 96], BF16, name=f"stage{i}") for i in range(NL)]
    for i in range(NL):
        P.op("pool", lambda e, i=i: e.memset(stages[i][:], 0.0), writes=[("stage", i)])

    def gate_lane(ln):
        bank, bres = lane_banks[ln], lane_res[ln]
        bank_b = bank[:].bitcast(BF16)
        gm, top8, selm, stage = gms[ln], top8s[ln], selms[ln], stages[ln]
        rg, rt, rs, rst = ("gm", ln), ("top8", ln), ("selm", ln), ("stage", ln)

        def tile(ti):
            qb = ti // 2
            c0 = 128 * ti
            P.op("pe", lambda e: e.matmul(bank[0:128, 0:32], lhsT=q32[0:64, c0:c0 + 128], rhs=km32[0:64, 0:32], start=True, stop=True),
                 reads=[("q32", ti // 4), "km32"], writes=[bres])
            yield
            P.op("dve", lambda e: e.tensor_tensor(out=gm[:], in0=bank[0:128, 0:32], in1=blkc[:, 0, qb, :], op=ALU.add), reads=["blkc"], writes=[rg, bres])
            yield
            P.op("dve", lambda e: e.max(out=top8[:], in_=gm[:]), reads=[rg], writes=[rt])
            yield
            P.op("dve", lambda e: e.tensor_scalar(out=selm[:], in0=gm[:], scalar1=top8[:, 2:3], scalar2=None, op0=ALU.is_ge), reads=[rg, rt], writes=[rs])
            yield
            P.op("dve", lambda e: e.tensor_tensor(out=selm[:], in0=selm[:], in1=blkc[:, 1, qb, :], op=ALU.mult), reads=[rs, "blkc"], writes=[rs])
            yield
            P.op("dve", lambda e: e.tensor_tensor(out=selm[:], in0=selm[:], in1=blkc[:, 2, qb, :], op=ALU.add), reads=[rs, "blkc"], writes=[rs])
            yield
            P.op("dve", lambda e: e.tensor_scalar(out=stage[:, 64:96], in0=selm[:], scalar1=-1.0, scalar2=30000.0, op0=ALU.add, op1=ALU.mult), reads=[rs], writes=[rst])
            yield
            P.op("pe", lambda e: e.transpose(bank_b[0:96, 0:128], stage[:, 0:96], C.ident_b[:, :]), reads=[rst, "ident_b"], writes=[bres])
            yield
            P.op("act", lambda e: e.activation(out=qs[64:96, c0:c0 + 128], in_=bank_b[64:96, 0:128], func=AF.Copy), reads=[], writes=[("qsel", ti), bres])
            yield

        for ti in range(ln, 64, NL):
            yield from tile(ti)

    alive = [gate_lane(ln) for ln in range(NL)]
    while alive:
        nxt = []
        for gen in alive:
            try:
                next(gen)
                nxt.append(gen)
            except StopIteration:
                pass
        alive = nxt

    make_vprime(C, vT, lambda blk: [("vT", blk // 4)], vp, "vp", 64, lambda blk: slice(128 * blk, 128 * blk + 128))
    onum = [P.sbuf([65, 512], F32, name=f"onum{i}") for i in range(2)]
    cb = norm_store(C, onum, out_d, "a")
    causal_attention(C, qs, lambda I: [("qs", I)] + [("qsel", 4 * I + k) for k in range(4)], ke, lambda j: [("ke", j // 4), "ke_ind"], 96, vp,
                     lambda j: [("vp", (j // 8) * 8), ("vp", "ones")], negc, None, cb, "a")
    if standalone:
        P.op("sp", lambda e: None, reads=C.out_res)
        P.emit()
    return nc


DILS = (1, 4, 16)


def build_dil(nc=None, C=None, io=None):
    standalone = C is None
    if standalone:
        C = Ctx(nc)
        io = {"xT": C.dram_in("xT", [128, 8, S]), "w": C.dram_in("w", [128, 8, 448]), "bias": C.dram_in("bias", [64, 8]),
              "out": C.dram_out("oT", [64, S])}
        C.band = C.load_const(C.dram_in("band", [128, 256]), [128, 256], "band")
        C.identf, C.ident_b = C.load_const(C.dram_in("ident", [128, 128]), [128, 128], "ident", cast=BF16)
        C.pm_b = C.pm[:].bitcast(BF16)
    P = C.P
    xT_d, w_d, b_d, out_d = io["xT"], io["w"], io["bias"], io["out"]
    band = C.band
    bia = C.load_const(b_d, [64, 8], "bia")
    bq8 = P.sbuf([64, 8], F32, name="bq8")
    P.op("dve", lambda e: e.tensor_scalar(out=bq8[:], in0=bia[:], scalar1=0.125, scalar2=None, op0=ALU.mult), reads=["bia"], writes=["bq8"])
    wb = load_weights(C, w_d, 448, "w")

    qTs = [P.sbuf([64, S], BF16, name=f"qT{g}") for g in range(3)]
    kTs = [P.sbuf([64, S], BF16, name=f"kT{g}") for g in range(3)]
    vT = P.sbuf([64, S], BF16, name="vT")
    vp = P.sbuf([128, 64, 65], BF16, name="vp")
    acc = P.sbuf([65, S], F32, name="acc")
    pts = [P.sbuf([128, 256], BF16, name=f"ptb{i}") for i in range(3)]
    tmps = [P.sbuf([128, 256], F32, name=f"tmpb{i}") for i in range(3)]
    cnt = {"pt": 0}

    def mk_ev(g):
        def ev_q(pt, pres, ci, t0):
            P.op("act", lambda e: e.activation(out=qTs[g][0:64, t0:t0 + 512], in_=pt[0:64, :], func=AF.Identity, bias=bq8[:, 2 * g:2 * g + 1], scale=0.125),
                 reads=[pres, "bq8"], writes=[("qT", g, ci)])

        def ev_k(pt, pres, ci, t0):
            P.op("act", lambda e: e.activation(out=kTs[g][0:64, t0:t0 + 512], in_=pt[0:64, :], func=AF.Identity, bias=bia[:, 2 * g + 1:2 * g + 2], scale=1.0),
                 reads=[pres, "bia"], writes=[("kT", g, ci)])
        return ev_q, ev_k

    def ev_v(pt, pres, ci, t0):
        P.op("act", lambda e: e.activation(out=vT[0:64, t0:t0 + 512], in_=pt[0:64, :], func=AF.Identity, bias=bia[:, 6:7], scale=1.0),
             reads=[pres, "bia"], writes=[("vT", ci)])

    specs = []
    for g in range(3):
        eq_, ek_ = mk_ev(g)
        specs.append((128 * g, 128, [(0, eq_), (64, ek_)]))
    specs.append((384, 64, ev_v))
    proj_pass(C, xT_d, wb, "w", specs, "b", xTb_d=io.get("xTb"))

    for g, d in enumerate(DILS):
        L = S // d
        qT, kT = qTs[g], kTs[g]
        allq = [("qT", g, ci) for ci in range(NCHK)]
        allk = [("kT", g, ci) for ci in range(NCHK)]
        allv = [("vT", ci) for ci in range(NCHK)]
        nlb = L // 128

        def colsel(blk, d=d, nlb=nlb):
            r, lb = divmod(blk, nlb)
            st = r + d * 128 * lb
            return slice(st, st + d * 127 + 1, d)

        make_vprime(C, vT, (lambda blk: allv) if d > 1 else (lambda blk: [("vT", blk // 4)]), vp, ("vp", g), 64, colsel)
        vpres = [(("vp", g), g0) for g0 in range(0, 64, 8)] + [(("vp", g), "ones")]
        tasks = []
        chunks = []
        for r in range(d):
            for c0 in range(0, L, 512):
                ck = (r, c0, C.next_po())
                chunks.append(ck)
                cl = [c for c in range(-1, 4) if c0 + 128 * c >= 0]
                for c in cl:
                    tasks.append((ck, c, c == cl[-1]))

        def stage1(t, ck, c, last, qT=qT, kT=kT, d=d):
            r, c0, (po, po_res) = ck
            qst = r + d * c0
            kl = c0 + 128 * c
            lo = max(0, 128 * c)
            hi = min(512, 128 * c + 256)
            n = hi - lo
            mlo = 128 if c == -1 else 0
            pst, ps_res = C.ps[t % 3], ("ps", t % 3)
            i = t % 3
            pt, tmp = pts[i], tmps[i]
            pt_res, tmp_res = ("ptb", i), ("tmpb", i)
            kst = r + d * kl
            ksl = slice(kst, kst + d * 127 + 1, d)
            qsl = slice(qst + d * lo, qst + d * (hi - 1) + 1, d)
            P.op("pe", lambda e: e.matmul(pst[:, 0:n], lhsT=kT[0:64, ksl], rhs=qT[0:64, qsl], start=True, stop=True),
                 reads=(allq + allk) if d > 1 else [("qT", g, c0 // 512), ("kT", g, kl // 512)], writes=[ps_res])
            P.op("dve", lambda e: e.tensor_tensor(out=tmp[:, 0:n], in0=pst[:, 0:n], in1=band[:, mlo:mlo + n], op=ALU.add), reads=[ps_res, "band"], writes=[tmp_res])
            P.op("act", lambda e: e.activation(out=pt[:, 0:n], in_=tmp[:, 0:n], func=AF.Exp), reads=[tmp_res], writes=[pt_res])

        def stage2(t, ck, c, last, g=g, d=d, nlb=nlb, vpres=vpres):
            r, c0, (po, po_res) = ck
            qst = r + d * c0
            kl = c0 + 128 * c
            i = t % 3
            pt, pt_res = pts[i], ("ptb", i)
            vidx = r * nlb + kl // 128
            if c >= 0:
                first = (c == 0 and c0 == 0)
                P.op("pe", lambda e: e.matmul(po[0:65, 128 * c:128 * c + 128], lhsT=vp[:, vidx, 0:65], rhs=pt[:, 0:128], start=first, stop=True, skip_group_check=True),
                     reads=vpres + [pt_res], writes=[po_res])
            if c <= 2:
                off = 0 if c == -1 else 128
                P.op("pe", lambda e: e.matmul(po[0:65, 128 * (c + 1):128 * (c + 1) + 128], lhsT=vp[:, vidx, 0:65], rhs=pt[:, off:off + 128], start=True, stop=False, skip_group_check=True),
                     reads=vpres + [pt_res], writes=[po_res])
            if last:
                asl = slice(qst, qst + d * 511 + 1, d)
                if g == 0:
                    P.op("act", lambda e: e.activation(out=acc[0:65, asl], in_=po[0:65, :], func=AF.Copy), reads=[po_res], writes=["acc"])
                else:
                    P.op("dve", lambda e: e.tensor_tensor(out=acc[0:65, asl], in0=acc[0:65, asl], in1=po[0:65, :], op=ALU.add), reads=[po_res, "acc"], writes=["acc"])

        LA = 2
        nt = len(tasks)
        for t in range(nt + LA):
            if t < nt:
                stage1(t, *tasks[t])
            if t - LA >= 0:
                stage2(t - LA, *tasks[t - LA])
    for I in range(NCHK):
        sl = slice(512 * I, 512 * I + 512)
        P.op("dve", lambda e, sl=sl: e.reciprocal(out=acc[64:65, sl], in_=acc[64:65, sl]), reads=["acc"], writes=["acc"])
        P.op("pe", lambda e, sl=sl: e.matmul(C.pm[0:64, :], lhsT=C.ones_f[64:65, 0:64], rhs=acc[64:65, sl], start=True, stop=True),
             reads=["ones_f", "acc"], writes=["pm"])
        P.op("dve", lambda e, sl=sl: e.tensor_tensor(out=acc[0:64, sl], in0=acc[0:64, sl], in1=C.pm[0:64, :], op=ALU.mult),
             reads=["acc", "pm"], writes=["acc"])
        P.dma("sp", lambda e, sl=sl: [e.dma_start(out=out_d[:, sl], in_=acc[0:64, sl])], reads=["acc"], writes=[("out", I)], lane="ost")
        C.out_res.append(("out", I))
    if standalone:
        P.op("sp", lambda e: None, reads=C.out_res)
        P.emit()
    return nc


NC_ = 64


def gdn_phase1(C, io, hd, st8, scr):
    P = C.P
    xT_d, w_d, b_d, cw_d, sc_d = io["xT"], io["w"], io["bias"], io["convw"], io["scal"]
    tri = C.tri
    bia = C.load_const(b_d, [64, 8], "bia")
    cw = C.load_const(cw_d, [64, 12], "cw")
    scal = C.load_const(sc_d, [128, 2], "scal")
    wb = load_weights(C, w_d, 194, "w")
    Utri = tri[:, 0, :]
    nA = P.sbuf([1, 1], F32, name="nA")
    bdec = P.sbuf([1, 1], F32, name="bdec")
    P.op("act", lambda e: e.activation(out=nA[:], in_=scal[0:1, 0:1], func=AF.Exp), reads=["scal"], writes=["nA"])
    P.op("dve", lambda e: e.tensor_scalar(out=nA[:], in0=nA[:], scalar1=-1.0, scalar2=None, op0=ALU.mult), reads=["nA"], writes=["nA"])
    P.op("dve", lambda e: e.tensor_tensor(out=bdec[:], in0=bia[0:1, 4:5], in1=scal[0:1, 1:2], op=ALU.add), reads=["bia", "scal"], writes=["bdec"])

    stg3 = [[P.sbuf([64, 512], F32, name=f"qkvst{i}_{b}") for b in range(2)] for i in range(3)]
    brow = P.sbuf([1, 512], F32, name="brow")
    grow = P.sbuf([1, 512], F32, name="grow")
    pBG = C.po[0]
    raws = [P.sbuf([64, 515], F32, name=f"raw{i}") for i in range(3)]
    cys = [P.sbuf([64, 512], F32, name="cy")] * 3
    czs = [P.sbuf([64, 512], F32, name="cz")] * 2
    sqs = [P.sbuf([64, 512], F32, name="sq")] * 2
    rns = [P.sbuf([64, 512], F32, name="rn")] * 2
    nbanks = [(C.pm, "pm"), (C.pm, "pm")]
    erow = P.sbuf([1, 512], F32, name="erow")
    eps_t = P.sbuf([128, 1], F32, name="eps_t")
    P.op("pool", lambda e: e.memset(eps_t[:], 1e-6), writes=["eps_t"])
    for i in range(3):
        P.op("pool", lambda e, i=i: e.memset(raws[i][:, 0:3], 0.0), writes=[("raw", i)])

    def ev_qkv(i):
        def ev(pt, pres, ci, t0):
            dst = stg3[i][ci % 2]
            dres = ("qkvst", i, ci % 2)
            cy, rcy = cys[i], "cy"
            if i < 2:
                cz, sq, rn = czs[i], sqs[i], rns[i]
                rcz, rsq, rrn = "cz", "sq", "rn"
                nb, rnb = nbanks[i]
            raw = raws[i]
            rr = ("raw", i)
            P.op("act", lambda e: e.activation(out=raw[:, 3:515], in_=pt[0:64, :], func=AF.Identity, bias=bia[:, i:i + 1], scale=1.0),
                 reads=[pres, "bia", rr], writes=[rr])
            P.op("dve", lambda e: e.tensor_scalar(out=cy[:], in0=raw[:, 0:512], scalar1=cw[:, 4 * i:4 * i + 1], scalar2=None, op0=ALU.mult),
                 reads=[rr, "cw"], writes=[rcy])
            for j in range(1, 4):
                P.op("dve", lambda e, j=j: e.scalar_tensor_tensor(out=cy[:], in0=raw[:, j:j + 512], scalar=cw[:, 4 * i + j:4 * i + j + 1], in1=cy[:],
                                                                 op0=ALU.mult, op1=ALU.add), reads=[rr, "cw", rcy], writes=[rcy])
            P.op("dve", lambda e: e.tensor_copy(out=raw[:, 0:3], in_=raw[:, 512:515]), reads=[rr], writes=[rr])
            if i == 2:
                P.op("act", lambda e: e.activation(out=dst[:, :], in_=cy[:], func=AF.Silu), reads=[rcy], writes=[dres])
                P.dma("sp", lambda e: [e.dma_start(out=scr[i, :, t0:t0 + 512], in_=dst[:, :])], reads=[dres], writes=[("scr", hd, i, ci)], lane=("qkvst", i, ci % 2))
                return
            P.op("act", lambda e: e.activation(out=cz[:], in_=cy[:], func=AF.Silu), reads=[rcy], writes=[rcz])
            P.op("pool", lambda e: e.tensor_tensor(out=sq[:], in0=cz[:], in1=cz[:], op=ALU.mult), reads=[rcz], writes=[rsq])
            P.op("pe", lambda e: e.matmul(nb[0:64, :], lhsT=C.ones_f[0:64, 0:64], rhs=sq[:], start=True, stop=True), reads=["ones_f", rsq], writes=[rnb])
            P.op("act", lambda e: e.activation(out=rn[:], in_=nb[0:64, :], func=AF.Ln, bias=eps_t[0:64, 0:1], scale=1.0), reads=[rnb, "eps_t"], writes=[rrn])
            P.op("act", lambda e: e.activation(out=rn[:], in_=rn[:], func=AF.Exp, scale=-0.5), reads=[rrn], writes=[rrn])
            if i == 0:
                P.op("dve", lambda e: e.scalar_tensor_tensor(out=dst[:, :], in0=cz[:], scalar=0.125, in1=rn[:], op0=ALU.mult, op1=ALU.mult),
                     reads=[rcz, rrn], writes=[dres])
            else:
                P.op("dve", lambda e: e.tensor_tensor(out=dst[:, :], in0=cz[:], in1=rn[:], op=ALU.mult), reads=[rcz, rrn], writes=[dres])
            P.dma("sp", lambda e: [e.dma_start(out=scr[i, :, t0:t0 + 512], in_=dst[:, :])], reads=[dres], writes=[("scr", hd, i, ci)], lane=("qkvst", i, ci % 2))
        return ev

    def ev_beta(pt, pres, ci, t0):
        P.op("act", lambda e: e.activation(out=brow[0:1, :], in_=pt[0:1, :], func=AF.Sigmoid, bias=bia[0:1, 3:4], scale=1.0),
             reads=[pres, "bia"], writes=["brow"])
        for j in range(4):
            P.op("pe", lambda e, j=j: e.matmul(pBG[0:128, 4 * ci + j:4 * ci + j + 1], lhsT=brow[0:1, 128 * j:128 * j + 128], rhs=C.ones_f[0:1, 0:1], start=True, stop=True),
                 reads=["brow", "ones_f"], writes=["pBG"])

    def ev_dec(pt, pres, ci, t0):
        P.op("act", lambda e: e.activation(out=erow[:], in_=pt[0:1, :], func=AF.Exp, bias=bdec[0:1, 0:1], scale=1.0), reads=[pres, "bdec"], writes=["erow"])
        P.op("act", lambda e: e.activation(out=erow[:], in_=erow[:], func=AF.Ln, bias=1.0, scale=1.0), reads=["erow"], writes=["erow"])
        P.op("dve", lambda e: e.tensor_scalar(out=grow[0:1, :], in0=erow[:], scalar1=nA[0:1, 0:1], scalar2=None, op0=ALU.mult),
             reads=["erow", "nA"], writes=["grow"])
        for j in range(4):
            P.op("pe", lambda e, j=j: e.matmul(pBG[0:128, 64 + 4 * ci + j:64 + 4 * ci + j + 1], lhsT=grow[0:1, 128 * j:128 * j + 128], rhs=C.ones_f[0:1, 0:1], start=True, stop=True),
                 reads=["grow", "ones_f"], writes=["pBG"])

    specs = [(0, 128, [(0, ev_qkv(0)), (64, ev_qkv(1))]), (128, 64, ev_qkv(2)), (192, 1, ev_beta), (193, 1, ev_dec)]
    proj_pass(C, xT_d, wb, "w", specs, "d", xTb_d=io.get("xTb"))

    betaT, gT, gcT, gl, egc, e2, egl, be = [st8[:, k, :] for k in range(8)]
    P.op("dve", lambda e: e.tensor_copy(out=betaT, in_=pBG[:, 0:64]), reads=["pBG"], writes=[("betaT", hd)])
    P.op("dve", lambda e: e.tensor_copy(out=gT, in_=pBG[:, 64:128]), reads=["pBG"], writes=[("gT", hd)])
    P.op("pe", lambda e: e.matmul(C.pm[:, 0:NC_], lhsT=Utri, rhs=gT, start=True, stop=True), reads=["tri", ("gT", hd)], writes=["pm"])
    P.op("dve", lambda e: e.tensor_copy(out=gcT, in_=C.pm[:, 0:NC_]), reads=["pm"], writes=[("gcT", hd)])
    P.op("pe", lambda e: e.matmul(C.pm[:, 64:64 + NC_], lhsT=C.ones_f[:, :], rhs=gT, start=True, stop=True), reads=["ones_f", ("gT", hd)], writes=["pm"])
    P.op("dve", lambda e: e.tensor_copy(out=gl, in_=C.pm[:, 64:64 + NC_]), reads=["pm"], writes=[("gl", hd)])
    P.op("act", lambda e: e.activation(out=egc, in_=gcT, func=AF.Exp), reads=[("gcT", hd)], writes=[("egc", hd)])
    P.op("act", lambda e: e.activation(out=egl, in_=gl, func=AF.Exp), reads=[("gl", hd)], writes=[("egl", hd)])
    P.op("dve", lambda e: e.tensor_tensor(out=e2, in0=gl, in1=gcT, op=ALU.subtract), reads=[("gl", hd), ("gcT", hd)], writes=[("e2", hd)])
    P.op("act", lambda e: e.activation(out=e2, in_=e2, func=AF.Exp), reads=[("e2", hd)], writes=[("e2", hd)])
    P.op("dve", lambda e: e.tensor_tensor(out=be, in0=betaT, in1=egc, op=ALU.mult), reads=[("betaT", hd), ("egc", hd)], writes=[("be", hd)])


def psum_slot(C, k):
    banks = C.pp + C.ps + C.po + [C.pm]
    return banks[k // 4][:, 128 * (k % 4):128 * (k % 4) + 128], ("slot", k)


def gdn_chunk_gen(C, hd, st8, scr, gnw, out_d, slots, T):
    P = C.P
    tri = C.tri
    identf = C.identf
    Utri, negSU, negSL = tri[:, 0, :], tri[:, 1, :], tri[:, 2, :]
    betaT, gT, gcT, gl, egc, e2, egl, be = [st8[:, k, :] for k in range(8)]
    R = lambda nm: (nm, hd)
    (sA, rA), (sB, rB), (sC, rC), (sD, rD), (sE, rE) = slots
    bank_keys = {rA, rB, rC, rD, rE}
    _P = P

    class _W:
        def op(self, eng, fn, reads=(), writes=()):
            rd = [r for r in reads if r not in bank_keys]
            wr = list(writes) + [r for r in reads if r in bank_keys]
            _P.op(eng, fn, reads=rd, writes=wr)

        def dma(self, *a, **k):
            _P.dma(*a, **k)
    P = _W()
    kv_t, kb_t, R32, GU, T1, T2, decST, decS, decTI, kbf = T["kv_t"], T["kb_t"], T["R32"], T["GU"], T["T1"], T["T2"], T["decST"], T["decS"], T["decTI"], T["kbf"]
    Mk, Nk, wf, u32, qkT, kdec_t = T["Mk"], T["Nk"], T["wf"], T["u32"], T["qkT"], T["kdec_t"]
    S32, v32, tmpB, o32, osq, ss, on32, ostage, qkv = T["S32"], T["v32"], T["tmpB"], T["o32"], T["osq"], T["ss"], T["on32"], T["ostage"], T["qkv"]
    eps_t = C.eps6
    P.op("pool", lambda e: e.memset(S32[:], 0.0), writes=[R("S32")])

    def chunk(c):
        cs = slice(128 * c, 128 * c + 128)
        cc = slice(c, c + 1)
        b2 = c % 2
        qc, kc, vc = qkv[b2][:, 0, :], qkv[b2][:, 1, :], qkv[b2][:, 2, :]
        rin = ("qkvc", hd, b2)
        P.dma("sp", lambda e: [e.dma_start(out=qkv[b2][:, :, :], in_=scr[:, :, cs].rearrange("i d t -> d i t"))], writes=[rin])
        yield
        P.op("pe", lambda e: e.transpose(sA[:, 0:64], kc, identf[0:64, 0:64]), reads=[rin, "ident"], writes=[rA])
        P.op("pe", lambda e: e.transpose(sA[:, 64:128], vc, identf[0:64, 0:64]), reads=[rin, "ident"], writes=[rA])
        yield
        P.op("act", lambda e: e.activation(out=kv_t[:], in_=sA[:, 0:128], func=AF.Copy), reads=[rA], writes=[R("kv_t")])
        yield
        P.op("dve", lambda e: e.tensor_scalar(out=kb_t[:], in0=kv_t[:, 0:64], scalar1=betaT[:, cc], scalar2=None, op0=ALU.mult), reads=[R("kv_t"), R("betaT")], writes=[R("kb_t")])
        P.op("pool", lambda e: e.tensor_scalar(out=kdec_t[b2][:], in0=kv_t[:, 0:64], scalar1=e2[:, cc], scalar2=None, op0=ALU.mult), reads=[R("kv_t"), R("e2")], writes=[("kdec_t", hd, b2)])
        P.op("dve", lambda e: e.tensor_scalar(out=R32[:, 0:64], in0=kv_t[:, 64:128], scalar1=betaT[:, cc], scalar2=None, op0=ALU.mult), reads=[R("kv_t"), R("betaT")], writes=[R("R32")])
        P.op("dve", lambda e: e.tensor_scalar(out=R32[:, 64:128], in0=kv_t[:, 0:64], scalar1=be[:, cc], scalar2=None, op0=ALU.mult), reads=[R("kv_t"), R("be")], writes=[R("R32")])
        P.op("pool", lambda e: e.tensor_scalar(out=GU[:], in0=Utri, scalar1=gT[:, cc], scalar2=None, op0=ALU.mult), reads=["tri", R("gT")], writes=[R("GU")])
        yield
        P.op("pe", lambda e: e.transpose(sA[0:64, 0:128], kb_t[:, 0:64], identf[:, :]), reads=[R("kb_t"), "ident"], writes=[rA])
        P.op("pe", lambda e: e.matmul(sB[:, 0:128], lhsT=C.ones_f[:, :], rhs=GU[:], start=True, stop=True), reads=["ones_f", R("GU")], writes=[rB])
        yield
        P.op("act", lambda e: e.activation(out=kbf[:], in_=sA[0:64, 0:128], func=AF.Copy), reads=[rA], writes=[R("kbf")])
        P.op("dve", lambda e: e.scalar_tensor_tensor(out=T1[:], in0=sB[:, 0:128], scalar=gcT[:, cc], in1=negSU, op0=ALU.subtract, op1=ALU.add),
             reads=[rB, R("gcT"), "tri"], writes=[R("T1")])
        P.op("dve", lambda e: e.scalar_tensor_tensor(out=T2[:], in0=sB[:, 0:128], scalar=gcT[:, cc], in1=negSL, op0=ALU.subtract, op1=ALU.subtract),
             reads=[rB, R("gcT"), "tri"], writes=[R("T2")])
        yield
        P.op("act", lambda e: e.activation(out=decST[:], in_=T1[:], func=AF.Exp), reads=[R("T1")], writes=[R("decST")])
        P.op("act", lambda e: e.activation(out=decS[:], in_=T2[:], func=AF.Exp, scale=-1.0), reads=[R("T2")], writes=[R("decS")])
        P.op("pe", lambda e: e.matmul(sC[:, 0:128], lhsT=kbf[:, :], rhs=kc, start=True, stop=True), reads=[R("kbf"), rin], writes=[rC])
        P.op("pe", lambda e: e.matmul(sD[:, 0:128], lhsT=kc, rhs=kbf[:, :], start=True, stop=True), reads=[R("kbf"), rin], writes=[rD])
        P.op("pe", lambda e: e.matmul(sE[:, 0:128], lhsT=kc, rhs=qc, start=True, stop=True), reads=[rin], writes=[rE])
        yield
        P.op("pool", lambda e: e.tensor_tensor(out=decTI[:], in0=decST[:], in1=identf[:], op=ALU.add), reads=[R("decST"), "ident"], writes=[R("decTI")])
        P.op("dve", lambda e: e.tensor_tensor(out=Mk[0][:], in0=sC[:, 0:128], in1=decS[:], op=ALU.mult), reads=[rC, R("decS")], writes=[("Mk", hd, 0)])
        P.op("dve", lambda e: e.tensor_tensor(out=Nk[0][:], in0=sD[:, 0:128], in1=decST[:], op=ALU.mult), reads=[rD, R("decST")], writes=[("Nk", hd, 0)])
        yield
        P.op("dve", lambda e: e.tensor_tensor(out=qkT[b2][:], in0=sE[:, 0:128], in1=decTI[:], op=ALU.mult), reads=[rE, R("decTI")], writes=[("qkT", hd, b2)])
        for k in range(1, 7):
            P.op("pe", lambda e, k=k: e.matmul(sD[:, 0:128], lhsT=Mk[k - 1][:], rhs=Nk[k - 1][:], start=True, stop=True), reads=[("Mk", hd, k - 1), ("Nk", hd, k - 1)], writes=[rD])
            if k < 6:
                P.op("pe", lambda e, k=k: e.matmul(sC[:, 0:128], lhsT=Nk[k - 1][:], rhs=Mk[k - 1][:], start=True, stop=True), reads=[("Mk", hd, k - 1), ("Nk", hd, k - 1)], writes=[rC])
            yield
            P.op("act", lambda e, k=k: e.activation(out=Nk[k][:], in_=sD[:, 0:128], func=AF.Copy), reads=[rD], writes=[("Nk", hd, k)])
            if k < 6:
                P.op("dve", lambda e, k=k: e.tensor_copy(out=Mk[k][:], in_=sC[:, 0:128]), reads=[rC], writes=[("Mk", hd, k)])
            yield
        for k in range(6, -1, -1):
            P.op("pe", lambda e, k=k: e.matmul(sB[:, 0:128], lhsT=Nk[k][:], rhs=R32[:], start=True, stop=True), reads=[("Nk", hd, k), R("R32")], writes=[rB])
            yield
            op = ALU.add if k > 0 else ALU.subtract
            P.op("dve", lambda e, op=op: e.tensor_tensor(out=R32[:], in0=R32[:], in1=sB[:, 0:128], op=op), reads=[rB, R("R32")], writes=[R("R32")])
            yield
        P.op("pool", lambda e: e.tensor_copy(out=u32[b2][:], in_=R32[:, 0:64]), reads=[R("R32")], writes=[("u32", hd, b2)])
        P.op("pe", lambda e: e.transpose(sA[0:64, 0:128], R32[:, 64:128], identf[:, :]), reads=[R("R32"), "ident"], writes=[rA])
        yield
        P.op("act", lambda e: e.activation(out=wf[b2][:], in_=sA[0:64, 0:128], func=AF.Copy), reads=[rA], writes=[("wf", hd, b2)])
        yield
        P.op("pe", lambda e: e.matmul(sE[:, 0:64], lhsT=wf[b2][:, :], rhs=S32[:, :], start=True, stop=True), reads=[("wf", hd, b2), R("S32")], writes=[rE])
        P.op("pe", lambda e: e.matmul(sC[:, 0:64], lhsT=qc, rhs=S32[:, :], start=True, stop=True), reads=[rin, R("S32")], writes=[rC])
        yield
        P.op("dve", lambda e: e.tensor_tensor(out=v32[:], in0=u32[b2][:], in1=sE[:, 0:64], op=ALU.subtract), reads=[("u32", hd, b2), rE], writes=[R("v32")])
        yield
        P.op("pe", lambda e: e.matmul(sC[:, 64:128], lhsT=qkT[b2][:, :], rhs=v32[:, :], start=True, stop=True), reads=[("qkT", hd, b2), R("v32")], writes=[rC])
        P.op("pe", lambda e: e.matmul(sE[0:64, 64:128], lhsT=kdec_t[b2][:, :], rhs=v32[:, :], start=True, stop=True), reads=[("kdec_t", hd, b2), R("v32")], writes=[rE])
        yield
        P.op("dve", lambda e: e.scalar_tensor_tensor(out=S32[:], in0=S32[:], scalar=egl[0:64, cc], in1=sE[0:64, 64:128], op0=ALU.mult, op1=ALU.add),
             reads=[R("S32"), R("egl"), rE], writes=[R("S32")])
        P.op("act", lambda e: e.activation(out=tmpB[:], in_=sC[:, 64:128], func=AF.Copy), reads=[rC], writes=[R("tmpB")])
        yield
        P.op("dve", lambda e: e.scalar_tensor_tensor(out=o32[:], in0=sC[:, 0:64], scalar=egc[:, cc], in1=tmpB[:], op0=ALU.mult, op1=ALU.add),
             reads=[rC, R("egc"), R("tmpB")], writes=[R("o32")])
        yield
        P.op("pool", lambda e: e.tensor_tensor(out=osq[:], in0=o32[:], in1=o32[:], op=ALU.mult), reads=[R("o32")], writes=[R("osq")])
        yield
        P.op("dve", lambda e: e.tensor_reduce(out=ss[:], in_=osq[:], axis=AX.X, op=ALU.add), reads=[R("osq")], writes=[R("ss")])
        yield
        P.op("act", lambda e: e.activation(out=ss[:], in_=ss[:], func=AF.Ln, bias=eps_t[:, 0:1], scale=1.0 / 64.0), reads=[R("ss"), "eps6"], writes=[R("ss")])
        P.op("act", lambda e: e.activation(out=ss[:], in_=ss[:], func=AF.Exp, scale=-0.5), reads=[R("ss")], writes=[R("ss")])
        yield
        P.op("dve", lambda e: e.scalar_tensor_tensor(out=on32[:], in0=o32[:], scalar=ss[:, 0:1], in1=gnw[:], op0=ALU.mult, op1=ALU.mult), reads=[R("o32"), R("ss"), "gnw"], writes=[R("on32")])
        yield
        og = (c // 4) % 2
        P.op("pe", lambda e: e.transpose(sD[0:64, 0:128], on32[:, 0:64], identf[:, :]), reads=[R("on32"), "ident"], writes=[rD])
        yield
        P.op("act", lambda e: e.activation(out=ostage[og][:, 128 * (c % 4):128 * (c % 4) + 128], in_=sD[0:64, 0:128], func=AF.Copy),
             reads=[rD], writes=[("ostage", hd, og)])
        if c % 4 == 3:
            I = c // 4
            P.dma("sp", lambda e: [e.dma_start(out=out_d[:, 512 * I:512 * I + 512], in_=ostage[og][:, :])], reads=[("ostage", hd, og)], writes=[("gout", hd, I)],
                  lane=("gost", hd, og))
            C.out_res.append(("gout", hd, I))
        yield

    import os
    lim = int(os.environ.get("GDN_LIM", "1000000000"))
    cnt = 0
    for c in range(int(os.environ.get("GDN_NCH", NC_))):
        for _ in chunk(c):
            cnt += 1
            if cnt >= lim:
                return
            yield


def gdn_alloc_tiles(P, hd):
    sb = lambda shape, nm: P.sbuf(shape, F32, name=f"{nm}_{hd}")
    T = {}
    for nm in ("kv_t", "R32", "GU", "T1", "T2", "decST", "decS", "decTI"):
        T[nm] = sb([128, 128], nm)
    T["kb_t"] = sb([128, 64], "kb_t")
    T["kbf"] = sb([64, 128], "kbf")
    T["kdec_t"] = [sb([128, 64], f"kdec{i}") for i in range(2)]
    T["Mk"] = [sb([128, 128], f"Mk{i}") for i in range(7)]
    T["Nk"] = [sb([128, 128], f"Nk{i}") for i in range(7)]
    T["wf"] = [sb([64, 128], f"wf{i}") for i in range(2)]
    T["u32"] = [sb([128, 64], f"u32{i}") for i in range(2)]
    T["qkT"] = [sb([128, 128], f"qkT{i}") for i in range(2)]
    T["S32"] = sb([64, 64], "S32")
    for nm in ("v32", "tmpB", "o32", "osq", "on32"):
        T[nm] = sb([128, 64], nm)
    T["ss"] = sb([128, 1], "ss")
    T["ostage"] = [sb([64, 512], f"ostage{i}") for i in range(2)]
    T["qkv"] = [sb([64, 3, 128], f"qkvc{i}") for i in range(2)]
    return T


def gdn_group(C, ios, scrs, gnw_d, outs):
    import itertools
    P = C.P
    G = len(ios)
    C.eps6 = P.sbuf([128, 1], F32, name="eps6")
    P.op("pool", lambda e: e.memset(C.eps6[:], 1e-6), writes=["eps6"])
    st8s = [P.sbuf([128, 8, NC_], F32, name=f"st8_{g}") for g in range(G)]
    gnw = C.load_const(gnw_d, [128, 64], "gnw")
    for g in range(G):
        with P.scope():
            C.new_scope()
            gdn_phase1(C, ios[g], g, st8s[g], scrs[g])
    assert G <= 8
    Ts = [gdn_alloc_tiles(P, g) for g in range(G)]
    banks = C.pp + C.ps + C.po + [C.pm]

    def slots_for(g):
        bk, key = banks[g], ("gbank", g)
        sl = [(bk[:, 128 * k:128 * k + 128], key) for k in range(4)]
        return sl + [sl[0]]

    gens = [gdn_chunk_gen(C, g, st8s[g], scrs[g], gnw, outs[g], slots_for(g), Ts[g]) for g in range(G)]
    alive = list(gens)
    while alive:
        nxt = []
        for gen in alive:
            try:
                next(gen)
                nxt.append(gen)
            except StopIteration:
                pass
        alive = nxt


def build_gdn(nc=None, C=None, io=None):
    C = Ctx(nc)
    io = {"xT": C.dram_in("xT", [128, 8, S]), "w": C.dram_in("w", [2, 128, 8, 128]), "bias": C.dram_in("bias", [64, 8]),
          "convw": C.dram_in("convw", [64, 12]), "scal": C.dram_in("scal", [128, 2])}
    gnw_d = C.dram_in("gnw", [128, 64])
    out_d = C.dram_out("oT", [64, S])
    C.tri = C.load_const(C.dram_in("tri", [128, 3, 128]), [128, 3, 128], "tri")
    C.identf, C.ident_b = C.load_const(C.dram_in("ident", [128, 128]), [128, 128], "ident", cast=BF16)
    C.pm_b = C.pm[:].bitcast(BF16)
    scr = nc.dram_tensor("gdn_scr", [3, 64, S], F32).ap()
    gdn_group(C, [io], [scr], gnw_d, [out_d])
    C.P.barrier()
    C.P.op("sp", lambda e: None)
    C.P.emit()
    return nc


T_ = 1024
NTC = 2
ALPHA = (2 * 2) ** 0.25
LN_EPS = 1e-5


def stream_w(C, w_d, c0, n, tag):
    P = C.P
    L = C.loc
    if not hasattr(L, "ws_bufs"):
        L.ws_stg = [P.sbuf([128, 8, 128], F32, name=f"wsstg{i}") for i in range(2)]
        L.ws_bufs = [P.sbuf([128, 8, 128], BF16, name=f"wsb{i}") for i in range(2)]
        L.ws_i = 0
    i = L.ws_i
    L.ws_i += 1
    st, rs = L.ws_stg[i % 2], ("wsstg", i % 2)
    wt, rw = L.ws_bufs[i % 2], ("wsb", i % 2)
    P.dma("act", lambda e: [e.dma_start(out=st[:, :, :], in_=w_d[c0])], writes=[rs])
    P.op("pool", lambda e: e.tensor_copy(out=wt[:, :, 0:n], in_=st[:, :, 0:n]), reads=[rs], writes=[rw])
    return wt, rw


def build_merge(nc=None, C=None, io=None):
    standalone = C is None
    if standalone:
        C = Ctx(nc)
        io = {"xT": C.dram_in("xT", [128, 8, T_]), "xtok": C.dram_in("xtok", [T_, 1024]), "oT_in": C.dram_in("oT_in", [4, 384, T_]),
              "wz": C.dram_in("wz", [128, 8, 1920]), "bz": C.dram_in("bz", [128, 16]), "wm": C.dram_in("wm", [128, 8, 5120]),
              "bm": C.dram_in("bm", [128, 40]), "weq": C.dram_in("weq", [128, 8, 384]), "beq": C.dram_in("beq", [96, 4]),
              "mem": C.dram_in("mem", [256, 1024]), "memg": C.dram_in("memg", [128, 2, 1024]), "wkv": C.dram_in("wkv", [128, 8, 768]),
              "wbr": C.dram_in("wbr", [128, 12, 1024]), "wbe": C.dram_in("wbe", [96, 4, 1024]), "wo": C.dram_in("wo", [128, 8, 1024]),
              "lng": C.dram_in("lng", [128, 2, 1024]), "out": C.dram_out("out", [T_, 1024]), "outT": None}
        C.identf, C.ident_b = C.load_const(C.dram_in("ident", [128, 128]), [128, 128], "ident", cast=BF16)
        C.pm_b = C.pm[:].bitcast(BF16)
    P = C.P
    wz_d, bz_d, wm_d, bm_d = io["wz"], io["bz"], io["wm"], io["bm"]
    weq_d, beq_d, mem_d, mg_d, wkv_d, wbr_d, wbe_d = io["weq"], io["beq"], io["mem"], io["memg"], io["wkv"], io["wbr"], io["wbe"]
    wo_d, lng_d = io["wo"], io["lng"]
    shards = io.get("shards") or [{"xT": io["xT"], "xtok": io["xtok"], "oT_in": io["oT_in"], "out": io["out"], "outT": io["outT"]}]
    _cache = {}

    def A(shape, dtype, name=None):
        if name not in _cache:
            _cache[name] = P.sbuf(shape, dtype, name=name)
        return _cache[name]
    identf = C.identf
    bz = C.load_const(bz_d, [128, 16], "bz")
    bm = C.load_const(bm_d, [128, 40], "bm")
    beq = C.load_const(beq_d, [96, 4], "beq")
    SC = 96 ** -0.5
    beqs = A([96, 4], F32, name="beqs")
    P.op("dve", lambda e: e.tensor_scalar(out=beqs[:], in0=beq[:], scalar1=SC, scalar2=None, op0=ALU.mult), reads=["beq"], writes=["beqs"])
    eps_t = A([128, 1], F32, name="eps_t")
    P.op("pool", lambda e: e.memset(eps_t[:], LN_EPS), writes=["eps_t"])

    big = [A([128, 1024], F32, name=f"big{i}") for i in range(3)]
    stat = A([128, 4], F32, name="stat")

    def layer_norm(src, src_res, gb, gb_res, dst, dst_res, tmp, tmp_res):
        P.op("dve", lambda e: e.tensor_reduce(out=stat[:, 0:1], in_=src[:], axis=AX.X, op=ALU.add), reads=[src_res], writes=["stat0"])
        P.op("dve", lambda e: e.tensor_scalar(out=stat[:, 0:1], in0=stat[:, 0:1], scalar1=1.0 / 1024.0, scalar2=None, op0=ALU.mult), reads=["stat0"], writes=["stat0"])
        P.op("dve", lambda e: e.tensor_scalar(out=src[:], in0=src[:], scalar1=stat[:, 0:1], scalar2=None, op0=ALU.subtract), reads=[src_res, "stat0"], writes=[src_res])
        P.op("pool", lambda e: e.tensor_tensor(out=tmp[:], in0=src[:], in1=src[:], op=ALU.mult), reads=[src_res], writes=[tmp_res])
        P.op("dve", lambda e: e.tensor_reduce(out=stat[:, 1:2], in_=tmp[:], axis=AX.X, op=ALU.add), reads=[tmp_res], writes=["stat1"])
        P.op("act", lambda e: e.activation(out=stat[:, 1:2], in_=stat[:, 1:2], func=AF.Ln, bias=eps_t[:, 0:1], scale=1.0 / 1024.0), reads=["stat1", "eps_t"], writes=["stat1"])
        P.op("act", lambda e: e.activation(out=stat[:, 1:2], in_=stat[:, 1:2], func=AF.Exp, scale=-0.5), reads=["stat1"], writes=["stat1"])
        P.op("dve", lambda e: e.scalar_tensor_tensor(out=tmp[:], in0=src[:], scalar=stat[:, 1:2], in1=gb[:, 0, :], op0=ALU.mult, op1=ALU.mult),
             reads=[src_res, "stat1", gb_res], writes=[tmp_res])
        P.op("pool", lambda e: e.tensor_tensor(out=dst[:], in0=tmp[:], in1=gb[:, 1, :], op=ALU.add), reads=[tmp_res, gb_res], writes=[dst_res])

    gbt = A([128, 2, 1024], F32, name="gbt")
    P.dma("sp", lambda e: [e.dma_start(out=gbt[:], in_=mg_d)], writes=["gbt"])
    mg = gbt
    memT = A([128, 8, 256], BF16, name="memT")
    memn_b = A([128, 1024], BF16, name="memn_b")
    for t in range(2):
        P.dma("sp", lambda e, t=t: [e.dma_start(out=big[0][:], in_=mem_d[128 * t:128 * t + 128, :])], writes=["big0"])
        layer_norm(big[0], "big0", mg, "gbt", big[1], "big1", big[2], "big2")
        P.op("act", lambda e: e.activation(out=memn_b[:], in_=big[1][:], func=AF.Copy), reads=["big1"], writes=["memn_b"])
        for kc in range(8):
            P.op("pe", lambda e, kc=kc: e.transpose(C.pm_b[:, 128 * kc:128 * kc + 128], memn_b[:, 128 * kc:128 * kc + 128], C.ident_b[:, :]),
                 reads=["memn_b", "ident_b"], writes=["pm"])
        P.op("act", lambda e, t=t: e.activation(out=memT[:, :, 128 * t:128 * t + 128], in_=C.pm_b[:, 0:1024].rearrange("p (k m) -> p k m", k=8), func=AF.Copy),
             reads=["pm"], writes=[("memT", t)])
    wbig = A([128, 8, 1024], BF16, name="wbig")
    wkv = load_weights(C, wkv_d, 768, "wbig", wb=wbig)
    kmT = A([96, 4, 256], BF16, name="kmT")
    vm = A([128, 2, 4, 97], BF16, name="vm")
    P.op("pool", lambda e: e.memset(vm[:, :, :, 96:97], 1.0), writes=["vm_ones"])
    for h in range(4):
        pt, pres = C.next_pp()
        for kc in range(8):
            P.op("pe", lambda e, pt=pt, kc=kc, h=h: e.matmul(pt[0:96, 0:256], lhsT=wkv[:, kc, 96 * h:96 * h + 96], rhs=memT[:, kc, :], start=(kc == 0), stop=(kc == 7)),
                 reads=["wbig", ("memT", 0), ("memT", 1)], writes=[pres])
        P.op("act", lambda e, pt=pt, h=h: e.activation(out=kmT[:, h, :], in_=pt[0:96, 0:256], func=AF.Copy), reads=[pres], writes=[("kmT", h)])
    for t in range(2):
        pt, pres = C.next_pp()
        for kc in range(8):
            P.op("pe", lambda e, pt=pt, kc=kc, t=t: e.matmul(pt[:, 0:384], lhsT=memT[:, kc, 128 * t:128 * t + 128], rhs=wkv[:, kc, 384:768], start=(kc == 0), stop=(kc == 7)),
                 reads=["wbig", ("memT", t)], writes=[pres])
        P.op("act", lambda e, pt=pt, t=t: e.activation(out=vm[:, t, :, 0:96], in_=pt[:, 0:384].rearrange("p (h d) -> p h d", h=4), func=AF.Copy),
             reads=[pres, "vm_ones"], writes=[("vm", t)])

    wbr = A([128, 12, 1024], BF16, name="wbr")
    wbe = A([96, 4, 1024], BF16, name="wbe")
    for q in range(16):
        bt, br = big[1 + q % 2], f"big{1 + q % 2}"
        if q < 12:
            P.dma("act", lambda e, q=q, bt=bt: [e.dma_start(out=bt[:, :], in_=wbr_d[:, q, :])], writes=[br])
            P.op("pool", lambda e, q=q, bt=bt: e.tensor_copy(out=wbr[:, q, :], in_=bt[:, :]), reads=[br], writes=["wbr"])
        else:
            P.dma("act", lambda e, q=q, bt=bt: [e.dma_start(out=bt[0:96, :], in_=wbe_d[:, q - 12, :])], writes=[br])
            P.op("pool", lambda e, q=q, bt=bt: e.tensor_copy(out=wbe[:, q - 12, :], in_=bt[0:96, :]), reads=[br], writes=["wbe"])
    wo = load_weights(C, wo_d, 1024, "wbig", wb=wbig)
    P.dma("sp", lambda e: [e.dma_start(out=gbt[:], in_=lng_d)], reads=[], writes=["gbt"])
    lng = gbt
    def do_shard(sh):
        xT_d, xtok_d, oT_d, out_d, outT_d = sh["xT"], sh["xtok"], sh["oT_in"], sh["out"], sh["outT"]
        xb = A([128, 8, T_], BF16, name="xb")
        xs = [A([128, 2, 512], F32, name=f"xs{i}") for i in range(2)]
        for ci in range(NTC):
            t0 = 512 * ci
            for q4 in range(4):
                hh = q4 % 2
                P.dma("sp", lambda e, hh=hh, q4=q4, t0=t0: [e.dma_start(out=xs[hh][:, :, :], in_=xT_d[:, 2 * q4:2 * q4 + 2, t0:t0 + 512])], writes=[("xs", hh)])
                P.op("pool" if hh == 0 else "dve", lambda e, hh=hh, q4=q4, t0=t0: e.tensor_copy(out=xb[:, 2 * q4:2 * q4 + 2, t0:t0 + 512], in_=xs[hh][:, :, :]), reads=[("xs", hh)], writes=[("xb", ci, q4)])

        def xres(ci):
            return [("xb", ci, q4) for q4 in range(4)]


        gE = A([96, 4, T_], BF16, name="gE")
        eq = A([96, 512], BF16, name="eq")
        szE = A([96, 512], F32, name="szE")
        ptE = [A([128, 512], BF16, name=f"ptE{i}") for i in range(2)]
        onE = A([97, 512], F32, name="onE")
        rdenE = A([1, 512], F32, name="rdenE")
        for h in range(4):
            wq_t, wq_r = stream_w(C, weq_d, h, 96, "weq")
            wz_t, wz_r = stream_w(C, wz_d, 12 + h, 96, "wzE")
            for ci in range(NTC):
                cs = slice(512 * ci, 512 * ci + 512)
                pt, pres = C.next_pp()
                for kc in range(8):
                    P.op("pe", lambda e, pt=pt, kc=kc, cs=cs: e.matmul(pt[0:96, :], lhsT=wq_t[:, kc, 0:96], rhs=xb[:, kc, cs], start=(kc == 0), stop=(kc == 7)),
                         reads=[wq_r] + xres(ci), writes=[pres])
                P.op("act", lambda e, pt=pt, h=h: e.activation(out=eq[:], in_=pt[0:96, :], func=AF.Identity, bias=beqs[:, h:h + 1], scale=SC), reads=[pres, "beqs"], writes=["eq"])
                pt2, pres2 = C.next_pp()
                for kc in range(8):
                    P.op("pe", lambda e, pt2=pt2, kc=kc, cs=cs: e.matmul(pt2[0:96, :], lhsT=wz_t[:, kc, 0:96], rhs=xb[:, kc, cs], start=(kc == 0), stop=(kc == 7)),
                         reads=[wz_r] + xres(ci), writes=[pres2])
                P.op("act", lambda e, pt2=pt2, h=h: e.activation(out=szE[:], in_=pt2[0:96, :], func=AF.Silu, bias=bz[0:96, 12 + h:13 + h], scale=1.0), reads=[pres2, "bz"], writes=["szE"])
                po, po_res = C.next_po()
                for t in range(2):
                    pst, ps_res = C.next_ps()
                    P.op("pe", lambda e, pst=pst, t=t, h=h: e.matmul(pst[:, :], lhsT=kmT[:, h, 128 * t:128 * t + 128], rhs=eq[:, :], start=True, stop=True),
                         reads=[("kmT", h), "eq"], writes=[ps_res])
                    P.op("act", lambda e, pst=pst, t=t: e.activation(out=ptE[t][:], in_=pst[:, :], func=AF.Exp), reads=[ps_res], writes=[("ptE", t)])
                    P.op("pe", lambda e, po=po, t=t, h=h: e.matmul(po[0:97, :], lhsT=vm[:, t, h, 0:97], rhs=ptE[t][:], start=(t == 0), stop=(t == 1)),
                         reads=[("vm", t), "vm_ones", ("ptE", t)], writes=[po_res])
                P.op("act", lambda e, po=po: e.activation(out=onE[0:97, :], in_=po[0:97, :], func=AF.Copy), reads=[po_res], writes=["onE"])
                P.op("dve", lambda e: e.reciprocal(out=rdenE[0:1, :], in_=onE[96:97, :]), reads=["onE"], writes=["rdenE"])
                P.op("pe", lambda e: e.matmul(C.pm[0:96, :], lhsT=C.ones_f[0:1, 0:96], rhs=rdenE[0:1, :], start=True, stop=True), reads=["ones_f", "rdenE"], writes=["pm"])
                P.op("dve", lambda e: e.tensor_tensor(out=onE[0:96, :], in0=onE[0:96, :], in1=C.pm[0:96, :], op=ALU.mult), reads=["onE", "pm"], writes=["onE"])
                P.op("dve", lambda e, h=h, cs=cs: e.tensor_tensor(out=gE[:, h, cs], in0=onE[0:96, :], in1=szE[:], op=ALU.mult), reads=["onE", "szE"], writes=[("gE", h, ci)])

        g4 = A([128, 12, T_], BF16, name="g4")
        oin = [A([128, 512], F32, name=f"oin{i}") for i in range(2)]
        sz = [A([128, 512], F32, name=f"sz{i}") for i in range(2)]
        it = 0
        for n in range(4):
            for k in range(3):
                wz_t, wz_r = stream_w(C, wz_d, 3 * n + k, 128, "wz")
                for ci in range(NTC):
                    cs = slice(512 * ci, 512 * ci + 512)
                    b = it % 2
                    it += 1
                    P.dma("sp", lambda e, b=b, n=n, k=k, cs=cs: [e.dma_start(out=oin[b][:], in_=oT_d[n, 128 * k:128 * k + 128, cs])], writes=[("oin", b)])
                    pt, pres = C.next_pp()
                    for kc in range(8):
                        P.op("pe", lambda e, pt=pt, kc=kc, cs=cs, wz_t=wz_t: e.matmul(pt[:, :], lhsT=wz_t[:, kc, 0:128], rhs=xb[:, kc, cs], start=(kc == 0), stop=(kc == 7)),
                             reads=[wz_r] + xres(ci), writes=[pres])
                    P.op("act", lambda e, pt=pt, b=b, n=n, k=k: e.activation(out=sz[b][:], in_=pt[:, :], func=AF.Silu, bias=bz[:, 3 * n + k:3 * n + k + 1], scale=1.0),
                         reads=[pres, "bz"], writes=[("sz", b)])
                    P.op("dve", lambda e, b=b, n=n, k=k, cs=cs: e.tensor_tensor(out=g4[:, 3 * n + k, cs], in0=oin[b][:], in1=sz[b][:], op=ALU.mult),
                         reads=[("oin", b), ("sz", b)], writes=[("g4", n, k, ci)])

        wbr, wbe = _cache["wbr"], _cache["wbe"]
        yb = A([128, 8, T_], BF16, name="yb")
        yacc = A([128, T_], F32, name="yacc")
        sig = [A([128, 512], F32, name=f"sig{i}") for i in range(2)]
        ytmp = [A([128, 512], F32, name=f"ytmp{i}") for i in range(2)]
        it = 0
        for dmt in range(8):
            dsl = slice(128 * dmt, 128 * dmt + 128)
            for n in range(5):
                wm_t, wm_r = stream_w(C, wm_d, 8 * n + dmt, 128, "wm")
                for ci in range(NTC):
                    cs = slice(512 * ci, 512 * ci + 512)
                    b = it % 2
                    it += 1
                    pt, pres = C.next_pp()
                    for kc in range(8):
                        P.op("pe", lambda e, pt=pt, kc=kc, cs=cs, wm_t=wm_t: e.matmul(pt[:, :], lhsT=wm_t[:, kc, 0:128], rhs=xb[:, kc, cs], start=(kc == 0), stop=(kc == 7)),
                             reads=[wm_r] + xres(ci), writes=[pres])
                    P.op("act", lambda e, pt=pt, b=b, n=n, dmt=dmt: e.activation(out=sig[b][:], in_=pt[:, :], func=AF.Sigmoid, bias=bm[:, 8 * n + dmt:8 * n + dmt + 1], scale=1.0),
                         reads=[pres, "bm"], writes=[("sig", b)])
                    pst, ps_res = C.next_ps()
                    if n < 4:
                        for k in range(3):
                            P.op("pe", lambda e, pst=pst, k=k, n=n, cs=cs, dsl=dsl: e.matmul(pst[:, :], lhsT=wbr[:, 3 * n + k, dsl], rhs=g4[:, 3 * n + k, cs], start=(k == 0), stop=(k == 2)),
                                 reads=["wbr", ("g4", n, k, ci)], writes=[ps_res])
                    else:
                        for h in range(4):
                            P.op("pe", lambda e, pst=pst, h=h, cs=cs, dsl=dsl: e.matmul(pst[:, :], lhsT=wbe[:, h, dsl], rhs=gE[:, h, cs], start=(h == 0), stop=(h == 3)),
                                 reads=["wbe", ("gE", h, ci)], writes=[ps_res])
                    if n == 0:
                        P.op("dve", lambda e, pst=pst, b=b, cs=cs: e.tensor_tensor(out=yacc[:, cs], in0=sig[b][:], in1=pst[:, :], op=ALU.mult), reads=[("sig", b), ps_res], writes=[("yacc", ci)])
                    else:
                        P.op("dve", lambda e, pst=pst, b=b: e.tensor_tensor(out=ytmp[b][:], in0=sig[b][:], in1=pst[:, :], op=ALU.mult), reads=[("sig", b), ps_res], writes=[("ytmp", b)])
                        P.op("pool", lambda e, b=b, cs=cs: e.tensor_tensor(out=yacc[:, cs], in0=yacc[:, cs], in1=ytmp[b][:], op=ALU.add), reads=[("ytmp", b), ("yacc", ci)], writes=[("yacc", ci)])
            for ci in range(NTC):
                cs = slice(512 * ci, 512 * ci + 512)
                P.op("act", lambda e, dmt=dmt, cs=cs: e.activation(out=yb[:, dmt, cs], in_=yacc[:, cs], func=AF.Copy), reads=[("yacc", ci)], writes=[("yb", dmt, ci)])

        wo, lng = wbig, gbt
        for tt in range(T_ // 128):
            ts = slice(128 * tt, 128 * tt + 128)
            ci = tt // 4
            P.dma("sp", lambda e, ts=ts: [e.dma_start(out=big[0][:], in_=xtok_d[ts, :])], writes=["big0"])
            for half in range(2):
                pt, pres = C.next_pp()
                hs = slice(512 * half, 512 * half + 512)
                for kc in range(8):
                    P.op("pe", lambda e, pt=pt, kc=kc, ts=ts, hs=hs: e.matmul(pt[:, :], lhsT=yb[:, kc, ts], rhs=wo[:, kc, hs], start=(kc == 0), stop=(kc == 7)),
                         reads=["wbig"] + [("yb", kc, ci) for kc in range(8)], writes=[pres])
                P.op("dve", lambda e, pt=pt, hs=hs: e.scalar_tensor_tensor(out=big[1][:, hs], in0=big[0][:, hs], scalar=ALPHA, in1=pt[:, :], op0=ALU.mult, op1=ALU.add),
                     reads=["big0", pres], writes=["big1"])
            layer_norm(big[1], "big1", lng, "gbt", big[0], "big0", big[2], "big2")
            P.dma("sp", lambda e, ts=ts: [e.dma_start(out=out_d[ts, :], in_=big[0][:])], reads=["big0"], writes=[("out", tt)], lane="ost")
            C.out_res.append(("out", tt))
            if outT_d is not None:
                xTn = big[2][:, :].rearrange("p (k t) -> p k t", k=8)
                for half in range(2):
                    for k4 in range(4):
                        kc = 4 * half + k4
                        P.op("pe", lambda e, kc=kc, k4=k4: e.transpose(C.pm[:, 128 * k4:128 * k4 + 128], big[0][:, 128 * kc:128 * kc + 128], identf[:, :]),
                             reads=["big0", "ident"], writes=["pm"])
                    P.op("act", lambda e, half=half: e.activation(out=xTn[:, 4 * half:4 * half + 4, :], in_=C.pm[:, 0:512].rearrange("p (k t) -> p k t", k=4), func=AF.Copy),
                         reads=["pm"], writes=["big2"])
                P.dma("sp", lambda e, ts=ts: [e.dma_start(out=outT_d[:, :, ts], in_=xTn)], reads=["big2"], writes=[("outT", tt)], lane="ostT")

    for sh in shards:
        do_shard(sh)
    if standalone:
        P.op("sp", lambda e: None, reads=C.out_res)
        P.emit()
    return nc


DEPTH = 2
NH = 6


def build_fused(nc, depth=DEPTH, heads=NH, nshards=S // T_):
    C = Ctx(nc)
    P = C.P
    di = C.dram_in
    xT0 = di("xT0", [128, 8, S])
    xtok0 = di("xtok0", [S, 1024])
    mem = di("mem", [256, 1024])
    memg = di("memg", [128, 2, 1024])
    wfox, bfox = di("wfox", [DEPTH, NH, 2, 128, 8, 128]), di("bfox", [DEPTH, NH, 64, 4])
    wmoba, bmoba = di("wmoba", [DEPTH, NH, 2, 128, 8, 128]), di("bmoba", [DEPTH, NH, 64, 4])
    wdil, bdil = di("wdil", [DEPTH, NH, 4, 128, 8, 128]), di("bdil", [DEPTH, NH, 64, 8])
    wgdn, bgdn = di("wgdn", [DEPTH, NH, 2, 128, 8, 128]), di("bgdn", [DEPTH, NH, 64, 8])
    convw, scal, gnw = di("convw", [DEPTH, NH, 64, 12]), di("scal", [DEPTH, NH, 128, 2]), di("gnw", [DEPTH, 128, 64])
    wz, bz = di("wz", [DEPTH, 16, 128, 8, 128]), di("bz", [DEPTH, 128, 16])
    wm, bm = di("wm", [DEPTH, 40, 128, 8, 128]), di("bm", [DEPTH, 128, 40])
    weq, beq = di("weq", [DEPTH, 4, 128, 8, 128]), di("beq", [DEPTH, 96, 4])
    wkv = di("wkv", [DEPTH, 6, 128, 8, 128])
    wbr, wbe = di("wbr", [DEPTH, 128, 12, 1024]), di("wbe", [DEPTH, 96, 4, 1024])
    wo, lng = di("wo", [DEPTH, 8, 128, 8, 128]), di("lng", [DEPTH, 128, 2, 1024])
    ind, blkc = di("ind", [32, S]), di("blkc", [128, 3, 32, 32])
    out = C.dram_out("out", [S, 1024])
    oT_all = nc.dram_tensor("oT_all", [4, 384, S], F32).ap()
    xT1 = nc.dram_tensor("xT1", [128, 8, S], F32).ap()
    xtok1 = nc.dram_tensor("xtok1", [S, 1024], F32).ap()
    gscr = nc.dram_tensor("gdn_scr", [NH, 3, 64, S], F32).ap()
    xTb = nc.dram_tensor("xTb", [128, 8, S], BF16).ap()
    C.negc = C.load_const(di("negc", [128, 128]), [128, 128], "negc")
    C.band = C.load_const(di("band", [128, 256]), [128, 256], "band")
    C.tri = C.load_const(di("tri", [128, 3, 128]), [128, 3, 128], "tri")
    C.identf, C.ident_b = C.load_const(di("ident", [128, 128]), [128, 128], "ident", cast=BF16)
    C.pm_b = C.pm[:].bitcast(BF16)
    for l in range(depth):
        xT = xT0 if l == 0 else xT1
        xtok = xtok0 if l == 0 else xtok1
        with P.scope():
            C.new_scope()
            convert_xT(C, xT, xTb)
        for h in range(heads):
            osl = lambda n: oT_all[n, 64 * h:64 * h + 64, :]
            with P.scope():
                C.new_scope()
                build_moba(C=C, io={"xT": xT, "w": wmoba[l, h], "bias": bmoba[l, h], "ind": ind, "blkc": blkc, "out": osl(0)})
            with P.scope():
                C.new_scope()
                build_dil(C=C, io={"xT": xT, "xTb": xTb, "w": wdil[l, h], "bias": bdil[l, h], "out": osl(1)})
            with P.scope():
                C.new_scope()
                build_fox(C=C, io={"xT": xT, "xTb": xTb, "w": wfox[l, h], "bias": bfox[l, h], "out": osl(2)})
        with P.scope():
            C.new_scope()
            gdn_group(C, [{"xT": xT, "xTb": xTb, "w": wgdn[l, h], "bias": bgdn[l, h], "convw": convw[l, h], "scal": scal[l, h]} for h in range(heads)],
                      [gscr[h] for h in range(heads)], gnw[l], [oT_all[3, 64 * h:64 * h + 64, :] for h in range(heads)])
        last = (l == depth - 1)
        shards = []
        for s_ in range(nshards):
            ts = slice(T_ * s_, T_ * s_ + T_)
            shards.append({"xT": xT[:, :, ts], "xtok": xtok[ts, :], "oT_in": oT_all[:, :, ts], "out": (out if last else xtok1)[ts, :],
                           "outT": None if last else xT1[:, :, ts]})
        with P.scope():
            C.new_scope()
            build_merge(C=C, io={"shards": shards, "wz": wz[l], "bz": bz[l], "wm": wm[l], "bm": bm[l], "weq": weq[l], "beq": beq[l], "mem": mem, "memg": memg,
                                 "wkv": wkv[l], "wbr": wbr[l], "wbe": wbe[l], "wo": wo[l], "lng": lng[l]})
    P.barrier()
    P.op("sp", lambda e: None)
    print("fused ops:", len(P.ops), flush=True)
    P.emit()
    return nc


H = 6
BW = 384
OFF_A = 0
OFF_BQK = 1152
OFF_BV = 3456
OFF_C = 3840
OFF_CF = 4992
OFF_D = 4998
OFF_DBETA = 6150
OFF_DDEC = 6156
OFF_EQ = 6162
OFF_Z = 6546
OFF_M = 8466


def _xT_layout(xb):
    T = xb.shape[0]
    return np.ascontiguousarray(xb.T.reshape(8, 128, T).transpose(1, 0, 2))


def _w_layout(Wc):
    n = Wc.shape[1]
    return np.ascontiguousarray(Wc.reshape(8, 128, n).transpose(1, 0, 2))


def _w_tiles(Wc, bounds=None):
    n = Wc.shape[1]
    if bounds is None:
        bounds = [(c0, min(c0 + 128, n)) for c0 in range(0, n, 128)]
    out = np.zeros((len(bounds), 128, 8, 128), np.float32)
    for t, (a, b) in enumerate(bounds):
        out[t, :, :, 0:b - a] = Wc[:, a:b].reshape(8, 128, b - a).transpose(1, 0, 2)
    return out


def _consts():
    s_ = np.arange(128)[:, None]
    t_ = np.arange(128)[None, :]
    c = {}
    c["negc"] = np.where(t_ >= s_, 0.0, -30000.0).astype(np.float32)
    c["band"] = np.concatenate([np.where(t_ >= s_, 0.0, -30000.0), np.where(t_ <= s_, 0.0, -30000.0)], 1).astype(np.float32)
    c["ind"] = (np.arange(S)[None, :] // 256 == np.arange(32)[:, None]).astype(np.float32)
    qb = np.arange(32)[:, None]
    n = np.arange(32)[None, :]
    blk = np.stack([np.where(n < qb, 0.0, -1e9), (n < qb).astype(np.float32), (n == qb).astype(np.float32)]).astype(np.float32)
    c["blkc"] = np.ascontiguousarray(np.broadcast_to(blk[None], (128, 3, 32, 32)))
    c["ident"] = np.eye(128, dtype=np.float32)
    c["tri"] = np.stack([(s_ <= t_).astype(np.float32), np.where(t_ > s_, 0.0, -30000.0), np.where(s_ > t_, 0.0, -30000.0)], 1).astype(np.float32)
    return c


def _pack_weights(w_in, b_in, conv_w, a_log, dt_bias, gdn_norm_w, w_mem_kv, w_branch, w_out, ln_g, ln_b, mem_ln_g, mem_ln_b):
    f32 = np.float32
    L = w_in.shape[0]
    d = {k: [] for k in ("wfox", "bfox", "wmoba", "bmoba", "wdil", "bdil", "wgdn", "bgdn", "convw", "scal", "gnw",
                         "wz", "bz", "wm", "bm", "weq", "beq", "wkv", "wbr", "wbe", "wo", "lng")}
    rep2 = lambda a, b: np.ascontiguousarray(np.broadcast_to(np.stack([np.asarray(a, f32), np.asarray(b, f32)])[None], (128, 2, 1024)))
    for l in range(L):
        Wl = np.asarray(w_in[l], f32)
        bl = np.asarray(b_in[l], f32)
        cwl = np.asarray(conv_w[l], f32)
        per = {k: [] for k in ("wfox", "bfox", "wmoba", "bmoba", "wdil", "bdil", "wgdn", "bgdn", "convw", "scal")}
        for h in range(H):
            cols = np.concatenate([OFF_C + k * BW + 64 * h + np.arange(64) for k in range(3)] + [np.array([OFF_CF + h])])
            bias = np.zeros((64, 4), f32)
            for k in range(3):
                bias[:, k] = bl[cols[64 * k:64 * k + 64]]
            bias[0, 3] = bl[OFF_CF + h]
            per["wfox"].append(_w_tiles(Wl[:, cols]))
            per["bfox"].append(bias)
            cols = np.concatenate([OFF_A + k * BW + 64 * h + np.arange(64) for k in range(3)])
            bias = np.zeros((64, 4), f32)
            for k in range(3):
                bias[:, k] = bl[cols[64 * k:64 * k + 64]]
            per["wmoba"].append(_w_tiles(Wl[:, cols]))
            per["bmoba"].append(bias)
            segs = []
            for g in range(3):
                segs.append(OFF_BQK + ((0 * 3 + g) * H + h) * 64 + np.arange(64))
                segs.append(OFF_BQK + ((1 * 3 + g) * H + h) * 64 + np.arange(64))
            segs.append(OFF_BV + 64 * h + np.arange(64))
            cols = np.concatenate(segs)
            bias = np.zeros((64, 8), f32)
            for k in range(7):
                bias[:, k] = bl[cols[64 * k:64 * k + 64]]
            per["wdil"].append(_w_tiles(Wl[:, cols]))
            per["bdil"].append(bias)
            cols = np.concatenate([OFF_D + k * BW + 64 * h + np.arange(64) for k in range(3)] + [np.array([OFF_DBETA + h, OFF_DDEC + h])])
            bias = np.zeros((64, 8), f32)
            for k in range(3):
                bias[:, k] = bl[cols[64 * k:64 * k + 64]]
            bias[0, 3] = bl[OFF_DBETA + h]
            bias[0, 4] = bl[OFF_DDEC + h]
            cw = np.zeros((64, 12), f32)
            for i in range(3):
                for j in range(4):
                    cw[:, 4 * i + j] = cwl[j, i * BW + 64 * h:i * BW + 64 * h + 64]
            sc = np.zeros((128, 2), f32)
            sc[:, 0] = np.asarray(a_log[l], f32)[h]
            sc[:, 1] = np.asarray(dt_bias[l], f32)[h]
            per["wgdn"].append(_w_tiles(Wl[:, cols]))
            per["bgdn"].append(bias)
            per["convw"].append(cw)
            per["scal"].append(sc)
        for k in per:
            d[k].append(np.stack(per[k]))
        d["gnw"].append(np.ascontiguousarray(np.broadcast_to(np.asarray(gdn_norm_w[l], f32)[None], (128, 64))))
        bzv = bl[OFF_Z:OFF_Z + 1920]
        bmv = bl[OFF_M:OFF_M + 5120]
        beqv = bl[OFF_EQ:OFF_EQ + 384]
        bz = np.zeros((128, 16), f32)
        for i in range(12):
            bz[:, i] = bzv[128 * i:128 * i + 128]
        for h in range(4):
            bz[0:96, 12 + h] = bzv[1536 + 96 * h:1536 + 96 * h + 96]
        bm = np.zeros((128, 40), f32)
        for n in range(5):
            for dd in range(8):
                bm[:, 8 * n + dd] = bmv[1024 * n + 128 * dd:1024 * n + 128 * dd + 128]
        Wbr = np.asarray(w_branch[l], f32)
        d["wz"].append(_w_tiles(Wl[:, OFF_Z:OFF_Z + 1920], [(128 * i, 128 * i + 128) for i in range(12)] + [(1536 + 96 * i, 1536 + 96 * i + 96) for i in range(4)]))
        d["bz"].append(bz)
        d["wm"].append(_w_tiles(Wl[:, OFF_M:OFF_M + 5120]))
        d["bm"].append(bm)
        d["weq"].append(_w_tiles(Wl[:, OFF_EQ:OFF_EQ + 384], [(96 * i, 96 * i + 96) for i in range(4)]))
        d["beq"].append(np.ascontiguousarray(beqv.reshape(4, 96).T))
        d["wkv"].append(_w_tiles(np.asarray(w_mem_kv[l], f32)))
        d["wbr"].append(np.ascontiguousarray(Wbr[0:4].reshape(4, 3, 128, 1024).transpose(2, 0, 1, 3).reshape(128, 12, 1024)))
        d["wbe"].append(np.ascontiguousarray(Wbr[4].reshape(4, 96, 1024).transpose(1, 0, 2)))
        d["wo"].append(_w_tiles(np.asarray(w_out[l], f32)))
        d["lng"].append(rep2(ln_g[l], ln_b[l]))
    out = {k: np.ascontiguousarray(np.stack(v)).astype(f32) for k, v in d.items()}
    out["memg"] = rep2(mem_ln_g, mem_ln_b)
    return out


def kernel(x, mem, mem_ln_g, mem_ln_b, w_in, b_in, conv_w, a_log, dt_bias, gdn_norm_w, w_mem_kv, w_branch, w_out, ln_g, ln_b):
    f32 = np.float32
    x = np.asarray(x, f32)
    mem = np.asarray(mem, f32)
    B = x.shape[0]
    shared = _pack_weights(np.asarray(w_in), np.asarray(b_in), np.asarray(conv_w), np.asarray(a_log), np.asarray(dt_bias), np.asarray(gdn_norm_w),
                           np.asarray(w_mem_kv), np.asarray(w_branch), np.asarray(w_out), np.asarray(ln_g), np.asarray(ln_b), mem_ln_g, mem_ln_b)
    shared.update(_consts())
    nc = bass.Bass("TRN2", target_bir_lowering=False)
    build_fused(nc)
    per_batch = []
    for b in range(B):
        d = dict(shared)
        d["xT0"] = _xT_layout(x[b])
        d["xtok0"] = np.ascontiguousarray(x[b])
        d["mem"] = np.ascontiguousarray(mem[b])
        per_batch.append(d)
    n_cores = 8
    grp = n_cores // B
    res = run_bass_kernel_spmd(nc, [per_batch[c // grp] for c in range(n_cores)], core_ids=list(range(n_cores)))
    return np.stack([np.asarray(res.results[b * grp]["out"]) for b in range(B)]).astype(f32)
```

```python
import os
import numpy as np
from concourse.bass_utils import run_bass_kernel_spmd
import contextlib
import numpy as np
import concourse.bass as bass
import concourse.mybir as mybir

F32 = mybir.dt.float32
BF16 = mybir.dt.bfloat16
AF = mybir.ActivationFunctionType
ALU = mybir.AluOpType
AX = mybir.AxisListType

ENGS = ("pe", "act", "dve", "pool", "sp")


class _Op:
    __slots__ = ("eng", "fn", "reads", "writes", "dma", "lane", "n", "idx", "waits", "signal", "cnt")


class Prog:
    def __init__(self, nc, same_engine_raw=True):
        self.nc = nc
        self.ops = []
        self.last_w = {}
        self.readers = {}
        self.same_engine_raw = same_engine_raw
        self.es = contextlib.ExitStack()
        self.cur_es = self.es
        self._tn = 0
        self.bar_deps = ()
        self.bar_seen = set()
        self.last_on_eng = {}
        self.last_on_lane = {}

    def sbuf(self, shape, dtype, name=None):
        self._tn += 1
        return self.cur_es.enter_context(self.nc.sbuf_tensor(f"s{self._tn}_" + (name or "t"), list(shape), dtype))

    def psum(self, shape, dtype=F32, name=None):
        self._tn += 1
        return self.es.enter_context(self.nc.psum_tensor("p_" + (name or f"ps{self._tn}"), list(shape), dtype))

    @contextlib.contextmanager
    def scope(self):
        prev = self.cur_es
        with contextlib.ExitStack() as es:
            self.cur_es = es
            try:
                yield
            finally:
                self.cur_es = prev
        self.barrier()

    def barrier(self):
        self.bar_deps = tuple(self.last_on_eng.values()) + tuple(self.last_on_lane.values())
        self.bar_seen = set()

    def op(self, eng, fn, reads=(), writes=()):
        o = _Op()
        o.eng, o.fn, o.reads, o.writes = eng, fn, tuple(reads), tuple(writes)
        o.dma, o.lane, o.n = False, None, 1
        self._add(o)

    def dma(self, eng, fn, reads=(), writes=(), lane=None, n=1):
        o = _Op()
        o.eng, o.fn, o.reads, o.writes = eng, fn, tuple(reads), tuple(writes)
        o.dma, o.n = True, n
        o.lane = lane if lane is not None else (o.writes[0] if o.writes else o.reads[0])
        self._add(o)

    def _add(self, o):
        o.idx = len(self.ops)
        deps = set()
        for r in o.reads:
            w = self.last_w.get(r)
            if w is not None:
                deps.add(w)
        for r in o.writes:
            w = self.last_w.get(r)
            if w is not None:
                deps.add(w)
            for rd in self.readers.get(r, ()):
                deps.add(rd)
        if o.eng not in self.bar_seen:
            self.bar_seen.add(o.eng)
            deps.update(self.bar_deps)
        deps.discard(o.idx)
        self.last_on_eng[o.eng] = o.idx
        if o.dma:
            self.last_on_lane[o.lane] = o.idx
        o.waits = deps
        o.signal = False
        for r in o.reads:
            self.readers.setdefault(r, []).append(o.idx)
        for r in o.writes:
            self.last_w[r] = o.idx
            self.readers[r] = []
        self.ops.append(o)

    def emit(self, final_wait_eng="sp"):
        nc = self.nc
        ops = self.ops
        for o in ops:
            keep = set()
            for d in o.waits:
                p = ops[d]
                if p.dma:
                    keep.add(d)
                elif p.eng != o.eng or o.dma:
                    keep.add(d)
                else:
                    if self.same_engine_raw and o.eng != "pe":
                        if any(r in p.writes for r in o.reads):
                            keep.add(d)
            o.waits = keep
            for d in keep:
                ops[d].signal = True
        eng_cnt = {e: 0 for e in ENGS}
        lane_cnt = {}
        for o in ops:
            if o.dma:
                lane_cnt[o.lane] = lane_cnt.get(o.lane, 0) + 16 * o.n
                o.cnt = lane_cnt[o.lane]
            elif o.signal:
                eng_cnt[o.eng] += 1
                o.cnt = eng_cnt[o.eng]
            else:
                o.cnt = None
        lanes = list(lane_cnt.keys())
        es = self.es
        eng_sem = {e: es.enter_context(nc.semaphore(f"s_{e}")) for e in ENGS}
        lane_sem = {l: es.enter_context(nc.semaphore(f"l_{i}")) for i, l in enumerate(lanes)}
        self.n_sems = len(lanes) + len(ENGS)
        per_eng = {e: [] for e in ENGS}
        for o in ops:
            per_eng[o.eng].append(o)

        def run(e, engine):
            seen = {}
            for o in per_eng[e]:
                need = {}
                for d in o.waits:
                    p = ops[d]
                    if p.dma:
                        key = ("l", p.lane)
                        sem = lane_sem[p.lane]
                    else:
                        key = ("e", p.eng)
                        sem = eng_sem[p.eng]
                    if need.get(key, (None, -1))[1] < p.cnt:
                        need[key] = (sem, p.cnt)
                for key, (sem, val) in need.items():
                    if seen.get(key, -1) < val:
                        engine.wait_ge(sem, val)
                        seen[key] = val
                if o.dma:
                    instrs = o.fn(engine)
                    assert len(instrs) == o.n, (len(instrs), o.n)
                    for ins in instrs:
                        ins.then_inc(lane_sem[o.lane], 16)
                else:
                    ins = o.fn(engine)
                    if o.signal:
                        ins.then_inc(eng_sem[o.eng], 1)

        with nc.Block() as block:
            @block.tensor
            def _(eng):
                run("pe", eng)

            @block.scalar
            def _(eng):
                run("act", eng)

            @block.vector
            def _(eng):
                run("dve", eng)

            @block.gpsimd
            def _(eng):
                run("pool", eng)

            @block.sync
            def _(eng):
                run("sp", eng)
        self.es.close()


import types
import numpy as np

S = 8192
NCHK = 16
NEG = -30000.0


class Ctx:
    def __init__(self, nc):
        self.nc = nc
        self.P = Prog(nc)
        P = self.P
        self.pp = [P.psum([128, 512], F32, name=f"pp{i}") for i in range(2)]
        self.ps = [P.psum([128, 512], F32, name=f"pss{i}") for i in range(3)]
        self.po = [P.psum([128, 512], F32, name=f"po{i}") for i in range(2)]
        self.pm = P.psum([128, 512], F32, name="pm")
        self.ipp = 0
        self.ips = 0
        self.ipo = 0
        self.loc = types.SimpleNamespace()
        self.out_res = []
        self.ones_f = P.sbuf([128, 128], F32, name="ones_f")
        self.ones_b = P.sbuf([128, 128], BF16, name="ones_b")
        P.op("pool", lambda e: e.memset(self.ones_f[:], 1.0), writes=["ones_f"])
        P.op("pool", lambda e: e.memset(self.ones_b[:], 1.0), writes=["ones_b"])

    def new_scope(self):
        self.loc = types.SimpleNamespace()

    def dram_in(self, name, shape, dtype=F32):
        return self.nc.dram_tensor(name, list(shape), dtype, kind="ExternalInput").ap()

    def dram_out(self, name, shape, dtype=F32):
        return self.nc.dram_tensor(name, list(shape), dtype, kind="ExternalOutput").ap()

    def load_const(self, dram_ap, shape, name, dtype=F32, cast=None):
        P = self.P
        t = P.sbuf(shape, dtype, name=name)
        P.dma("sp", lambda e: [e.dma_start(out=t[:], in_=dram_ap)], writes=[name])
        if cast is not None:
            tb = P.sbuf(shape, cast, name=name + "_b")
            P.op("pool", lambda e: e.tensor_copy(out=tb[:], in_=t[:]), reads=[name], writes=[name + "_b"])
            return t, tb
        return t

    def next_pp(self):
        i = self.ipp
        self.ipp = (i + 1) % 2
        return self.pp[i], ("pp", i)

    def next_ps(self):
        i = self.ips
        self.ips = (i + 1) % 3
        return self.ps[i], ("ps", i)

    def next_po(self):
        i = self.ipo
        self.ipo = (i + 1) % 2
        return self.po[i], ("po", i)


def load_weights(C, w_d, ncols, name, wb=None):
    P = C.P
    if wb is None:
        wb = P.sbuf([128, 8, ncols], BF16, name=name)
    if not hasattr(C.loc, "lw_stg"):
        C.loc.lw_stg = [P.sbuf([128, 8, 128], F32, name=f"lw_stg{i}") for i in range(2)]
        C.loc.lw_k = 0
    stg = C.loc.lw_stg
    k = C.loc.lw_k
    for c0 in range(0, ncols, 128):
        n = min(128, ncols - c0)
        st = stg[k % 2]
        rs = ("lw_stg", k % 2)
        P.dma("act", lambda e, st=st, c0=c0, n=n: [e.dma_start(out=st[:, :, :], in_=w_d[c0 // 128])], writes=[rs])
        P.op("pool", lambda e, st=st, c0=c0, n=n: e.tensor_copy(out=wb[:, :, c0:c0 + n], in_=st[:, :, 0:n]), reads=[rs], writes=[name])
        k += 1
    C.loc.lw_k = k
    return wb


class _Shift:
    def __init__(self, t, off):
        self.t, self.off = t, off

    def __getitem__(self, key):
        ps, rest = key[0], key[1:]
        return self.t[(slice(ps.start + self.off, ps.stop + self.off),) + tuple(rest)]


def convert_xT(C, xT_d, xTb_d):
    P = C.P
    xs = [P.sbuf([128, 8, 512], F32, name=f"cv_xs{i}") for i in range(2)]
    xb = [P.sbuf([128, 8, 512], BF16, name=f"cv_xb{i}") for i in range(2)]
    for ci in range(NCHK):
        t0, b = ci * 512, ci % 2
        P.dma("sp", lambda e, b=b, t0=t0: [e.dma_start(out=xs[b][:, 0:4, :], in_=xT_d[:, 0:4, t0:t0 + 512]),
                                          e.dma_start(out=xs[b][:, 4:8, :], in_=xT_d[:, 4:8, t0:t0 + 512])], writes=[("cvxs", b)], n=2)
        P.op("pool", lambda e, b=b: e.tensor_copy(out=xb[b][:, 0:3, :], in_=xs[b][:, 0:3, :]), reads=[("cvxs", b)], writes=[("cvxb", b, 0)])
        P.op("dve", lambda e, b=b: e.tensor_copy(out=xb[b][:, 3:6, :], in_=xs[b][:, 3:6, :]), reads=[("cvxs", b)], writes=[("cvxb", b, 1)])
        P.op("act", lambda e, b=b: e.activation(out=xb[b][:, 6:8, :], in_=xs[b][:, 6:8, :], func=AF.Copy), reads=[("cvxs", b)], writes=[("cvxb", b, 2)])
        P.dma("act", lambda e, b=b, t0=t0: [e.dma_start(out=xTb_d[:, :, t0:t0 + 512], in_=xb[b][:, :, :])], reads=[("cvxb", b, k) for k in range(3)],
              writes=[("xTb", ci)], lane=("cvst", b))


def proj_pass(C, xT_d, wb, wname, specs, tag, hook=None, xTb_d=None):
    P = C.P
    use_b = xTb_d is not None and hook is None and all(len(sp) == 3 for sp in specs)
    if not hasattr(C.loc, "xb_bufs"):
        C.loc.xb_bufs = [P.sbuf([128, 8, 512], BF16, name=f"xb_{i}") for i in range(3 if use_b else 2)]
        if not use_b:
            C.loc.xs_bufs = [P.sbuf([128, 8, 512], F32, name=f"xs_{i}") for i in range(2)]
    xb = C.loc.xb_bufs
    xs = None if use_b else C.loc.xs_bufs
    tag = "x"
    if use_b:
        for ci in range(NCHK):
            t0 = ci * 512
            b = ci % 3
            rxb = ("xb", tag, b)
            P.dma("sp", lambda e, b=b, t0=t0: [e.dma_start(out=xb[b][:, 0:4, :], in_=xTb_d[:, 0:4, t0:t0 + 512]),
                                              e.dma_start(out=xb[b][:, 4:8, :], in_=xTb_d[:, 4:8, t0:t0 + 512])], writes=[(rxb, 0), (rxb, 1)], n=2,
                  lane=("xbl", b))
            for spec in specs:
                col0, M, evac = spec
                pt, pres = C.next_pp()
                for kc in range(8):
                    P.op("pe", lambda e, pt=pt, kc=kc, col0=col0, M=M, b=b: e.matmul(
                        pt[0:M, :], lhsT=wb[:, kc, col0:col0 + M], rhs=xb[b][:, kc, :], start=(kc == 0), stop=(kc == 7)),
                        reads=[wname, (rxb, 0), (rxb, 1)], writes=[pres])
                if callable(evac):
                    evac(pt, pres, ci, t0)
                else:
                    for off, fn in evac:
                        fn(pt if off == 0 else _Shift(pt, off), pres, ci, t0)
        return
    for ci in range(NCHK):
        t0 = ci * 512
        b = ci % 2
        rxs = ("xs", tag, b)
        rxb = ("xb", tag, b)
        P.dma("sp", lambda e, b=b, t0=t0: [e.dma_start(out=xs[b][:, 0:4, :], in_=xT_d[:, 0:4, t0:t0 + 512]),
                                          e.dma_start(out=xs[b][:, 4:8, :], in_=xT_d[:, 4:8, t0:t0 + 512])],
              writes=[rxs], n=2)
        P.op("pool", lambda e, b=b: e.tensor_copy(out=xb[b][:, 0:4, :], in_=xs[b][:, 0:4, :]), reads=[rxs], writes=[(rxb, 0)])
        P.op("dve", lambda e, b=b: e.tensor_copy(out=xb[b][:, 4:8, :], in_=xs[b][:, 4:8, :]), reads=[rxs], writes=[(rxb, 1)])
        if hook is not None:
            hook(ci, xs[b], rxs)
        for spec in specs:
            col0, M, evac = spec[0], spec[1], spec[2]
            pt, pres = C.next_pp()
            if len(spec) > 3:
                w32, w32_res = spec[3], spec[4]
                for kc in range(8):
                    P.op("pe", lambda e, pt=pt, kc=kc, M=M, b=b, w32=w32: e.matmul(
                        pt[0:M, :], lhsT=w32[:, kc, 0:M], rhs=xs[b][:, kc, :], start=(kc == 0), stop=(kc == 7)),
                        reads=[w32_res, rxs], writes=[pres])
            else:
                for kc in range(8):
                    P.op("pe", lambda e, pt=pt, kc=kc, col0=col0, M=M, b=b: e.matmul(
                        pt[0:M, :], lhsT=wb[:, kc, col0:col0 + M], rhs=xb[b][:, kc, :], start=(kc == 0), stop=(kc == 7)),
                        reads=[wname, (rxb, 0), (rxb, 1)], writes=[pres])
            if callable(evac):
                evac(pt, pres, ci, t0)
            else:
                for off, fn in evac:
                    fn(pt if off == 0 else _Shift(pt, off), pres, ci, t0)


def make_vprime(C, vT, vT_res, vp, vp_res, nblk, colsel, dh=64):
    P = C.P
    P.op("pool", lambda e: e.memset(vp[:, :, dh:dh + 1], 1.0), writes=[(vp_res, "ones")])
    pmb = C.pm_b
    G = 8
    for g0 in range(0, nblk, G):
        for k in range(G):
            blk = g0 + k
            P.op("pe", lambda e, blk=blk, k=k: e.transpose(pmb[:, k * dh:(k + 1) * dh], vT[0:dh, colsel(blk)], C.ident_b[0:dh, 0:dh]),
                 reads=vT_res(blk) + ["ident_b"], writes=["pm"])
        P.op("act", lambda e, g0=g0: e.activation(out=vp[:, g0:g0 + G, 0:dh], in_=pmb[:, 0:G * dh].rearrange("p (g d) -> p g d", g=G), func=AF.Copy),
             reads=["pm", (vp_res, "ones")], writes=[(vp_res, g0)])


def attn_block(C, S_mm, nq_lo, nq_hi, bias_ap, bias_res, masks, PV_mm, po_res, extra_reads=()):
    raise NotImplementedError


def causal_attention(C, qT, q_res, kT, k_res, KC, vp, vp_res, negc, bias_fn, out_cb, tag, LA=3):
    P = C.P
    NB = LA + 1
    pts = [P.sbuf([128, 512], BF16, name=f"pt_{tag}{i}") for i in range(NB)]
    tmps = [P.sbuf([128, 128], F32, name=f"tmp_{tag}{i}") for i in range(2)]
    st = {"itmp": 0}
    tasks = []
    pos = {}
    for I in range(NCHK):
        pos[I] = C.next_po()
        for j in range(4 * I + 4):
            tasks.append((I, j))

    def stage1(t, I, j):
        r = j - 4 * I
        lo = 128 * r if r > 0 else 0
        pst, ps_res = C.ps[t % 3], ("ps", t % 3)
        pt = pts[t % NB]
        pt_res = ("pt", tag, t % NB)
        q0 = 512 * I
        P.op("pe", lambda e: e.matmul(pst[:, lo:512], lhsT=kT[0:KC, 128 * j:128 * j + 128], rhs=qT[0:KC, q0 + lo:q0 + 512], start=True, stop=True),
             reads=q_res(I) + k_res(j), writes=[ps_res])
        b = bias_fn(I, j) if bias_fn is not None else None
        bkw = {} if b is None else {"bias": b[0]}
        brd = [] if b is None else [b[1]]
        if r >= 0:
            tmp = tmps[st["itmp"] % 2]
            tmp_res = ("tmp", tag, st["itmp"] % 2)
            st["itmp"] += 1
            P.op("dve", lambda e: e.tensor_tensor(out=tmp[:], in0=pst[:, lo:lo + 128], in1=negc[:], op=ALU.add), reads=[ps_res, "negc"], writes=[tmp_res])
            P.op("act", lambda e: e.activation(out=pt[:, lo:lo + 128], in_=tmp[:], func=AF.Exp, **bkw), reads=[tmp_res] + brd, writes=[(pt_res, "d")])
            if lo + 128 < 512:
                P.op("act", lambda e: e.activation(out=pt[:, lo + 128:512], in_=pst[:, lo + 128:512], func=AF.Exp, **bkw), reads=[ps_res] + brd, writes=[(pt_res, "o")])
        else:
            P.op("act", lambda e: e.activation(out=pt[:, :], in_=pst[:, :], func=AF.Exp, **bkw), reads=[ps_res] + brd, writes=[(pt_res, "d"), (pt_res, "o")])

    def stage2(t, I, j):
        r = j - 4 * I
        lo = 128 * r if r > 0 else 0
        nkb = 4 * I + 4
        po, po_res = pos[I]
        pt = pts[t % NB]
        pt_res = ("pt", tag, t % NB)
        P.op("pe", lambda e: e.matmul(po[0:65, lo:512], lhsT=vp[:, j, 0:65], rhs=pt[:, lo:512], start=(j == 0), stop=(j == nkb - 1), skip_group_check=True),
             reads=vp_res(j) + [(pt_res, "d"), (pt_res, "o")], writes=[po_res])
        if j == nkb - 1:
            out_cb(I, po, po_res)

    n = len(tasks)
    for t in range(n + LA):
        if t < n:
            stage1(t, *tasks[t])
        if t - LA >= 0:
            stage2(t - LA, *tasks[t - LA])


def norm_store(C, onum_pool, out_d, tag):
    P = C.P
    st = {"i": 0}

    def cb(I, po, po_res):
        i = st["i"] % 2
        st["i"] += 1
        on = onum_pool[i]
        on_res = ("onum", tag, i)
        P.op("act", lambda e: e.activation(out=on[0:65, :], in_=po[0:65, :], func=AF.Copy), reads=[po_res], writes=[on_res])
        P.op("dve", lambda e: e.reciprocal(out=on[64:65, :], in_=on[64:65, :]), reads=[on_res], writes=[on_res])
        P.op("pe", lambda e: e.matmul(C.pm[0:64, :], lhsT=C.ones_f[64:65, 0:64], rhs=on[64:65, :], start=True, stop=True),
             reads=["ones_f", on_res], writes=["pm"])
        P.op("dve", lambda e: e.tensor_tensor(out=on[0:64, :], in0=on[0:64, :], in1=C.pm[0:64, :], op=ALU.mult),
             reads=[on_res, "pm"], writes=[on_res])
        P.dma("sp", lambda e: [e.dma_start(out=out_d[:, 512 * I:512 * I + 512], in_=on[0:64, :])], reads=[on_res], writes=[("out", tag, I)],
              lane=("onum_st", tag, i))
        C.out_res.append(("out", tag, I))
    return cb


def build_fox(nc=None, C=None, io=None):
    standalone = C is None
    if standalone:
        C = Ctx(nc)
        io = {"xT": C.dram_in("xT", [128, 8, S]), "w": C.dram_in("w", [128, 8, 193]), "bias": C.dram_in("bias", [64, 4]),
              "out": C.dram_out("oT", [64, S])}
        C.negc = C.load_const(C.dram_in("negc", [128, 128]), [128, 128], "negc")
        C.identf, C.ident_b = C.load_const(C.dram_in("ident", [128, 128]), [128, 128], "ident", cast=BF16)
        C.pm_b = C.pm[:].bitcast(BF16)
    P = C.P
    xT_d, w_d, b_d, out_d = io["xT"], io["w"], io["bias"], io["out"]
    negc = C.negc
    bia = C.load_const(b_d, [64, 4], "bia")
    bq8 = P.sbuf([64, 1], F32, name="bq8")
    nbcf = P.sbuf([1, 1], F32, name="nbcf")
    P.op("dve", lambda e: e.tensor_scalar(out=bq8[:], in0=bia[:, 0:1], scalar1=0.125, scalar2=None, op0=ALU.mult), reads=["bia"], writes=["bq8"])
    P.op("dve", lambda e: e.tensor_scalar(out=nbcf[:], in0=bia[0:1, 3:4], scalar1=-1.0, scalar2=None, op0=ALU.mult), reads=["bia"], writes=["nbcf"])
    wb = load_weights(C, w_d, 193, "w")

    qC = P.sbuf([65, S], BF16, name="qC")
    kC = P.sbuf([65, S], BF16, name="kC")
    vT = P.sbuf([64, S], BF16, name="vT")
    Frow = P.sbuf([1, S], F32, name="Frow")
    erow = P.sbuf([1, 512], F32, name="erow")
    vp = P.sbuf([128, 64, 65], BF16, name="vp")

    def ev_q(pt, pres, ci, t0):
        P.op("act", lambda e: e.activation(out=qC[0:64, t0:t0 + 512], in_=pt[0:64, :], func=AF.Identity, bias=bq8[:, 0:1], scale=0.125),
             reads=[pres, "bq8"], writes=[("qC", ci)])

    def ev_k(pt, pres, ci, t0):
        P.op("act", lambda e: e.activation(out=kC[0:64, t0:t0 + 512], in_=pt[0:64, :], func=AF.Identity, bias=bia[:, 1:2], scale=1.0),
             reads=[pres, "bia"], writes=[("kC", ci)])

    def ev_v(pt, pres, ci, t0):
        P.op("act", lambda e: e.activation(out=vT[0:64, t0:t0 + 512], in_=pt[0:64, :], func=AF.Identity, bias=bia[:, 2:3], scale=1.0),
             reads=[pres, "bia"], writes=[("vT", ci)])

    def ev_cf(pt, pres, ci, t0):
        P.op("act", lambda e: e.activation(out=erow[:], in_=pt[0:1, :], func=AF.Exp, bias=nbcf[:, 0:1], scale=-1.0),
             reads=[pres, "nbcf"], writes=["erow"])
        P.op("act", lambda e: e.activation(out=erow[:], in_=erow[:], func=AF.Ln, bias=1.0, scale=1.0), reads=["erow"], writes=["erow"])
        init = 0.0 if ci == 0 else Frow[0:1, t0 - 1:t0]
        P.op("dve", lambda e: e.tensor_tensor_scan(out=Frow[0:1, t0:t0 + 512], data0=ones_row[0:1, :], data1=erow[:],
                                                  initial=init, op0=ALU.mult, op1=ALU.subtract),
             reads=["erow", "ones_row", "Frow"], writes=["Frow"])

    ones_row = P.sbuf([1, 512], F32, name="ones_row")
    P.op("pool", lambda e: e.memset(ones_row[:], 1.0), writes=["ones_row"])
    specs = [(0, 128, [(0, ev_q), (64, ev_k)]), (128, 65, [(0, ev_v), (64, ev_cf)])]
    proj_pass(C, xT_d, wb, "w", specs, "c", xTb_d=io.get("xTb"))
    P.op("pool", lambda e: e.memset(kC[64:65, :], 1.0), writes=["kC64"])
    for I in range(NCHK):
        P.op("dve", lambda e, I=I: e.tensor_scalar(out=qC[64:65, 512 * I:512 * I + 512], in0=Frow[0:1, 512 * I:512 * I + 512],
                                                   scalar1=Frow[0:1, 512 * I:512 * I + 1], scalar2=None, op0=ALU.subtract),
             reads=["Frow"], writes=["qC64"])
    for j in range(64):
        P.op("pe", lambda e, j=j: e.matmul(C.pm[0:128, j:j + 1], lhsT=Frow[0:1, 128 * j:128 * j + 128], rhs=C.ones_f[0:1, 0:1], start=True, stop=True),
             reads=["Frow", "ones_f"], writes=["pm"])
    P.op("pe", lambda e: e.matmul(C.pm[0:128, 64:80], lhsT=C.ones_f[0:1, 0:128], rhs=Frow[0:1, 0:S:512], start=True, stop=True),
         reads=["Frow", "ones_f"], writes=["pm"])
    Ftr = P.sbuf([128, 80], F32, name="Ftr")
    biasC = P.sbuf([128, 16, 64], F32, name="biasC")
    P.op("dve", lambda e: e.tensor_copy(out=Ftr[:], in_=C.pm[:, 0:80]), reads=["pm"], writes=["Ftr"])
    for I in range(NCHK):
        P.op("dve", lambda e, I=I: e.tensor_scalar(out=biasC[:, I, :], in0=Ftr[:, 0:64], scalar1=-1.0, scalar2=Ftr[:, 64 + I:65 + I], op0=ALU.mult, op1=ALU.add),
             reads=["Ftr"], writes=["biasC"])
    make_vprime(C, vT, lambda blk: [("vT", blk // 4)], vp, "vp", 64, lambda blk: slice(128 * blk, 128 * blk + 128))
    onum = [P.sbuf([65, 512], F32, name=f"onum{i}") for i in range(2)]
    cb = norm_store(C, onum, out_d, "c")
    causal_attention(C, qC, lambda I: [("qC", I), "qC64"], kC, lambda j: [("kC", j // 4), "kC64"], 65, vp,
                     lambda j: [("vp", (j // 8) * 8), ("vp", "ones")], negc,
                     lambda I, j: (biasC[:, I, j:j + 1], "biasC"), cb, "c")
    if standalone:
        P.op("sp", lambda e: None, reads=C.out_res)
        P.emit()
    return nc


def build_moba(nc=None, C=None, io=None):
    standalone = C is None
    if standalone:
        C = Ctx(nc)
        io = {"xT": C.dram_in("xT", [128, 8, S]), "w": C.dram_in("w", [2, 128, 8, 128]), "bias": C.dram_in("bias", [64, 4]),
              "ind": C.dram_in("ind", [32, S]), "blkc": C.dram_in("blkc", [128, 3, 32, 32]), "out": C.dram_out("oT", [64, S])}
        C.negc = C.load_const(C.dram_in("negc", [128, 128]), [128, 128], "negc")
        C.identf, C.ident_b = C.load_const(C.dram_in("ident", [128, 128]), [128, 128], "ident", cast=BF16)
        C.pm_b = C.pm[:].bitcast(BF16)
    P = C.P
    xT_d, w_d, b_d, ind_d, blk_d, out_d = io["xT"], io["w"], io["bias"], io["ind"], io["blkc"], io["out"]
    negc = C.negc
    bia = C.load_const(b_d, [64, 4], "bia")
    blkc = C.load_const(blk_d, [128, 3, 32, 32], "blkc")
    bq8 = P.sbuf([64, 1], F32, name="bq8")
    P.op("dve", lambda e: e.tensor_scalar(out=bq8[:], in0=bia[:, 0:1], scalar1=0.125, scalar2=None, op0=ALU.mult), reads=["bia"], writes=["bq8"])
    wb = load_weights(C, w_d, 192, "w")

    qs = P.sbuf([96, S], BF16, name="qs")
    ke = P.sbuf([96, S], BF16, name="ke")
    vT = P.sbuf([64, S], BF16, name="vT")
    vp = P.sbuf([128, 64, 65], BF16, name="vp")
    ksum = P.sbuf([64, 32], F32, name="ksum")
    kmT = P.sbuf([64, 32], BF16, name="kmT")
    indst = P.sbuf([32, 2048], F32, name="indst")
    for i4 in range(4):
        P.dma("act", lambda e, i4=i4: [e.dma_start(out=indst[:, :], in_=ind_d[:, 2048 * i4:2048 * i4 + 2048])], writes=["indst"])
        P.op("pool", lambda e, i4=i4: e.tensor_copy(out=ke[64:96, 2048 * i4:2048 * i4 + 2048], in_=indst[:]), reads=["indst"], writes=["ke_ind"])
    q32 = P.sbuf([64, S], F32, name="q32")
    w32q = P.sbuf([128, 8, 64], F32, name="w32q")
    w32k = P.sbuf([128, 8, 64], F32, name="w32k")
    P.dma("act", lambda e: [e.dma_start(out=w32q[:], in_=w_d[0][:, :, 0:64])], writes=["w32q"])
    P.dma("act", lambda e: [e.dma_start(out=w32k[:], in_=w_d[0][:, :, 64:128])], writes=["w32k"])
    xsum = P.sbuf([128, 8, 32], F32, name="xsum")
    km32 = P.sbuf([64, 32], F32, name="km32")

    def ev_q32(pt, pres, ci, t0):
        P.op("act", lambda e: e.activation(out=q32[0:64, t0:t0 + 512], in_=pt[0:64, :], func=AF.Identity, bias=bq8[:, 0:1], scale=0.125),
             reads=[pres, "bq8"], writes=[("q32", ci)])

    def xhook(ci, xs_t, rxs):
        P.op("dve", lambda e: e.tensor_reduce(out=xsum[:, :, 2 * ci:2 * ci + 2], in_=xs_t[:, :, :].rearrange("p k (a b) -> p k a b", a=2), axis=AX.X, op=ALU.add),
             reads=[rxs], writes=["xsum"])

    def ev_q(pt, pres, ci, t0):
        P.op("act", lambda e: e.activation(out=qs[0:64, t0:t0 + 512], in_=pt[0:64, :], func=AF.Identity, bias=bq8[:, 0:1], scale=0.125),
             reads=[pres, "bq8"], writes=[("qs", ci)])

    def ev_k(pt, pres, ci, t0):
        P.op("act", lambda e: e.activation(out=ke[0:64, t0:t0 + 512], in_=pt[0:64, :], func=AF.Identity, bias=bia[:, 1:2], scale=1.0),
             reads=[pres, "bia"], writes=[("ke", ci)])

    def ev_v(pt, pres, ci, t0):
        P.op("act", lambda e: e.activation(out=vT[0:64, t0:t0 + 512], in_=pt[0:64, :], func=AF.Identity, bias=bia[:, 2:3], scale=1.0),
             reads=[pres, "bia"], writes=[("vT", ci)])

    specs = [(0, 128, [(0, ev_q), (64, ev_k)]), (128, 64, ev_v), (0, 64, ev_q32, w32q, "w32q")]
    proj_pass(C, xT_d, wb, "w", specs, "a", hook=xhook)
    P.op("dve", lambda e: e.tensor_scalar(out=xsum[:], in0=xsum[:], scalar1=1.0 / 256.0, scalar2=None, op0=ALU.mult), reads=["xsum"], writes=["xsum"])
    for kc in range(8):
        P.op("pe", lambda e, kc=kc: e.matmul(C.pm[0:64, 0:32], lhsT=w32k[:, kc, 0:64], rhs=xsum[:, kc, 0:32], start=(kc == 0), stop=(kc == 7)),
             reads=["w32k", "xsum"], writes=["pm"])
    P.op("dve", lambda e: e.tensor_scalar(out=km32[:], in0=C.pm[0:64, 0:32], scalar1=bia[:, 1:2], scalar2=None, op0=ALU.add), reads=["pm", "bia"], writes=["km32"])
    NL = 4
    lane_banks = [C.pp[0], C.pp[1], C.po[0], C.po[1]]
    lane_res = [("pp", 0), ("pp", 1), ("po", 0), ("po", 1)]
    gms = [P.sbuf([128, 32], F32, name=f"gm{i}") for i in range(NL)]
    top8s = [P.sbuf([128, 8], F32, name=f"top8{i}") for i in range(NL)]
    selms = [P.sbuf([128, 32], F32, name=f"selm{i}") for i in range(NL)]
    stages = [P.sbuf([128, 96], BF16, name=f"stage{i}") for i in range(NL)]
    for i in range(NL):
        P.op("pool", lambda e, i=i: e.memset(stages[i][:], 0.0), writes=[("stage", i)])

    def gate_lane(ln):
        bank, bres = lane_banks[ln], lane_res[ln]
        bank_b = bank[:].bitcast(BF16)
        gm, top8, selm, stage = gms[ln], top8s[ln], selms[ln], stages[ln]
        rg, rt, rs, rst = ("gm", ln), ("top8", ln), ("selm", ln), ("stage", ln)

        def tile(ti):
            qb = ti // 2
            c0 = 128 * ti
            P.op("pe", lambda e: e.matmul(bank[0:128, 0:32], lhsT=q32[0:64, c0:c0 + 128], rhs=km32[0:64, 0:32], start=True, stop=True),
                 reads=[("q32", ti // 4), "km32"], writes=[bres])
            yield
            P.op("dve", lambda e: e.tensor_tensor(out=gm[:], in0=bank[0:128, 0:32], in1=blkc[:, 0, qb, :], op=ALU.add), reads=["blkc"], writes=[rg, bres])
            yield
            P.op("dve", lambda e: e.max(out=top8[:], in_=gm[:]), reads=[rg], writes=[rt])
            yield
            P.op("dve", lambda e: e.tensor_scalar(out=selm[:], in0=gm[:], scalar1=top8[:, 2:3], scalar2=None, op0=ALU.is_ge), reads=[rg, rt], writes=[rs])
            yield
            P.op("dve", lambda e: e.tensor_tensor(out=selm[:], in0=selm[:], in1=blkc[:, 1, qb, :], op=ALU.mult), reads=[rs, "blkc"], writes=[rs])
            yield
            P.op("dve", lambda e: e.tensor_tensor(out=selm[:], in0=selm[:], in1=blkc[:, 2, qb, :], op=ALU.add), reads=[rs, "blkc"], writes=[rs])
            yield
            P.op("dve", lambda e: e.tensor_scalar(out=stage[:, 64:96], in0=selm[:], scalar1=-1.0, scalar2=30000.0, op0=ALU.add, op1=ALU.mult), reads=[rs], writes=[rst])
            yield
            P.op("pe", lambda e: e.transpose(bank_b[0:96, 0:128], stage[:, 0:96], C.ident_b[:, :]), reads=[rst, "ident_b"], writes=[bres])
            yield
            P.op("act", lambda e: e.activation(out=qs[64:96, c0:c0 + 128], in_=bank_b[64:96, 0:128], func=AF.Copy), reads=[], writes=[("qsel", ti), bres])
            yield

        for ti in range(ln, 64, NL):
            yield from tile(ti)

    alive = [gate_lane(ln) for ln in range(NL)]
    while alive:
        nxt = []
        for gen in alive:
            try:
                next(gen)
                nxt.append(gen)
            except StopIteration:
                pass
        alive = nxt

    make_vprime(C, vT, lambda blk: [("vT", blk // 4)], vp, "vp", 64, lambda blk: slice(128 * blk, 128 * blk + 128))
    onum = [P.sbuf([65, 512], F32, name=f"onum{i}") for i in range(2)]
    cb = norm_store(C, onum, out_d, "a")
    causal_attention(C, qs, lambda I: [("qs", I)] + [("qsel", 4 * I + k) for k in range(4)], ke, lambda j: [("ke", j // 4), "ke_ind"], 96, vp,
                     lambda j: [("vp", (j // 8) * 8), ("vp", "ones")], negc, None, cb, "a")
    if standalone:
        P.op("sp", lambda e: None, reads=C.out_res)
        P.emit()
    return nc


DILS = (1, 4, 16)


def build_dil(nc=None, C=None, io=None):
    standalone = C is None
    if standalone:
        C = Ctx(nc)
        io = {"xT": C.dram_in("xT", [128, 8, S]), "w": C.dram_in("w", [128, 8, 448]), "bias": C.dram_in("bias", [64, 8]),
              "out": C.dram_out("oT", [64, S])}
        C.band = C.load_const(C.dram_in("band", [128, 256]), [128, 256], "band")
        C.identf, C.ident_b = C.load_const(C.dram_in("ident", [128, 128]), [128, 128], "ident", cast=BF16)
        C.pm_b = C.pm[:].bitcast(BF16)
    P = C.P
    xT_d, w_d, b_d, out_d = io["xT"], io["w"], io["bias"], io["out"]
    band = C.band
    bia = C.load_const(b_d, [64, 8], "bia")
    bq8 = P.sbuf([64, 8], F32, name="bq8")
    P.op("dve", lambda e: e.tensor_scalar(out=bq8[:], in0=bia[:], scalar1=0.125, scalar2=None, op0=ALU.mult), reads=["bia"], writes=["bq8"])
    wb = load_weights(C, w_d, 448, "w")

    qTs = [P.sbuf([64, S], BF16, name=f"qT{g}") for g in range(3)]
    kTs = [P.sbuf([64, S], BF16, name=f"kT{g}") for g in range(3)]
    vT = P.sbuf([64, S], BF16, name="vT")
    vp = P.sbuf([128, 64, 65], BF16, name="vp")
    acc = P.sbuf([65, S], F32, name="acc")
    pts = [P.sbuf([128, 256], BF16, name=f"ptb{i}") for i in range(3)]
    tmps = [P.sbuf([128, 256], F32, name=f"tmpb{i}") for i in range(3)]
    cnt = {"pt": 0}

    def mk_ev(g):
        def ev_q(pt, pres, ci, t0):
            P.op("act", lambda e: e.activation(out=qTs[g][0:64, t0:t0 + 512], in_=pt[0:64, :], func=AF.Identity, bias=bq8[:, 2 * g:2 * g + 1], scale=0.125),
                 reads=[pres, "bq8"], writes=[("qT", g, ci)])

        def ev_k(pt, pres, ci, t0):
            P.op("act", lambda e: e.activation(out=kTs[g][0:64, t0:t0 + 512], in_=pt[0:64, :], func=AF.Identity, bias=bia[:, 2 * g + 1:2 * g + 2], scale=1.0),
                 reads=[pres, "bia"], writes=[("kT", g, ci)])
        return ev_q, ev_k

    def ev_v(pt, pres, ci, t0):
        P.op("act", lambda e: e.activation(out=vT[0:64, t0:t0 + 512], in_=pt[0:64, :], func=AF.Identity, bias=bia[:, 6:7], scale=1.0),
             reads=[pres, "bia"], writes=[("vT", ci)])

    specs = []
    for g in range(3):
        eq_, ek_ = mk_ev(g)
        specs.append((128 * g, 128, [(0, eq_), (64, ek_)]))
    specs.append((384, 64, ev_v))
    proj_pass(C, xT_d, wb, "w", specs, "b", xTb_d=io.get("xTb"))

    for g, d in enumerate(DILS):
        L = S // d
        qT, kT = qTs[g], kTs[g]
        allq = [("qT", g, ci) for ci in range(NCHK)]
        allk = [("kT", g, ci) for ci in range(NCHK)]
        allv = [("vT", ci) for ci in range(NCHK)]
        nlb = L // 128

        def colsel(blk, d=d, nlb=nlb):
            r, lb = divmod(blk, nlb)
            st = r + d * 128 * lb
            return slice(st, st + d * 127 + 1, d)

        make_vprime(C, vT, (lambda blk: allv) if d > 1 else (lambda blk: [("vT", blk // 4)]), vp, ("vp", g), 64, colsel)
        vpres = [(("vp", g), g0) for g0 in range(0, 64, 8)] + [(("vp", g), "ones")]
        tasks = []
        chunks = []
        for r in range(d):
            for c0 in range(0, L, 512):
                ck = (r, c0, C.next_po())
                chunks.append(ck)
                cl = [c for c in range(-1, 4) if c0 + 128 * c >= 0]
                for c in cl:
                    tasks.append((ck, c, c == cl[-1]))

        def stage1(t, ck, c, last, qT=qT, kT=kT, d=d):
            r, c0, (po, po_res) = ck
            qst = r + d * c0
            kl = c0 + 128 * c
            lo = max(0, 128 * c)
            hi = min(512, 128 * c + 256)
            n = hi - lo
            mlo = 128 if c == -1 else 0
            pst, ps_res = C.ps[t % 3], ("ps", t % 3)
            i = t % 3
            pt, tmp = pts[i], tmps[i]
            pt_res, tmp_res = ("ptb", i), ("tmpb", i)
            kst = r + d * kl
            ksl = slice(kst, kst + d * 127 + 1, d)
            qsl = slice(qst + d * lo, qst + d * (hi - 1) + 1, d)
            P.op("pe", lambda e: e.matmul(pst[:, 0:n], lhsT=kT[0:64, ksl], rhs=qT[0:64, qsl], start=True, stop=True),
                 reads=(allq + allk) if d > 1 else [("qT", g, c0 // 512), ("kT", g, kl // 512)], writes=[ps_res])
            P.op("dve", lambda e: e.tensor_tensor(out=tmp[:, 0:n], in0=pst[:, 0:n], in1=band[:, mlo:mlo + n], op=ALU.add), reads=[ps_res, "band"], writes=[tmp_res])
            P.op("act", lambda e: e.activation(out=pt[:, 0:n], in_=tmp[:, 0:n], func=AF.Exp), reads=[tmp_res], writes=[pt_res])

        def stage2(t, ck, c, last, g=g, d=d, nlb=nlb, vpres=vpres):
            r, c0, (po, po_res) = ck
            qst = r + d * c0
            kl = c0 + 128 * c
            i = t % 3
            pt, pt_res = pts[i], ("ptb", i)
            vidx = r * nlb + kl // 128
            if c >= 0:
                first = (c == 0 and c0 == 0)
                P.op("pe", lambda e: e.matmul(po[0:65, 128 * c:128 * c + 128], lhsT=vp[:, vidx, 0:65], rhs=pt[:, 0:128], start=first, stop=True, skip_group_check=True),
                     reads=vpres + [pt_res], writes=[po_res])
            if c <= 2:
                off = 0 if c == -1 else 128
                P.op("pe", lambda e: e.matmul(po[0:65, 128 * (c + 1):128 * (c + 1) + 128], lhsT=vp[:, vidx, 0:65], rhs=pt[:, off:off + 128], start=True, stop=False, skip_group_check=True),
                     reads=vpres + [pt_res], writes=[po_res])
            if last:
                asl = slice(qst, qst + d * 511 + 1, d)
                if g == 0:
                    P.op("act", lambda e: e.activation(out=acc[0:65, asl], in_=po[0:65, :], func=AF.Copy), reads=[po_res], writes=["acc"])
                else:
                    P.op("dve", lambda e: e.tensor_tensor(out=acc[0:65, asl], in0=acc[0:65, asl], in1=po[0:65, :], op=ALU.add), reads=[po_res, "acc"], writes=["acc"])

        LA = 2
        nt = len(tasks)
        for t in range(nt + LA):
            if t < nt:
                stage1(t, *tasks[t])
            if t - LA >= 0:
                stage2(t - LA, *tasks[t - LA])
    for I in range(NCHK):
        sl = slice(512 * I, 512 * I + 512)
        P.op("dve", lambda e, sl=sl: e.reciprocal(out=acc[64:65, sl], in_=acc[64:65, sl]), reads=["acc"], writes=["acc"])
        P.op("pe", lambda e, sl=sl: e.matmul(C.pm[0:64, :], lhsT=C.ones_f[64:65, 0:64], rhs=acc[64:65, sl], start=True, stop=True),
             reads=["ones_f", "acc"], writes=["pm"])
        P.op("dve", lambda e, sl=sl: e.tensor_tensor(out=acc[0:64, sl], in0=acc[0:64, sl], in1=C.pm[0:64, :], op=ALU.mult),
             reads=["acc", "pm"], writes=["acc"])
        P.dma("sp", lambda e, sl=sl: [e.dma_start(out=out_d[:, sl], in_=acc[0:64, sl])], reads=["acc"], writes=[("out", I)], lane="ost")
        C.out_res.append(("out", I))
    if standalone:
        P.op("sp", lambda e: None, reads=C.out_res)
        P.emit()
    return nc


NC_ = 64


def gdn_phase1(C, io, hd, st8, scr):
    P = C.P
    xT_d, w_d, b_d, cw_d, sc_d = io["xT"], io["w"], io["bias"], io["convw"], io["scal"]
    tri = C.tri
    bia = C.load_const(b_d, [64, 8], "bia")
    cw = C.load_const(cw_d, [64, 12], "cw")
    scal = C.load_const(sc_d, [128, 2], "scal")
    wb = load_weights(C, w_d, 194, "w")
    Utri = tri[:, 0, :]
    nA = P.sbuf([1, 1], F32, name="nA")
    bdec = P.sbuf([1, 1], F32, name="bdec")
    P.op("act", lambda e: e.activation(out=nA[:], in_=scal[0:1, 0:1], func=AF.Exp), reads=["scal"], writes=["nA"])
    P.op("dve", lambda e: e.tensor_scalar(out=nA[:], in0=nA[:], scalar1=-1.0, scalar2=None, op0=ALU.mult), reads=["nA"], writes=["nA"])
    P.op("dve", lambda e: e.tensor_tensor(out=bdec[:], in0=bia[0:1, 4:5], in1=scal[0:1, 1:2], op=ALU.add), reads=["bia", "scal"], writes=["bdec"])

    stg3 = [[P.sbuf([64, 512], F32, name=f"qkvst{i}_{b}") for b in range(2)] for i in range(3)]
    brow = P.sbuf([1, 512], F32, name="brow")
    grow = P.sbuf([1, 512], F32, name="grow")
    pBG = C.po[0]
    raws = [P.sbuf([64, 515], F32, name=f"raw{i}") for i in range(3)]
    cys = [P.sbuf([64, 512], F32, name="cy")] * 3
    czs = [P.sbuf([64, 512], F32, name="cz")] * 2
    sqs = [P.sbuf([64, 512], F32, name="sq")] * 2
    rns = [P.sbuf([64, 512], F32, name="rn")] * 2
    nbanks = [(C.pm, "pm"), (C.pm, "pm")]
    erow = P.sbuf([1, 512], F32, name="erow")
    eps_t = P.sbuf([128, 1], F32, name="eps_t")
    P.op("pool", lambda e: e.memset(eps_t[:], 1e-6), writes=["eps_t"])
    for i in range(3):
        P.op("pool", lambda e, i=i: e.memset(raws[i][:, 0:3], 0.0), writes=[("raw", i)])

    def ev_qkv(i):
        def ev(pt, pres, ci, t0):
            dst = stg3[i][ci % 2]
            dres = ("qkvst", i, ci % 2)
            cy, rcy = cys[i], "cy"
            if i < 2:
                cz, sq, rn = czs[i], sqs[i], rns[i]
                rcz, rsq, rrn = "cz", "sq", "rn"
                nb, rnb = nbanks[i]
            raw = raws[i]
            rr = ("raw", i)
            P.op("act", lambda e: e.activation(out=raw[:, 3:515], in_=pt[0:64, :], func=AF.Identity, bias=bia[:, i:i + 1], scale=1.0),
                 reads=[pres, "bia", rr], writes=[rr])
            P.op("dve", lambda e: e.tensor_scalar(out=cy[:], in0=raw[:, 0:512], scalar1=cw[:, 4 * i:4 * i + 1], scalar2=None, op0=ALU.mult),
                 reads=[rr, "cw"], writes=[rcy])
            for j in range(1, 4):
                P.op("dve", lambda e, j=j: e.scalar_tensor_tensor(out=cy[:], in0=raw[:, j:j + 512], scalar=cw[:, 4 * i + j:4 * i + j + 1], in1=cy[:],
                                                                 op0=ALU.mult, op1=ALU.add), reads=[rr, "cw", rcy], writes=[rcy])
            P.op("dve", lambda e: e.tensor_copy(out=raw[:, 0:3], in_=raw[:, 512:515]), reads=[rr], writes=[rr])
            if i == 2:
                P.op("act", lambda e: e.activation(out=dst[:, :], in_=cy[:], func=AF.Silu), reads=[rcy], writes=[dres])
                P.dma("sp", lambda e: [e.dma_start(out=scr[i, :, t0:t0 + 512], in_=dst[:, :])], reads=[dres], writes=[("scr", hd, i, ci)], lane=("qkvst", i, ci % 2))
                return
            P.op("act", lambda e: e.activation(out=cz[:], in_=cy[:], func=AF.Silu), reads=[rcy], writes=[rcz])
            P.op("pool", lambda e: e.tensor_tensor(out=sq[:], in0=cz[:], in1=cz[:], op=ALU.mult), reads=[rcz], writes=[rsq])
            P.op("pe", lambda e: e.matmul(nb[0:64, :], lhsT=C.ones_f[0:64, 0:64], rhs=sq[:], start=True, stop=True), reads=["ones_f", rsq], writes=[rnb])
            P.op("act", lambda e: e.activation(out=rn[:], in_=nb[0:64, :], func=AF.Ln, bias=eps_t[0:64, 0:1], scale=1.0), reads=[rnb, "eps_t"], writes=[rrn])
            P.op("act", lambda e: e.activation(out=rn[:], in_=rn[:], func=AF.Exp, scale=-0.5), reads=[rrn], writes=[rrn])
            if i == 0:
                P.op("dve", lambda e: e.scalar_tensor_tensor(out=dst[:, :], in0=cz[:], scalar=0.125, in1=rn[:], op0=ALU.mult, op1=ALU.mult),
                     reads=[rcz, rrn], writes=[dres])
            else:
                P.op("dve", lambda e: e.tensor_tensor(out=dst[:, :], in0=cz[:], in1=rn[:], op=ALU.mult), reads=[rcz, rrn], writes=[dres])
            P.dma("sp", lambda e: [e.dma_start(out=scr[i, :, t0:t0 + 512], in_=dst[:, :])], reads=[dres], writes=[("scr", hd, i, ci)], lane=("qkvst", i, ci % 2))
        return ev

    def ev_beta(pt, pres, ci, t0):
        P.op("act", lambda e: e.activation(out=brow[0:1, :], in_=pt[0:1, :], func=AF.Sigmoid, bias=bia[0:1, 3:4], scale=1.0),
             reads=[pres, "bia"], writes=["brow"])
        for j in range(4):
            P.op("pe", lambda e, j=j: e.matmul(pBG[0:128, 4 * ci + j:4 * ci + j + 1], lhsT=brow[0:1, 128 * j:128 * j + 128], rhs=C.ones_f[0:1, 0:1], start=True, stop=True),
                 reads=["brow", "ones_f"], writes=["pBG"])

    def ev_dec(pt, pres, ci, t0):
        P.op("act", lambda e: e.activation(out=erow[:], in_=pt[0:1, :], func=AF.Exp, bias=bdec[0:1, 0:1], scale=1.0), reads=[pres, "bdec"], writes=["erow"])
        P.op("act", lambda e: e.activation(out=erow[:], in_=erow[:], func=AF.Ln, bias=1.0, scale=1.0), reads=["erow"], writes=["erow"])
        P.op("dve", lambda e: e.tensor_scalar(out=grow[0:1, :], in0=erow[:], scalar1=nA[0:1, 0:1], scalar2=None, op0=ALU.mult),
             reads=["erow", "nA"], writes=["grow"])
        for j in range(4):
            P.op("pe", lambda e, j=j: e.matmul(pBG[0:128, 64 + 4 * ci + j:64 + 4 * ci + j + 1], lhsT=grow[0:1, 128 * j:128 * j + 128], rhs=C.ones_f[0:1, 0:1], start=True, stop=True),
                 reads=["grow", "ones_f"], writes=["pBG"])

    specs = [(0, 128, [(0, ev_qkv(0)), (64, ev_qkv(1))]), (128, 64, ev_qkv(2)), (192, 1, ev_beta), (193, 1, ev_dec)]
    proj_pass(C, xT_d, wb, "w", specs, "d", xTb_d=io.get("xTb"))

    betaT, gT, gcT, gl, egc, e2, egl, be = [st8[:, k, :] for k in range(8)]
    P.op("dve", lambda e: e.tensor_copy(out=betaT, in_=pBG[:, 0:64]), reads=["pBG"], writes=[("betaT", hd)])
    P.op("dve", lambda e: e.tensor_copy(out=gT, in_=pBG[:, 64:128]), reads=["pBG"], writes=[("gT", hd)])
    P.op("pe", lambda e: e.matmul(C.pm[:, 0:NC_], lhsT=Utri, rhs=gT, start=True, stop=True), reads=["tri", ("gT", hd)], writes=["pm"])
    P.op("dve", lambda e: e.tensor_copy(out=gcT, in_=C.pm[:, 0:NC_]), reads=["pm"], writes=[("gcT", hd)])
    P.op("pe", lambda e: e.matmul(C.pm[:, 64:64 + NC_], lhsT=C.ones_f[:, :], rhs=gT, start=True, stop=True), reads=["ones_f", ("gT", hd)], writes=["pm"])
    P.op("dve", lambda e: e.tensor_copy(out=gl, in_=C.pm[:, 64:64 + NC_]), reads=["pm"], writes=[("gl", hd)])
    P.op("act", lambda e: e.activation(out=egc, in_=gcT, func=AF.Exp), reads=[("gcT", hd)], writes=[("egc", hd)])
    P.op("act", lambda e: e.activation(out=egl, in_=gl, func=AF.Exp), reads=[("gl", hd)], writes=[("egl", hd)])
    P.op("dve", lambda e: e.tensor_tensor(out=e2, in0=gl, in1=gcT, op=ALU.subtract), reads=[("gl", hd), ("gcT", hd)], writes=[("e2", hd)])
    P.op("act", lambda e: e.activation(out=e2, in_=e2, func=AF.Exp), reads=[("e2", hd)], writes=[("e2", hd)])
    P.op("dve", lambda e: e.tensor_tensor(out=be, in0=betaT, in1=egc, op=ALU.mult), reads=[("betaT", hd), ("egc", hd)], writes=[("be", hd)])


def psum_slot(C, k):
    banks = C.pp + C.ps + C.po + [C.pm]
    return banks[k // 4][:, 128 * (k % 4):128 * (k % 4) + 128], ("slot", k)


def gdn_chunk_gen(C, hd, st8, scr, gnw, out_d, slots, T):
    P = C.P
    tri = C.tri
    identf = C.identf
    Utri, negSU, negSL = tri[:, 0, :], tri[:, 1, :], tri[:, 2, :]
    betaT, gT, gcT, gl, egc, e2, egl, be = [st8[:, k, :] for k in range(8)]
    R = lambda nm: (nm, hd)
    (sA, rA), (sB, rB), (sC, rC), (sD, rD), (sE, rE) = slots
    bank_keys = {rA, rB, rC, rD, rE}
    _P = P

    class _W:
        def op(self, eng, fn, reads=(), writes=()):
            rd = [r for r in reads if r not in bank_keys]
            wr = list(writes) + [r for r in reads if r in bank_keys]
            _P.op(eng, fn, reads=rd, writes=wr)

        def dma(self, *a, **k):
            _P.dma(*a, **k)
    P = _W()
    kv_t, kb_t, R32, GU, T1, T2, decST, decS, decTI, kbf = T["kv_t"], T["kb_t"], T["R32"], T["GU"], T["T1"], T["T2"], T["decST"], T["decS"], T["decTI"], T["kbf"]
    Mk, Nk, wf, u32, qkT, kdec_t = T["Mk"], T["Nk"], T["wf"], T["u32"], T["qkT"], T["kdec_t"]
    S32, v32, tmpB, o32, osq, ss, on32, ostage, qkv = T["S32"], T["v32"], T["tmpB"], T["o32"], T["osq"], T["ss"], T["on32"], T["ostage"], T["qkv"]
    eps_t = C.eps6
    P.op("pool", lambda e: e.memset(S32[:], 0.0), writes=[R("S32")])

    def chunk(c):
        cs = slice(128 * c, 128 * c + 128)
        cc = slice(c, c + 1)
        b2 = c % 2
        qc, kc, vc = qkv[b2][:, 0, :], qkv[b2][:, 1, :], qkv[b2][:, 2, :]
        rin = ("qkvc", hd, b2)
        P.dma("sp", lambda e: [e.dma_start(out=qkv[b2][:, :, :], in_=scr[:, :, cs].rearrange("i d t -> d i t"))], writes=[rin])
        yield
        P.op("pe", lambda e: e.transpose(sA[:, 0:64], kc, identf[0:64, 0:64]), reads=[rin, "ident"], writes=[rA])
        P.op("pe", lambda e: e.transpose(sA[:, 64:128], vc, identf[0:64, 0:64]), reads=[rin, "ident"], writes=[rA])
        yield
        P.op("act", lambda e: e.activation(out=kv_t[:], in_=sA[:, 0:128], func=AF.Copy), reads=[rA], writes=[R("kv_t")])
        yield
        P.op("dve", lambda e: e.tensor_scalar(out=kb_t[:], in0=kv_t[:, 0:64], scalar1=betaT[:, cc], scalar2=None, op0=ALU.mult), reads=[R("kv_t"), R("betaT")], writes=[R("kb_t")])
        P.op("pool", lambda e: e.tensor_scalar(out=kdec_t[b2][:], in0=kv_t[:, 0:64], scalar1=e2[:, cc], scalar2=None, op0=ALU.mult), reads=[R("kv_t"), R("e2")], writes=[("kdec_t", hd, b2)])
        P.op("dve", lambda e: e.tensor_scalar(out=R32[:, 0:64], in0=kv_t[:, 64:128], scalar1=betaT[:, cc], scalar2=None, op0=ALU.mult), reads=[R("kv_t"), R("betaT")], writes=[R("R32")])
        P.op("dve", lambda e: e.tensor_scalar(out=R32[:, 64:128], in0=kv_t[:, 0:64], scalar1=be[:, cc], scalar2=None, op0=ALU.mult), reads=[R("kv_t"), R("be")], writes=[R("R32")])
        P.op("pool", lambda e: e.tensor_scalar(out=GU[:], in0=Utri, scalar1=gT[:, cc], scalar2=None, op0=ALU.mult), reads=["tri", R("gT")], writes=[R("GU")])
        yield
        P.op("pe", lambda e: e.transpose(sA[0:64, 0:128], kb_t[:, 0:64], identf[:, :]), reads=[R("kb_t"), "ident"], writes=[rA])
        P.op("pe", lambda e: e.matmul(sB[:, 0:128], lhsT=C.ones_f[:, :], rhs=GU[:], start=True, stop=True), reads=["ones_f", R("GU")], writes=[rB])
        yield
        P.op("act", lambda e: e.activation(out=kbf[:], in_=sA[0:64, 0:128], func=AF.Copy), reads=[rA], writes=[R("kbf")])
        P.op("dve", lambda e: e.scalar_tensor_tensor(out=T1[:], in0=sB[:, 0:128], scalar=gcT[:, cc], in1=negSU, op0=ALU.subtract, op1=ALU.add),
             reads=[rB, R("gcT"), "tri"], writes=[R("T1")])
        P.op("dve", lambda e: e.scalar_tensor_tensor(out=T2[:], in0=sB[:, 0:128], scalar=gcT[:, cc], in1=negSL, op0=ALU.subtract, op1=ALU.subtract),
             reads=[rB, R("gcT"), "tri"], writes=[R("T2")])
        yield
        P.op("act", lambda e: e.activation(out=decST[:], in_=T1[:], func=AF.Exp), reads=[R("T1")], writes=[R("decST")])
        P.op("act", lambda e: e.activation(out=decS[:], in_=T2[:], func=AF.Exp, scale=-1.0), reads=[R("T2")], writes=[R("decS")])
        P.op("pe", lambda e: e.matmul(sC[:, 0:128], lhsT=kbf[:, :], rhs=kc, start=True, stop=True), reads=[R("kbf"), rin], writes=[rC])
        P.op("pe", lambda e: e.matmul(sD[:, 0:128], lhsT=kc, rhs=kbf[:, :], start=True, stop=True), reads=[R("kbf"), rin], writes=[rD])
        P.op("pe", lambda e: e.matmul(sE[:, 0:128], lhsT=kc, rhs=qc, start=True, stop=True), reads=[rin], writes=[rE])
        yield
        P.op("pool", lambda e: e.tensor_tensor(out=decTI[:], in0=decST[:], in1=identf[:], op=ALU.add), reads=[R("decST"), "ident"], writes=[R("decTI")])
        P.op("dve", lambda e: e.tensor_tensor(out=Mk[0][:], in0=sC[:, 0:128], in1=decS[:], op=ALU.mult), reads=[rC, R("decS")], writes=[("Mk", hd, 0)])
        P.op("dve", lambda e: e.tensor_tensor(out=Nk[0][:], in0=sD[:, 0:128], in1=decST[:], op=ALU.mult), reads=[rD, R("decST")], writes=[("Nk", hd, 0)])
        yield
        P.op("dve", lambda e: e.tensor_tensor(out=qkT[b2][:], in0=sE[:, 0:128], in1=decTI[:], op=ALU.mult), reads=[rE, R("decTI")], writes=[("qkT", hd, b2)])
        for k in range(1, 7):
            P.op("pe", lambda e, k=k: e.matmul(sD[:, 0:128], lhsT=Mk[k - 1][:], rhs=Nk[k - 1][:], start=True, stop=True), reads=[("Mk", hd, k - 1), ("Nk", hd, k - 1)], writes=[rD])
            if k < 6:
                P.op("pe", lambda e, k=k: e.matmul(sC[:, 0:128], lhsT=Nk[k - 1][:], rhs=Mk[k - 1][:], start=True, stop=True), reads=[("Mk", hd, k - 1), ("Nk", hd, k - 1)], writes=[rC])
            yield
            P.op("act", lambda e, k=k: e.activation(out=Nk[k][:], in_=sD[:, 0:128], func=AF.Copy), reads=[rD], writes=[("Nk", hd, k)])
            if k < 6:
                P.op("dve", lambda e, k=k: e.tensor_copy(out=Mk[k][:], in_=sC[:, 0:128]), reads=[rC], writes=[("Mk", hd, k)])
            yield
        for k in range(6, -1, -1):
            P.op("pe", lambda e, k=k: e.matmul(sB[:, 0:128], lhsT=Nk[k][:], rhs=R32[:], start=True, stop=True), reads=[("Nk", hd, k), R("R32")], writes=[rB])
            yield
            op = ALU.add if k > 0 else ALU.subtract
            P.op("dve", lambda e, op=op: e.tensor_tensor(out=R32[:], in0=R32[:], in1=sB[:, 0:128], op=op), reads=[rB, R("R32")], writes=[R("R32")])
            yield
        P.op("pool", lambda e: e.tensor_copy(out=u32[b2][:], in_=R32[:, 0:64]), reads=[R("R32")], writes=[("u32", hd, b2)])
        P.op("pe", lambda e: e.transpose(sA[0:64, 0:128], R32[:, 64:128], identf[:, :]), reads=[R("R32"), "ident"], writes=[rA])
        yield
        P.op("act", lambda e: e.activation(out=wf[b2][:], in_=sA[0:64, 0:128], func=AF.Copy), reads=[rA], writes=[("wf", hd, b2)])
        yield
        P.op("pe", lambda e: e.matmul(sE[:, 0:64], lhsT=wf[b2][:, :], rhs=S32[:, :], start=True, stop=True), reads=[("wf", hd, b2), R("S32")], writes=[rE])
        P.op("pe", lambda e: e.matmul(sC[:, 0:64], lhsT=qc, rhs=S32[:, :], start=True, stop=True), reads=[rin, R("S32")], writes=[rC])
        yield
        P.op("dve", lambda e: e.tensor_tensor(out=v32[:], in0=u32[b2][:], in1=sE[:, 0:64], op=ALU.subtract), reads=[("u32", hd, b2), rE], writes=[R("v32")])
        yield
        P.op("pe", lambda e: e.matmul(sC[:, 64:128], lhsT=qkT[b2][:, :], rhs=v32[:, :], start=True, stop=True), reads=[("qkT", hd, b2), R("v32")], writes=[rC])
        P.op("pe", lambda e: e.matmul(sE[0:64, 64:128], lhsT=kdec_t[b2][:, :], rhs=v32[:, :], start=True, stop=True), reads=[("kdec_t", hd, b2), R("v32")], writes=[rE])
        yield
        P.op("dve", lambda e: e.scalar_tensor_tensor(out=S32[:], in0=S32[:], scalar=egl[0:64, cc], in1=sE[0:64, 64:128], op0=ALU.mult, op1=ALU.add),
             reads=[R("S32"), R("egl"), rE], writes=[R("S32")])
        P.op("act", lambda e: e.activation(out=tmpB[:], in_=sC[:, 64:128], func=AF.Copy), reads=[rC], writes=[R("tmpB")])
        yield
        P.op("dve", lambda e: e.scalar_tensor_tensor(out=o32[:], in0=sC[:, 0:64], scalar=egc[:, cc], in1=tmpB[:], op0=ALU.mult, op1=ALU.add),
             reads=[rC, R("egc"), R("tmpB")], writes=[R("o32")])
        yield
        P.op("pool", lambda e: e.tensor_tensor(out=osq[:], in0=o32[:], in1=o32[:], op=ALU.mult), reads=[R("o32")], writes=[R("osq")])
        yield
        P.op("dve", lambda e: e.tensor_reduce(out=ss[:], in_=osq[:], axis=AX.X, op=ALU.add), reads=[R("osq")], writes=[R("ss")])
        yield
        P.op("act", lambda e: e.activation(out=ss[:], in_=ss[:], func=AF.Ln, bias=eps_t[:, 0:1], scale=1.0 / 64.0), reads=[R("ss"), "eps6"], writes=[R("ss")])
        P.op("act", lambda e: e.activation(out=ss[:], in_=ss[:], func=AF.Exp, scale=-0.5), reads=[R("ss")], writes=[R("ss")])
        yield
        P.op("dve", lambda e: e.scalar_tensor_tensor(out=on32[:], in0=o32[:], scalar=ss[:, 0:1], in1=gnw[:], op0=ALU.mult, op1=ALU.mult), reads=[R("o32"), R("ss"), "gnw"], writes=[R("on32")])
        yield
        og = (c // 4) % 2
        P.op("pe", lambda e: e.transpose(sD[0:64, 0:128], on32[:, 0:64], identf[:, :]), reads=[R("on32"), "ident"], writes=[rD])
        yield
        P.op("act", lambda e: e.activation(out=ostage[og][:, 128 * (c % 4):128 * (c % 4) + 128], in_=sD[0:64, 0:128], func=AF.Copy),
             reads=[rD], writes=[("ostage", hd, og)])
        if c % 4 == 3:
            I = c // 4
            P.dma("sp", lambda e: [e.dma_start(out=out_d[:, 512 * I:512 * I + 512], in_=ostage[og][:, :])], reads=[("ostage", hd, og)], writes=[("gout", hd, I)],
                  lane=("gost", hd, og))
            C.out_res.append(("gout", hd, I))
        yield

    import os
    lim = int(os.environ.get("GDN_LIM", "1000000000"))
    cnt = 0
    for c in range(int(os.environ.get("GDN_NCH", NC_))):
        for _ in chunk(c):
            cnt += 1
            if cnt >= lim:
                return
            yield


def gdn_alloc_tiles(P, hd):
    sb = lambda shape, nm: P.sbuf(shape, F32, name=f"{nm}_{hd}")
    T = {}
    for nm in ("kv_t", "R32", "GU", "T1", "T2", "decST", "decS", "decTI"):
        T[nm] = sb([128, 128], nm)
    T["kb_t"] = sb([128, 64], "kb_t")
    T["kbf"] = sb([64, 128], "kbf")
    T["kdec_t"] = [sb([128, 64], f"kdec{i}") for i in range(2)]
    T["Mk"] = [sb([128, 128], f"Mk{i}") for i in range(7)]
    T["Nk"] = [sb([128, 128], f"Nk{i}") for i in range(7)]
    T["wf"] = [sb([64, 128], f"wf{i}") for i in range(2)]
    T["u32"] = [sb([128, 64], f"u32{i}") for i in range(2)]
    T["qkT"] = [sb([128, 128], f"qkT{i}") for i in range(2)]
    T["S32"] = sb([64, 64], "S32")
    for nm in ("v32", "tmpB", "o32", "osq", "on32"):
        T[nm] = sb([128, 64], nm)
    T["ss"] = sb([128, 1], "ss")
    T["ostage"] = [sb([64, 512], f"ostage{i}") for i in range(2)]
    T["qkv"] = [sb([64, 3, 128], f"qkvc{i}") for i in range(2)]
    return T


def gdn_group(C, ios, scrs, gnw_d, outs):
    import itertools
    P = C.P
    G = len(ios)
    C.eps6 = P.sbuf([128, 1], F32, name="eps6")
    P.op("pool", lambda e: e.memset(C.eps6[:], 1e-6), writes=["eps6"])
    st8s = [P.sbuf([128, 8, NC_], F32, name=f"st8_{g}") for g in range(G)]
    gnw = C.load_const(gnw_d, [128, 64], "gnw")
    for g in range(G):
        with P.scope():
            C.new_scope()
            gdn_phase1(C, ios[g], g, st8s[g], scrs[g])
    assert G <= 8
    Ts = [gdn_alloc_tiles(P, g) for g in range(G)]
    banks = C.pp + C.ps + C.po + [C.pm]

    def slots_for(g):
        bk, key = banks[g], ("gbank", g)
        sl = [(bk[:, 128 * k:128 * k + 128], key) for k in range(4)]
        return sl + [sl[0]]

    gens = [gdn_chunk_gen(C, g, st8s[g], scrs[g], gnw, outs[g], slots_for(g), Ts[g]) for g in range(G)]
    alive = list(gens)
    while alive:
        nxt = []
        for gen in alive:
            try:
                next(gen)
                nxt.append(gen)
            except StopIteration:
                pass
        alive = nxt


def build_gdn(nc=None, C=None, io=None):
    C = Ctx(nc)
    io = {"xT": C.dram_in("xT", [128, 8, S]), "w": C.dram_in("w", [2, 128, 8, 128]), "bias": C.dram_in("bias", [64, 8]),
          "convw": C.dram_in("convw", [64, 12]), "scal": C.dram_in("scal", [128, 2])}
    gnw_d = C.dram_in("gnw", [128, 64])
    out_d = C.dram_out("oT", [64, S])
    C.tri = C.load_const(C.dram_in("tri", [128, 3, 128]), [128, 3, 128], "tri")
    C.identf, C.ident_b = C.load_const(C.dram_in("ident", [128, 128]), [128, 128], "ident", cast=BF16)
    C.pm_b = C.pm[:].bitcast(BF16)
    scr = nc.dram_tensor("gdn_scr", [3, 64, S], F32).ap()
    gdn_group(C, [io], [scr], gnw_d, [out_d])
    C.P.barrier()
    C.P.op("sp", lambda e: None)
    C.P.emit()
    return nc


T_ = 1024
NTC = 2
ALPHA = (2 * 2) ** 0.25
LN_EPS = 1e-5


def stream_w(C, w_d, c0, n, tag):
    P = C.P
    L = C.loc
    if not hasattr(L, "ws_bufs"):
        L.ws_stg = [P.sbuf([128, 8, 128], F32, name=f"wsstg{i}") for i in range(2)]
        L.ws_bufs = [P.sbuf([128, 8, 128], BF16, name=f"wsb{i}") for i in range(2)]
        L.ws_i = 0
    i = L.ws_i
    L.ws_i += 1
    st, rs = L.ws_stg[i % 2], ("wsstg", i % 2)
    wt, rw = L.ws_bufs[i % 2], ("wsb", i % 2)
    P.dma("act", lambda e: [e.dma_start(out=st[:, :, :], in_=w_d[c0])], writes=[rs])
    P.op("pool", lambda e: e.tensor_copy(out=wt[:, :, 0:n], in_=st[:, :, 0:n]), reads=[rs], writes=[rw])
    return wt, rw


def build_merge(nc=None, C=None, io=None):
    standalone = C is None
    if standalone:
        C = Ctx(nc)
        io = {"xT": C.dram_in("xT", [128, 8, T_]), "xtok": C.dram_in("xtok", [T_, 1024]), "oT_in": C.dram_in("oT_in", [4, 384, T_]),
              "wz": C.dram_in("wz", [128, 8, 1920]), "bz": C.dram_in("bz", [128, 16]), "wm": C.dram_in("wm", [128, 8, 5120]),
              "bm": C.dram_in("bm", [128, 40]), "weq": C.dram_in("weq", [128, 8, 384]), "beq": C.dram_in("beq", [96, 4]),
              "mem": C.dram_in("mem", [256, 1024]), "memg": C.dram_in("memg", [128, 2, 1024]), "wkv": C.dram_in("wkv", [128, 8, 768]),
              "wbr": C.dram_in("wbr", [128, 12, 1024]), "wbe": C.dram_in("wbe", [96, 4, 1024]), "wo": C.dram_in("wo", [128, 8, 1024]),
              "lng": C.dram_in("lng", [128, 2, 1024]), "out": C.dram_out("out", [T_, 1024]), "outT": None}
        C.identf, C.ident_b = C.load_const(C.dram_in("ident", [128, 128]), [128, 128], "ident", cast=BF16)
        C.pm_b = C.pm[:].bitcast(BF16)
    P = C.P
    wz_d, bz_d, wm_d, bm_d = io["wz"], io["bz"], io["wm"], io["bm"]
    weq_d, beq_d, mem_d, mg_d, wkv_d, wbr_d, wbe_d = io["weq"], io["beq"], io["mem"], io["memg"], io["wkv"], io["wbr"], io["wbe"]
    wo_d, lng_d = io["wo"], io["lng"]
    shards = io.get("shards") or [{"xT": io["xT"], "xtok": io["xtok"], "oT_in": io["oT_in"], "out": io["out"], "outT": io["outT"]}]
    _cache = {}

    def A(shape, dtype, name=None):
        if name not in _cache:
            _cache[name] = P.sbuf(shape, dtype, name=name)
        return _cache[name]
    identf = C.identf
    bz = C.load_const(bz_d, [128, 16], "bz")
    bm = C.load_const(bm_d, [128, 40], "bm")
    beq = C.load_const(beq_d, [96, 4], "beq")
    SC = 96 ** -0.5
    beqs = A([96, 4], F32, name="beqs")
    P.op("dve", lambda e: e.tensor_scalar(out=beqs[:], in0=beq[:], scalar1=SC, scalar2=None, op0=ALU.mult), reads=["beq"], writes=["beqs"])
    eps_t = A([128, 1], F32, name="eps_t")
    P.op("pool", lambda e: e.memset(eps_t[:], LN_EPS), writes=["eps_t"])

    big = [A([128, 1024], F32, name=f"big{i}") for i in range(3)]
    stat = A([128, 4], F32, name="stat")

    def layer_norm(src, src_res, gb, gb_res, dst, dst_res, tmp, tmp_res):
        P.op("dve", lambda e: e.tensor_reduce(out=stat[:, 0:1], in_=src[:], axis=AX.X, op=ALU.add), reads=[src_res], writes=["stat0"])
        P.op("dve", lambda e: e.tensor_scalar(out=stat[:, 0:1], in0=stat[:, 0:1], scalar1=1.0 / 1024.0, scalar2=None, op0=ALU.mult), reads=["stat0"], writes=["stat0"])
        P.op("dve", lambda e: e.tensor_scalar(out=src[:], in0=src[:], scalar1=stat[:, 0:1], scalar2=None, op0=ALU.subtract), reads=[src_res, "stat0"], writes=[src_res])
        P.op("pool", lambda e: e.tensor_tensor(out=tmp[:], in0=src[:], in1=src[:], op=ALU.mult), reads=[src_res], writes=[tmp_res])
        P.op("dve", lambda e: e.tensor_reduce(out=stat[:, 1:2], in_=tmp[:], axis=AX.X, op=ALU.add), reads=[tmp_res], writes=["stat1"])
        P.op("act", lambda e: e.activation(out=stat[:, 1:2], in_=stat[:, 1:2], func=AF.Ln, bias=eps_t[:, 0:1], scale=1.0 / 1024.0), reads=["stat1", "eps_t"], writes=["stat1"])
        P.op("act", lambda e: e.activation(out=stat[:, 1:2], in_=stat[:, 1:2], func=AF.Exp, scale=-0.5), reads=["stat1"], writes=["stat1"])
        P.op("dve", lambda e: e.scalar_tensor_tensor(out=tmp[:], in0=src[:], scalar=stat[:, 1:2], in1=gb[:, 0, :], op0=ALU.mult, op1=ALU.mult),
             reads=[src_res, "stat1", gb_res], writes=[tmp_res])
        P.op("pool", lambda e: e.tensor_tensor(out=dst[:], in0=tmp[:], in1=gb[:, 1, :], op=ALU.add), reads=[tmp_res, gb_res], writes=[dst_res])

    gbt = A([128, 2, 1024], F32, name="gbt")
    P.dma("sp", lambda e: [e.dma_start(out=gbt[:], in_=mg_d)], writes=["gbt"])
    mg = gbt
    memT = A([128, 8, 256], BF16, name="memT")
    memn_b = A([128, 1024], BF16, name="memn_b")
    for t in range(2):
        P.dma("sp", lambda e, t=t: [e.dma_start(out=big[0][:], in_=mem_d[128 * t:128 * t + 128, :])], writes=["big0"])
        layer_norm(big[0], "big0", mg, "gbt", big[1], "big1", big[2], "big2")
        P.op("act", lambda e: e.activation(out=memn_b[:], in_=big[1][:], func=AF.Copy), reads=["big1"], writes=["memn_b"])
        for kc in range(8):
            P.op("pe", lambda e, kc=kc: e.transpose(C.pm_b[:, 128 * kc:128 * kc + 128], memn_b[:, 128 * kc:128 * kc + 128], C.ident_b[:, :]),
                 reads=["memn_b", "ident_b"], writes=["pm"])
        P.op("act", lambda e, t=t: e.activation(out=memT[:, :, 128 * t:128 * t + 128], in_=C.pm_b[:, 0:1024].rearrange("p (k m) -> p k m", k=8), func=AF.Copy),
             reads=["pm"], writes=[("memT", t)])
    wbig = A([128, 8, 1024], BF16, name="wbig")
    wkv = load_weights(C, wkv_d, 768, "wbig", wb=wbig)
    kmT = A([96, 4, 256], BF16, name="kmT")
    vm = A([128, 2, 4, 97], BF16, name="vm")
    P.op("pool", lambda e: e.memset(vm[:, :, :, 96:97], 1.0), writes=["vm_ones"])
    for h in range(4):
        pt, pres = C.next_pp()
        for kc in range(8):
            P.op("pe", lambda e, pt=pt, kc=kc, h=h: e.matmul(pt[0:96, 0:256], lhsT=wkv[:, kc, 96 * h:96 * h + 96], rhs=memT[:, kc, :], start=(kc == 0), stop=(kc == 7)),
                 reads=["wbig", ("memT", 0), ("memT", 1)], writes=[pres])
        P.op("act", lambda e, pt=pt, h=h: e.activation(out=kmT[:, h, :], in_=pt[0:96, 0:256], func=AF.Copy), reads=[pres], writes=[("kmT", h)])
    for t in range(2):
        pt, pres = C.next_pp()
        for kc in range(8):
            P.op("pe", lambda e, pt=pt, kc=kc, t=t: e.matmul(pt[:, 0:384], lhsT=memT[:, kc, 128 * t:128 * t + 128], rhs=wkv[:, kc, 384:768], start=(kc == 0), stop=(kc == 7)),
                 reads=["wbig", ("memT", t)], writes=[pres])
        P.op("act", lambda e, pt=pt, t=t: e.activation(out=vm[:, t, :, 0:96], in_=pt[:, 0:384].rearrange("p (h d) -> p h d", h=4), func=AF.Copy),
             reads=[pres, "vm_ones"], writes=[("vm", t)])

    wbr = A([128, 12, 1024], BF16, name="wbr")
    wbe = A([96, 4, 1024], BF16, name="wbe")
    for q in range(16):
        bt, br = big[1 + q % 2], f"big{1 + q % 2}"
        if q < 12:
            P.dma("act", lambda e, q=q, bt=bt: [e.dma_start(out=bt[:, :], in_=wbr_d[:, q, :])], writes=[br])
            P.op("pool", lambda e, q=q, bt=bt: e.tensor_copy(out=wbr[:, q, :], in_=bt[:, :]), reads=[br], writes=["wbr"])
        else:
            P.dma("act", lambda e, q=q, bt=bt: [e.dma_start(out=bt[0:96, :], in_=wbe_d[:, q - 12, :])], writes=[br])
            P.op("pool", lambda e, q=q, bt=bt: e.tensor_copy(out=wbe[:, q - 12, :], in_=bt[0:96, :]), reads=[br], writes=["wbe"])
    wo = load_weights(C, wo_d, 1024, "wbig", wb=wbig)
    P.dma("sp", lambda e: [e.dma_start(out=gbt[:], in_=lng_d)], reads=[], writes=["gbt"])
    lng = gbt
    def do_shard(sh):
        xT_d, xtok_d, oT_d, out_d, outT_d = sh["xT"], sh["xtok"], sh["oT_in"], sh["out"], sh["outT"]
        xb = A([128, 8, T_], BF16, name="xb")
        xs = [A([128, 2, 512], F32, name=f"xs{i}") for i in range(2)]
        for ci in range(NTC):
            t0 = 512 * ci
            for q4 in range(4):
                hh = q4 % 2
                P.dma("sp", lambda e, hh=hh, q4=q4, t0=t0: [e.dma_start(out=xs[hh][:, :, :], in_=xT_d[:, 2 * q4:2 * q4 + 2, t0:t0 + 512])], writes=[("xs", hh)])
                P.op("pool" if hh == 0 else "dve", lambda e, hh=hh, q4=q4, t0=t0: e.tensor_copy(out=xb[:, 2 * q4:2 * q4 + 2, t0:t0 + 512], in_=xs[hh][:, :, :]), reads=[("xs", hh)], writes=[("xb", ci, q4)])

        def xres(ci):
            return [("xb", ci, q4) for q4 in range(4)]


        gE = A([96, 4, T_], BF16, name="gE")
        eq = A([96, 512], BF16, name="eq")
        szE = A([96, 512], F32, name="szE")
        ptE = [A([128, 512], BF16, name=f"ptE{i}") for i in range(2)]
        onE = A([97, 512], F32, name="onE")
        rdenE = A([1, 512], F32, name="rdenE")
        for h in range(4):
            wq_t, wq_r = stream_w(C, weq_d, h, 96, "weq")
            wz_t, wz_r = stream_w(C, wz_d, 12 + h, 96, "wzE")
            for ci in range(NTC):
                cs = slice(512 * ci, 512 * ci + 512)
                pt, pres = C.next_pp()
                for kc in range(8):
                    P.op("pe", lambda e, pt=pt, kc=kc, cs=cs: e.matmul(pt[0:96, :], lhsT=wq_t[:, kc, 0:96], rhs=xb[:, kc, cs], start=(kc == 0), stop=(kc == 7)),
                         reads=[wq_r] + xres(ci), writes=[pres])
                P.op("act", lambda e, pt=pt, h=h: e.activation(out=eq[:], in_=pt[0:96, :], func=AF.Identity, bias=beqs[:, h:h + 1], scale=SC), reads=[pres, "beqs"], writes=["eq"])
                pt2, pres2 = C.next_pp()
                for kc in range(8):
                    P.op("pe", lambda e, pt2=pt2, kc=kc, cs=cs: e.matmul(pt2[0:96, :], lhsT=wz_t[:, kc, 0:96], rhs=xb[:, kc, cs], start=(kc == 0), stop=(kc == 7)),
                         reads=[wz_r] + xres(ci), writes=[pres2])
                P.op("act", lambda e, pt2=pt2, h=h: e.activation(out=szE[:], in_=pt2[0:96, :], func=AF.Silu, bias=bz[0:96, 12 + h:13 + h], scale=1.0), reads=[pres2, "bz"], writes=["szE"])
                po, po_res = C.next_po()
                for t in range(2):
                    pst, ps_res = C.next_ps()
                    P.op("pe", lambda e, pst=pst, t=t, h=h: e.matmul(pst[:, :], lhsT=kmT[:, h, 128 * t:128 * t + 128], rhs=eq[:, :], start=True, stop=True),
                         reads=[("kmT", h), "eq"], writes=[ps_res])
                    P.op("act", lambda e, pst=pst, t=t: e.activation(out=ptE[t][:], in_=pst[:, :], func=AF.Exp), reads=[ps_res], writes=[("ptE", t)])
                    P.op("pe", lambda e, po=po, t=t, h=h: e.matmul(po[0:97, :], lhsT=vm[:, t, h, 0:97], rhs=ptE[t][:], start=(t == 0), stop=(t == 1)),
                         reads=[("vm", t), "vm_ones", ("ptE", t)], writes=[po_res])
                P.op("act", lambda e, po=po: e.activation(out=onE[0:97, :], in_=po[0:97, :], func=AF.Copy), reads=[po_res], writes=["onE"])
                P.op("dve", lambda e: e.reciprocal(out=rdenE[0:1, :], in_=onE[96:97, :]), reads=["onE"], writes=["rdenE"])
                P.op("pe", lambda e: e.matmul(C.pm[0:96, :], lhsT=C.ones_f[0:1, 0:96], rhs=rdenE[0:1, :], start=True, stop=True), reads=["ones_f", "rdenE"], writes=["pm"])
                P.op("dve", lambda e: e.tensor_tensor(out=onE[0:96, :], in0=onE[0:96, :], in1=C.pm[0:96, :], op=ALU.mult), reads=["onE", "pm"], writes=["onE"])
                P.op("dve", lambda e, h=h, cs=cs: e.tensor_tensor(out=gE[:, h, cs], in0=onE[0:96, :], in1=szE[:], op=ALU.mult), reads=["onE", "szE"], writes=[("gE", h, ci)])

        g4 = A([128, 12, T_], BF16, name="g4")
        oin = [A([128, 512], F32, name=f"oin{i}") for i in range(2)]
        sz = [A([128, 512], F32, name=f"sz{i}") for i in range(2)]
        it = 0
        for n in range(4):
            for k in range(3):
                wz_t, wz_r = stream_w(C, wz_d, 3 * n + k, 128, "wz")
                for ci in range(NTC):
                    cs = slice(512 * ci, 512 * ci + 512)
                    b = it % 2
                    it += 1
                    P.dma("sp", lambda e, b=b, n=n, k=k, cs=cs: [e.dma_start(out=oin[b][:], in_=oT_d[n, 128 * k:128 * k + 128, cs])], writes=[("oin", b)])
                    pt, pres = C.next_pp()
                    for kc in range(8):
                        P.op("pe", lambda e, pt=pt, kc=kc, cs=cs, wz_t=wz_t: e.matmul(pt[:, :], lhsT=wz_t[:, kc, 0:128], rhs=xb[:, kc, cs], start=(kc == 0), stop=(kc == 7)),
                             reads=[wz_r] + xres(ci), writes=[pres])
                    P.op("act", lambda e, pt=pt, b=b, n=n, k=k: e.activation(out=sz[b][:], in_=pt[:, :], func=AF.Silu, bias=bz[:, 3 * n + k:3 * n + k + 1], scale=1.0),
                         reads=[pres, "bz"], writes=[("sz", b)])
                    P.op("dve", lambda e, b=b, n=n, k=k, cs=cs: e.tensor_tensor(out=g4[:, 3 * n + k, cs], in0=oin[b][:], in1=sz[b][:], op=ALU.mult),
                         reads=[("oin", b), ("sz", b)], writes=[("g4", n, k, ci)])

        wbr, wbe = _cache["wbr"], _cache["wbe"]
        yb = A([128, 8, T_], BF16, name="yb")
        yacc = A([128, T_], F32, name="yacc")
        sig = [A([128, 512], F32, name=f"sig{i}") for i in range(2)]
        ytmp = [A([128, 512], F32, name=f"ytmp{i}") for i in range(2)]
        it = 0
        for dmt in range(8):
            dsl = slice(128 * dmt, 128 * dmt + 128)
            for n in range(5):
                wm_t, wm_r = stream_w(C, wm_d, 8 * n + dmt, 128, "wm")
                for ci in range(NTC):
                    cs = slice(512 * ci, 512 * ci + 512)
                    b = it % 2
                    it += 1
                    pt, pres = C.next_pp()
                    for kc in range(8):
                        P.op("pe", lambda e, pt=pt, kc=kc, cs=cs, wm_t=wm_t: e.matmul(pt[:, :], lhsT=wm_t[:, kc, 0:128], rhs=xb[:, kc, cs], start=(kc == 0), stop=(kc == 7)),
                             reads=[wm_r] + xres(ci), writes=[pres])
                    P.op("act", lambda e, pt=pt, b=b, n=n, dmt=dmt: e.activation(out=sig[b][:], in_=pt[:, :], func=AF.Sigmoid, bias=bm[:, 8 * n + dmt:8 * n + dmt + 1], scale=1.0),
                         reads=[pres, "bm"], writes=[("sig", b)])
                    pst, ps_res = C.next_ps()
                    if n < 4:
                        for k in range(3):
                            P.op("pe", lambda e, pst=pst, k=k, n=n, cs=cs, dsl=dsl: e.matmul(pst[:, :], lhsT=wbr[:, 3 * n + k, dsl], rhs=g4[:, 3 * n + k, cs], start=(k == 0), stop=(k == 2)),
                                 reads=["wbr", ("g4", n, k, ci)], writes=[ps_res])
                    else:
                        for h in range(4):
                            P.op("pe", lambda e, pst=pst, h=h, cs=cs, dsl=dsl: e.matmul(pst[:, :], lhsT=wbe[:, h, dsl], rhs=gE[:, h, cs], start=(h == 0), stop=(h == 3)),
                                 reads=["wbe", ("gE", h, ci)], writes=[ps_res])
                    if n == 0:
                        P.op("dve", lambda e, pst=pst, b=b, cs=cs: e.tensor_tensor(out=yacc[:, cs], in0=sig[b][:], in1=pst[:, :], op=ALU.mult), reads=[("sig", b), ps_res], writes=[("yacc", ci)])
                    else:
                        P.op("dve", lambda e, pst=pst, b=b: e.tensor_tensor(out=ytmp[b][:], in0=sig[b][:], in1=pst[:, :], op=ALU.mult), reads=[("sig", b), ps_res], writes=[("ytmp", b)])
                        P.op("pool", lambda e, b=b, cs=cs: e.tensor_tensor(out=yacc[:, cs], in0=yacc[:, cs], in1=ytmp[b][:], op=ALU.add), reads=[("ytmp", b), ("yacc", ci)], writes=[("yacc", ci)])
            for ci in range(NTC):
                cs = slice(512 * ci, 512 * ci + 512)
                P.op("act", lambda e, dmt=dmt, cs=cs: e.activation(out=yb[:, dmt, cs], in_=yacc[:, cs], func=AF.Copy), reads=[("yacc", ci)], writes=[("yb", dmt, ci)])

        wo, lng = wbig, gbt
        for tt in range(T_ // 128):
            ts = slice(128 * tt, 128 * tt + 128)
            ci = tt // 4
            P.dma("sp", lambda e, ts=ts: [e.dma_start(out=big[0][:], in_=xtok_d[ts, :])], writes=["big0"])
            for half in range(2):
                pt, pres = C.next_pp()
                hs = slice(512 * half, 512 * half + 512)
                for kc in range(8):
                    P.op("pe", lambda e, pt=pt, kc=kc, ts=ts, hs=hs: e.matmul(pt[:, :], lhsT=yb[:, kc, ts], rhs=wo[:, kc, hs], start=(kc == 0), stop=(kc == 7)),
                         reads=["wbig"] + [("yb", kc, ci) for kc in range(8)], writes=[pres])
                P.op("dve", lambda e, pt=pt, hs=hs: e.scalar_tensor_tensor(out=big[1][:, hs], in0=big[0][:, hs], scalar=ALPHA, in1=pt[:, :], op0=ALU.mult, op1=ALU.add),
                     reads=["big0", pres], writes=["big1"])
            layer_norm(big[1], "big1", lng, "gbt", big[0], "big0", big[2], "big2")
            P.dma("sp", lambda e, ts=ts: [e.dma_start(out=out_d[ts, :], in_=big[0][:])], reads=["big0"], writes=[("out", tt)], lane="ost")
            C.out_res.append(("out", tt))
            if outT_d is not None:
                xTn = big[2][:, :].rearrange("p (k t) -> p k t", k=8)
                for half in range(2):
                    for k4 in range(4):
                        kc = 4 * half + k4
                        P.op("pe", lambda e, kc=kc, k4=k4: e.transpose(C.pm[:, 128 * k4:128 * k4 + 128], big[0][:, 128 * kc:128 * kc + 128], identf[:, :]),
                             reads=["big0", "ident"], writes=["pm"])
                    P.op("act", lambda e, half=half: e.activation(out=xTn[:, 4 * half:4 * half + 4, :], in_=C.pm[:, 0:512].rearrange("p (k t) -> p k t", k=4), func=AF.Copy),
                         reads=["pm"], writes=["big2"])
                P.dma("sp", lambda e, ts=ts: [e.dma_start(out=outT_d[:, :, ts], in_=xTn)], reads=["big2"], writes=[("outT", tt)], lane="ostT")

    for sh in shards:
        do_shard(sh)
    if standalone:
        P.op("sp", lambda e: None, reads=C.out_res)
        P.emit()
    return nc


DEPTH = 2
NH = 6


def build_fused(nc, depth=DEPTH, heads=NH, nshards=S // T_):
    C = Ctx(nc)
    P = C.P
    di = C.dram_in
    xT0 = di("xT0", [128, 8, S])
    xtok0 = di("xtok0", [S, 1024])
    mem = di("mem", [256, 1024])
    memg = di("memg", [128, 2, 1024])
    wfox, bfox = di("wfox", [DEPTH, NH, 2, 128, 8, 128]), di("bfox", [DEPTH, NH, 64, 4])
    wmoba, bmoba = di("wmoba", [DEPTH, NH, 2, 128, 8, 128]), di("bmoba", [DEPTH, NH, 64, 4])
    wdil, bdil = di("wdil", [DEPTH, NH, 4, 128, 8, 128]), di("bdil", [DEPTH, NH, 64, 8])
    wgdn, bgdn = di("wgdn", [DEPTH, NH, 2, 128, 8, 128]), di("bgdn", [DEPTH, NH, 64, 8])
    convw, scal, gnw = di("convw", [DEPTH, NH, 64, 12]), di("scal", [DEPTH, NH, 128, 2]), di("gnw", [DEPTH, 128, 64])
    wz, bz = di("wz", [DEPTH, 16, 128, 8, 128]), di("bz", [DEPTH, 128, 16])
    wm, bm = di("wm", [DEPTH, 40, 128, 8, 128]), di("bm", [DEPTH, 128, 40])
    weq, beq = di("weq", [DEPTH, 4, 128, 8, 128]), di("beq", [DEPTH, 96, 4])
    wkv = di("wkv", [DEPTH, 6, 128, 8, 128])
    wbr, wbe = di("wbr", [DEPTH, 128, 12, 1024]), di("wbe", [DEPTH, 96, 4, 1024])
    wo, lng = di("wo", [DEPTH, 8, 128, 8, 128]), di("lng", [DEPTH, 128, 2, 1024])
    ind, blkc = di("ind", [32, S]), di("blkc", [128, 3, 32, 32])
    out = C.dram_out("out", [S, 1024])
    oT_all = nc.dram_tensor("oT_all", [4, 384, S], F32).ap()
    xT1 = nc.dram_tensor("xT1", [128, 8, S], F32).ap()
    xtok1 = nc.dram_tensor("xtok1", [S, 1024], F32).ap()
    gscr = nc.dram_tensor("gdn_scr", [NH, 3, 64, S], F32).ap()
    xTb = nc.dram_tensor("xTb", [128, 8, S], BF16).ap()
    C.negc = C.load_const(di("negc", [128, 128]), [128, 128], "negc")
    C.band = C.load_const(di("band", [128, 256]), [128, 256], "band")
    C.tri = C.load_const(di("tri", [128, 3, 128]), [128, 3, 128], "tri")
    C.identf, C.ident_b = C.load_const(di("ident", [128, 128]), [128, 128], "ident", cast=BF16)
    C.pm_b = C.pm[:].bitcast(BF16)
    for l in range(depth):
        xT = xT0 if l == 0 else xT1
        xtok = xtok0 if l == 0 else xtok1
        with P.scope():
            C.new_scope()
            convert_xT(C, xT, xTb)
        for h in range(heads):
            osl = lambda n: oT_all[n, 64 * h:64 * h + 64, :]
            with P.scope():
                C.new_scope()
                build_moba(C=C, io={"xT": xT, "w": wmoba[l, h], "bias": bmoba[l, h], "ind": ind, "blkc": blkc, "out": osl(0)})
            with P.scope():
                C.new_scope()
                build_dil(C=C, io={"xT": xT, "xTb": xTb, "w": wdil[l, h], "bias": bdil[l, h], "out": osl(1)})
            with P.scope():
                C.new_scope()
                build_fox(C=C, io={"xT": xT, "xTb": xTb, "w": wfox[l, h], "bias": bfox[l, h], "out": osl(2)})
        with P.scope():
            C.new_scope()
            gdn_group(C, [{"xT": xT, "xTb": xTb, "w": wgdn[l, h], "bias": bgdn[l, h], "convw": convw[l, h], "scal": scal[l, h]} for h in range(heads)],
                      [gscr[h] for h in range(heads)], gnw[l], [oT_all[3, 64 * h:64 * h + 64, :] for h in range(heads)])
        last = (l == depth - 1)
        shards = []
        for s_ in range(nshards):
            ts = slice(T_ * s_, T_ * s_ + T_)
            shards.append({"xT": xT[:, :, ts], "xtok": xtok[ts, :], "oT_in": oT_all[:, :, ts], "out": (out if last else xtok1)[ts, :],
                           "outT": None if last else xT1[:, :, ts]})
        with P.scope():
            C.new_scope()
            build_merge(C=C, io={"shards": shards, "wz": wz[l], "bz": bz[l], "wm": wm[l], "bm": bm[l], "weq": weq[l], "beq": beq[l], "mem": mem, "memg": memg,
                                 "wkv": wkv[l], "wbr": wbr[l], "wbe": wbe[l], "wo": wo[l], "lng": lng[l]})
    P.barrier()
    P.op("sp", lambda e: None)
    print("fused ops:", len(P.ops), flush=True)
    P.emit()
    return nc


H = 6
BW = 384
OFF_A = 0
OFF_BQK = 1152
OFF_BV = 3456
OFF_C = 3840
OFF_CF = 4992
OFF_D = 4998
OFF_DBETA = 6150
OFF_DDEC = 6156
OFF_EQ = 6162
OFF_Z = 6546
OFF_M = 8466


def _xT_layout(xb):
    T = xb.shape[0]
    return np.ascontiguousarray(xb.T.reshape(8, 128, T).transpose(1, 0, 2))


def _w_layout(Wc):
    n = Wc.shape[1]
    return np.ascontiguousarray(Wc.reshape(8, 128, n).transpose(1, 0, 2))


def _w_tiles(Wc, bounds=None):
    n = Wc.shape[1]
    if bounds is None:
        bounds = [(c0, min(c0 + 128, n)) for c0 in range(0, n, 128)]
    out = np.zeros((len(bounds), 128, 8, 128), np.float32)
    for t, (a, b) in enumerate(bounds):
        out[t, :, :, 0:b - a] = Wc[:, a:b].reshape(8, 128, b - a).transpose(1, 0, 2)
    return out


def _consts():
    s_ = np.arange(128)[:, None]
    t_ = np.arange(128)[None, :]
    c = {}
    c["negc"] = np.where(t_ >= s_, 0.0, -30000.0).astype(np.float32)
    c["band"] = np.concatenate([np.where(t_ >= s_, 0.0, -30000.0), np.where(t_ <= s_, 0.0, -30000.0)], 1).astype(np.float32)
    c["ind"] = (np.arange(S)[None, :] // 256 == np.arange(32)[:, None]).astype(np.float32)
    qb = np.arange(32)[:, None]
    n = np.arange(32)[None, :]
    blk = np.stack([np.where(n < qb, 0.0, -1e9), (n < qb).astype(np.float32), (n == qb).astype(np.float32)]).astype(np.float32)
    c["blkc"] = np.ascontiguousarray(np.broadcast_to(blk[None], (128, 3, 32, 32)))
    c["ident"] = np.eye(128, dtype=np.float32)
    c["tri"] = np.stack([(s_ <= t_).astype(np.float32), np.where(t_ > s_, 0.0, -30000.0), np.where(s_ > t_, 0.0, -30000.0)], 1).astype(np.float32)
    return c


def _pack_weights(w_in, b_in, conv_w, a_log, dt_bias, gdn_norm_w, w_mem_kv, w_branch, w_out, ln_g, ln_b, mem_ln_g, mem_ln_b):
    f32 = np.float32
    L = w_in.shape[0]
    d = {k: [] for k in ("wfox", "bfox", "wmoba", "bmoba", "wdil", "bdil", "wgdn", "bgdn", "convw", "scal", "gnw",
                         "wz", "bz", "wm", "bm", "weq", "beq", "wkv", "wbr", "wbe", "wo", "lng")}
    rep2 = lambda a, b: np.ascontiguousarray(np.broadcast_to(np.stack([np.asarray(a, f32), np.asarray(b, f32)])[None], (128, 2, 1024)))
    for l in range(L):
        Wl = np.asarray(w_in[l], f32)
        bl = np.asarray(b_in[l], f32)
        cwl = np.asarray(conv_w[l], f32)
        per = {k: [] for k in ("wfox", "bfox", "wmoba", "bmoba", "wdil", "bdil", "wgdn", "bgdn", "convw", "scal")}
        for h in range(H):
            cols = np.concatenate([OFF_C + k * BW + 64 * h + np.arange(64) for k in range(3)] + [np.array([OFF_CF + h])])
            bias = np.zeros((64, 4), f32)
            for k in range(3):
                bias[:, k] = bl[cols[64 * k:64 * k + 64]]
            bias[0, 3] = bl[OFF_CF + h]
            per["wfox"].append(_w_tiles(Wl[:, cols]))
            per["bfox"].append(bias)
            cols = np.concatenate([OFF_A + k * BW + 64 * h + np.arange(64) for k in range(3)])
            bias = np.zeros((64, 4), f32)
            for k in range(3):
                bias[:, k] = bl[cols[64 * k:64 * k + 64]]
            per["wmoba"].append(_w_tiles(Wl[:, cols]))
            per["bmoba"].append(bias)
            segs = []
            for g in range(3):
                segs.append(OFF_BQK + ((0 * 3 + g) * H + h) * 64 + np.arange(64))
                segs.append(OFF_BQK + ((1 * 3 + g) * H + h) * 64 + np.arange(64))
            segs.append(OFF_BV + 64 * h + np.arange(64))
            cols = np.concatenate(segs)
            bias = np.zeros((64, 8), f32)
            for k in range(7):
                bias[:, k] = bl[cols[64 * k:64 * k + 64]]
            per["wdil"].append(_w_tiles(Wl[:, cols]))
            per["bdil"].append(bias)
            cols = np.concatenate([OFF_D + k * BW + 64 * h + np.arange(64) for k in range(3)] + [np.array([OFF_DBETA + h, OFF_DDEC + h])])
            bias = np.zeros((64, 8), f32)
            for k in range(3):
                bias[:, k] = bl[cols[64 * k:64 * k + 64]]
            bias[0, 3] = bl[OFF_DBETA + h]
            bias[0, 4] = bl[OFF_DDEC + h]
            cw = np.zeros((64, 12), f32)
            for i in range(3):
                for j in range(4):
                    cw[:, 4 * i + j] = cwl[j, i * BW + 64 * h:i * BW + 64 * h + 64]
            sc = np.zeros((128, 2), f32)
            sc[:, 0] = np.asarray(a_log[l], f32)[h]
            sc[:, 1] = np.asarray(dt_bias[l], f32)[h]
            per["wgdn"].append(_w_tiles(Wl[:, cols]))
            per["bgdn"].append(bias)
            per["convw"].append(cw)
            per["scal"].append(sc)
        for k in per:
            d[k].append(np.stack(per[k]))
        d["gnw"].append(np.ascontiguousarray(np.broadcast_to(np.asarray(gdn_norm_w[l], f32)[None], (128, 64))))
        bzv = bl[OFF_Z:OFF_Z + 1920]
        bmv = bl[OFF_M:OFF_M + 5120]
        beqv = bl[OFF_EQ:OFF_EQ + 384]
        bz = np.zeros((128, 16), f32)
        for i in range(12):
            bz[:, i] = bzv[128 * i:128 * i + 128]
        for h in range(4):
            bz[0:96, 12 + h] = bzv[1536 + 96 * h:1536 + 96 * h + 96]
        bm = np.zeros((128, 40), f32)
        for n in range(5):
            for dd in range(8):
                bm[:, 8 * n + dd] = bmv[1024 * n + 128 * dd:1024 * n + 128 * dd + 128]
        Wbr = np.asarray(w_branch[l], f32)
        d["wz"].append(_w_tiles(Wl[:, OFF_Z:OFF_Z + 1920], [(128 * i, 128 * i + 128) for i in range(12)] + [(1536 + 96 * i, 1536 + 96 * i + 96) for i in range(4)]))
        d["bz"].append(bz)
        d["wm"].append(_w_tiles(Wl[:, OFF_M:OFF_M + 5120]))
        d["bm"].append(bm)
        d["weq"].append(_w_tiles(Wl[:, OFF_EQ:OFF_EQ + 384], [(96 * i, 96 * i + 96) for i in range(4)]))
        d["beq"].append(np.ascontiguousarray(beqv.reshape(4, 96).T))
        d["wkv"].append(_w_tiles(np.asarray(w_mem_kv[l], f32)))
        d["wbr"].append(np.ascontiguousarray(Wbr[0:4].reshape(4, 3, 128, 1024).transpose(2, 0, 1, 3).reshape(128, 12, 1024)))
        d["wbe"].append(np.ascontiguousarray(Wbr[4].reshape(4, 96, 1024).transpose(1, 0, 2)))
        d["wo"].append(_w_tiles(np.asarray(w_out[l], f32)))
        d["lng"].append(rep2(ln_g[l], ln_b[l]))
    out = {k: np.ascontiguousarray(np.stack(v)).astype(f32) for k, v in d.items()}
    out["memg"] = rep2(mem_ln_g, mem_ln_b)
    return out


def kernel(x, mem, mem_ln_g, mem_ln_b, w_in, b_in, conv_w, a_log, dt_bias, gdn_norm_w, w_mem_kv, w_branch, w_out, ln_g, ln_b):
    f32 = np.float32
    x = np.asarray(x, f32)
    mem = np.asarray(mem, f32)
    B = x.shape[0]
    shared = _pack_weights(np.asarray(w_in), np.asarray(b_in), np.asarray(conv_w), np.asarray(a_log), np.asarray(dt_bias), np.asarray(gdn_norm_w),
                           np.asarray(w_mem_kv), np.asarray(w_branch), np.asarray(w_out), np.asarray(ln_g), np.asarray(ln_b), mem_ln_g, mem_ln_b)
    shared.update(_consts())
    nc = bass.Bass("TRN2", target_bir_lowering=False)
    build_fused(nc)
    per_batch = []
    for b in range(B):
        d = dict(shared)
        d["xT0"] = _xT_layout(x[b])
        d["xtok0"] = np.ascontiguousarray(x[b])
        d["mem"] = np.ascontiguousarray(mem[b])
        per_batch.append(d)
    n_cores = 8
    grp = n_cores // B
    res = run_bass_kernel_spmd(nc, [per_batch[c // grp] for c in range(n_cores)], core_ids=list(range(n_cores)))
    return np.stack([np.asarray(res.results[b * grp]["out"]) for b in range(B)]).astype(f32)
```

```python
import os
import numpy as np
from concourse.bass_utils import run_bass_kernel_spmd
import contextlib
import numpy as np
import concourse.bass as bass
import concourse.mybir as mybir

F32 = mybir.dt.float32
BF16 = mybir.dt.bfloat16
AF = mybir.ActivationFunctionType
ALU = mybir.AluOpType
AX = mybir.AxisListType

ENGS = ("pe", "act", "dve", "pool", "sp")


class _Op:
    __slots__ = ("eng", "fn", "reads", "writes", "dma", "lane", "n", "idx", "waits", "signal", "cnt")


class Prog:
    def __init__(self, nc, same_engine_raw=True):
        self.nc = nc
        self.ops = []
        self.last_w = {}
        self.readers = {}
        self.same_engine_raw = same_engine_raw
        self.es = contextlib.ExitStack()
        self.cur_es = self.es
        self._tn = 0
        self.bar_deps = ()
        self.bar_seen = set()
        self.last_on_eng = {}
        self.last_on_lane = {}

    def sbuf(self, shape, dtype, name=None):
        self._tn += 1
        return self.cur_es.enter_context(self.nc.sbuf_tensor(f"s{self._tn}_" + (name or "t"), list(shape), dtype))

    def psum(self, shape, dtype=F32, name=None):
        self._tn += 1
        return self.es.enter_context(self.nc.psum_tensor("p_" + (name or f"ps{self._tn}"), list(shape), dtype))

    @contextlib.contextmanager
    def scope(self):
        prev = self.cur_es
        with contextlib.ExitStack() as es:
            self.cur_es = es
            try:
                yield
            finally:
                self.cur_es = prev
        self.barrier()

    def barrier(self):
        self.bar_deps = tuple(self.last_on_eng.values()) + tuple(self.last_on_lane.values())
        self.bar_seen = set()

    def op(self, eng, fn, reads=(), writes=()):
        o = _Op()
        o.eng, o.fn, o.reads, o.writes = eng, fn, tuple(reads), tuple(writes)
        o.dma, o.lane, o.n = False, None, 1
        self._add(o)

    def dma(self, eng, fn, reads=(), writes=(), lane=None, n=1):
        o = _Op()
        o.eng, o.fn, o.reads, o.writes = eng, fn, tuple(reads), tuple(writes)
        o.dma, o.n = True, n
        o.lane = lane if lane is not None else (o.writes[0] if o.writes else o.reads[0])
        self._add(o)

    def _add(self, o):
        o.idx = len(self.ops)
        deps = set()
        for r in o.reads:
            w = self.last_w.get(r)
            if w is not None:
                deps.add(w)
        for r in o.writes:
            w = self.last_w.get(r)
            if w is not None:
                deps.add(w)
            for rd in self.readers.get(r, ()):
                deps.add(rd)
        if o.eng not in self.bar_seen:
            self.bar_seen.add(o.eng)
            deps.update(self.bar_deps)
        deps.discard(o.idx)
        self.last_on_eng[o.eng] = o.idx
        if o.dma:
            self.last_on_lane[o.lane] = o.idx
        o.waits = deps
        o.signal = False
        for r in o.reads:
            self.readers.setdefault(r, []).append(o.idx)
        for r in o.writes:
            self.last_w[r] = o.idx
            self.readers[r] = []
        self.ops.append(o)

    def emit(self, final_wait_eng="sp"):
        nc = self.nc
        ops = self.ops
        for o in ops:
            keep = set()
            for d in o.waits:
                p = ops[d]
                if p.dma:
                    keep.add(d)
                elif p.eng != o.eng or o.dma:
                    keep.add(d)
                else:
                    if self.same_engine_raw and o.eng != "pe":
                        if any(r in p.writes for r in o.reads):
                            keep.add(d)
            o.waits = keep
            for d in keep:
                ops[d].signal = True
        eng_cnt = {e: 0 for e in ENGS}
        lane_cnt = {}
        for o in ops:
            if o.dma:
                lane_cnt[o.lane] = lane_cnt.get(o.lane, 0) + 16 * o.n
                o.cnt = lane_cnt[o.lane]
            elif o.signal:
                eng_cnt[o.eng] += 1
                o.cnt = eng_cnt[o.eng]
            else:
                o.cnt = None
        lanes = list(lane_cnt.keys())
        es = self.es
        eng_sem = {e: es.enter_context(nc.semaphore(f"s_{e}")) for e in ENGS}
        lane_sem = {l: es.enter_context(nc.semaphore(f"l_{i}")) for i, l in enumerate(lanes)}
        self.n_sems = len(lanes) + len(ENGS)
        per_eng = {e: [] for e in ENGS}
        for o in ops:
            per_eng[o.eng].append(o)

        def run(e, engine):
            seen = {}
            for o in per_eng[e]:
                need = {}
                for d in o.waits:
                    p = ops[d]
                    if p.dma:
                        key = ("l", p.lane)
                        sem = lane_sem[p.lane]
                    else:
                        key = ("e", p.eng)
                        sem = eng_sem[p.eng]
                    if need.get(key, (None, -1))[1] < p.cnt:
                        need[key] = (sem, p.cnt)
                for key, (sem, val) in need.items():
                    if seen.get(key, -1) < val:
                        engine.wait_ge(sem, val)
                        seen[key] = val
                if o.dma:
                    instrs = o.fn(engine)
                    assert len(instrs) == o.n, (len(instrs), o.n)
                    for ins in instrs:
                        ins.then_inc(lane_sem[o.lane], 16)
                else:
                    ins = o.fn(engine)
                    if o.signal:
                        ins.then_inc(eng_sem[o.eng], 1)

        with nc.Block() as block:
            @block.tensor
            def _(eng):
                run("pe", eng)

            @block.scalar
            def _(eng):
                run("act", eng)

            @block.vector
            def _(eng):
                run("dve", eng)

            @block.gpsimd
            def _(eng):
                run("pool", eng)

            @block.sync
            def _(eng):
                run("sp", eng)
        self.es.close()


import types
import numpy as np

S = 8192
NCHK = 16
NEG = -30000.0


class Ctx:
    def __init__(self, nc):
        self.nc = nc
        self.P = Prog(nc)
        P = self.P
        self.pp = [P.psum([128, 512], F32, name=f"pp{i}") for i in range(2)]
        self.ps = [P.psum([128, 512], F32, name=f"pss{i}") for i in range(3)]
        self.po = [P.psum([128, 512], F32, name=f"po{i}") for i in range(2)]
        self.pm = P.psum([128, 512], F32, name="pm")
        self.ipp = 0
        self.ips = 0
        self.ipo = 0
        self.loc = types.SimpleNamespace()
        self.out_res = []
        self.ones_f = P.sbuf([128, 128], F32, name="ones_f")
        self.ones_b = P.sbuf([128, 128], BF16, name="ones_b")
        P.op("pool", lambda e: e.memset(self.ones_f[:], 1.0), writes=["ones_f"])
        P.op("pool", lambda e: e.memset(self.ones_b[:], 1.0), writes=["ones_b"])

    def new_scope(self):
        self.loc = types.SimpleNamespace()

    def dram_in(self, name, shape, dtype=F32):
        return self.nc.dram_tensor(name, list(shape), dtype, kind="ExternalInput").ap()

    def dram_out(self, name, shape, dtype=F32):
        return self.nc.dram_tensor(name, list(shape), dtype, kind="ExternalOutput").ap()

    def load_const(self, dram_ap, shape, name, dtype=F32, cast=None):
        P = self.P
        t = P.sbuf(shape, dtype, name=name)
        P.dma("sp", lambda e: [e.dma_start(out=t[:], in_=dram_ap)], writes=[name])
        if cast is not None:
            tb = P.sbuf(shape, cast, name=name + "_b")
            P.op("pool", lambda e: e.tensor_copy(out=tb[:], in_=t[:]), reads=[name], writes=[name + "_b"])
            return t, tb
        return t

    def next_pp(self):
        i = self.ipp
        self.ipp = (i + 1) % 2
        return self.pp[i], ("pp", i)

    def next_ps(self):
        i = self.ips
        self.ips = (i + 1) % 3
        return self.ps[i], ("ps", i)

    def next_po(self):
        i = self.ipo
        self.ipo = (i + 1) % 2
        return self.po[i], ("po", i)


def load_weights(C, w_d, ncols, name, wb=None):
    P = C.P
    if wb is None:
        wb = P.sbuf([128, 8, ncols], BF16, name=name)
    if not hasattr(C.loc, "lw_stg"):
        C.loc.lw_stg = [P.sbuf([128, 8, 128], F32, name=f"lw_stg{i}") for i in range(2)]
        C.loc.lw_k = 0
    stg = C.loc.lw_stg
    k = C.loc.lw_k
    for c0 in range(0, ncols, 128):
        n = min(128, ncols - c0)
        st = stg[k % 2]
        rs = ("lw_stg", k % 2)
        P.dma("act", lambda e, st=st, c0=c0, n=n: [e.dma_start(out=st[:, :, :], in_=w_d[c0 // 128])], writes=[rs])
        P.op("pool", lambda e, st=st, c0=c0, n=n: e.tensor_copy(out=wb[:, :, c0:c0 + n], in_=st[:, :, 0:n]), reads=[rs], writes=[name])
        k += 1
    C.loc.lw_k = k
    return wb


class _Shift:
    def __init__(self, t, off):
        self.t, self.off = t, off

    def __getitem__(self, key):
        ps, rest = key[0], key[1:]
        return self.t[(slice(ps.start + self.off, ps.stop + self.off),) + tuple(rest)]


def convert_xT(C, xT_d, xTb_d):
    P = C.P
    xs = [P.sbuf([128, 8, 512], F32, name=f"cv_xs{i}") for i in range(2)]
    xb = [P.sbuf([128, 8, 512], BF16, name=f"cv_xb{i}") for i in range(2)]
    for ci in range(NCHK):
        t0, b = ci * 512, ci % 2
        P.dma("sp", lambda e, b=b, t0=t0: [e.dma_start(out=xs[b][:, 0:4, :], in_=xT_d[:, 0:4, t0:t0 + 512]),
                                          e.dma_start(out=xs[b][:, 4:8, :], in_=xT_d[:, 4:8, t0:t0 + 512])], writes=[("cvxs", b)], n=2)
        P.op("pool", lambda e, b=b: e.tensor_copy(out=xb[b][:, 0:3, :], in_=xs[b][:, 0:3, :]), reads=[("cvxs", b)], writes=[("cvxb", b, 0)])
        P.op("dve", lambda e, b=b: e.tensor_copy(out=xb[b][:, 3:6, :], in_=xs[b][:, 3:6, :]), reads=[("cvxs", b)], writes=[("cvxb", b, 1)])
        P.op("act", lambda e, b=b: e.activation(out=xb[b][:, 6:8, :], in_=xs[b][:, 6:8, :], func=AF.Copy), reads=[("cvxs", b)], writes=[("cvxb", b, 2)])
        P.dma("act", lambda e, b=b, t0=t0: [e.dma_start(out=xTb_d[:, :, t0:t0 + 512], in_=xb[b][:, :, :])], reads=[("cvxb", b, k) for k in range(3)],
              writes=[("xTb", ci)], lane=("cvst", b))


def proj_pass(C, xT_d, wb, wname, specs, tag, hook=None, xTb_d=None):
    P = C.P
    use_b = xTb_d is not None and hook is None and all(len(sp) == 3 for sp in specs)
    if not hasattr(C.loc, "xb_bufs"):
        C.loc.xb_bufs = [P.sbuf([128, 8, 512], BF16, name=f"xb_{i}") for i in range(3 if use_b else 2)]
        if not use_b:
            C.loc.xs_bufs = [P.sbuf([128, 8, 512], F32, name=f"xs_{i}") for i in range(2)]
    xb = C.loc.xb_bufs
    xs = None if use_b else C.loc.xs_bufs
    tag = "x"
    if use_b:
        for ci in range(NCHK):
            t0 = ci * 512
            b = ci % 3
            rxb = ("xb", tag, b)
            P.dma("sp", lambda e, b=b, t0=t0: [e.dma_start(out=xb[b][:, 0:4, :], in_=xTb_d[:, 0:4, t0:t0 + 512]),
                                              e.dma_start(out=xb[b][:, 4:8, :], in_=xTb_d[:, 4:8, t0:t0 + 512])], writes=[(rxb, 0), (rxb, 1)], n=2,
                  lane=("xbl", b))
            for spec in specs:
                col0, M, evac = spec
                pt, pres = C.next_pp()
                for kc in range(8):
                    P.op("pe", lambda e, pt=pt, kc=kc, col0=col0, M=M, b=b: e.matmul(
                        pt[0:M, :], lhsT=wb[:, kc, col0:col0 + M], rhs=xb[b][:, kc, :], start=(kc == 0), stop=(kc == 7)),
                        reads=[wname, (rxb, 0), (rxb, 1)], writes=[pres])
                if callable(evac):
                    evac(pt, pres, ci, t0)
                else:
                    for off, fn in evac:
                        fn(pt if off == 0 else _Shift(pt, off), pres, ci, t0)
        return
    for ci in range(NCHK):
        t0 = ci * 512
        b = ci % 2
        rxs = ("xs", tag, b)
        rxb = ("xb", tag, b)
        P.dma("sp", lambda e, b=b, t0=t0: [e.dma_start(out=xs[b][:, 0:4, :], in_=xT_d[:, 0:4, t0:t0 + 512]),
                                          e.dma_start(out=xs[b][:, 4:8, :], in_=xT_d[:, 4:8, t0:t0 + 512])],
              writes=[rxs], n=2)
        P.op("pool", lambda e, b=b: e.tensor_copy(out=xb[b][:, 0:4, :], in_=xs[b][:, 0:4, :]), reads=[rxs], writes=[(rxb, 0)])
        P.op("dve", lambda e, b=b: e.tensor_copy(out=xb[b][:, 4:8, :], in_=xs[b][:, 4:8, :]), reads=[rxs], writes=[(rxb, 1)])
        if hook is not None:
            hook(ci, xs[b], rxs)
        for spec in specs:
            col0, M, evac = spec[0], spec[1], spec[2]
            pt, pres = C.next_pp()
            if len(spec) > 3:
                w32, w32_res = spec[3], spec[4]
                for kc in range(8):
                    P.op("pe", lambda e, pt=pt, kc=kc, M=M, b=b, w32=w32: e.matmul(
                        pt[0:M, :], lhsT=w32[:, kc, 0:M], rhs=xs[b][:, kc, :], start=(kc == 0), stop=(kc == 7)),
                        reads=[w32_res, rxs], writes=[pres])
            else:
                for kc in range(8):
                    P.op("pe", lambda e, pt=pt, kc=kc, col0=col0, M=M, b=b: e.matmul(
                        pt[0:M, :], lhsT=wb[:, kc, col0:col0 + M], rhs=xb[b][:, kc, :], start=(kc == 0), stop=(kc == 7)),
                        reads=[wname, (rxb, 0), (rxb, 1)], writes=[pres])
            if callable(evac):
                evac(pt, pres, ci, t0)
            else:
                for off, fn in evac:
                    fn(pt if off == 0 else _Shift(pt, off), pres, ci, t0)


def make_vprime(C, vT, vT_res, vp, vp_res, nblk, colsel, dh=64):
    P = C.P
    P.op("pool", lambda e: e.memset(vp[:, :, dh:dh + 1], 1.0), writes=[(vp_res, "ones")])
    pmb = C.pm_b
    G = 8
    for g0 in range(0, nblk, G):
        for k in range(G):
            blk = g0 + k
            P.op("pe", lambda e, blk=blk, k=k: e.transpose(pmb[:, k * dh:(k + 1) * dh], vT[0:dh, colsel(blk)], C.ident_b[0:dh, 0:dh]),
                 reads=vT_res(blk) + ["ident_b"], writes=["pm"])
        P.op("act", lambda e, g0=g0: e.activation(out=vp[:, g0:g0 + G, 0:dh], in_=pmb[:, 0:G * dh].rearrange("p (g d) -> p g d", g=G), func=AF.Copy),
             reads=["pm", (vp_res, "ones")], writes=[(vp_res, g0)])


def attn_block(C, S_mm, nq_lo, nq_hi, bias_ap, bias_res, masks, PV_mm, po_res, extra_reads=()):
    raise NotImplementedError


def causal_attention(C, qT, q_res, kT, k_res, KC, vp, vp_res, negc, bias_fn, out_cb, tag, LA=2):
    P = C.P
    NB = LA + 1
    pts = [P.sbuf([128, 512], BF16, name=f"pt_{tag}{i}") for i in range(NB)]
    tmps = [P.sbuf([128, 128], F32, name=f"tmp_{tag}{i}") for i in range(2)]
    st = {"itmp": 0}
    tasks = []
    pos = {}
    for I in range(NCHK):
        pos[I] = C.next_po()
        for j in range(4 * I + 4):
            tasks.append((I, j))

    def stage1(t, I, j):
        r = j - 4 * I
        lo = 128 * r if r > 0 else 0
        pst, ps_res = C.ps[t % 3], ("ps", t % 3)
        pt = pts[t % NB]
        pt_res = ("pt", tag, t % NB)
        q0 = 512 * I
        P.op("pe", lambda e: e.matmul(pst[:, lo:512], lhsT=kT[0:KC, 128 * j:128 * j + 128], rhs=qT[0:KC, q0 + lo:q0 + 512], start=True, stop=True),
             reads=q_res(I) + k_res(j), writes=[ps_res])
        b = bias_fn(I, j) if bias_fn is not None else None
        bkw = {} if b is None else {"bias": b[0]}
        brd = [] if b is None else [b[1]]
        if r >= 0:
            tmp = tmps[st["itmp"] % 2]
            tmp_res = ("tmp", tag, st["itmp"] % 2)
            st["itmp"] += 1
            P.op("dve", lambda e: e.tensor_tensor(out=tmp[:], in0=pst[:, lo:lo + 128], in1=negc[:], op=ALU.add), reads=[ps_res, "negc"], writes=[tmp_res])
            P.op("act", lambda e: e.activation(out=pt[:, lo:lo + 128], in_=tmp[:], func=AF.Exp, **bkw), reads=[tmp_res] + brd, writes=[(pt_res, "d")])
            if lo + 128 < 512:
                P.op("act", lambda e: e.activation(out=pt[:, lo + 128:512], in_=pst[:, lo + 128:512], func=AF.Exp, **bkw), reads=[ps_res] + brd, writes=[(pt_res, "o")])
        else:
            P.op("act", lambda e: e.activation(out=pt[:, :], in_=pst[:, :], func=AF.Exp, **bkw), reads=[ps_res] + brd, writes=[(pt_res, "d"), (pt_res, "o")])

    def stage2(t, I, j):
        r = j - 4 * I
        lo = 128 * r if r > 0 else 0
        nkb = 4 * I + 4
        po, po_res = pos[I]
        pt = pts[t % NB]
        pt_res = ("pt", tag, t % NB)
        P.op("pe", lambda e: e.matmul(po[0:65, lo:512], lhsT=vp[:, j, 0:65], rhs=pt[:, lo:512], start=(j == 0), stop=(j == nkb - 1), skip_group_check=True),
             reads=vp_res(j) + [(pt_res, "d"), (pt_res, "o")], writes=[po_res])
        if j == nkb - 1:
            out_cb(I, po, po_res)

    n = len(tasks)
    for t in range(n + LA):
        if t < n:
            stage1(t, *tasks[t])
        if t - LA >= 0:
            stage2(t - LA, *tasks[t - LA])


def norm_store(C, onum_pool, out_d, tag):
    P = C.P
    st = {"i": 0}

    def cb(I, po, po_res):
        i = st["i"] % 2
        st["i"] += 1
        on = onum_pool[i]
        on_res = ("onum", tag, i)
        P.op("act", lambda e: e.activation(out=on[0:65, :], in_=po[0:65, :], func=AF.Copy), reads=[po_res], writes=[on_res])
        P.op("dve", lambda e: e.reciprocal(out=on[64:65, :], in_=on[64:65, :]), reads=[on_res], writes=[on_res])
        P.op("pe", lambda e: e.matmul(C.pm[0:64, :], lhsT=C.ones_f[64:65, 0:64], rhs=on[64:65, :], start=True, stop=True),
             reads=["ones_f", on_res], writes=["pm"])
        P.op("dve", lambda e: e.tensor_tensor(out=on[0:64, :], in0=on[0:64, :], in1=C.pm[0:64, :], op=ALU.mult),
             reads=[on_res, "pm"], writes=[on_res])
        P.dma("sp", lambda e: [e.dma_start(out=out_d[:, 512 * I:512 * I + 512], in_=on[0:64, :])], reads=[on_res], writes=[("out", tag, I)],
              lane=("onum_st", tag, i))
        C.out_res.append(("out", tag, I))
    return cb


def build_fox(nc=None, C=None, io=None):
    standalone = C is None
    if standalone:
        C = Ctx(nc)
        io = {"xT": C.dram_in("xT", [128, 8, S]), "w": C.dram_in("w", [128, 8, 193]), "bias": C.dram_in("bias", [64, 4]),
              "out": C.dram_out("oT", [64, S])}
        C.negc = C.load_const(C.dram_in("negc", [128, 128]), [128, 128], "negc")
        C.identf, C.ident_b = C.load_const(C.dram_in("ident", [128, 128]), [128, 128], "ident", cast=BF16)
        C.pm_b = C.pm[:].bitcast(BF16)
    P = C.P
    xT_d, w_d, b_d, out_d = io["xT"], io["w"], io["bias"], io["out"]
    negc = C.negc
    bia = C.load_const(b_d, [64, 4], "bia")
    bq8 = P.sbuf([64, 1], F32, name="bq8")
    nbcf = P.sbuf([1, 1], F32, name="nbcf")
    P.op("dve", lambda e: e.tensor_scalar(out=bq8[:], in0=bia[:, 0:1], scalar1=0.125, scalar2=None, op0=ALU.mult), reads=["bia"], writes=["bq8"])
    P.op("dve", lambda e: e.tensor_scalar(out=nbcf[:], in0=bia[0:1, 3:4], scalar1=-1.0, scalar2=None, op0=ALU.mult), reads=["bia"], writes=["nbcf"])
    wb = load_weights(C, w_d, 193, "w")

    qC = P.sbuf([65, S], BF16, name="qC")
    kC = P.sbuf([65, S], BF16, name="kC")
    vT = P.sbuf([64, S], BF16, name="vT")
    Frow = P.sbuf([1, S], F32, name="Frow")
    erow = P.sbuf([1, 512], F32, name="erow")
    vp = P.sbuf([128, 64, 65], BF16, name="vp")

    def ev_q(pt, pres, ci, t0):
        P.op("act", lambda e: e.activation(out=qC[0:64, t0:t0 + 512], in_=pt[0:64, :], func=AF.Identity, bias=bq8[:, 0:1], scale=0.125),
             reads=[pres, "bq8"], writes=[("qC", ci)])

    def ev_k(pt, pres, ci, t0):
        P.op("act", lambda e: e.activation(out=kC[0:64, t0:t0 + 512], in_=pt[0:64, :], func=AF.Identity, bias=bia[:, 1:2], scale=1.0),
             reads=[pres, "bia"], writes=[("kC", ci)])

    def ev_v(pt, pres, ci, t0):
        P.op("act", lambda e: e.activation(out=vT[0:64, t0:t0 + 512], in_=pt[0:64, :], func=AF.Identity, bias=bia[:, 2:3], scale=1.0),
             reads=[pres, "bia"], writes=[("vT", ci)])

    def ev_cf(pt, pres, ci, t0):
        P.op("act", lambda e: e.activation(out=erow[:], in_=pt[0:1, :], func=AF.Exp, bias=nbcf[:, 0:1], scale=-1.0),
             reads=[pres, "nbcf"], writes=["erow"])
        P.op("act", lambda e: e.activation(out=erow[:], in_=erow[:], func=AF.Ln, bias=1.0, scale=1.0), reads=["erow"], writes=["erow"])
        init = 0.0 if ci == 0 else Frow[0:1, t0 - 1:t0]
        P.op("dve", lambda e: e.tensor_tensor_scan(out=Frow[0:1, t0:t0 + 512], data0=ones_row[0:1, :], data1=erow[:],
                                                  initial=init, op0=ALU.mult, op1=ALU.subtract),
             reads=["erow", "ones_row", "Frow"], writes=["Frow"])

    ones_row = P.sbuf([1, 512], F32, name="ones_row")
    P.op("pool", lambda e: e.memset(ones_row[:], 1.0), writes=["ones_row"])
    specs = [(0, 128, [(0, ev_q), (64, ev_k)]), (128, 65, [(0, ev_v), (64, ev_cf)])]
    proj_pass(C, xT_d, wb, "w", specs, "c", xTb_d=io.get("xTb"))
    P.op("pool", lambda e: e.memset(kC[64:65, :], 1.0), writes=["kC64"])
    for I in range(NCHK):
        P.op("dve", lambda e, I=I: e.tensor_scalar(out=qC[64:65, 512 * I:512 * I + 512], in0=Frow[0:1, 512 * I:512 * I + 512],
                                                   scalar1=Frow[0:1, 512 * I:512 * I + 1], scalar2=None, op0=ALU.subtract),
             reads=["Frow"], writes=["qC64"])
    for j in range(64):
        P.op("pe", lambda e, j=j: e.matmul(C.pm[0:128, j:j + 1], lhsT=Frow[0:1, 128 * j:128 * j + 128], rhs=C.ones_f[0:1, 0:1], start=True, stop=True),
             reads=["Frow", "ones_f"], writes=["pm"])
    P.op("pe", lambda e: e.matmul(C.pm[0:128, 64:80], lhsT=C.ones_f[0:1, 0:128], rhs=Frow[0:1, 0:S:512], start=True, stop=True),
         reads=["Frow", "ones_f"], writes=["pm"])
    Ftr = P.sbuf([128, 80], F32, name="Ftr")
    biasC = P.sbuf([128, 16, 64], F32, name="biasC")
    P.op("dve", lambda e: e.tensor_copy(out=Ftr[:], in_=C.pm[:, 0:80]), reads=["pm"], writes=["Ftr"])
    for I in range(NCHK):
        P.op("dve", lambda e, I=I: e.tensor_scalar(out=biasC[:, I, :], in0=Ftr[:, 0:64], scalar1=-1.0, scalar2=Ftr[:, 64 + I:65 + I], op0=ALU.mult, op1=ALU.add),
             reads=["Ftr"], writes=["biasC"])
    make_vprime(C, vT, lambda blk: [("vT", blk // 4)], vp, "vp", 64, lambda blk: slice(128 * blk, 128 * blk + 128))
    onum = [P.sbuf([65, 512], F32, name=f"onum{i}") for i in range(2)]
    cb = norm_store(C, onum, out_d, "c")
    causal_attention(C, qC, lambda I: [("qC", I), "qC64"], kC, lambda j: [("kC", j // 4), "kC64"], 65, vp,
                     lambda j: [("vp", (j // 8) * 8), ("vp", "ones")], negc,
                     lambda I, j: (biasC[:, I, j:j + 1], "biasC"), cb, "c")
    if standalone:
        P.op("sp", lambda e: None, reads=C.out_res)
        P.emit()
    return nc


def build_moba(nc=None, C=None, io=None):
    standalone = C is None
    if standalone:
        C = Ctx(nc)
        io = {"xT": C.dram_in("xT", [128, 8, S]), "w": C.dram_in("w", [2, 128, 8, 128]), "bias": C.dram_in("bias", [64, 4]),
              "ind": C.dram_in("ind", [32, S]), "blkc": C.dram_in("blkc", [128, 3, 32, 32]), "out": C.dram_out("oT", [64, S])}
        C.negc = C.load_const(C.dram_in("negc", [128, 128]), [128, 128], "negc")
        C.identf, C.ident_b = C.load_const(C.dram_in("ident", [128, 128]), [128, 128], "ident", cast=BF16)
        C.pm_b = C.pm[:].bitcast(BF16)
    P = C.P
    xT_d, w_d, b_d, ind_d, blk_d, out_d = io["xT"], io["w"], io["bias"], io["ind"], io["blkc"], io["out"]
    negc = C.negc
    bia = C.load_const(b_d, [64, 4], "bia")
    blkc = C.load_const(blk_d, [128, 3, 32, 32], "blkc")
    bq8 = P.sbuf([64, 1], F32, name="bq8")
    P.op("dve", lambda e: e.tensor_scalar(out=bq8[:], in0=bia[:, 0:1], scalar1=0.125, scalar2=None, op0=ALU.mult), reads=["bia"], writes=["bq8"])
    wb = load_weights(C, w_d, 192, "w")

    qs = P.sbuf([96, S], BF16, name="qs")
    ke = P.sbuf([96, S], BF16, name="ke")
    vT = P.sbuf([64, S], BF16, name="vT")
    vp = P.sbuf([128, 64, 65], BF16, name="vp")
    ksum = P.sbuf([64, 32], F32, name="ksum")
    kmT = P.sbuf([64, 32], BF16, name="kmT")
    indst = P.sbuf([32, 2048], F32, name="indst")
    for i4 in range(4):
        P.dma("act", lambda e, i4=i4: [e.dma_start(out=indst[:, :], in_=ind_d[:, 2048 * i4:2048 * i4 + 2048])], writes=["indst"])
        P.op("pool", lambda e, i4=i4: e.tensor_copy(out=ke[64:96, 2048 * i4:2048 * i4 + 2048], in_=indst[:]), reads=["indst"], writes=["ke_ind"])
    q32 = P.sbuf([64, S], F32, name="q32")
    w32q = P.sbuf([128, 8, 64], F32, name="w32q")
    w32k = P.sbuf([128, 8, 64], F32, name="w32k")
    P.dma("act", lambda e: [e.dma_start(out=w32q[:], in_=w_d[0][:, :, 0:64])], writes=["w32q"])
    P.dma("act", lambda e: [e.dma_start(out=w32k[:], in_=w_d[0][:, :, 64:128])], writes=["w32k"])
    xsum = P.sbuf([128, 8, 32], F32, name="xsum")
    km32 = P.sbuf([64, 32], F32, name="km32")

    def ev_q32(pt, pres, ci, t0):
        P.op("act", lambda e: e.activation(out=q32[0:64, t0:t0 + 512], in_=pt[0:64, :], func=AF.Identity, bias=bq8[:, 0:1], scale=0.125),
             reads=[pres, "bq8"], writes=[("q32", ci)])

    def xhook(ci, xs_t, rxs):
        P.op("dve", lambda e: e.tensor_reduce(out=xsum[:, :, 2 * ci:2 * ci + 2], in_=xs_t[:, :, :].rearrange("p k (a b) -> p k a b", a=2), axis=AX.X, op=ALU.add),
             reads=[rxs], writes=["xsum"])

    def ev_q(pt, pres, ci, t0):
        P.op("act", lambda e: e.activation(out=qs[0:64, t0:t0 + 512], in_=pt[0:64, :], func=AF.Identity, bias=bq8[:, 0:1], scale=0.125),
             reads=[pres, "bq8"], writes=[("qs", ci)])

    def ev_k(pt, pres, ci, t0):
        P.op("act", lambda e: e.activation(out=ke[0:64, t0:t0 + 512], in_=pt[0:64, :], func=AF.Identity, bias=bia[:, 1:2], scale=1.0),
             reads=[pres, "bia"], writes=[("ke", ci)])

    def ev_v(pt, pres, ci, t0):
        P.op("act", lambda e: e.activation(out=vT[0:64, t0:t0 + 512], in_=pt[0:64, :], func=AF.Identity, bias=bia[:, 2:3], scale=1.0),
             reads=[pres, "bia"], writes=[("vT", ci)])

    specs = [(0, 128, [(0, ev_q), (64, ev_k)]), (128, 64, ev_v), (0, 64, ev_q32, w32q, "w32q")]
    proj_pass(C, xT_d, wb, "w", specs, "a", hook=xhook)
    P.op("dve", lambda e: e.tensor_scalar(out=xsum[:], in0=xsum[:], scalar1=1.0 / 256.0, scalar2=None, op0=ALU.mult), reads=["xsum"], writes=["xsum"])
    for kc in range(8):
        P.op("pe", lambda e, kc=kc: e.matmul(C.pm[0:64, 0:32], lhsT=w32k[:, kc, 0:64], rhs=xsum[:, kc, 0:32], start=(kc == 0), stop=(kc == 7)),
             reads=["w32k", "xsum"], writes=["pm"])
    P.op("dve", lambda e: e.tensor_scalar(out=km32[:], in0=C.pm[0:64, 0:32], scalar1=bia[:, 1:2], scalar2=None, op0=ALU.add), reads=["pm", "bia"], writes=["km32"])
    NL = 4
    lane_banks = [C.pp[0], C.pp[1], C.po[0], C.po[1]]
    lane_res = [("pp", 0), ("pp", 1), ("po", 0), ("po", 1)]
    gms = [P.sbuf([128, 32], F32, name=f"gm{i}") for i in range(NL)]
    top8s = [P.sbuf([128, 8], F32, name=f"top8{i}") for i in range(NL)]
    selms = [P.sbuf([128, 32], F32, name=f"selm{i}") for i in range(NL)]
    stages = [P.sbuf([128, 96], BF16, name=f"stage{i}") for i in range(NL)]
    for i in range(NL):
        P.op("pool", lambda e, i=i: e.memset(stages[i][:], 0.0), writes=[("stage", i)])

    def gate_lane(ln):
        bank, bres = lane_banks[ln], lane_res[ln]
        bank_b = bank[:].bitcast(BF16)
        gm, top8, selm, stage = gms[ln], top8s[ln], selms[ln], stages[ln]
        rg, rt, rs, rst = ("gm", ln), ("top8", ln), ("selm", ln), ("stage", ln)

        def tile(ti):
            qb = ti // 2
            c0 = 128 * ti
            P.op("pe", lambda e: e.matmul(bank[0:128, 0:32], lhsT=q32[0:64, c0:c0 + 128], rhs=km32[0:64, 0:32], start=True, stop=True),
                 reads=[("q32", ti // 4), "km32"], writes=[bres])
            yield
            P.op("dve", lambda e: e.tensor_tensor(out=gm[:], in0=bank[0:128, 0:32], in1=blkc[:, 0, qb, :], op=ALU.add), reads=["blkc"], writes=[rg, bres])
            yield
            P.op("dve", lambda e: e.max(out=top8[:], in_=gm[:]), reads=[rg], writes=[rt])
            yield
            P.op("dve", lambda e: e.tensor_scalar(out=selm[:], in0=gm[:], scalar1=top8[:, 2:3], scalar2=None, op0=ALU.is_ge), reads=[rg, rt], writes=[rs])
            yield
            P.op("dve", lambda e: e.tensor_tensor(out=selm[:], in0=selm[:], in1=blkc[:, 1, qb, :], op=ALU.mult), reads=[rs, "blkc"], writes=[rs])
            yield
            P.op("dve", lambda e: e.tensor_tensor(out=selm[:], in0=selm[:], in1=blkc[:, 2, qb, :], op=ALU.add), reads=[rs, "blkc"], writes=[rs])
            yield
            P.op("dve", lambda e: e.tensor_scalar(out=stage[:, 64:96], in0=selm[:], scalar1=-1.0, scalar2=30000.0, op0=ALU.add, op1=ALU.mult), reads=[rs], writes=[rst])
            yield
            P.op("pe", lambda e: e.transpose(bank_b[0:96, 0:128], stage[:, 0:96], C.ident_b[:, :]), reads=[rst, "ident_b"], writes=[bres])
            yield
            P.op("act", lambda e: e.activation(out=qs[64:96, c0:c0 + 128], in_=bank_b[64:96, 0:128], func=AF.Copy), reads=[], writes=[("qsel", ti), bres])
            yield

        for ti in range(ln, 64, NL):
            yield from tile(ti)

    alive = [gate_lane(ln) for ln in range(NL)]
    while alive:
        nxt = []
        for gen in alive:
            try:
                next(gen)
                nxt.append(gen)
            except StopIteration:
                pass
        alive = nxt

    make_vprime(C, vT, lambda blk: [("vT", blk // 4)], vp, "vp", 64, lambda blk: slice(128 * blk, 128 * blk + 128))
    onum = [P.sbuf([65, 512], F32, name=f"onum{i}") for i in range(2)]
    cb = norm_store(C, onum, out_d, "a")
    causal_attention(C, qs, lambda I: [("qs", I)] + [("qsel", 4 * I + k) for k in range(4)], ke, lambda j: [("ke", j // 4), "ke_ind"], 96, vp,
                     lambda j: [("vp", (j // 8) * 8), ("vp", "ones")], negc, None, cb, "a")
    if standalone:
        P.op("sp", lambda e: None, reads=C.out_res)
        P.emit()
    return nc


DILS = (1, 4, 16)


def build_dil(nc=None, C=None, io=None):
    standalone = C is None
    if standalone:
        C = Ctx(nc)
        io = {"xT": C.dram_in("xT", [128, 8, S]), "w": C.dram_in("w", [128, 8, 448]), "bias": C.dram_in("bias", [64, 8]),
              "out": C.dram_out("oT", [64, S])}
        C.band = C.load_const(C.dram_in("band", [128, 256]), [128, 256], "band")
        C.identf, C.ident_b = C.load_const(C.dram_in("ident", [128, 128]), [128, 128], "ident", cast=BF16)
        C.pm_b = C.pm[:].bitcast(BF16)
    P = C.P
    xT_d, w_d, b_d, out_d = io["xT"], io["w"], io["bias"], io["out"]
    band = C.band
    bia = C.load_const(b_d, [64, 8], "bia")
    bq8 = P.sbuf([64, 8], F32, name="bq8")
    P.op("dve", lambda e: e.tensor_scalar(out=bq8[:], in0=bia[:], scalar1=0.125, scalar2=None, op0=ALU.mult), reads=["bia"], writes=["bq8"])
    wb = load_weights(C, w_d, 448, "w")

    qTs = [P.sbuf([64, S], BF16, name=f"qT{g}") for g in range(3)]
    kTs = [P.sbuf([64, S], BF16, name=f"kT{g}") for g in range(3)]
    vT = P.sbuf([64, S], BF16, name="vT")
    vp = P.sbuf([128, 64, 65], BF16, name="vp")
    acc = P.sbuf([65, S], F32, name="acc")
    pts = [P.sbuf([128, 256], BF16, name=f"ptb{i}") for i in range(3)]
    tmps = [P.sbuf([128, 256], F32, name=f"tmpb{i}") for i in range(3)]
    cnt = {"pt": 0}

    def mk_ev(g):
        def ev_q(pt, pres, ci, t0):
            P.op("act", lambda e: e.activation(out=qTs[g][0:64, t0:t0 + 512], in_=pt[0:64, :], func=AF.Identity, bias=bq8[:, 2 * g:2 * g + 1], scale=0.125),
                 reads=[pres, "bq8"], writes=[("qT", g, ci)])

        def ev_k(pt, pres, ci, t0):
            P.op("act", lambda e: e.activation(out=kTs[g][0:64, t0:t0 + 512], in_=pt[0:64, :], func=AF.Identity, bias=bia[:, 2 * g + 1:2 * g + 2], scale=1.0),
                 reads=[pres, "bia"], writes=[("kT", g, ci)])
        return ev_q, ev_k

    def ev_v(pt, pres, ci, t0):
        P.op("act", lambda e: e.activation(out=vT[0:64, t0:t0 + 512], in_=pt[0:64, :], func=AF.Identity, bias=bia[:, 6:7], scale=1.0),
             reads=[pres, "bia"], writes=[("vT", ci)])

    specs = []
    for g in range(3):
        eq_, ek_ = mk_ev(g)
        specs.append((128 * g, 128, [(0, eq_), (64, ek_)]))
    specs.append((384, 64, ev_v))
    proj_pass(C, xT_d, wb, "w", specs, "b", xTb_d=io.get("xTb"))

    for g, d in enumerate(DILS):
        L = S // d
        qT, kT = qTs[g], kTs[g]
        allq = [("qT", g, ci) for ci in range(NCHK)]
        allk = [("kT", g, ci) for ci in range(NCHK)]
        allv = [("vT", ci) for ci in range(NCHK)]
        nlb = L // 128

        def colsel(blk, d=d, nlb=nlb):
            r, lb = divmod(blk, nlb)
            st = r + d * 128 * lb
            return slice(st, st + d * 127 + 1, d)

        make_vprime(C, vT, (lambda blk: allv) if d > 1 else (lambda blk: [("vT", blk // 4)]), vp, ("vp", g), 64, colsel)
        vpres = [(("vp", g), g0) for g0 in range(0, 64, 8)] + [(("vp", g), "ones")]
        tasks = []
        chunks = []
        for r in range(d):
            for c0 in range(0, L, 512):
                ck = (r, c0, C.next_po())
                chunks.append(ck)
                cl = [c for c in range(-1, 4) if c0 + 128 * c >= 0]
                for c in cl:
                    tasks.append((ck, c, c == cl[-1]))

        def stage1(t, ck, c, last, qT=qT, kT=kT, d=d):
            r, c0, (po, po_res) = ck
            qst = r + d * c0
            kl = c0 + 128 * c
            lo = max(0, 128 * c)
            hi = min(512, 128 * c + 256)
            n = hi - lo
            mlo = 128 if c == -1 else 0
            pst, ps_res = C.ps[t % 3], ("ps", t % 3)
            i = t % 3
            pt, tmp = pts[i], tmps[i]
            pt_res, tmp_res = ("ptb", i), ("tmpb", i)
            kst = r + d * kl
            ksl = slice(kst, kst + d * 127 + 1, d)
            qsl = slice(qst + d * lo, qst + d * (hi - 1) + 1, d)
            P.op("pe", lambda e: e.matmul(pst[:, 0:n], lhsT=kT[0:64, ksl], rhs=qT[0:64, qsl], start=True, stop=True),
                 reads=(allq + allk) if d > 1 else [("qT", g, c0 // 512), ("kT", g, kl // 512)], writes=[ps_res])
            P.op("dve", lambda e: e.tensor_tensor(out=tmp[:, 0:n], in0=pst[:, 0:n], in1=band[:, mlo:mlo + n], op=ALU.add), reads=[ps_res, "band"], writes=[tmp_res])
            P.op("act", lambda e: e.activation(out=pt[:, 0:n], in_=tmp[:, 0:n], func=AF.Exp), reads=[tmp_res], writes=[pt_res])

        def stage2(t, ck, c, last, g=g, d=d, nlb=nlb, vpres=vpres):
            r, c0, (po, po_res) = ck
            qst = r + d * c0
            kl = c0 + 128 * c
            i = t % 3
            pt, pt_res = pts[i], ("ptb", i)
            vidx = r * nlb + kl // 128
            if c >= 0:
                first = (c == 0 and c0 == 0)
                P.op("pe", lambda e: e.matmul(po[0:65, 128 * c:128 * c + 128], lhsT=vp[:, vidx, 0:65], rhs=pt[:, 0:128], start=first, stop=True, skip_group_check=True),
                     reads=vpres + [pt_res], writes=[po_res])
            if c <= 2:
                off = 0 if c == -1 else 128
                P.op("pe", lambda e: e.matmul(po[0:65, 128 * (c + 1):128 * (c + 1) + 128], lhsT=vp[:, vidx, 0:65], rhs=pt[:, off:off + 128], start=True, stop=False, skip_group_check=True),
                     reads=vpres + [pt_res], writes=[po_res])
            if last:
                asl = slice(qst, qst + d * 511 + 1, d)
                if g == 0:
                    P.op("act", lambda e: e.activation(out=acc[0:65, asl], in_=po[0:65, :], func=AF.Copy), reads=[po_res], writes=["acc"])
                else:
                    P.op("dve", lambda e: e.tensor_tensor(out=acc[0:65, asl], in0=acc[0:65, asl], in1=po[0:65, :], op=ALU.add), reads=[po_res, "acc"], writes=["acc"])

        LA = 2
        nt = len(tasks)
        for t in range(nt + LA):
            if t < nt:
                stage1(t, *tasks[t])
            if t - LA >= 0:
                stage2(t - LA, *tasks[t - LA])
    for I in range(NCHK):
        sl = slice(512 * I, 512 * I + 512)
        P.op("dve", lambda e, sl=sl: e.reciprocal(out=acc[64:65, sl], in_=acc[64:65, sl]), reads=["acc"], writes=["acc"])
        P.op("pe", lambda e, sl=sl: e.matmul(C.pm[0:64, :], lhsT=C.ones_f[64:65, 0:64], rhs=acc[64:65, sl], start=True, stop=True),
             reads=["ones_f", "acc"], writes=["pm"])
        P.op("dve", lambda e, sl=sl: e.tensor_tensor(out=acc[0:64, sl], in0=acc[0:64, sl], in1=C.pm[0:64, :], op=ALU.mult),
             reads=["acc", "pm"], writes=["acc"])
        P.dma("sp", lambda e, sl=sl: [e.dma_start(out=out_d[:, sl], in_=acc[0:64, sl])], reads=["acc"], writes=[("out", I)], lane="ost")
        C.out_res.append(("out", I))
    if standalone:
        P.op("sp", lambda e: None, reads=C.out_res)
        P.emit()
    return nc


NC_ = 64


def gdn_phase1(C, io, hd, st8, scr):
    P = C.P
    xT_d, w_d, b_d, cw_d, sc_d = io["xT"], io["w"], io["bias"], io["convw"], io["scal"]
    tri = C.tri
    bia = C.load_const(b_d, [64, 8], "bia")
    cw = C.load_const(cw_d, [64, 12], "cw")
    scal = C.load_const(sc_d, [128, 2], "scal")
    wb = load_weights(C, w_d, 194, "w")
    Utri = tri[:, 0, :]
    nA = P.sbuf([1, 1], F32, name="nA")
    bdec = P.sbuf([1, 1], F32, name="bdec")
    P.op("act", lambda e: e.activation(out=nA[:], in_=scal[0:1, 0:1], func=AF.Exp), reads=["scal"], writes=["nA"])
    P.op("dve", lambda e: e.tensor_scalar(out=nA[:], in0=nA[:], scalar1=-1.0, scalar2=None, op0=ALU.mult), reads=["nA"], writes=["nA"])
    P.op("dve", lambda e: e.tensor_tensor(out=bdec[:], in0=bia[0:1, 4:5], in1=scal[0:1, 1:2], op=ALU.add), reads=["bia", "scal"], writes=["bdec"])

    stg3 = [[P.sbuf([64, 512], F32, name=f"qkvst{i}_{b}") for b in range(2)] for i in range(3)]
    brow = P.sbuf([1, 512], F32, name="brow")
    grow = P.sbuf([1, 512], F32, name="grow")
    pBG = C.po[0]
    raws = [P.sbuf([64, 515], F32, name=f"raw{i}") for i in range(3)]
    cys = [P.sbuf([64, 512], F32, name="cy")] * 3
    czs = [P.sbuf([64, 512], F32, name="cz")] * 2
    sqs = [P.sbuf([64, 512], F32, name="sq")] * 2
    rns = [P.sbuf([64, 512], F32, name="rn")] * 2
    nbanks = [(C.pm, "pm"), (C.pm, "pm")]
    erow = P.sbuf([1, 512], F32, name="erow")
    eps_t = P.sbuf([128, 1], F32, name="eps_t")
    P.op("pool", lambda e: e.memset(eps_t[:], 1e-6), writes=["eps_t"])
    for i in range(3):
        P.op("pool", lambda e, i=i: e.memset(raws[i][:, 0:3], 0.0), writes=[("raw", i)])

    def ev_qkv(i):
        def ev(pt, pres, ci, t0):
            dst = stg3[i][ci % 2]
            dres = ("qkvst", i, ci % 2)
            cy, rcy = cys[i], "cy"
            if i < 2:
                cz, sq, rn = czs[i], sqs[i], rns[i]
                rcz, rsq, rrn = "cz", "sq", "rn"
                nb, rnb = nbanks[i]
            raw = raws[i]
            rr = ("raw", i)
            P.op("act", lambda e: e.activation(out=raw[:, 3:515], in_=pt[0:64, :], func=AF.Identity, bias=bia[:, i:i + 1], scale=1.0),
                 reads=[pres, "bia", rr], writes=[rr])
            P.op("dve", lambda e: e.tensor_scalar(out=cy[:], in0=raw[:, 0:512], scalar1=cw[:, 4 * i:4 * i + 1], scalar2=None, op0=ALU.mult),
                 reads=[rr, "cw"], writes=[rcy])
            for j in range(1, 4):
                P.op("dve", lambda e, j=j: e.scalar_tensor_tensor(out=cy[:], in0=raw[:, j:j + 512], scalar=cw[:, 4 * i + j:4 * i + j + 1], in1=cy[:],
                                                                 op0=ALU.mult, op1=ALU.add), reads=[rr, "cw", rcy], writes=[rcy])
            P.op("dve", lambda e: e.tensor_copy(out=raw[:, 0:3], in_=raw[:, 512:515]), reads=[rr], writes=[rr])
            if i == 2:
                P.op("act", lambda e: e.activation(out=dst[:, :], in_=cy[:], func=AF.Silu), reads=[rcy], writes=[dres])
                P.dma("sp", lambda e: [e.dma_start(out=scr[i, :, t0:t0 + 512], in_=dst[:, :])], reads=[dres], writes=[("scr", hd, i, ci)], lane=("qkvst", i, ci % 2))
                return
            P.op("act", lambda e: e.activation(out=cz[:], in_=cy[:], func=AF.Silu), reads=[rcy], writes=[rcz])
            P.op("pool", lambda e: e.tensor_tensor(out=sq[:], in0=cz[:], in1=cz[:], op=ALU.mult), reads=[rcz], writes=[rsq])
            P.op("pe", lambda e: e.matmul(nb[0:64, :], lhsT=C.ones_f[0:64, 0:64], rhs=sq[:], start=True, stop=True), reads=["ones_f", rsq], writes=[rnb])
            P.op("act", lambda e: e.activation(out=rn[:], in_=nb[0:64, :], func=AF.Ln, bias=eps_t[0:64, 0:1], scale=1.0), reads=[rnb, "eps_t"], writes=[rrn])
            P.op("act", lambda e: e.activation(out=rn[:], in_=rn[:], func=AF.Exp, scale=-0.5), reads=[rrn], writes=[rrn])
            if i == 0:
                P.op("dve", lambda e: e.scalar_tensor_tensor(out=dst[:, :], in0=cz[:], scalar=0.125, in1=rn[:], op0=ALU.mult, op1=ALU.mult),
                     reads=[rcz, rrn], writes=[dres])
            else:
                P.op("dve", lambda e: e.tensor_tensor(out=dst[:, :], in0=cz[:], in1=rn[:], op=ALU.mult), reads=[rcz, rrn], writes=[dres])
            P.dma("sp", lambda e: [e.dma_start(out=scr[i, :, t0:t0 + 512], in_=dst[:, :])], reads=[dres], writes=[("scr", hd, i, ci)], lane=("qkvst", i, ci % 2))
        return ev

    def ev_beta(pt, pres, ci, t0):
        P.op("act", lambda e: e.activation(out=brow[0:1, :], in_=pt[0:1, :], func=AF.Sigmoid, bias=bia[0:1, 3:4], scale=1.0),
             reads=[pres, "bia"], writes=["brow"])
        for j in range(4):
            P.op("pe", lambda e, j=j: e.matmul(pBG[0:128, 4 * ci + j:4 * ci + j + 1], lhsT=brow[0:1, 128 * j:128 * j + 128], rhs=C.ones_f[0:1, 0:1], start=True, stop=True),
                 reads=["brow", "ones_f"], writes=["pBG"])

    def ev_dec(pt, pres, ci, t0):
        P.op("act", lambda e: e.activation(out=erow[:], in_=pt[0:1, :], func=AF.Exp, bias=bdec[0:1, 0:1], scale=1.0), reads=[pres, "bdec"], writes=["erow"])
        P.op("act", lambda e: e.activation(out=erow[:], in_=erow[:], func=AF.Ln, bias=1.0, scale=1.0), reads=["erow"], writes=["erow"])
        P.op("dve", lambda e: e.tensor_scalar(out=grow[0:1, :], in0=erow[:], scalar1=nA[0:1, 0:1], scalar2=None, op0=ALU.mult),
             reads=["erow", "nA"], writes=["grow"])
        for j in range(4):
            P.op("pe", lambda e, j=j: e.matmul(pBG[0:128, 64 + 4 * ci + j:64 + 4 * ci + j + 1], lhsT=grow[0:1, 128 * j:128 * j + 128], rhs=C.ones_f[0:1, 0:1], start=True, stop=True),
                 reads=["grow", "ones_f"], writes=["pBG"])

    specs = [(0, 128, [(0, ev_qkv(0)), (64, ev_qkv(1))]), (128, 64, ev_qkv(2)), (192, 1, ev_beta), (193, 1, ev_dec)]
    proj_pass(C, xT_d, wb, "w", specs, "d", xTb_d=io.get("xTb"))

    betaT, gT, gcT, gl, egc, e2, egl, be = [st8[:, k, :] for k in range(8)]
    P.op("dve", lambda e: e.tensor_copy(out=betaT, in_=pBG[:, 0:64]), reads=["pBG"], writes=[("betaT", hd)])
    P.op("dve", lambda e: e.tensor_copy(out=gT, in_=pBG[:, 64:128]), reads=["pBG"], writes=[("gT", hd)])
    P.op("pe", lambda e: e.matmul(C.pm[:, 0:NC_], lhsT=Utri, rhs=gT, start=True, stop=True), reads=["tri", ("gT", hd)], writes=["pm"])
    P.op("dve", lambda e: e.tensor_copy(out=gcT, in_=C.pm[:, 0:NC_]), reads=["pm"], writes=[("gcT", hd)])
    P.op("pe", lambda e: e.matmul(C.pm[:, 64:64 + NC_], lhsT=C.ones_f[:, :], rhs=gT, start=True, stop=True), reads=["ones_f", ("gT", hd)], writes=["pm"])
    P.op("dve", lambda e: e.tensor_copy(out=gl, in_=C.pm[:, 64:64 + NC_]), reads=["pm"], writes=[("gl", hd)])
    P.op("act", lambda e: e.activation(out=egc, in_=gcT, func=AF.Exp), reads=[("gcT", hd)], writes=[("egc", hd)])
    P.op("act", lambda e: e.activation(out=egl, in_=gl, func=AF.Exp), reads=[("gl", hd)], writes=[("egl", hd)])
    P.op("dve", lambda e: e.tensor_tensor(out=e2, in0=gl, in1=gcT, op=ALU.subtract), reads=[("gl", hd), ("gcT", hd)], writes=[("e2", hd)])
    P.op("act", lambda e: e.activation(out=e2, in_=e2, func=AF.Exp), reads=[("e2", hd)], writes=[("e2", hd)])
    P.op("dve", lambda e: e.tensor_tensor(out=be, in0=betaT, in1=egc, op=ALU.mult), reads=[("betaT", hd), ("egc", hd)], writes=[("be", hd)])


def psum_slot(C, k):
    banks = C.pp + C.ps + C.po + [C.pm]
    return banks[k // 4][:, 128 * (k % 4):128 * (k % 4) + 128], ("slot", k)


def gdn_chunk_gen(C, hd, st8, scr, gnw, out_d, slots, T):
    P = C.P
    tri = C.tri
    identf = C.identf
    Utri, negSU, negSL = tri[:, 0, :], tri[:, 1, :], tri[:, 2, :]
    betaT, gT, gcT, gl, egc, e2, egl, be = [st8[:, k, :] for k in range(8)]
    R = lambda nm: (nm, hd)
    (sA, rA), (sB, rB), (sC, rC), (sD, rD), (sE, rE) = slots
    bank_keys = {rA, rB, rC, rD, rE}
    _P = P

    class _W:
        def op(self, eng, fn, reads=(), writes=()):
            rd = [r for r in reads if r not in bank_keys]
            wr = list(writes) + [r for r in reads if r in bank_keys]
            _P.op(eng, fn, reads=rd, writes=wr)

        def dma(self, *a, **k):
            _P.dma(*a, **k)
    P = _W()
    kv_t, kb_t, R32, GU, T1, T2, decST, decS, decTI, kbf = T["kv_t"], T["kb_t"], T["R32"], T["GU"], T["T1"], T["T2"], T["decST"], T["decS"], T["decTI"], T["kbf"]
    Mk, Nk, wf, u32, qkT, kdec_t = T["Mk"], T["Nk"], T["wf"], T["u32"], T["qkT"], T["kdec_t"]
    S32, v32, tmpB, o32, osq, ss, on32, ostage, qkv = T["S32"], T["v32"], T["tmpB"], T["o32"], T["osq"], T["ss"], T["on32"], T["ostage"], T["qkv"]
    eps_t = C.eps6
    P.op("pool", lambda e: e.memset(S32[:], 0.0), writes=[R("S32")])

    def chunk(c):
        cs = slice(128 * c, 128 * c + 128)
        cc = slice(c, c + 1)
        b2 = c % 2
        qc, kc, vc = qkv[b2][:, 0, :], qkv[b2][:, 1, :], qkv[b2][:, 2, :]
        rin = ("qkvc", hd, b2)
        P.dma("sp", lambda e: [e.dma_start(out=qkv[b2][:, :, :], in_=scr[:, :, cs].rearrange("i d t -> d i t"))], writes=[rin])
        yield
        P.op("pe", lambda e: e.transpose(sA[:, 0:64], kc, identf[0:64, 0:64]), reads=[rin, "ident"], writes=[rA])
        P.op("pe", lambda e: e.transpose(sA[:, 64:128], vc, identf[0:64, 0:64]), reads=[rin, "ident"], writes=[rA])
        yield
        P.op("act", lambda e: e.activation(out=kv_t[:], in_=sA[:, 0:128], func=AF.Copy), reads=[rA], writes=[R("kv_t")])
        yield
        P.op("dve", lambda e: e.tensor_scalar(out=kb_t[:], in0=kv_t[:, 0:64], scalar1=betaT[:, cc], scalar2=None, op0=ALU.mult), reads=[R("kv_t"), R("betaT")], writes=[R("kb_t")])
        P.op("pool", lambda e: e.tensor_scalar(out=kdec_t[b2][:], in0=kv_t[:, 0:64], scalar1=e2[:, cc], scalar2=None, op0=ALU.mult), reads=[R("kv_t"), R("e2")], writes=[("kdec_t", hd, b2)])
        P.op("dve", lambda e: e.tensor_scalar(out=R32[:, 0:64], in0=kv_t[:, 64:128], scalar1=betaT[:, cc], scalar2=None, op0=ALU.mult), reads=[R("kv_t"), R("betaT")], writes=[R("R32")])
        P.op("dve", lambda e: e.tensor_scalar(out=R32[:, 64:128], in0=kv_t[:, 0:64], scalar1=be[:, cc], scalar2=None, op0=ALU.mult), reads=[R("kv_t"), R("be")], writes=[R("R32")])
        P.op("pool", lambda e: e.tensor_scalar(out=GU[:], in0=Utri, scalar1=gT[:, cc], scalar2=None, op0=ALU.mult), reads=["tri", R("gT")], writes=[R("GU")])
        yield
        P.op("pe", lambda e: e.transpose(sA[0:64, 0:128], kb_t[:, 0:64], identf[:, :]), reads=[R("kb_t"), "ident"], writes=[rA])
        P.op("pe", lambda e: e.matmul(sB[:, 0:128], lhsT=C.ones_f[:, :], rhs=GU[:], start=True, stop=True), reads=["ones_f", R("GU")], writes=[rB])
        yield
        P.op("act", lambda e: e.activation(out=kbf[:], in_=sA[0:64, 0:128], func=AF.Copy), reads=[rA], writes=[R("kbf")])
        P.op("dve", lambda e: e.scalar_tensor_tensor(out=T1[:], in0=sB[:, 0:128], scalar=gcT[:, cc], in1=negSU, op0=ALU.subtract, op1=ALU.add),
             reads=[rB, R("gcT"), "tri"], writes=[R("T1")])
        P.op("dve", lambda e: e.scalar_tensor_tensor(out=T2[:], in0=sB[:, 0:128], scalar=gcT[:, cc], in1=negSL, op0=ALU.subtract, op1=ALU.subtract),
             reads=[rB, R("gcT"), "tri"], writes=[R("T2")])
        yield
        P.op("act", lambda e: e.activation(out=decST[:], in_=T1[:], func=AF.Exp), reads=[R("T1")], writes=[R("decST")])
        P.op("act", lambda e: e.activation(out=decS[:], in_=T2[:], func=AF.Exp, scale=-1.0), reads=[R("T2")], writes=[R("decS")])
        P.op("pe", lambda e: e.matmul(sC[:, 0:128], lhsT=kbf[:, :], rhs=kc, start=True, stop=True), reads=[R("kbf"), rin], writes=[rC])
        P.op("pe", lambda e: e.matmul(sD[:, 0:128], lhsT=kc, rhs=kbf[:, :], start=True, stop=True), reads=[R("kbf"), rin], writes=[rD])
        P.op("pe", lambda e: e.matmul(sE[:, 0:128], lhsT=kc, rhs=qc, start=True, stop=True), reads=[rin], writes=[rE])
        yield
        P.op("pool", lambda e: e.tensor_tensor(out=decTI[:], in0=decST[:], in1=identf[:], op=ALU.add), reads=[R("decST"), "ident"], writes=[R("decTI")])
        P.op("dve", lambda e: e.tensor_tensor(out=Mk[0][:], in0=sC[:, 0:128], in1=decS[:], op=ALU.mult), reads=[rC, R("decS")], writes=[("Mk", hd, 0)])
        P.op("dve", lambda e: e.tensor_tensor(out=Nk[0][:], in0=sD[:, 0:128], in1=decST[:], op=ALU.mult), reads=[rD, R("decST")], writes=[("Nk", hd, 0)])
        yield
        P.op("dve", lambda e: e.tensor_tensor(out=qkT[b2][:], in0=sE[:, 0:128], in1=decTI[:], op=ALU.mult), reads=[rE, R("decTI")], writes=[("qkT", hd, b2)])
        for k in range(1, 7):
            P.op("pe", lambda e, k=k: e.matmul(sD[:, 0:128], lhsT=Mk[k - 1][:], rhs=Nk[k - 1][:], start=True, stop=True), reads=[("Mk", hd, k - 1), ("Nk", hd, k - 1)], writes=[rD])
            if k < 6:
                P.op("pe", lambda e, k=k: e.matmul(sC[:, 0:128], lhsT=Nk[k - 1][:], rhs=Mk[k - 1][:], start=True, stop=True), reads=[("Mk", hd, k - 1), ("Nk", hd, k - 1)], writes=[rC])
            yield
            P.op("act", lambda e, k=k: e.activation(out=Nk[k][:], in_=sD[:, 0:128], func=AF.Copy), reads=[rD], writes=[("Nk", hd, k)])
            if k < 6:
                P.op("dve", lambda e, k=k: e.tensor_copy(out=Mk[k][:], in_=sC[:, 0:128]), reads=[rC], writes=[("Mk", hd, k)])
            yield
        for k in range(6, -1, -1):
            P.op("pe", lambda e, k=k: e.matmul(sB[:, 0:128], lhsT=Nk[k][:], rhs=R32[:], start=True, stop=True), reads=[("Nk", hd, k), R("R32")], writes=[rB])
            yield
            op = ALU.add if k > 0 else ALU.subtract
            P.op("dve", lambda e, op=op: e.tensor_tensor(out=R32[:], in0=R32[:], in1=sB[:, 0:128], op=op), reads=[rB, R("R32")], writes=[R("R32")])
            yield
        P.op("pool", lambda e: e.tensor_copy(out=u32[b2][:], in_=R32[:, 0:64]), reads=[R("R32")], writes=[("u32", hd, b2)])
        P.op("pe", lambda e: e.transpose(sA[0:64, 0:128], R32[:, 64:128], identf[:, :]), reads=[R("R32"), "ident"], writes=[rA])
        yield
        P.op("act", lambda e: e.activation(out=wf[b2][:], in_=sA[0:64, 0:128], func=AF.Copy), reads=[rA], writes=[("wf", hd, b2)])
        yield
        P.op("pe", lambda e: e.matmul(sE[:, 0:64], lhsT=wf[b2][:, :], rhs=S32[:, :], start=True, stop=True), reads=[("wf", hd, b2), R("S32")], writes=[rE])
        P.op("pe", lambda e: e.matmul(sC[:, 0:64], lhsT=qc, rhs=S32[:, :], start=True, stop=True), reads=[rin, R("S32")], writes=[rC])
        yield
        P.op("dve", lambda e: e.tensor_tensor(out=v32[:], in0=u32[b2][:], in1=sE[:, 0:64], op=ALU.subtract), reads=[("u32", hd, b2), rE], writes=[R("v32")])
        yield
        P.op("pe", lambda e: e.matmul(sC[:, 64:128], lhsT=qkT[b2][:, :], rhs=v32[:, :], start=True, stop=True), reads=[("qkT", hd, b2), R("v32")], writes=[rC])
        P.op("pe", lambda e: e.matmul(sE[0:64, 64:128], lhsT=kdec_t[b2][:, :], rhs=v32[:, :], start=True, stop=True), reads=[("kdec_t", hd, b2), R("v32")], writes=[rE])
        yield
        P.op("dve", lambda e: e.scalar_tensor_tensor(out=S32[:], in0=S32[:], scalar=egl[0:64, cc], in1=sE[0:64, 64:128], op0=ALU.mult, op1=ALU.add),
             reads=[R("S32"), R("egl"), rE], writes=[R("S32")])
        P.op("act", lambda e: e.activation(out=tmpB[:], in_=sC[:, 64:128], func=AF.Copy), reads=[rC], writes=[R("tmpB")])
        yield
        P.op("dve", lambda e: e.scalar_tensor_tensor(out=o32[:], in0=sC[:, 0:64], scalar=egc[:, cc], in1=tmpB[:], op0=ALU.mult, op1=ALU.add),
             reads=[rC, R("egc"), R("tmpB")], writes=[R("o32")])
        yield
        P.op("pool", lambda e: e.tensor_tensor(out=osq[:], in0=o32[:], in1=o32[:], op=ALU.mult), reads=[R("o32")], writes=[R("osq")])
        yield
        P.op("dve", lambda e: e.tensor_reduce(out=ss[:], in_=osq[:], axis=AX.X, op=ALU.add), reads=[R("osq")], writes=[R("ss")])
        yield
        P.op("act", lambda e: e.activation(out=ss[:], in_=ss[:], func=AF.Ln, bias=eps_t[:, 0:1], scale=1.0 / 64.0), reads=[R("ss"), "eps6"], writes=[R("ss")])
        P.op("act", lambda e: e.activation(out=ss[:], in_=ss[:], func=AF.Exp, scale=-0.5), reads=[R("ss")], writes=[R("ss")])
        yield
        P.op("dve", lambda e: e.scalar_tensor_tensor(out=on32[:], in0=o32[:], scalar=ss[:, 0:1], in1=gnw[:], op0=ALU.mult, op1=ALU.mult), reads=[R("o32"), R("ss"), "gnw"], writes=[R("on32")])
        yield
        og = (c // 4) % 2
        P.op("pe", lambda e: e.transpose(sD[0:64, 0:128], on32[:, 0:64], identf[:, :]), reads=[R("on32"), "ident"], writes=[rD])
        yield
        P.op("act", lambda e: e.activation(out=ostage[og][:, 128 * (c % 4):128 * (c % 4) + 128], in_=sD[0:64, 0:128], func=AF.Copy),
             reads=[rD], writes=[("ostage", hd, og)])
        if c % 4 == 3:
            I = c // 4
            P.dma("sp", lambda e: [e.dma_start(out=out_d[:, 512 * I:512 * I + 512], in_=ostage[og][:, :])], reads=[("ostage", hd, og)], writes=[("gout", hd, I)],
                  lane=("gost", hd, og))
            C.out_res.append(("gout", hd, I))
        yield

    import os
    lim = int(os.environ.get("GDN_LIM", "1000000000"))
    cnt = 0
    for c in range(int(os.environ.get("GDN_NCH", NC_))):
        for _ in chunk(c):
            cnt += 1
            if cnt >= lim:
                return
            yield


def gdn_alloc_tiles(P, hd):
    sb = lambda shape, nm: P.sbuf(shape, F32, name=f"{nm}_{hd}")
    T = {}
    for nm in ("kv_t", "R32", "GU", "T1", "T2", "decST", "decS", "decTI"):
        T[nm] = sb([128, 128], nm)
    T["kb_t"] = sb([128, 64], "kb_t")
    T["kbf"] = sb([64, 128], "kbf")
    T["kdec_t"] = [sb([128, 64], f"kdec{i}") for i in range(2)]
    T["Mk"] = [sb([128, 128], f"Mk{i}") for i in range(7)]
    T["Nk"] = [sb([128, 128], f"Nk{i}") for i in range(7)]
    T["wf"] = [sb([64, 128], f"wf{i}") for i in range(2)]
    T["u32"] = [sb([128, 64], f"u32{i}") for i in range(2)]
    T["qkT"] = [sb([128, 128], f"qkT{i}") for i in range(2)]
    T["S32"] = sb([64, 64], "S32")
    for nm in ("v32", "tmpB", "o32", "osq", "on32"):
        T[nm] = sb([128, 64], nm)
    T["ss"] = sb([128, 1], "ss")
    T["ostage"] = [sb([64, 512], f"ostage{i}") for i in range(2)]
    T["qkv"] = [sb([64, 3, 128], f"qkvc{i}") for i in range(2)]
    return T


def gdn_group(C, ios, scrs, gnw_d, outs):
    import itertools
    P = C.P
    G = len(ios)
    C.eps6 = P.sbuf([128, 1], F32, name="eps6")
    P.op("pool", lambda e: e.memset(C.eps6[:], 1e-6), writes=["eps6"])
    st8s = [P.sbuf([128, 8, NC_], F32, name=f"st8_{g}") for g in range(G)]
    gnw = C.load_const(gnw_d, [128, 64], "gnw")
    for g in range(G):
        with P.scope():
            C.new_scope()
            gdn_phase1(C, ios[g], g, st8s[g], scrs[g])
    assert G <= 8
    Ts = [gdn_alloc_tiles(P, g) for g in range(G)]
    banks = C.pp + C.ps + C.po + [C.pm]

    def slots_for(g):
        bk, key = banks[g], ("gbank", g)
        sl = [(bk[:, 128 * k:128 * k + 128], key) for k in range(4)]
        return sl + [sl[0]]

    gens = [gdn_chunk_gen(C, g, st8s[g], scrs[g], gnw, outs[g], slots_for(g), Ts[g]) for g in range(G)]
    alive = list(gens)
    while alive:
        nxt = []
        for gen in alive:
            try:
                next(gen)
                nxt.append(gen)
            except StopIteration:
                pass
        alive = nxt


def build_gdn(nc=None, C=None, io=None):
    C = Ctx(nc)
    io = {"xT": C.dram_in("xT", [128, 8, S]), "w": C.dram_in("w", [2, 128, 8, 128]), "bias": C.dram_in("bias", [64, 8]),
          "convw": C.dram_in("convw", [64, 12]), "scal": C.dram_in("scal", [128, 2])}
    gnw_d = C.dram_in("gnw", [128, 64])
    out_d = C.dram_out("oT", [64, S])
    C.tri = C.load_const(C.dram_in("tri", [128, 3, 128]), [128, 3, 128], "tri")
    C.identf, C.ident_b = C.load_const(C.dram_in("ident", [128, 128]), [128, 128], "ident", cast=BF16)
    C.pm_b = C.pm[:].bitcast(BF16)
    scr = nc.dram_tensor("gdn_scr", [3, 64, S], F32).ap()
    gdn_group(C, [io], [scr], gnw_d, [out_d])
    C.P.barrier()
    C.P.op("sp", lambda e: None)
    C.P.emit()
    return nc


T_ = 1024
NTC = 2
ALPHA = (2 * 2) ** 0.25
LN_EPS = 1e-5


def stream_w(C, w_d, c0, n, tag):
    P = C.P
    L = C.loc
    if not hasattr(L, "ws_bufs"):
        L.ws_stg = [P.sbuf([128, 8, 128], F32, name=f"wsstg{i}") for i in range(2)]
        L.ws_bufs = [P.sbuf([128, 8, 128], BF16, name=f"wsb{i}") for i in range(2)]
        L.ws_i = 0
    i = L.ws_i
    L.ws_i += 1
    st, rs = L.ws_stg[i % 2], ("wsstg", i % 2)
    wt, rw = L.ws_bufs[i % 2], ("wsb", i % 2)
    P.dma("act", lambda e: [e.dma_start(out=st[:, :, :], in_=w_d[c0])], writes=[rs])
    P.op("pool", lambda e: e.tensor_copy(out=wt[:, :, 0:n], in_=st[:, :, 0:n]), reads=[rs], writes=[rw])
    return wt, rw


def build_merge(nc=None, C=None, io=None):
    standalone = C is None
    if standalone:
        C = Ctx(nc)
        io = {"xT": C.dram_in("xT", [128, 8, T_]), "xtok": C.dram_in("xtok", [T_, 1024]), "oT_in": C.dram_in("oT_in", [4, 384, T_]),
              "wz": C.dram_in("wz", [128, 8, 1920]), "bz": C.dram_in("bz", [128, 16]), "wm": C.dram_in("wm", [128, 8, 5120]),
              "bm": C.dram_in("bm", [128, 40]), "weq": C.dram_in("weq", [128, 8, 384]), "beq": C.dram_in("beq", [96, 4]),
              "mem": C.dram_in("mem", [256, 1024]), "memg": C.dram_in("memg", [128, 2, 1024]), "wkv": C.dram_in("wkv", [128, 8, 768]),
              "wbr": C.dram_in("wbr", [128, 12, 1024]), "wbe": C.dram_in("wbe", [96, 4, 1024]), "wo": C.dram_in("wo", [128, 8, 1024]),
              "lng": C.dram_in("lng", [128, 2, 1024]), "out": C.dram_out("out", [T_, 1024]), "outT": None}
        C.identf, C.ident_b = C.load_const(C.dram_in("ident", [128, 128]), [128, 128], "ident", cast=BF16)
        C.pm_b = C.pm[:].bitcast(BF16)
    P = C.P
    wz_d, bz_d, wm_d, bm_d = io["wz"], io["bz"], io["wm"], io["bm"]
    weq_d, beq_d, mem_d, mg_d, wkv_d, wbr_d, wbe_d = io["weq"], io["beq"], io["mem"], io["memg"], io["wkv"], io["wbr"], io["wbe"]
    wo_d, lng_d = io["wo"], io["lng"]
    shards = io.get("shards") or [{"xT": io["xT"], "xtok": io["xtok"], "oT_in": io["oT_in"], "out": io["out"], "outT": io["outT"]}]
    _cache = {}

    def A(shape, dtype, name=None):
        if name not in _cache:
            _cache[name] = P.sbuf(shape, dtype, name=name)
        return _cache[name]
    identf = C.identf
    bz = C.load_const(bz_d, [128, 16], "bz")
    bm = C.load_const(bm_d, [128, 40], "bm")
    beq = C.load_const(beq_d, [96, 4], "beq")
    SC = 96 ** -0.5
    beqs = A([96, 4], F32, name="beqs")
    P.op("dve", lambda e: e.tensor_scalar(out=beqs[:], in0=beq[:], scalar1=SC, scalar2=None, op0=ALU.mult), reads=["beq"], writes=["beqs"])
    eps_t = A([128, 1], F32, name="eps_t")
    P.op("pool", lambda e: e.memset(eps_t[:], LN_EPS), writes=["eps_t"])

    big = [A([128, 1024], F32, name=f"big{i}") for i in range(3)]
    stat = A([128, 4], F32, name="stat")

    def layer_norm(src, src_res, gb, gb_res, dst, dst_res, tmp, tmp_res):
        P.op("dve", lambda e: e.tensor_reduce(out=stat[:, 0:1], in_=src[:], axis=AX.X, op=ALU.add), reads=[src_res], writes=["stat0"])
        P.op("dve", lambda e: e.tensor_scalar(out=stat[:, 0:1], in0=stat[:, 0:1], scalar1=1.0 / 1024.0, scalar2=None, op0=ALU.mult), reads=["stat0"], writes=["stat0"])
        P.op("dve", lambda e: e.tensor_scalar(out=src[:], in0=src[:], scalar1=stat[:, 0:1], scalar2=None, op0=ALU.subtract), reads=[src_res, "stat0"], writes=[src_res])
        P.op("act", lambda e: e.activation(out=tmp[:], in_=src[:], func=AF.Square), reads=[src_res], writes=[tmp_res])
        P.op("dve", lambda e: e.tensor_reduce(out=stat[:, 1:2], in_=tmp[:], axis=AX.X, op=ALU.add), reads=[tmp_res], writes=["stat1"])
        P.op("act", lambda e: e.activation(out=stat[:, 1:2], in_=stat[:, 1:2], func=AF.Ln, bias=eps_t[:, 0:1], scale=1.0 / 1024.0), reads=["stat1", "eps_t"], writes=["stat1"])
        P.op("act", lambda e: e.activation(out=stat[:, 1:2], in_=stat[:, 1:2], func=AF.Exp, scale=-0.5), reads=["stat1"], writes=["stat1"])
        P.op("dve", lambda e: e.scalar_tensor_tensor(out=tmp[:], in0=src[:], scalar=stat[:, 1:2], in1=gb[:, 0, :], op0=ALU.mult, op1=ALU.mult),
             reads=[src_res, "stat1", gb_res], writes=[tmp_res])
        P.op("dve", lambda e: e.tensor_tensor(out=dst[:], in0=tmp[:], in1=gb[:, 1, :], op=ALU.add), reads=[tmp_res, gb_res], writes=[dst_res])

    gbt = A([128, 2, 1024], F32, name="gbt")
    P.dma("sp", lambda e: [e.dma_start(out=gbt[:], in_=mg_d)], writes=["gbt"])
    mg = gbt
    memT = A([128, 8, 256], BF16, name="memT")
    memn_b = A([128, 1024], BF16, name="memn_b")
    for t in range(2):
        P.dma("sp", lambda e, t=t: [e.dma_start(out=big[0][:], in_=mem_d[128 * t:128 * t + 128, :])], writes=["big0"])
        layer_norm(big[0], "big0", mg, "gbt", big[1], "big1", big[2], "big2")
        P.op("act", lambda e: e.activation(out=memn_b[:], in_=big[1][:], func=AF.Copy), reads=["big1"], writes=["memn_b"])
        for kc in range(8):
            P.op("pe", lambda e, kc=kc: e.transpose(C.pm_b[:, 128 * kc:128 * kc + 128], memn_b[:, 128 * kc:128 * kc + 128], C.ident_b[:, :]),
                 reads=["memn_b", "ident_b"], writes=["pm"])
        P.op("act", lambda e, t=t: e.activation(out=memT[:, :, 128 * t:128 * t + 128], in_=C.pm_b[:, 0:1024].rearrange("p (k m) -> p k m", k=8), func=AF.Copy),
             reads=["pm"], writes=[("memT", t)])
    wbig = A([128, 8, 1024], BF16, name="wbig")
    wkv = load_weights(C, wkv_d, 768, "wbig", wb=wbig)
    kmT = A([96, 4, 256], BF16, name="kmT")
    vm = A([128, 2, 4, 97], BF16, name="vm")
    P.op("pool", lambda e: e.memset(vm[:, :, :, 96:97], 1.0), writes=["vm_ones"])
    for h in range(4):
        pt, pres = C.next_pp()
        for kc in range(8):
            P.op("pe", lambda e, pt=pt, kc=kc, h=h: e.matmul(pt[0:96, 0:256], lhsT=wkv[:, kc, 96 * h:96 * h + 96], rhs=memT[:, kc, :], start=(kc == 0), stop=(kc == 7)),
                 reads=["wbig", ("memT", 0), ("memT", 1)], writes=[pres])
        P.op("act", lambda e, pt=pt, h=h: e.activation(out=kmT[:, h, :], in_=pt[0:96, 0:256], func=AF.Copy), reads=[pres], writes=[("kmT", h)])
    for t in range(2):
        pt, pres = C.next_pp()
        for kc in range(8):
            P.op("pe", lambda e, pt=pt, kc=kc, t=t: e.matmul(pt[:, 0:384], lhsT=memT[:, kc, 128 * t:128 * t + 128], rhs=wkv[:, kc, 384:768], start=(kc == 0), stop=(kc == 7)),
                 reads=["wbig", ("memT", t)], writes=[pres])
        P.op("act", lambda e, pt=pt, t=t: e.activation(out=vm[:, t, :, 0:96], in_=pt[:, 0:384].rearrange("p (h d) -> p h d", h=4), func=AF.Copy),
             reads=[pres, "vm_ones"], writes=[("vm", t)])

    wbr = A([128, 12, 1024], BF16, name="wbr")
    wbe = A([96, 4, 1024], BF16, name="wbe")
    for q in range(16):
        bt, br = big[1 + q % 2], f"big{1 + q % 2}"
        if q < 12:
            P.dma("act", lambda e, q=q, bt=bt: [e.dma_start(out=bt[:, :], in_=wbr_d[:, q, :])], writes=[br])
            P.op("pool", lambda e, q=q, bt=bt: e.tensor_copy(out=wbr[:, q, :], in_=bt[:, :]), reads=[br], writes=["wbr"])
        else:
            P.dma("act", lambda e, q=q, bt=bt: [e.dma_start(out=bt[0:96, :], in_=wbe_d[:, q - 12, :])], writes=[br])
            P.op("pool", lambda e, q=q, bt=bt: e.tensor_copy(out=wbe[:, q - 12, :], in_=bt[0:96, :]), reads=[br], writes=["wbe"])
    wo = load_weights(C, wo_d, 1024, "wbig", wb=wbig)
    P.dma("sp", lambda e: [e.dma_start(out=gbt[:], in_=lng_d)], reads=[], writes=["gbt"])
    lng = gbt
    def do_shard(sh):
        xT_d, xtok_d, oT_d, out_d, outT_d = sh["xT"], sh["xtok"], sh["oT_in"], sh["out"], sh["outT"]
        xb = A([128, 8, T_], BF16, name="xb")
        xs = [A([128, 2, 512], F32, name=f"xs{i}") for i in range(2)]
        for ci in range(NTC):
            t0 = 512 * ci
            for q4 in range(4):
                hh = q4 % 2
                P.dma("sp", lambda e, hh=hh, q4=q4, t0=t0: [e.dma_start(out=xs[hh][:, :, :], in_=xT_d[:, 2 * q4:2 * q4 + 2, t0:t0 + 512])], writes=[("xs", hh)])
                P.op("pool" if hh == 0 else "dve", lambda e, hh=hh, q4=q4, t0=t0: e.tensor_copy(out=xb[:, 2 * q4:2 * q4 + 2, t0:t0 + 512], in_=xs[hh][:, :, :]), reads=[("xs", hh)], writes=[("xb", ci, q4)])

        def xres(ci):
            return [("xb", ci, q4) for q4 in range(4)]


        gE = A([96, 4, T_], BF16, name="gE")
        eq = A([96, 512], BF16, name="eq")
        szE = A([96, 512], F32, name="szE")
        ptE = [A([128, 512], BF16, name=f"ptE{i}") for i in range(2)]
        onE = A([97, 512], F32, name="onE")
        rdenE = A([1, 512], F32, name="rdenE")
        for h in range(4):
            wq_t, wq_r = stream_w(C, weq_d, h, 96, "weq")
            wz_t, wz_r = stream_w(C, wz_d, 12 + h, 96, "wzE")
            for ci in range(NTC):
                cs = slice(512 * ci, 512 * ci + 512)
                pt, pres = C.next_pp()
                for kc in range(8):
                    P.op("pe", lambda e, pt=pt, kc=kc, cs=cs: e.matmul(pt[0:96, :], lhsT=wq_t[:, kc, 0:96], rhs=xb[:, kc, cs], start=(kc == 0), stop=(kc == 7)),
                         reads=[wq_r] + xres(ci), writes=[pres])
                P.op("act", lambda e, pt=pt, h=h: e.activation(out=eq[:], in_=pt[0:96, :], func=AF.Identity, bias=beqs[:, h:h + 1], scale=SC), reads=[pres, "beqs"], writes=["eq"])
                pt2, pres2 = C.next_pp()
                for kc in range(8):
                    P.op("pe", lambda e, pt2=pt2, kc=kc, cs=cs: e.matmul(pt2[0:96, :], lhsT=wz_t[:, kc, 0:96], rhs=xb[:, kc, cs], start=(kc == 0), stop=(kc == 7)),
                         reads=[wz_r] + xres(ci), writes=[pres2])
                P.op("act", lambda e, pt2=pt2, h=h: e.activation(out=szE[:], in_=pt2[0:96, :], func=AF.Silu, bias=bz[0:96, 12 + h:13 + h], scale=1.0), reads=[pres2, "bz"], writes=["szE"])
                po, po_res = C.next_po()
                for t in range(2):
                    pst, ps_res = C.next_ps()
                    P.op("pe", lambda e, pst=pst, t=t, h=h: e.matmul(pst[:, :], lhsT=kmT[:, h, 128 * t:128 * t + 128], rhs=eq[:, :], start=True, stop=True),
                         reads=[("kmT", h), "eq"], writes=[ps_res])
                    P.op("act", lambda e, pst=pst, t=t: e.activation(out=ptE[t][:], in_=pst[:, :], func=AF.Exp), reads=[ps_res], writes=[("ptE", t)])
                    P.op("pe", lambda e, po=po, t=t, h=h: e.matmul(po[0:97, :], lhsT=vm[:, t, h, 0:97], rhs=ptE[t][:], start=(t == 0), stop=(t == 1)),
                         reads=[("vm", t), "vm_ones", ("ptE", t)], writes=[po_res])
                P.op("act", lambda e, po=po: e.activation(out=onE[0:97, :], in_=po[0:97, :], func=AF.Copy), reads=[po_res], writes=["onE"])
                P.op("dve", lambda e: e.reciprocal(out=rdenE[0:1, :], in_=onE[96:97, :]), reads=["onE"], writes=["rdenE"])
                P.op("pe", lambda e: e.matmul(C.pm[0:96, :], lhsT=C.ones_f[0:1, 0:96], rhs=rdenE[0:1, :], start=True, stop=True), reads=["ones_f", "rdenE"], writes=["pm"])
                P.op("dve", lambda e: e.tensor_tensor(out=onE[0:96, :], in0=onE[0:96, :], in1=C.pm[0:96, :], op=ALU.mult), reads=["onE", "pm"], writes=["onE"])
                P.op("dve", lambda e, h=h, cs=cs: e.tensor_tensor(out=gE[:, h, cs], in0=onE[0:96, :], in1=szE[:], op=ALU.mult), reads=["onE", "szE"], writes=[("gE", h, ci)])

        g4 = A([128, 12, T_], BF16, name="g4")
        oin = [A([128, 512], F32, name=f"oin{i}") for i in range(2)]
        sz = [A([128, 512], F32, name=f"sz{i}") for i in range(2)]
        it = 0
        for n in range(4):
            for k in range(3):
                wz_t, wz_r = stream_w(C, wz_d, 3 * n + k, 128, "wz")
                for ci in range(NTC):
                    cs = slice(512 * ci, 512 * ci + 512)
                    b = it % 2
                    it += 1
                    P.dma("sp", lambda e, b=b, n=n, k=k, cs=cs: [e.dma_start(out=oin[b][:], in_=oT_d[n, 128 * k:128 * k + 128, cs])], writes=[("oin", b)])
                    pt, pres = C.next_pp()
                    for kc in range(8):
                        P.op("pe", lambda e, pt=pt, kc=kc, cs=cs, wz_t=wz_t: e.matmul(pt[:, :], lhsT=wz_t[:, kc, 0:128], rhs=xb[:, kc, cs], start=(kc == 0), stop=(kc == 7)),
                             reads=[wz_r] + xres(ci), writes=[pres])
                    P.op("act", lambda e, pt=pt, b=b, n=n, k=k: e.activation(out=sz[b][:], in_=pt[:, :], func=AF.Silu, bias=bz[:, 3 * n + k:3 * n + k + 1], scale=1.0),
                         reads=[pres, "bz"], writes=[("sz", b)])
                    P.op("dve", lambda e, b=b, n=n, k=k, cs=cs: e.tensor_tensor(out=g4[:, 3 * n + k, cs], in0=oin[b][:], in1=sz[b][:], op=ALU.mult),
                         reads=[("oin", b), ("sz", b)], writes=[("g4", n, k, ci)])

        wbr, wbe = _cache["wbr"], _cache["wbe"]
        yb = A([128, 8, T_], BF16, name="yb")
        yacc = A([128, T_], F32, name="yacc")
        sig = [A([128, 512], F32, name=f"sig{i}") for i in range(2)]
        ytmp = [A([128, 512], F32, name=f"ytmp{i}") for i in range(2)]
        it = 0
        for dmt in range(8):
            dsl = slice(128 * dmt, 128 * dmt + 128)
            for n in range(5):
                wm_t, wm_r = stream_w(C, wm_d, 8 * n + dmt, 128, "wm")
                for ci in range(NTC):
                    cs = slice(512 * ci, 512 * ci + 512)
                    b = it % 2
                    it += 1
                    pt, pres = C.next_pp()
                    for kc in range(8):
                        P.op("pe", lambda e, pt=pt, kc=kc, cs=cs, wm_t=wm_t: e.matmul(pt[:, :], lhsT=wm_t[:, kc, 0:128], rhs=xb[:, kc, cs], start=(kc == 0), stop=(kc == 7)),
                             reads=[wm_r] + xres(ci), writes=[pres])
                    P.op("act", lambda e, pt=pt, b=b, n=n, dmt=dmt: e.activation(out=sig[b][:], in_=pt[:, :], func=AF.Sigmoid, bias=bm[:, 8 * n + dmt:8 * n + dmt + 1], scale=1.0),
                         reads=[pres, "bm"], writes=[("sig", b)])
                    pst, ps_res = C.next_ps()
                    if n < 4:
                        for k in range(3):
                            P.op("pe", lambda e, pst=pst, k=k, n=n, cs=cs, dsl=dsl: e.matmul(pst[:, :], lhsT=wbr[:, 3 * n + k, dsl], rhs=g4[:, 3 * n + k, cs], start=(k == 0), stop=(k == 2)),
                                 reads=["wbr", ("g4", n, k, ci)], writes=[ps_res])
                    else:
                        for h in range(4):
                            P.op("pe", lambda e, pst=pst, h=h, cs=cs, dsl=dsl: e.matmul(pst[:, :], lhsT=wbe[:, h, dsl], rhs=gE[:, h, cs], start=(h == 0), stop=(h == 3)),
                                 reads=["wbe", ("gE", h, ci)], writes=[ps_res])
                    if n == 0:
                        P.op("dve", lambda e, pst=pst, b=b, cs=cs: e.tensor_tensor(out=yacc[:, cs], in0=sig[b][:], in1=pst[:, :], op=ALU.mult), reads=[("sig", b), ps_res], writes=[("yacc", ci)])
                    else:
                        P.op("dve", lambda e, pst=pst, b=b: e.tensor_tensor(out=ytmp[b][:], in0=sig[b][:], in1=pst[:, :], op=ALU.mult), reads=[("sig", b), ps_res], writes=[("ytmp", b)])
                        P.op("pool", lambda e, b=b, cs=cs: e.tensor_tensor(out=yacc[:, cs], in0=yacc[:, cs], in1=ytmp[b][:], op=ALU.add), reads=[("ytmp", b), ("yacc", ci)], writes=[("yacc", ci)])
            for ci in range(NTC):
                cs = slice(512 * ci, 512 * ci + 512)
                P.op("act", lambda e, dmt=dmt, cs=cs: e.activation(out=yb[:, dmt, cs], in_=yacc[:, cs], func=AF.Copy), reads=[("yacc", ci)], writes=[("yb", dmt, ci)])

        wo, lng = wbig, gbt
        for tt in range(T_ // 128):
            ts = slice(128 * tt, 128 * tt + 128)
            ci = tt // 4
            P.dma("sp", lambda e, ts=ts: [e.dma_start(out=big[0][:], in_=xtok_d[ts, :])], writes=["big0"])
            for half in range(2):
                pt, pres = C.next_pp()
                hs = slice(512 * half, 512 * half + 512)
                for kc in range(8):
                    P.op("pe", lambda e, pt=pt, kc=kc, ts=ts, hs=hs: e.matmul(pt[:, :], lhsT=yb[:, kc, ts], rhs=wo[:, kc, hs], start=(kc == 0), stop=(kc == 7)),
                         reads=["wbig"] + [("yb", kc, ci) for kc in range(8)], writes=[pres])
                P.op("dve", lambda e, pt=pt, hs=hs: e.scalar_tensor_tensor(out=big[1][:, hs], in0=big[0][:, hs], scalar=ALPHA, in1=pt[:, :], op0=ALU.mult, op1=ALU.add),
                     reads=["big0", pres], writes=["big1"])
            layer_norm(big[1], "big1", lng, "gbt", big[0], "big0", big[2], "big2")
            P.dma("sp", lambda e, ts=ts: [e.dma_start(out=out_d[ts, :], in_=big[0][:])], reads=["big0"], writes=[("out", tt)], lane="ost")
            C.out_res.append(("out", tt))
            if outT_d is not None:
                xTn = big[2][:, :].rearrange("p (k t) -> p k t", k=8)
                for half in range(2):
                    for k4 in range(4):
                        kc = 4 * half + k4
                        P.op("pe", lambda e, kc=kc, k4=k4: e.transpose(C.pm[:, 128 * k4:128 * k4 + 128], big[0][:, 128 * kc:128 * kc + 128], identf[:, :]),
                             reads=["big0", "ident"], writes=["pm"])
                    P.op("act", lambda e, half=half: e.activation(out=xTn[:, 4 * half:4 * half + 4, :], in_=C.pm[:, 0:512].rearrange("p (k t) -> p k t", k=4), func=AF.Copy),
                         reads=["pm"], writes=["big2"])
                P.dma("sp", lambda e, ts=ts: [e.dma_start(out=outT_d[:, :, ts], in_=xTn)], reads=["big2"], writes=[("outT", tt)], lane="ostT")

    for sh in shards:
        do_shard(sh)
    if standalone:
        P.op("sp", lambda e: None, reads=C.out_res)
        P.emit()
    return nc


DEPTH = 2
NH = 6


def build_fused(nc, depth=DEPTH, heads=NH, nshards=S // T_):
    C = Ctx(nc)
    P = C.P
    di = C.dram_in
    xT0 = di("xT0", [128, 8, S])
    xtok0 = di("xtok0", [S, 1024])
    mem = di("mem", [256, 1024])
    memg = di("memg", [128, 2, 1024])
    wfox, bfox = di("wfox", [DEPTH, NH, 2, 128, 8, 128]), di("bfox", [DEPTH, NH, 64, 4])
    wmoba, bmoba = di("wmoba", [DEPTH, NH, 2, 128, 8, 128]), di("bmoba", [DEPTH, NH, 64, 4])
    wdil, bdil = di("wdil", [DEPTH, NH, 4, 128, 8, 128]), di("bdil", [DEPTH, NH, 64, 8])
    wgdn, bgdn = di("wgdn", [DEPTH, NH, 2, 128, 8, 128]), di("bgdn", [DEPTH, NH, 64, 8])
    convw, scal, gnw = di("convw", [DEPTH, NH, 64, 12]), di("scal", [DEPTH, NH, 128, 2]), di("gnw", [DEPTH, 128, 64])
    wz, bz = di("wz", [DEPTH, 16, 128, 8, 128]), di("bz", [DEPTH, 128, 16])
    wm, bm = di("wm", [DEPTH, 40, 128, 8, 128]), di("bm", [DEPTH, 128, 40])
    weq, beq = di("weq", [DEPTH, 4, 128, 8, 128]), di("beq", [DEPTH, 96, 4])
    wkv = di("wkv", [DEPTH, 6, 128, 8, 128])
    wbr, wbe = di("wbr", [DEPTH, 128, 12, 1024]), di("wbe", [DEPTH, 96, 4, 1024])
    wo, lng = di("wo", [DEPTH, 8, 128, 8, 128]), di("lng", [DEPTH, 128, 2, 1024])
    ind, blkc = di("ind", [32, S]), di("blkc", [128, 3, 32, 32])
    out = C.dram_out("out", [S, 1024])
    oT_all = nc.dram_tensor("oT_all", [4, 384, S], F32).ap()
    xT1 = nc.dram_tensor("xT1", [128, 8, S], F32).ap()
    xtok1 = nc.dram_tensor("xtok1", [S, 1024], F32).ap()
    gscr = nc.dram_tensor("gdn_scr", [NH, 3, 64, S], F32).ap()
    xTb = nc.dram_tensor("xTb", [128, 8, S], BF16).ap()
    C.negc = C.load_const(di("negc", [128, 128]), [128, 128], "negc")
    C.band = C.load_const(di("band", [128, 256]), [128, 256], "band")
    C.tri = C.load_const(di("tri", [128, 3, 128]), [128, 3, 128], "tri")
    C.identf, C.ident_b = C.load_const(di("ident", [128, 128]), [128, 128], "ident", cast=BF16)
    C.pm_b = C.pm[:].bitcast(BF16)
    for l in range(depth):
        xT = xT0 if l == 0 else xT1
        xtok = xtok0 if l == 0 else xtok1
        with P.scope():
            C.new_scope()
            convert_xT(C, xT, xTb)
        for h in range(heads):
            osl = lambda n: oT_all[n, 64 * h:64 * h + 64, :]
            with P.scope():
                C.new_scope()
                build_moba(C=C, io={"xT": xT, "w": wmoba[l, h], "bias": bmoba[l, h], "ind": ind, "blkc": blkc, "out": osl(0)})
            with P.scope():
                C.new_scope()
                build_dil(C=C, io={"xT": xT, "xTb": xTb, "w": wdil[l, h], "bias": bdil[l, h], "out": osl(1)})
            with P.scope():
                C.new_scope()
                build_fox(C=C, io={"xT": xT, "xTb": xTb, "w": wfox[l, h], "bias": bfox[l, h], "out": osl(2)})
        with P.scope():
            C.new_scope()
            gdn_group(C, [{"xT": xT, "xTb": xTb, "w": wgdn[l, h], "bias": bgdn[l, h], "convw": convw[l, h], "scal": scal[l, h]} for h in range(heads)],
                      [gscr[h] for h in range(heads)], gnw[l], [oT_all[3, 64 * h:64 * h + 64, :] for h in range(heads)])
        last = (l == depth - 1)
        shards = []
        for s_ in range(nshards):
            ts = slice(T_ * s_, T_ * s_ + T_)
            shards.append({"xT": xT[:, :, ts], "xtok": xtok[ts, :], "oT_in": oT_all[:, :, ts], "out": (out if last else xtok1)[ts, :],
                           "outT": None if last else xT1[:, :, ts]})
        with P.scope():
            C.new_scope()
            build_merge(C=C, io={"shards": shards, "wz": wz[l], "bz": bz[l], "wm": wm[l], "bm": bm[l], "weq": weq[l], "beq": beq[l], "mem": mem, "memg": memg,
                                 "wkv": wkv[l], "wbr": wbr[l], "wbe": wbe[l], "wo": wo[l], "lng": lng[l]})
    P.barrier()
    P.op("sp", lambda e: None)
    print("fused ops:", len(P.ops), flush=True)
    P.emit()
    return nc


H = 6
BW = 384
OFF_A = 0
OFF_BQK = 1152
OFF_BV = 3456
OFF_C = 3840
OFF_CF = 4992
OFF_D = 4998
OFF_DBETA = 6150
OFF_DDEC = 6156
OFF_EQ = 6162
OFF_Z = 6546
OFF_M = 8466


def _xT_layout(xb):
    T = xb.shape[0]
    return np.ascontiguousarray(xb.T.reshape(8, 128, T).transpose(1, 0, 2))


def _w_layout(Wc):
    n = Wc.shape[1]
    return np.ascontiguousarray(Wc.reshape(8, 128, n).transpose(1, 0, 2))


def _w_tiles(Wc, bounds=None):
    n = Wc.shape[1]
    if bounds is None:
        bounds = [(c0, min(c0 + 128, n)) for c0 in range(0, n, 128)]
    out = np.zeros((len(bounds), 128, 8, 128), np.float32)
    for t, (a, b) in enumerate(bounds):
        out[t, :, :, 0:b - a] = Wc[:, a:b].reshape(8, 128, b - a).transpose(1, 0, 2)
    return out


def _consts():
    s_ = np.arange(128)[:, None]
    t_ = np.arange(128)[None, :]
    c = {}
    c["negc"] = np.where(t_ >= s_, 0.0, -30000.0).astype(np.float32)
    c["band"] = np.concatenate([np.where(t_ >= s_, 0.0, -30000.0), np.where(t_ <= s_, 0.0, -30000.0)], 1).astype(np.float32)
    c["ind"] = (np.arange(S)[None, :] // 256 == np.arange(32)[:, None]).astype(np.float32)
    qb = np.arange(32)[:, None]
    n = np.arange(32)[None, :]
    blk = np.stack([np.where(n < qb, 0.0, -1e9), (n < qb).astype(np.float32), (n == qb).astype(np.float32)]).astype(np.float32)
    c["blkc"] = np.ascontiguousarray(np.broadcast_to(blk[None], (128, 3, 32, 32)))
    c["ident"] = np.eye(128, dtype=np.float32)
    c["tri"] = np.stack([(s_ <= t_).astype(np.float32), np.where(t_ > s_, 0.0, -30000.0), np.where(s_ > t_, 0.0, -30000.0)], 1).astype(np.float32)
    return c


def _pack_weights(w_in, b_in, conv_w, a_log, dt_bias, gdn_norm_w, w_mem_kv, w_branch, w_out, ln_g, ln_b, mem_ln_g, mem_ln_b):
    f32 = np.float32
    L = w_in.shape[0]
    d = {k: [] for k in ("wfox", "bfox", "wmoba", "bmoba", "wdil", "bdil", "wgdn", "bgdn", "convw", "scal", "gnw",
                         "wz", "bz", "wm", "bm", "weq", "beq", "wkv", "wbr", "wbe", "wo", "lng")}
    rep2 = lambda a, b: np.ascontiguousarray(np.broadcast_to(np.stack([np.asarray(a, f32), np.asarray(b, f32)])[None], (128, 2, 1024)))
    for l in range(L):
        Wl = np.asarray(w_in[l], f32)
        bl = np.asarray(b_in[l], f32)
        cwl = np.asarray(conv_w[l], f32)
        per = {k: [] for k in ("wfox", "bfox", "wmoba", "bmoba", "wdil", "bdil", "wgdn", "bgdn", "convw", "scal")}
        for h in range(H):
            cols = np.concatenate([OFF_C + k * BW + 64 * h + np.arange(64) for k in range(3)] + [np.array([OFF_CF + h])])
            bias = np.zeros((64, 4), f32)
            for k in range(3):
                bias[:, k] = bl[cols[64 * k:64 * k + 64]]
            bias[0, 3] = bl[OFF_CF + h]
            per["wfox"].append(_w_tiles(Wl[:, cols]))
            per["bfox"].append(bias)
            cols = np.concatenate([OFF_A + k * BW + 64 * h + np.arange(64) for k in range(3)])
            bias = np.zeros((64, 4), f32)
            for k in range(3):
                bias[:, k] = bl[cols[64 * k:64 * k + 64]]
            per["wmoba"].append(_w_tiles(Wl[:, cols]))
            per["bmoba"].append(bias)
            segs = []
            for g in range(3):
                segs.append(OFF_BQK + ((0 * 3 + g) * H + h) * 64 + np.arange(64))
                segs.append(OFF_BQK + ((1 * 3 + g) * H + h) * 64 + np.arange(64))
            segs.append(OFF_BV + 64 * h + np.arange(64))
            cols = np.concatenate(segs)
            bias = np.zeros((64, 8), f32)
            for k in range(7):
                bias[:, k] = bl[cols[64 * k:64 * k + 64]]
            per["wdil"].append(_w_tiles(Wl[:, cols]))
            per["bdil"].append(bias)
            cols = np.concatenate([OFF_D + k * BW + 64 * h + np.arange(64) for k in range(3)] + [np.array([OFF_DBETA + h, OFF_DDEC + h])])
            bias = np.zeros((64, 8), f32)
            for k in range(3):
                bias[:, k] = bl[cols[64 * k:64 * k + 64]]
            bias[0, 3] = bl[OFF_DBETA + h]
            bias[0, 4] = bl[OFF_DDEC + h]
            cw = np.zeros((64, 12), f32)
            for i in range(3):
                for j in range(4):
                    cw[:, 4 * i + j] = cwl[j, i * BW + 64 * h:i * BW + 64 * h + 64]
            sc = np.zeros((128, 2), f32)
            sc[:, 0] = np.asarray(a_log[l], f32)[h]
            sc[:, 1] = np.asarray(dt_bias[l], f32)[h]
            per["wgdn"].append(_w_tiles(Wl[:, cols]))
            per["bgdn"].append(bias)
            per["convw"].append(cw)
            per["scal"].append(sc)
        for k in per:
            d[k].append(np.stack(per[k]))
        d["gnw"].append(np.ascontiguousarray(np.broadcast_to(np.asarray(gdn_norm_w[l], f32)[None], (128, 64))))
        bzv = bl[OFF_Z:OFF_Z + 1920]
        bmv = bl[OFF_M:OFF_M + 5120]
        beqv = bl[OFF_EQ:OFF_EQ + 384]
        bz = np.zeros((128, 16), f32)
        for i in range(12):
            bz[:, i] = bzv[128 * i:128 * i + 128]
        for h in range(4):
            bz[0:96, 12 + h] = bzv[1536 + 96 * h:1536 + 96 * h + 96]
        bm = np.zeros((128, 40), f32)
        for n in range(5):
            for dd in range(8):
                bm[:, 8 * n + dd] = bmv[1024 * n + 128 * dd:1024 * n + 128 * dd + 128]
        Wbr = np.asarray(w_branch[l], f32)
        d["wz"].append(_w_tiles(Wl[:, OFF_Z:OFF_Z + 1920], [(128 * i, 128 * i + 128) for i in range(12)] + [(1536 + 96 * i, 1536 + 96 * i + 96) for i in range(4)]))
        d["bz"].append(bz)
        d["wm"].append(_w_tiles(Wl[:, OFF_M:OFF_M + 5120]))
        d["bm"].append(bm)
        d["weq"].append(_w_tiles(Wl[:, OFF_EQ:OFF_EQ + 384], [(96 * i, 96 * i + 96) for i in range(4)]))
        d["beq"].append(np.ascontiguousarray(beqv.reshape(4, 96).T))
        d["wkv"].append(_w_tiles(np.asarray(w_mem_kv[l], f32)))
        d["wbr"].append(np.ascontiguousarray(Wbr[0:4].reshape(4, 3, 128, 1024).transpose(2, 0, 1, 3).reshape(128, 12, 1024)))
        d["wbe"].append(np.ascontiguousarray(Wbr[4].reshape(4, 96, 1024).transpose(1, 0, 2)))
        d["wo"].append(_w_tiles(np.asarray(w_out[l], f32)))
        d["lng"].append(rep2(ln_g[l], ln_b[l]))
    out = {k: np.ascontiguousarray(np.stack(v)).astype(f32) for k, v in d.items()}
    out["memg"] = rep2(mem_ln_g, mem_ln_b)
    return out


def kernel(x, mem, mem_ln_g, mem_ln_b, w_in, b_in, conv_w, a_log, dt_bias, gdn_norm_w, w_mem_kv, w_branch, w_out, ln_g, ln_b):
    f32 = np.float32
    x = np.asarray(x, f32)
    mem = np.asarray(mem, f32)
    B = x.shape[0]
    shared = _pack_weights(np.asarray(w_in), np.asarray(b_in), np.asarray(conv_w), np.asarray(a_log), np.asarray(dt_bias), np.asarray(gdn_norm_w),
                           np.asarray(w_mem_kv), np.asarray(w_branch), np.asarray(w_out), np.asarray(ln_g), np.asarray(ln_b), mem_ln_g, mem_ln_b)
    shared.update(_consts())
    nc = bass.Bass("TRN2", target_bir_lowering=False)
    build_fused(nc)
    per_batch = []
    for b in range(B):
        d = dict(shared)
        d["xT0"] = _xT_layout(x[b])
        d["xtok0"] = np.ascontiguousarray(x[b])
        d["mem"] = np.ascontiguousarray(mem[b])
        per_batch.append(d)
    n_cores = 8
    grp = n_cores // B
    res = run_bass_kernel_spmd(nc, [per_batch[c // grp] for c in range(n_cores)], core_ids=list(range(n_cores)))
    return np.stack([np.asarray(res.results[b * grp]["out"]) for b in range(B)]).astype(f32)
```
